# Optimizing a Trainium2 kernel written in Bass

```python
import math
import jax, jax.numpy as jnp
from jax import lax
import numpy as np

D_MODEL = 1024
BATCH = 8
SEQ = 8192
DEPTH = 4
DEC_BATCH = 4
DEC_SEQ = 4096
PAST_LEN = 128

MIX = D_MODEL
RET_WIDTH = D_MODEL // 4
RET_HEADS = 4
RET_HD = RET_WIDTH // RET_HEADS
RET_CHUNK = 128
RWKV_WIDTH = D_MODEL // 4
RWKV_HD = 64
RWKV_HEADS = RWKV_WIDTH // RWKV_HD
DECAY_LORA = 64
ICLR_LORA = 64
GATE_LORA = 128
RWKV_IN = 3 * RWKV_WIDTH + DECAY_LORA + ICLR_LORA + GATE_LORA
S5_WIDTH = MIX - RET_WIDTH - RWKV_WIDTH
S5_GROUP = 16
S5_GROUPS = S5_WIDTH // S5_GROUP
S5_STATE = 64
P_IN = 4 * RET_WIDTH + RWKV_IN + S5_WIDTH
D_FF = -(-8 * D_MODEL // (3 * 256)) * 256
RMS_EPS = 1e-6
RET_GN_EPS = 1e-5
RWKV_GN_EPS = 64e-5

kernel_name = "hybrid_bidir_retention_rwkv7_s5_encoder"


def rmsnorm(x, g):
    xf = x.astype(jnp.float32)
    y = xf * lax.rsqrt(jnp.mean(xf * xf, axis=-1, keepdims=True) + RMS_EPS)
    return (y * g.astype(jnp.float32)).astype(x.dtype)


def head_layernorm(y, w, b, n_heads, eps):
    B, T, W = y.shape
    yf = y.astype(jnp.float32).reshape(B, T, n_heads, W // n_heads)
    mu = jnp.mean(yf, axis=-1, keepdims=True)
    var = jnp.mean(jnp.square(yf - mu), axis=-1, keepdims=True)
    yn = ((yf - mu) * lax.rsqrt(var + eps)).reshape(B, T, W)
    return (yn * w.astype(jnp.float32) + b.astype(jnp.float32)).astype(y.dtype)


def rotary(x):
    T, d = x.shape[1], x.shape[-1]
    inv = 1.0 / (10000.0 ** jnp.linspace(0.0, 1.0, d // 2, dtype=jnp.float32))
    ang = jnp.arange(T, dtype=jnp.float32)[:, None] * inv[None, :]
    cos = jnp.cos(ang)[:, None, :]
    sin = jnp.sin(ang)[:, None, :]
    xf = x.astype(jnp.float32)
    x1, x2 = xf[..., : d // 2], xf[..., d // 2:]
    return jnp.concatenate([x1 * cos - x2 * sin, x1 * sin + x2 * cos], axis=-1).astype(x.dtype)


def _retention_causal(q, k, v, log_gamma, include_diag):
    B, H, T, d = q.shape
    C = RET_CHUNK
    n = T // C
    dt = q.dtype
    qc = q.reshape(B, H, n, C, d)
    kc = k.reshape(B, H, n, C, d)
    vc = v.reshape(B, H, n, C, d)
    i = jnp.arange(C, dtype=jnp.float32)
    diff = i[:, None] - i[None, :]
    causal = (diff >= 0) if include_diag else (diff > 0)
    lg = log_gamma[:, None, None]
    mask = jnp.where(causal[None], jnp.exp(jnp.where(causal, diff, 0.0)[None] * lg), 0.0).astype(dt)
    scores = jnp.einsum('bhnid,bhnjd->bhnij', qc, kc) * mask[:, None]
    intra = jnp.einsum('bhnij,bhnjd->bhnid', scores, vc)
    k_decay = jnp.exp((C - 1 - i)[None, :] * log_gamma[:, None]).astype(dt)
    kv = jnp.einsum('bhnjd,bhnje->bhnde', kc * k_decay[:, None, :, None], vc)
    chunk_decay = jnp.exp(C * log_gamma).astype(kv.dtype)[None, :, None, None]

    def step(R, kv_n):
        return R * chunk_decay + kv_n, R

    _, r_excl = lax.scan(step, jnp.zeros((B, H, d, d), kv.dtype), jnp.moveaxis(kv, 2, 0))
    r_excl = jnp.moveaxis(r_excl, 0, 2)
    q_decay = jnp.exp((i + 1)[None, :] * log_gamma[:, None]).astype(dt)
    cross = jnp.einsum('bhnid,bhnde->bhnie', qc * q_decay[:, None, :, None], r_excl)
    return (intra + cross).reshape(B, H, T, d)


def retention_mixer(q, k, v, g, gn_w, gn_b):
    B, T, W = q.shape
    heads = lambda t: t.reshape(B, T, RET_HEADS, RET_HD)
    qh = rotary(heads(q)) * (RET_HD ** -0.5)
    kh = rotary(heads(k))
    qh, kh, vh = (jnp.transpose(t, (0, 2, 1, 3)) for t in (qh, kh, heads(v)))
    log_gamma = jnp.log(1.0 - 2.0 ** (-5.0 - jnp.arange(RET_HEADS, dtype=jnp.float32)))
    fwd = _retention_causal(qh, kh, vh, log_gamma, True)
    flip = lambda t: jnp.flip(t, axis=2)
    bwd = flip(_retention_causal(flip(qh), flip(kh), flip(vh), log_gamma, False))
    y = jnp.transpose(fwd + bwd, (0, 2, 1, 3)).reshape(B, T, W)
    y = head_layernorm(y, gn_w, gn_b, RET_HEADS, RET_GN_EPS)
    return jax.nn.silu(g) * y


def centred_shift(p):
    prev = jnp.pad(p[:, :-1], ((0, 0), (1, 0), (0, 0)))
    nxt = jnp.pad(p[:, 1:], ((0, 0), (0, 1), (0, 0)))
    return 0.5 * (prev + nxt)


def _rwkv7_scan(r, w, k, v, a, b, reverse):
    B, T, H, N = r.shape
    xs = tuple(jnp.moveaxis(t.astype(jnp.float32), 1, 0) for t in (r, w, k, v, a, b))

    def step(S, inp):
        r_t, w_t, k_t, v_t, a_t, b_t = inp
        sa = jnp.einsum('bhvk,bhk->bhv', S, a_t)
        S = S * w_t[:, :, None, :] + sa[..., None] * b_t[:, :, None, :] + v_t[..., None] * k_t[:, :, None, :]
        return S, jnp.einsum('bhvk,bhk->bhv', S, r_t)

    _, ys = lax.scan(step, jnp.zeros((B, H, N, N), jnp.float32), xs, reverse=reverse)
    return jnp.moveaxis(ys, 0, 1).astype(r.dtype)


def rwkv7_mixer(p, mu, w0, w2, a0, a2, g2, k_k, k_a, r_k, gn_w, gn_b):
    p = p + (centred_shift(p) - p) * mu
    r, k, v, wd, ad, gd = jnp.split(p, [RWKV_WIDTH, 2 * RWKV_WIDTH, 3 * RWKV_WIDTH,
                                       3 * RWKV_WIDTH + DECAY_LORA,
                                       3 * RWKV_WIDTH + DECAY_LORA + ICLR_LORA], axis=-1)
    B, T, W = r.shape
    heads = lambda t: t.reshape(B, T, RWKV_HEADS, RWKV_HD)
    g = jax.nn.sigmoid(gd) @ g2
    kk = heads(k * k_k).astype(jnp.float32)
    kk = (kk / jnp.maximum(jnp.sqrt(jnp.sum(kk * kk, axis=-1, keepdims=True)), 1e-12)).astype(k.dtype)
    rh, vh = heads(r), heads(v)
    rk = r_k.reshape(RWKV_HEADS, RWKV_HD)
    y = None
    bonus = None
    for d in range(2):
        w = jnp.exp(-math.exp(-0.5) * jax.nn.sigmoid(w0[d] + jnp.tanh(wd) @ w2[d]))
        a = jax.nn.sigmoid(a0[d] + ad @ a2[d])
        kd = heads(k * (1.0 + (a - 1.0) * k_a))
        y_d = _rwkv7_scan(rh, heads(w), kd, vh, -kk, kk * heads(a), reverse=(d == 1))
        b_d = jnp.sum(rh * kd * rk, axis=-1, keepdims=True) * vh
        y = y_d if y is None else y + y_d
        bonus = b_d if bonus is None else bonus + b_d
    y = head_layernorm(y.reshape(B, T, W), gn_w, gn_b, RWKV_HEADS, RWKV_GN_EPS)
    return (y + bonus.reshape(B, T, W)) * g


def _linear_combine(left, right):
    a_l, b_l = left
    a_r, b_r = right
    return a_r * a_l, a_r * b_l + b_r


def _s5_scan(uf, lam_re, lam_im, log_step, b_re, b_im, c_re, c_im, reverse):
    f = jnp.float32
    lam = lax.complex(jnp.minimum(lam_re.astype(f), -1e-4), lam_im.astype(f))
    step = jnp.exp(log_step.astype(f))[:, None]
    a_bar = jnp.exp(lam * step)
    bmat = lax.complex(b_re.astype(f), b_im.astype(f))
    b_bar = ((a_bar - 1.0) / lam)[..., None] * bmat
    bu = jnp.einsum('btgc,gpc->btgp', uf.astype(jnp.complex64), b_bar)
    T = uf.shape[1]
    a_seq = jnp.broadcast_to(a_bar, (1, T) + a_bar.shape)
    _, xs = lax.associative_scan(_linear_combine, (a_seq, bu), reverse=reverse, axis=1)
    cmat = lax.complex(c_re.astype(f), c_im.astype(f))
    return jnp.real(jnp.einsum('btgp,gcp->btgc', xs, cmat))


def s5_mixer(u, lam_re, lam_im, log_step, b_re, b_im, c_re, c_im, d_skip, glu_w, glu_b):
    B, T, W = u.shape
    uf = u.astype(jnp.float32)
    ug = uf.reshape(B, T, S5_GROUPS, S5_GROUP)
    y = d_skip.astype(jnp.float32) * uf
    for d in range(2):
        y = y + _s5_scan(ug, lam_re[d], lam_im[d], log_step[d], b_re[d], b_im[d],
                         c_re[d], c_im[d], reverse=(d == 1)).reshape(B, T, W)
    y = jax.nn.gelu(y)
    out = y * jax.nn.sigmoid(y @ glu_w.astype(jnp.float32) + glu_b.astype(jnp.float32))
    return out.astype(u.dtype)


def _trunk(x, norm1_g, w_in, ret_gn_w, ret_gn_b, rwkv_mu, rwkv_w0, rwkv_w2, rwkv_a0, rwkv_a2,
           rwkv_g2, rwkv_k_k, rwkv_k_a, rwkv_r_k, rwkv_gn_w, rwkv_gn_b, s5_lam_re, s5_lam_im,
           s5_log_step, s5_b_re, s5_b_im, s5_c_re, s5_c_im, s5_d, s5_glu_w, s5_glu_b, w_out,
           norm2_g, ffn_w_gate, ffn_w_up, ffn_w_down, final_g):
    r_end = 4 * RET_WIDTH
    w_end = r_end + RWKV_IN
    for l in range(DEPTH):
        h = rmsnorm(x, norm1_g[l])
        p = h @ w_in[l]
        q, k, v, g = jnp.split(p[..., :r_end], 4, axis=-1)
        y_ret = retention_mixer(q, k, v, g, ret_gn_w[l], ret_gn_b[l])
        y_rwkv = rwkv7_mixer(p[..., r_end:w_end], rwkv_mu[l], rwkv_w0[l], rwkv_w2[l], rwkv_a0[l],
                             rwkv_a2[l], rwkv_g2[l], rwkv_k_k[l], rwkv_k_a[l], rwkv_r_k[l],
                             rwkv_gn_w[l], rwkv_gn_b[l])
        y_s5 = s5_mixer(p[..., w_end:], s5_lam_re[l], s5_lam_im[l], s5_log_step[l], s5_b_re[l],
                        s5_b_im[l], s5_c_re[l], s5_c_im[l], s5_d[l], s5_glu_w[l], s5_glu_b[l])
        x = x + jnp.concatenate([y_ret, y_rwkv, y_s5], axis=-1) @ w_out[l]
        h = rmsnorm(x, norm2_g[l])
        x = x + (jax.nn.silu(h @ ffn_w_gate[l]) * (h @ ffn_w_up[l])) @ ffn_w_down[l]
    return rmsnorm(x, final_g)


def setup_inputs(seed: int = 0) -> dict:
    key = jax.random.key(seed)
    ks = iter(jax.random.split(key, 48))
    nrm = lambda shape, s=1.0: s * jax.random.normal(next(ks), shape, jnp.float32)
    uni = lambda shape, lo, hi: jax.random.uniform(next(ks), shape, jnp.float32, lo, hi)
    L = DEPTH
    G, P, C = S5_GROUPS, S5_STATE, S5_GROUP
    lam_im_base = jnp.pi * jnp.arange(P, dtype=jnp.float32)
    return {
        "x_prompt": nrm((BATCH, SEQ, D_MODEL)),
        "x_sample": nrm((DEC_BATCH, DEC_SEQ, D_MODEL)),
        "norm1_g": 1.0 + nrm((L, D_MODEL), 0.05),
        "w_in": nrm((L, D_MODEL, P_IN), D_MODEL ** -0.5),
        "ret_gn_w": 1.0 + nrm((L, RET_WIDTH), 0.05),
        "ret_gn_b": nrm((L, RET_WIDTH), 0.02),
        "rwkv_mu": uni((L, RWKV_IN), 0.0, 1.0),
        "rwkv_w0": uni((L, 2, RWKV_WIDTH), -4.0, 2.0),
        "rwkv_w2": nrm((L, 2, DECAY_LORA, RWKV_WIDTH), 0.1 * DECAY_LORA ** -0.5),
        "rwkv_a0": nrm((L, 2, RWKV_WIDTH), 0.5),
        "rwkv_a2": nrm((L, 2, ICLR_LORA, RWKV_WIDTH), 0.1 * ICLR_LORA ** -0.5),
        "rwkv_g2": nrm((L, GATE_LORA, RWKV_WIDTH), GATE_LORA ** -0.5),
        "rwkv_k_k": 0.85 + nrm((L, RWKV_WIDTH), 0.05),
        "rwkv_k_a": 1.0 + nrm((L, RWKV_WIDTH), 0.05),
        "rwkv_r_k": nrm((L, RWKV_WIDTH), 0.1),
        "rwkv_gn_w": 1.0 + nrm((L, RWKV_WIDTH), 0.05),
        "rwkv_gn_b": nrm((L, RWKV_WIDTH), 0.02),
        "s5_lam_re": -0.5 + nrm((L, 2, G, P), 0.01),
        "s5_lam_im": lam_im_base + nrm((L, 2, G, P), 0.01),
        "s5_log_step": uni((L, 2, G), math.log(1e-3), math.log(1e-1)),
        "s5_b_re": nrm((L, 2, G, P, C), (2 * C) ** -0.5),
        "s5_b_im": nrm((L, 2, G, P, C), (2 * C) ** -0.5),
        "s5_c_re": nrm((L, 2, G, C, P), (2 * P) ** -0.5),
        "s5_c_im": nrm((L, 2, G, C, P), (2 * P) ** -0.5),
        "s5_d": nrm((L, S5_WIDTH), 0.5),
        "s5_glu_w": nrm((L, S5_WIDTH, S5_WIDTH), S5_WIDTH ** -0.5),
        "s5_glu_b": nrm((L, S5_WIDTH), 0.02),
        "w_out": nrm((L, MIX, D_MODEL), 0.5 * MIX ** -0.5),
        "norm2_g": 1.0 + nrm((L, D_MODEL), 0.05),
        "ffn_w_gate": nrm((L, D_MODEL, D_FF), D_MODEL ** -0.5),
        "ffn_w_up": nrm((L, D_MODEL, D_FF), D_MODEL ** -0.5),
        "ffn_w_down": nrm((L, D_FF, D_MODEL), 0.5 * D_FF ** -0.5),
        "final_g": 1.0 + nrm((D_MODEL,), 0.05),
    }


def reference(x_prompt, x_sample, norm1_g, w_in, ret_gn_w, ret_gn_b, rwkv_mu, rwkv_w0, rwkv_w2,
              rwkv_a0, rwkv_a2, rwkv_g2, rwkv_k_k, rwkv_k_a, rwkv_r_k, rwkv_gn_w, rwkv_gn_b,
              s5_lam_re, s5_lam_im, s5_log_step, s5_b_re, s5_b_im, s5_c_re, s5_c_im, s5_d,
              s5_glu_w, s5_glu_b, w_out, norm2_g, ffn_w_gate, ffn_w_up, ffn_w_down, final_g):
    weights = (norm1_g, w_in, ret_gn_w, ret_gn_b, rwkv_mu, rwkv_w0, rwkv_w2, rwkv_a0, rwkv_a2,
               rwkv_g2, rwkv_k_k, rwkv_k_a, rwkv_r_k, rwkv_gn_w, rwkv_gn_b, s5_lam_re, s5_lam_im,
               s5_log_step, s5_b_re, s5_b_im, s5_c_re, s5_c_im, s5_d, s5_glu_w, s5_glu_b, w_out,
               norm2_g, ffn_w_gate, ffn_w_up, ffn_w_down, final_g)
    y_prompt = _trunk(x_prompt, *weights)
    y_sample = _trunk(x_sample, *weights)
    return (y_prompt, y_sample)
```

```python
import math
import numpy as np
import concourse.bass as bass
import concourse.mybir as mybir
from concourse.bass_utils import run_bass_kernel_spmd
from contextlib import ExitStack

F32 = mybir.dt.float32
BF16 = mybir.dt.bfloat16
ALU = mybir.AluOpType
AF = mybir.ActivationFunctionType
AX = mybir.AxisListType

D = 1024
NH = 4
HD = 64
PIN = 3072
DFF = 2816
YW = 1536
CDEC = -math.exp(-0.5)
ENGS = ("pe", "act", "dve", "pool", "sp")


class Op:
    __slots__ = ("eng", "fn", "deps", "signal", "idx", "semkey", "is_dma", "emitted", "val")

    def __init__(self, eng, fn, semkey, is_dma):
        self.eng = eng
        self.fn = fn
        self.deps = []
        self.signal = is_dma
        self.idx = -1
        self.semkey = semkey
        self.is_dma = is_dma
        self.emitted = False
        self.val = None


class Sched:
    def __init__(self, nc, es):
        self.nc = nc
        self.es = es
        self.ops = {e: [] for e in ENGS}
        self.semidx = {}
        self.sems = {}
        self.res = {}
        self.seen = {e: {} for e in ENGS}
        self.sigcount = {}
        self.last_emitted = {}
        self.last_op = {}
        self.nops = 0

    def _dep(self, op, d):
        if d is None or d is op:
            return
        if d.emitted and not d.signal:
            d = self.last_emitted[d.semkey]
        seen = self.seen[op.eng]
        if seen.get(d.semkey, -1) >= d.idx:
            return
        seen[d.semkey] = d.idx
        d.signal = True
        op.deps.append(d)

    def op(self, eng, fn, reads=(), writes=(), dma=None, extra=()):
        is_dma = dma is not None
        semkey = ("dma", dma) if is_dma else ("eng", eng)
        o = Op(eng, fn, semkey, is_dma)
        o.idx = self.semidx.get(semkey, 0)
        self.semidx[semkey] = o.idx + 1
        for r in reads:
            st = self.res.get(r)
            if st is not None:
                self._dep(o, st[0])
        for w in writes:
            st = self.res.get(w)
            if st is not None:
                self._dep(o, st[0])
                for rd in st[1]:
                    self._dep(o, rd)
        for d in extra:
            self._dep(o, d)
        for r in reads:
            st = self.res.setdefault(r, [None, []])
            st[1].append(o)
        for w in writes:
            self.res[w] = [o, []]
        self.ops[eng].append(o)
        self.last_op[semkey] = o
        self.nops += 1
        return o

    def barrier(self):
        lasts = list(self.last_op.values())
        for e in ENGS:
            self.op(e, lambda en: en.nop(), extra=lasts)

    def flush(self, final=()):
        nc = self.nc
        for key in self.semidx:
            if key not in self.sems:
                nm = ("s_" + "_".join(map(str, key)))[:48]
                self.sems[key] = self.es.enter_context(nc.semaphore(nm))
        for e in ENGS:
            if self.ops[e]:
                for o in reversed(self.ops[e]):
                    if not o.is_dma:
                        o.signal = True
                        break
        for e in ENGS:
            for o in self.ops[e]:
                if o.is_dma:
                    o.val = 16 * (o.idx + 1)
                elif o.signal:
                    c = self.sigcount.get(o.semkey, 0) + 1
                    self.sigcount[o.semkey] = c
                    o.val = c
        sems = self.sems
        pending = self.ops

        def run(en, key):
            for o in pending[key]:
                for d in o.deps:
                    en.wait_ge(sems[d.semkey], d.val)
                ins = o.fn(en)
                if o.signal:
                    ins.then_inc(sems[o.semkey], 16 if o.is_dma else 1)
            if key == "sp":
                for d in final:
                    en.wait_ge(sems[d.semkey], d.val)

        with nc.Block() as block:
            @block.tensor
            def _(en):
                run(en, "pe")

            @block.scalar
            def _(en):
                run(en, "act")

            @block.vector
            def _(en):
                run(en, "dve")

            @block.gpsimd
            def _(en):
                run(en, "pool")

            @block.sync
            def _(en):
                run(en, "sp")
        for e in ENGS:
            for o in self.ops[e]:
                o.emitted = True
                if o.signal and not o.is_dma:
                    self.last_emitted[o.semkey] = o
            self.ops[e] = []


class Tl:
    def __init__(self, t, key):
        self.t = t
        self.k = key

    def __getitem__(self, idx):
        return self.t[idx]


def bc(ap, shape):
    return ap.to_broadcast(list(shape))


class Builder:
    def __init__(self, Tp, Ts, depth, flags=None):
        self.Tp, self.Ts, self.L = Tp, Ts, depth
        self.seqs = [(0, Tp), (Tp, Ts)]
        self.TT = Tp + Ts
        self.flags = flags or {}
        self.nc = bass.Bass("TRN2", target_bir_lowering=False)

    def sb(self, es, name, shape, dt=F32):
        self._uid = getattr(self, "_uid", 0) + 1
        name = f"{name}_{self._uid}"
        return Tl(es.enter_context(self.nc.sbuf_tensor(name, list(shape), dt)), name)

    def ps(self, es, name, shape, dt=F32):
        self._uid = getattr(self, "_uid", 0) + 1
        name = f"{name}_{self._uid}"
        return Tl(es.enter_context(self.nc.psum_tensor(name, list(shape), dt)), name)

    def O(self, eng, fn, reads=(), writes=(), dma=None):
        return self.S.op(eng, fn, [r.k if isinstance(r, Tl) else r for r in reads],
                         [w.k if isinstance(w, Tl) else w for w in writes], dma=dma)

    def load(self, dst, dst_ap, src_ap, key, reads=(), eng="sp"):
        return self.O(eng, lambda e, o=dst_ap, i=src_ap: e.dma_start(out=o, in_=i, allow_slow_non_contiguous=True), reads=reads, writes=[dst], dma=key)

    def store(self, src, dst_ap, src_ap, key, writes=(), eng="sp"):
        return self.O(eng, lambda e, o=dst_ap, i=src_ap: e.dma_start(out=o, in_=i), reads=[src], writes=writes, dma=key)

    def mm(self, out_t, out_ap, l_t, l_ap, r_t, r_ap, start=True, stop=True):
        return self.O("pe", lambda e, o=out_ap, a=l_ap, b=r_ap, s0=start, s1=stop: e.matmul(o, a, b, start=s0, stop=s1),
                      reads=[l_t, r_t] + ([] if start else [out_t]), writes=[out_t])

    def tr(self, out_t, out_ap, in_t, in_ap):
        idn = self.identf if in_ap.dtype == F32 else self.identb
        return self.O("pe", lambda e, o=out_ap, a=in_ap, i=idn: e.transpose(o, a, i[:]), reads=[in_t, idn], writes=[out_t])

    def act(self, out_t, out_ap, in_t, in_ap, func, scale=1.0, bias=None, accum=None, extra_r=(), extra_w=()):
        def fn(e, o=out_ap, i=in_ap, f=func, s=scale, b=bias, a=accum):
            kw = {}
            if b is not None:
                kw["bias"] = b
            if a is not None:
                kw["accum_out"] = a
            return e.activation(out=o, in_=i, func=f, scale=s, **kw)
        return self.O("act", fn, reads=[in_t] + list(extra_r), writes=[out_t] + list(extra_w))

    def tt(self, eng, out_t, out_ap, a_t, a_ap, b_t, b_ap, op):
        return self.O(eng, lambda e, o=out_ap, a=a_ap, b=b_ap, p=op: e.tensor_tensor(out=o, in0=a, in1=b, op=p),
                      reads=[a_t, b_t], writes=[out_t])

    def ts(self, eng, out_t, out_ap, a_t, a_ap, s1, s2, op0, op1=None, extra_r=()):
        def fn(e, o=out_ap, a=a_ap, x=s1, y=s2, p0=op0, p1=op1):
            if p1 is None:
                return e.tensor_single_scalar(out=o, in_=a, scalar=x, op=p0)
            return e.tensor_scalar(out=o, in0=a, scalar1=x, scalar2=y, op0=p0, op1=p1)
        return self.O(eng, fn, reads=[a_t] + list(extra_r), writes=[out_t])

    def stt(self, eng, out_t, out_ap, a_t, a_ap, scalar, b_t, b_ap, op0, op1, extra_r=()):
        eng = "dve"
        return self.O(eng, lambda e, o=out_ap, a=a_ap, s=scalar, b=b_ap, p0=op0, p1=op1:
                      e.scalar_tensor_tensor(out=o, in0=a, scalar=s, in1=b, op0=p0, op1=p1),
                      reads=[a_t, b_t] + list(extra_r), writes=[out_t])

    def cp(self, eng, out_t, out_ap, in_t, in_ap):
        if eng == "act":
            return self.act(out_t, out_ap, in_t, in_ap, AF.Copy)
        return self.O(eng, lambda e, o=out_ap, i=in_ap: e.tensor_copy(out=o, in_=i), reads=[in_t], writes=[out_t])

    def memset(self, eng, t, ap, v):
        return self.O(eng, lambda e, o=ap, x=v: e.memset(o, x), writes=[t])

    def load_w_bf16(self, es, name, dram_ap, K, C, stg, dst=None, dst_koff=0):
        if dst is None:
            dst = self.sb(es, name, [128, K, C], BF16)
        n = 0
        for k in range(K):
            for c0 in range(0, C, 1024):
                c1 = min(C, c0 + 1024)
                slot = self._stg_i % len(stg)
                st = stg[slot]
                self._stg_i += 1
                self.load(st, st[:, 0:c1 - c0], dram_ap[k * 128:(k + 1) * 128, c0:c1], f"stg{slot}")
                eng = ("pool", "dve", "act")[n % 3]
                n += 1
                self.cp(eng, dst, dst[:, dst_koff + k, c0:c1], st, st[:, 0:c1 - c0])
        return dst

    def bcast_load(self, es, name, dram_vec_ap, C, key="bcl"):
        t = self.sb(es, name, [128, C])
        self.load(t, t[:], dram_vec_ap.partition_broadcast(128), "setup")
        return t

    def rows(self, seq, i):
        off, T = self.seqs[seq]
        return off + i * 128

    def prow(self, seq, i):
        off, T = self.seqs[seq]
        return off + 2 * seq + 1 + i * 128

    def tiles(self, d=0):
        out = []
        for s, (off, T) in enumerate(self.seqs):
            n = T // 128
            rng = range(n) if d == 0 else range(n - 1, -1, -1)
            for j, i in enumerate(rng):
                out.append((s, i, j == 0))
        return out

    def build(self):
        nc = self.nc
        Tp, Ts, L, TT = self.Tp, self.Ts, self.L, self.TT
        di = {}

        def inp(name, shape):
            di[name] = nc.dram_tensor(name, list(shape), F32, kind="ExternalInput").ap()
            return di[name]
        inp("x_prompt", [Tp, D]); inp("x_sample", [Ts, D])
        inp("norm1_g", [L, D]); inp("w_in", [L, D, PIN])
        inp("ret_gn_w", [L, 256]); inp("ret_gn_b", [L, 256])
        inp("rwkv_mu", [L, 1024]); inp("rwkv_w0", [L, 2, 256]); inp("rwkv_w2", [L, 2, 64, 256])
        inp("rwkv_a0", [L, 2, 256]); inp("rwkv_a2", [L, 2, 64, 256]); inp("rwkv_g2", [L, 128, 256])
        inp("rwkv_k_k", [L, 256]); inp("rwkv_k_a", [L, 256]); inp("rwkv_r_k", [L, 256])
        inp("rwkv_gn_w", [L, 256]); inp("rwkv_gn_b", [L, 256])
        inp("s5_lam_re", [L, 2, 32, 64]); inp("s5_lam_im", [L, 2, 32, 64]); inp("s5_log_step", [L, 2, 32])
        inp("s5_b_re", [L, 2, 32, 64, 16]); inp("s5_b_im", [L, 2, 32, 64, 16])
        inp("s5_c_re", [L, 2, 32, 16, 64]); inp("s5_c_im", [L, 2, 32, 16, 64])
        inp("s5_d", [L, 512]); inp("s5_glu_w", [L, 512, 512]); inp("s5_glu_b", [L, 512])
        inp("w_out", [L, D, D]); inp("norm2_g", [L, D])
        inp("ffn_w_gate", [L, D, DFF]); inp("ffn_w_up", [L, D, DFF]); inp("ffn_w_down", [L, DFF, D])
        inp("final_g", [D])
        inp("c_ident", [128, 128]); inp("c_rot", [max(Tp, Ts), 256])
        inp("c_retmask", [128, 512]); inp("c_retqk", [2, 128, 8]); inp("c_retgc", [128, 256])
        inp("c_cs", [2, 128, 128]); inp("c_ones", [128, 128]); inp("c_msi", [2, 128, 256]); inp("c_mst", [2, 128, 128])
        inp("c_j1", [2, 128, 128]); inp("c_z0", [2, 128, 128]); inp("c_swap", [128, 128]); inp("c_sel", [128, 128])
        self.di = di
        yp = nc.dram_tensor("y_prompt", [Tp, D], F32, kind="ExternalOutput").ap()
        ys = nc.dram_tensor("y_sample", [Ts, D], F32, kind="ExternalOutput").ap()
        self.yout = [yp, ys]
        self.XS = nc.dram_tensor("XS", [TT, D], F32, kind="Internal").ap()
        self.PS = nc.dram_tensor("PSC", [TT + 4, PIN], F32, kind="Internal").ap()
        self.YD = nc.dram_tensor("YD", [2, TT, YW], F32, kind="Internal").ap()

        with ExitStack() as ges:
            self.S = Sched(nc, ges)
            self._stg_i = 0
            self.identf = self.sb(ges, "identf", [128, 128])
            self.identb = self.sb(ges, "identb", [128, 128], BF16)
            self.load(self.identf, self.identf[:], di["c_ident"][:, :], "setup")
            self.cp("dve", self.identb, self.identb[:], self.identf, self.identf[:])
            zero = self.sb(ges, "zero", [128, PIN])
            self.memset("pool", zero, zero[:], 0.0)
            self.O("sp", lambda e: e.dma_start(out=self.XS[0:Tp, :], in_=di["x_prompt"][:, :]), writes=["XSall"], dma="setup")
            self.O("sp", lambda e: e.dma_start(out=self.XS[Tp:TT, :], in_=di["x_sample"][:, :]), writes=["XSall"], dma="setup")
            for s, (off, T) in enumerate(self.seqs):
                for r in (off + 2 * s, off + 2 * s + T + 1):
                    self.store(zero, self.PS[r:r + 1, :], zero[0:1, :], "setup", writes=["PSall"])
            self.S.barrier()
            self.S.flush()
            final = []
            for l in range(L):
                self.phase_P(l)
                for d in range(2):
                    if not self.flags.get("no_ret"):
                        self.phase_ret(l, d)
                    if not self.flags.get("no_rwkv"):
                        self.phase_rwkv(l, d)
                    if not self.flags.get("no_s5"):
                        self.phase_s5(l, d)
                self.phase_O1(l)
                final = self.phase_O2(l, last=(l == L - 1))
            self.S.barrier()
            self.S.flush(final=final)
        return nc

    def end_phase(self):
        self.S.barrier()
        self.S.flush()

    def rmsnorm(self, xt, G, h_out, junk, ss, eng2="dve"):
        self.act(junk, junk[:], xt, xt[:], AF.Square, accum=ss[:, 0:1], extra_w=[ss])
        self.ts("dve", ss, ss[:, 1:2], ss, ss[:, 0:1], 1.0 / D, 1e-6, ALU.mult, ALU.add)
        self.act(ss, ss[:, 2:3], ss, ss[:, 1:2], AF.Sqrt)
        self.O("dve", lambda e, o=ss[:, 3:4], i=ss[:, 2:3]: e.reciprocal(out=o, in_=i), reads=[ss], writes=[ss])
        self.stt(eng2, h_out, h_out[:], xt, xt[:], ss[:, 3:4], G, G[:], ALU.mult, ALU.mult, extra_r=[ss])

    def transpose_to(self, src, ncol_tiles, psT, dstT, evac_engs=("act", "dve")):
        n = 0
        for g0 in range(0, ncol_tiles, 4):
            g1 = min(ncol_tiles, g0 + 4)
            pt = psT[(g0 // 4) % len(psT)]
            for c in range(g0, g1):
                self.tr(pt, pt[:, c - g0, :], src, src[:, c * 128:(c + 1) * 128])
            self.cp(evac_engs[n % len(evac_engs)], dstT, dstT[:, g0:g1, :], pt, pt[:, 0:g1 - g0, :])
            n += 1

    def phase_P(self, l):
        di = self.di
        with ExitStack() as es:
            stg = [self.sb(es, f"stg{i}", [128, 1024]) for i in range(3)]
            Wb = self.load_w_bf16(es, "Wb", di["w_in"][l], 8, PIN, stg)
            G1 = self.bcast_load(es, "G1", di["norm1_g"][l], D)
            xb = [self.sb(es, f"xb{i}", [128, D]) for i in range(2)]
            junk = self.sb(es, "junk", [128, D], BF16)
            ss = [self.sb(es, f"ss{i}", [128, 4]) for i in range(2)]
            h = [self.sb(es, f"h{i}", [128, D]) for i in range(2)]
            hT = [self.sb(es, f"hT{i}", [128, 8, 128], BF16) for i in range(2)]
            pt = [self.sb(es, f"pt{i}", [128, PIN]) for i in range(2)]
            psT = [self.ps(es, f"psT{i}", [128, 4, 128]) for i in range(2)]
            psP = [self.ps(es, f"psP{i}", [128, 512]) for i in range(3)]
            n = 0
            self.S.barrier()
            for it, (s, i, first) in enumerate(self.tiles()):
                b = it % 2
                r0 = self.rows(s, i)
                self.load(xb[b], xb[b][:], self.XS[r0:r0 + 128, :], f"l0{b}", reads=["XSall", ("XS", r0)])
                self.rmsnorm(xb[b], G1, h[b], junk, ss[b])
                self.transpose_to(h[b], 8, psT, hT[b])
                for cb in range(PIN // 512):
                    pp = psP[n % 3]
                    for k in range(8):
                        self.mm(pp, pp[:], hT[b], hT[b][:, k, :], Wb, Wb[:, k, cb * 512:(cb + 1) * 512], start=(k == 0), stop=(k == 7))
                    self.cp(("act", "dve")[n % 2], pt[b], pt[b][:, cb * 512:(cb + 1) * 512], pp, pp[:])
                    n += 1
                pr = self.prow(s, i)
                self.store(pt[b], self.PS[pr:pr + 128, :], pt[b][:], f"s0{b}", writes=[("PS", s, i)])
            self.end_phase()

    def phase_ret(self, l, d):
        di = self.di
        with ExitStack() as es:
            sb = lambda n, sh, dt=F32: self.sb(es, n, sh, dt)
            mask = sb("rmask", [128, 512]); self.load(mask, mask[:], di["c_retmask"][:, :], "setup")
            qk = sb("rqk", [128, 8]); self.load(qk, qk[:], di["c_retqk"][d], "setup")
            gc = sb("rgc", [128, 256]); self.load(gc, gc[:], di["c_retgc"][:, :], "setup")
            R = sb("R", [128, 256]); Rb = sb("Rb", [128, 256], BF16)
            pr = [sb(f"pr{i}", [128, 1280]) for i in range(2)]
            rot = [sb(f"rot{i}", [128, 256]) for i in range(2)]
            qh = sb("qh", [128, 512]); t1 = sb("rt1", [128, 512])
            qkT = sb("qkT", [128, 4, 128], BF16)
            sT = sb("sT", [128, 4, 128], BF16)
            vb = sb("vb", [128, 256], BF16); kdb = sb("kdb", [128, 256], BF16)
            tmp = sb("rtmp", [128, 256]); yo = [sb(f"ryo{i}", [128, 256]) for i in range(2)]
            psT = self.ps(es, "rpsT", [128, 4, 128]); psS = self.ps(es, "rpsS", [128, 4, 128])
            psO = self.ps(es, "rpsO", [128, 256]); psC = self.ps(es, "rpsC", [128, 256]); psR = self.ps(es, "rpsR", [128, 256])
            self.S.barrier()
            for it, (s, i, first) in enumerate(self.tiles(d)):
                b = it % 2
                p0 = self.prow(s, i)
                if first:
                    self.memset("pool", R, R[:], 0.0)
                    self.memset("pool", Rb, Rb[:], 0.0)
                self.load(pr[b], pr[b][:], self.PS[p0:p0 + 128, 0:1280], f"l0{b}", reads=[("PS", s, i)])
                self.load(rot[b], rot[b][:], di["c_rot"][i * 128:(i + 1) * 128, :], f"l1{b}")
                P, RT = pr[b], rot[b]
                for j, (c0, tb) in enumerate(((0, 0), (512, 128))):
                    o = j * 256
                    v4 = lambda ap: ap.rearrange("p (a b) -> p a b", a=4)
                    cosb = bc(RT[:, tb:tb + 64].unsqueeze(1), [128, 4, 64])
                    sinb = bc(RT[:, tb + 64:tb + 128].unsqueeze(1), [128, 4, 64])
                    self.tt("dve", qh, v4(qh[:, o:o + 256]), P, v4(P[:, c0:c0 + 256]), RT, cosb, ALU.mult)
                    self.tt("pool", t1, v4(t1[:, o:o + 256]), P, v4(P[:, c0 + 256:c0 + 512]), RT, sinb, ALU.mult)
                    self.tt("dve", qh, qh[:, o:o + 256], qh, qh[:, o:o + 256], t1, t1[:, o:o + 256], ALU.add)
                for c in range(4):
                    self.tr(psT, psT[:, c, :], qh, qh[:, c * 128:(c + 1) * 128])
                self.cp("act", qkT, qkT[:], psT, psT[:])
                self.cp("act", vb, vb[:], P, P[:, 1024:1280])
                kv = qh[:, 256:512].rearrange("p (a b) -> p a b", a=4)
                self.tt("dve", kdb, kdb[:].rearrange("p (a b) -> p a b", a=4), qh, kv, qk, bc(qk[:, 4:8].unsqueeze(2), [128, 4, 64]), ALU.mult)
                hp = lambda hh: slice((hh % 2) * 64, (hh % 2) * 64 + 64)
                if d == 0:
                    for hh in range(4):
                        self.mm(psS, psS[:, hh, :], qkT, qkT[hp(hh), 2 + hh // 2, :], qkT, qkT[hp(hh), hh // 2, :])
                    self.tt("dve", sT, sT[:], psS, psS[:], mask, mask[:].rearrange("p (a b) -> p a b", a=4), ALU.mult)
                    for hh in range(4):
                        self.mm(psO, psO[:, hh * 64:(hh + 1) * 64], sT, sT[:, hh, :], vb, vb[:, hh * 64:(hh + 1) * 64])
                    self.cp("act", tmp, tmp[:], psO, psO[:])
                for hh in range(4):
                    self.mm(psC, psC[:, hh * 64:(hh + 1) * 64], qkT, qkT[hp(hh), hh // 2, :], Rb,
                            Rb[hp(hh), (hh // 2) * 128 + (hh % 2) * 64:(hh // 2) * 128 + (hh % 2) * 64 + 64])
                y = yo[b]
                self.tt("dve", y, y[:].rearrange("p (a b) -> p a b", a=4), psC, psC[:].rearrange("p (a b) -> p a b", a=4),
                        qk, bc(qk[:, 0:4].unsqueeze(2), [128, 4, 64]), ALU.mult)
                if d == 0:
                    self.tt("pool", y, y[:], y, y[:], tmp, tmp[:], ALU.add)
                r0 = self.rows(s, i)
                self.store(y, self.YD[d, r0:r0 + 128, 0:256], y[:], f"s0{b}", writes=[("YDr", d, s, i)])
                for pr_ in range(2):
                    self.mm(psR, psR[:, pr_ * 128:(pr_ + 1) * 128], kdb, kdb[:, pr_ * 128:(pr_ + 1) * 128], vb, vb[:, pr_ * 128:(pr_ + 1) * 128])
                self.tt("dve", R, R[:], R, R[:], gc, gc[:], ALU.mult)
                self.tt("dve", R, R[:], R, R[:], psR, psR[:], ALU.add)
                self.cp("act", Rb, Rb[:], R, R[:])
            self.end_phase()

    def phase_rwkv(self, l, d):
        di = self.di
        with ExitStack() as es:
            sb = lambda n, sh, dt=F32: self.sb(es, n, sh, dt)
            v4 = lambda ap: ap.rearrange("p (a b) -> p a b", a=4)
            CS = sb("wCS", [128, 128]); self.load(CS, CS[:], di["c_cs"][d], "setup")
            ON = sb("wON", [128, 128]); self.load(ON, ON[:], di["c_ones"][:, :], "setup")
            MSI = sb("wMSI", [128, 256]); self.load(MSI, MSI[:], di["c_msi"][d], "setup")
            MST = sb("wMST", [128, 128]); self.load(MST, MST[:], di["c_mst"][d], "setup")
            MU = self.bcast_load(es, "wMU", di["rwkv_mu"][l], 1024)
            W0 = self.bcast_load(es, "wW0", di["rwkv_w0"][l, d], 256)
            A0 = self.bcast_load(es, "wA0", di["rwkv_a0"][l, d], 256)
            KK = self.bcast_load(es, "wKK", di["rwkv_k_k"][l], 256)
            KA = self.bcast_load(es, "wKA", di["rwkv_k_a"][l], 256)
            RK = self.bcast_load(es, "wRK", di["rwkv_r_k"][l], 256)
            stg = sb("wstg", [128, 256])
            LW = sb("wLW", [128, 256], BF16)
            self.load(stg, stg[0:64, :], di["rwkv_w2"][l, d], "setup")
            self.load(stg, stg[64:128, :], di["rwkv_a2"][l, d], "setup")
            stg2 = sb("wstg2", [128, 256])
            G2 = sb("wG2", [128, 256], BF16)
            self.load(stg2, stg2[:], di["rwkv_g2"][l], "setup")
            self.S.barrier()
            self.cp("dve", LW, LW[:], stg, stg[:])
            self.cp("dve", G2, G2[:], stg2, stg2[:])
            pm = [sb(f"wpm{i}", [128, 1024]) for i in range(2)]
            pc = [sb(f"wpc{i}", [128, 1024]) for i in range(2)]
            pn = [sb(f"wpn{i}", [128, 1024]) for i in range(2)]
            pp = sb("wpp", [128, 1024]); sh = sb("wsh", [128, 1024])
            ldT = sb("wldT", [128, 2, 128], BF16)
            targ = sb("wtarg", [128, 512]); sg = sb("wsg", [128, 256]); asg = sb("wasg", [128, 256])
            kk = sb("wkk", [128, 256]); kkr = sb("wkkr", [128, 256]); junk = sb("wjunk", [128, 64])
            ssq = sb("wssq", [128, 16]); kd = sb("wkd", [128, 256]); bv = sb("wbv", [128, 256])
            t1 = sb("wt1", [128, 256]); t2 = sb("wt2", [128, 256])
            yo = [sb(f"wyo{i}", [128, 768]) for i in range(2)]
            lws = sb("wlws", [128, 256]); lx = sb("wlx", [128, 256]); lt = sb("wlt", [128, 256])
            Wt = sb("wWt", [128, 256]); Wn = sb("wWn", [128, 256]); Wx = sb("wWx", [128, 256]); Wh = sb("wWh", [128, 256])
            X4 = sb("wX4", [128, 1024])
            Atb = sb("wAtb", [128, 256], BF16); Bhb = sb("wBhb", [128, 256], BF16); Khb = sb("wKhb", [128, 256], BF16)
            vb = sb("wvb", [128, 256], BF16)
            XT = sb("wXT", [128, 2, 4, 128], BF16)
            WCk = sb("wWCk", [128, 2])
            Nb = sb("wNb", [128, 4, 128], BF16); NTb = sb("wNTb", [128, 4, 128], BF16)
            Pb = [sb(f"wPb{i}", [128, 4, 128], BF16) for i in range(2)]
            PTb = [sb(f"wPTb{i}", [128, 4, 128], BF16) for i in range(2)]
            Mf = sb("wMf", [128, 4, 128]); Mb = sb("wMb", [128, 4, 128], BF16)
            Arb = sb("wArb", [128, 4, 128], BF16); Aak = sb("wAak", [128, 4, 128], BF16); Ark = sb("wArk", [128, 4, 128], BF16)
            Xb = sb("wXb", [128, 256], BF16); Ut = sb("wUt", [128, 256]); Ub = sb("wUb", [128, 256], BF16)
            WtT = sb("wWtT", [128, 4, 128], BF16)
            H = sb("wH", [128, 256]); Hb = sb("wHb", [128, 256], BF16)
            identq = sb("widq", [128, 4, 128])
            for hh in range(4):
                self.cp("pool", identq, identq[:, hh, :], self.identf, self.identf[:])
            psA = self.ps(es, "wpsA", [128, 512]); psB = self.ps(es, "wpsB", [128, 512])
            psC = self.ps(es, "wpsC", [128, 512]); psD = self.ps(es, "wpsD", [128, 512])
            psE = self.ps(es, "wpsE", [128, 512]); psF = self.ps(es, "wpsF", [128, 512])
            psG = self.ps(es, "wpsG", [128, 512]); psH = self.ps(es, "wpsH", [128, 512])
            hp = lambda hh: slice((hh % 2) * 64, (hh % 2) * 64 + 64)
            for it, (s, i, first) in enumerate(self.tiles(d)):
                b = it % 2
                p0 = self.prow(s, i)
                if first:
                    self.memset("pool", H, H[:], 0.0)
                    self.memset("pool", Hb, Hb[:], 0.0)
                self.load(pm[b], pm[b][:], self.PS[p0 - 1:p0 + 127, 1536:2560], f"l0{b}", reads=[("PS", s, i), ("PS", s, i - 1), "PSall"])
                self.load(pc[b], pc[b][:], self.PS[p0:p0 + 128, 1536:2560], f"l1{b}", reads=[("PS", s, i)])
                self.load(pn[b], pn[b][:], self.PS[p0 + 1:p0 + 129, 1536:2560], f"l2{b}", reads=[("PS", s, i), ("PS", s, i + 1), "PSall"])
                self.tt("pool", sh, sh[:], pm[b], pm[b][:], pn[b], pn[b][:], ALU.add)
                self.stt("dve", sh, sh[:], sh, sh[:], 0.5, pc[b], pc[b][:], ALU.mult, ALU.subtract)
                self.tt("pool", sh, sh[:], sh, sh[:], MU, MU[:], ALU.mult)
                self.tt("dve", pp, pp[:], pc[b], pc[b][:], sh, sh[:], ALU.add)
                r_, k_, v_ = pp[:, 0:256], pp[:, 256:512], pp[:, 512:768]
                Y = yo[b]
                self.tr(psA, psA[:, 0:128], pp, pp[:, 768:896])
                self.tr(psA, psA[:, 128:256], pp, pp[:, 896:1024])
                self.act(ldT, ldT[0:64, 0, :], psA, psA[0:64, 0:128], AF.Tanh)
                self.cp("dve", ldT, ldT[64:128, 0, :], psA, psA[64:128, 0:128])
                self.mm(psB, psB[:, 0:256], ldT, ldT[0:64, 0, :], LW, LW[0:64, :])
                self.mm(psB, psB[:, 256:512], ldT, ldT[64:128, 0, :], LW, LW[64:128, :])
                self.tt("dve", targ, targ[:, 0:256], psB, psB[:, 0:256], W0, W0[:], ALU.add)
                self.tt("dve", targ, targ[:, 256:512], psB, psB[:, 256:512], A0, A0[:], ALU.add)
                self.act(sg, sg[:], targ, targ[:, 0:256], AF.Sigmoid)
                self.act(asg, asg[:], targ, targ[:, 256:512], AF.Sigmoid)
                if d == 0:
                    self.act(ldT, ldT[:, 1, :], psA, psA[:, 128:256], AF.Sigmoid)
                    self.mm(psC, psC[:, 0:256], ldT, ldT[:, 1, :], G2, G2[:])
                    self.cp("act", Y, Y[:, 512:768], psC, psC[:, 0:256])
                self.tt("pool", kkr, kkr[:], pp, k_, KK, KK[:], ALU.mult)
                for hh in range(4):
                    self.act(junk, junk[:], kkr, kkr[:, hh * 64:(hh + 1) * 64], AF.Square, accum=ssq[:, hh:hh + 1], extra_w=[ssq])
                self.act(ssq, ssq[:, 4:8], ssq, ssq[:, 0:4], AF.Sqrt)
                self.ts("dve", ssq, ssq[:, 8:12], ssq, ssq[:, 4:8], 1e-12, None, ALU.max)
                self.O("dve", lambda e, o=ssq[:, 12:16], i_=ssq[:, 8:12]: e.reciprocal(out=o, in_=i_), reads=[ssq], writes=[ssq])
                self.tt("dve", kk, v4(kk[:]), kkr, v4(kkr[:]), ssq, bc(ssq[:, 12:16].unsqueeze(2), [128, 4, 64]), ALU.mult)
                self.stt("pool", t1, t1[:], asg, asg[:], -1.0, KA, KA[:], ALU.add, ALU.mult)
                self.stt("dve", kd, kd[:], t1, t1[:], 1.0, pp, k_, ALU.add, ALU.mult)
                self.tt("pool", bv, bv[:], kk, kk[:], asg, asg[:], ALU.mult)
                self.tt("dve", t1, t1[:], pp, r_, kd, kd[:], ALU.mult)
                self.tt("pool", t1, t1[:], t1, t1[:], RK, RK[:], ALU.mult)
                self.O("dve", lambda e, o=ssq[:, 0:4], i_=v4(t1[:]): e.tensor_reduce(out=o, in_=i_, axis=AX.X, op=ALU.add), reads=[t1], writes=[ssq])
                self.tt("dve", Y, v4(Y[:, 256:512]), pp, v4(v_), ssq, bc(ssq[:, 0:4].unsqueeze(2), [128, 4, 64]), ALU.mult)
                self.cp("act", vb, vb[:], pp, v_)
                self.mm(psC, psC[:, 256:512], CS, CS[:], sg, sg[:])
                self.mm(psD, psD[:, 0:256], ON, ON[:], sg, sg[:])
                for pr_ in range(2):
                    self.mm(psD, psD[:, 256 + pr_:257 + pr_], sg, sg[:, pr_ * 128:(pr_ + 1) * 128], ON, ON[:, 0:1])
                self.cp("act", lws, lws[:], psC, psC[:, 256:512])
                self.tt("dve", lx, lx[:], lws, lws[:], sg, sg[:], ALU.subtract)
                self.tt("dve", lt, lt[:], psD, psD[:, 0:256], lws, lws[:], ALU.subtract)
                self.act(Wt, Wt[:], lws, lws[:], AF.Exp, scale=CDEC)
                self.act(Wn, Wn[:], lws, lws[:], AF.Exp, scale=-CDEC)
                self.act(Wx, Wx[:], lx, lx[:], AF.Exp, scale=CDEC)
                self.act(Wh, Wh[:], lt, lt[:], AF.Exp, scale=CDEC)
                self.act(WCk, WCk[:], psD, psD[:, 256:258], AF.Exp, scale=CDEC)
                self.stt("dve", X4, X4[:, 0:256], kk, kk[:], -1.0, Wx, Wx[:], ALU.mult, ALU.mult)
                self.tt("pool", X4, X4[:, 256:512], pp, r_, Wt, Wt[:], ALU.mult)
                self.tt("dve", X4, X4[:, 512:768], bv, bv[:], Wn, Wn[:], ALU.mult)
                self.tt("pool", X4, X4[:, 768:1024], kd, kd[:], Wn, Wn[:], ALU.mult)
                self.cp("act", Atb, Atb[:], X4, X4[:, 0:256])
                self.tt("dve", Bhb, Bhb[:], bv, bv[:], Wh, Wh[:], ALU.mult)
                self.tt("pool", Khb, Khb[:], kd, kd[:], Wh, Wh[:], ALU.mult)
                for kind in range(4):
                    pt_ = psE if kind < 2 else psF
                    for pr_ in range(2):
                        c0 = kind * 256 + pr_ * 128
                        self.tr(pt_, pt_[:, ((kind % 2) * 2 + pr_) * 128:((kind % 2) * 2 + pr_ + 1) * 128], X4, X4[:, c0:c0 + 128])
                for kind in range(4):
                    pt_ = psE if kind < 2 else psF
                    self.cp(("act", "dve")[kind % 2], XT, XT[:, :, kind, :],
                            pt_, pt_[:, (kind % 2) * 256:(kind % 2) * 256 + 256].rearrange("p (a b) -> p a b", a=2))
                for hh in range(4):
                    pq = hh // 2
                    rhs = XT[hp(hh), pq, 0:2, :]
                    self.mm(psA, psA[:, hh * 128:(hh + 1) * 128], XT, XT[hp(hh), pq, 2, :], XT, XT[hp(hh), pq, 0, :])
                    self.mm(psB, psB[:, hh * 128:(hh + 1) * 128], XT, XT[hp(hh), pq, 2, :], XT, XT[hp(hh), pq, 1, :])
                    self.mm(psG, psG[:, hh * 128:(hh + 1) * 128], XT, XT[hp(hh), pq, 3, :], XT, XT[hp(hh), pq, 0, :])
                    self.mm(psH, psH[:, hh * 128:(hh + 1) * 128], XT, XT[hp(hh), pq, 3, :], XT, XT[hp(hh), pq, 1, :])
                    self.mm(psC, psC[:, hh * 128:(hh + 1) * 128], XT, XT[hp(hh), pq, 0, :], XT, XT[hp(hh), pq, 2, :])
                msb = bc(MSI[:, 0:128].unsqueeze(1), [128, 4, 128])
                mib = bc(MSI[:, 128:256].unsqueeze(1), [128, 4, 128])
                mtb = bc(MST[:].unsqueeze(1), [128, 4, 128])
                self.tt("dve", Nb, Nb[:], psA, v4(psA[:]), MSI, msb, ALU.mult)
                self.tt("dve", Mf, Mf[:], psA, v4(psA[:]), MSI, msb, ALU.mult)
                self.tt("dve", Arb, Arb[:], psB, v4(psB[:]), MSI, mib, ALU.mult)
                self.tt("dve", Aak, Aak[:], psG, v4(psG[:]), MSI, msb, ALU.mult)
                self.tt("dve", Ark, Ark[:], psH, v4(psH[:]), MSI, mib, ALU.mult)
                self.tt("dve", NTb, NTb[:], psC, v4(psC[:]), MST, mtb, ALU.mult)
                self.tt("pool", Mf, Mf[:], Mf, Mf[:], identq, identq[:], ALU.add)
                self.cp("act", Mb, Mb[:], Mf, Mf[:])
                Pc, PTc = Nb, NTb
                for st in range(6):
                    Pn, PTn = Pb[st % 2], PTb[st % 2]
                    for hh in range(4):
                        self.mm(psA, psA[:, hh * 128:(hh + 1) * 128], PTc, PTc[:, hh, :], Pc, Pc[:, hh, :])
                        self.mm(psB, psB[:, hh * 128:(hh + 1) * 128], Pc, Pc[:, hh, :], PTc, PTc[:, hh, :])
                    if st < 5:
                        self.cp("act", Pn, Pn[:], psA, v4(psA[:]))
                    self.cp("dve", PTn, PTn[:], psB, v4(psB[:]))
                    for hh in range(4):
                        self.mm(psG, psG[:, hh * 128:(hh + 1) * 128], PTn, PTn[:, hh, :], Mb, Mb[:, hh, :])
                    self.tt("dve", Mf, Mf[:], Mf, Mf[:], psG, v4(psG[:]), ALU.add)
                    self.cp("act", Mb, Mb[:], Mf, Mf[:])
                    Pc, PTc = Pn, PTn
                for hh in range(4):
                    self.mm(psH, psH[:, hh * 64:(hh + 1) * 64], Aak, Aak[:, hh, :], vb, vb[:, hh * 64:(hh + 1) * 64])
                self.cp("act", Xb, Xb[:], psH, psH[:, 0:256])
                for hh in range(4):
                    self.mm(psA, psA[:, hh * 64:(hh + 1) * 64], Mb, Mb[:, hh, :], Xb, Xb[:, hh * 64:(hh + 1) * 64])
                    self.mm(psB, psB[:, hh * 128:(hh + 1) * 128], Atb, Atb[:, (hh // 2) * 128:(hh // 2) * 128 + 128], Mb, Mb[:, hh, :])
                self.cp("act", Ut, Ut[:], psA, psA[:, 0:256])
                self.cp("dve", WtT, WtT[:], psB, v4(psB[:]))
                hcol = lambda hh: slice(hh * 64, (hh + 1) * 64)
                for hh in range(4):
                    self.mm(psC, psC[:, hcol(hh)], WtT, WtT[hp(hh), hh, :], Hb, Hb[hp(hh), hcol(hh)])
                self.tt("dve", Ut, Ut[:], Ut, Ut[:], psC, psC[:, 0:256], ALU.add)
                self.cp("act", Ub, Ub[:], Ut, Ut[:])
                for hh in range(4):
                    pq = hh // 2
                    self.mm(psD, psD[:, hcol(hh)], XT, XT[hp(hh), pq, 1, :], Hb, Hb[hp(hh), hcol(hh)], start=True, stop=False)
                    self.mm(psD, psD[:, hcol(hh)], Arb, Arb[:, hh, :], Ub, Ub[:, hcol(hh)], start=False, stop=False)
                    self.mm(psD, psD[:, hcol(hh)], Ark, Ark[:, hh, :], vb, vb[:, hcol(hh)], start=False, stop=True)
                self.cp("act", Y, Y[:, 0:256], psD, psD[:, 0:256])
                r0 = self.rows(s, i)
                ncol = 768 if d == 0 else 512
                self.store(Y, self.YD[d, r0:r0 + 128, 256:256 + ncol], Y[:, 0:ncol], f"s0{b}", writes=[("YDw", d, s, i)])
                for hh in range(4):
                    pq = hh // 2
                    self.mm(psE, psE[:, hcol(hh)], Bhb, Bhb[:, pq * 128:(pq + 1) * 128], Ub, Ub[:, hcol(hh)], start=True, stop=False)
                    self.mm(psE, psE[:, hcol(hh)], Khb, Khb[:, pq * 128:(pq + 1) * 128], vb, vb[:, hcol(hh)], start=False, stop=True)
                for pq in range(2):
                    self.ts("dve", H, H[:, pq * 128:(pq + 1) * 128], H, H[:, pq * 128:(pq + 1) * 128], WCk[:, pq:pq + 1], None, ALU.mult, extra_r=[WCk])
                self.tt("dve", H, H[:], H, H[:], psE, psE[:, 0:256], ALU.add)
                self.cp("act", Hb, Hb[:], H, H[:])
            self.end_phase()

    def phase_s5(self, l, d):
        di = self.di
        PI = math.pi
        with ExitStack() as es:
            sb = lambda n, sh, dt=F32: self.sb(es, n, sh, dt)
            COS = sb("sCOS", [128, 32, 128]); SIN = sb("sSIN", [128, 32, 128]); RHO = sb("sRHO", [128, 32, 128])
            BP1 = sb("sBP1", [128, 32, 128], BF16); BP2 = sb("sBP2", [128, 32, 128], BF16)
            W1 = sb("sW1", [128, 512], BF16); W2 = sb("sW2", [128, 512], BF16)
            RHOF = sb("sRHOF", [128, 32]); CL = sb("sCL", [128, 32]); SL = sb("sSL", [128, 32])
            SWAP = sb("sSWAP", [128, 128]); self.load(SWAP, SWAP[:], di["c_swap"][:, :], "p0")
            psA = self.ps(es, "spsA", [128, 1024]); psB = self.ps(es, "spsB", [128, 1024])
            psY = self.ps(es, "spsY", [128, 512]); psT = self.ps(es, "spsT", [128, 4, 128]); psX = self.ps(es, "spsX", [128, 512])
            with ExitStack() as es2:
                sb2 = lambda n, sh, dt=F32: self.sb(es2, n, sh, dt)
                J1 = sb2("sJ1", [128, 128]); self.load(J1, J1[:], di["c_j1"][d], "p1")
                Z0 = sb2("sZ0", [128, 128]); self.load(Z0, Z0[:], di["c_z0"][d], "p2")
                SEL = sb2("sSEL", [128, 128]); self.load(SEL, SEL[:], di["c_sel"][:, :], "p3")
                SELb = sb2("sSELb", [128, 128], BF16); self.cp("dve", SELb, SELb[:], SEL, SEL[:])
                negpi = sb2("snegpi", [128, 1]); self.memset("pool", negpi, negpi[:], -PI)
                lre = sb2("slre", [128, 32]); lim = sb2("slim", [128, 32]); stp = sb2("sstp", [128, 32])
                lam_re_T = di["s5_lam_re"][l, d].rearrange("g p -> p g")
                lam_im_T = di["s5_lam_im"][l, d].rearrange("g p -> p g")
                for hf in range(2):
                    self.O("sp", lambda e, o=lre[hf * 64:(hf + 1) * 64, :], i_=lam_re_T: e.dma_start(out=o, in_=i_, allow_slow_non_contiguous=True), writes=[lre], dma="p4")
                    self.O("sp", lambda e, o=lim[hf * 64:(hf + 1) * 64, :], i_=lam_im_T: e.dma_start(out=o, in_=i_, allow_slow_non_contiguous=True), writes=[lim], dma="p5")
                self.load(stp, stp[:], di["s5_log_step"][l, d].partition_broadcast(128), "p6")
                self.act(stp, stp[:], stp, stp[:], AF.Exp)
                self.ts("dve", lre, lre[:], lre, lre[:], -1e-4, None, ALU.min)
                al = sb2("sal", [128, 32]); th = sb2("sth", [128, 32])
                self.tt("dve", al, al[:], lre, lre[:], stp, stp[:], ALU.mult)
                self.tt("dve", th, th[:], lim, lim[:], stp, stp[:], ALU.mult)
                self.act(RHOF, RHOF[:], al, al[:], AF.Exp)
                arg = sb2("sarg", [128, 32, 128]); arg2 = sb2("sarg2", [128, 32, 128])
                argi = sb2("sargi", [128, 32, 128], mybir.dt.int32); argf = sb2("sargf", [128, 32, 128])

                def sinred(o_t, o_ap, a_t, a_ap, w_ap, wi_ap, wf_ap, off):
                    self.ts("dve", arg2, w_ap, a_t, a_ap, 1.0 / (2 * PI), off, ALU.mult, ALU.add)
                    self.cp("dve", argi, wi_ap, arg2, w_ap)
                    self.cp("dve", argf, wf_ap, argi, wi_ap)
                    self.tt("dve", arg2, w_ap, arg2, w_ap, argf, wf_ap, ALU.subtract)
                    self.act(o_t, o_ap, arg2, w_ap, AF.Sin, scale=2 * PI)
                self.tt("dve", arg, arg[:], th, bc(th[:].unsqueeze(2), [128, 32, 128]), J1, bc(J1[:].unsqueeze(1), [128, 32, 128]), ALU.mult)
                sinred(SIN, SIN[:], arg, arg[:], arg2[:], argi[:], argf[:], 0.0)
                sinred(COS, COS[:], arg, arg[:], arg2[:], argi[:], argf[:], 0.25)
                self.tt("dve", RHO, RHO[:], RHOF, bc(RHOF[:].unsqueeze(2), [128, 32, 128]), Z0, bc(Z0[:].unsqueeze(1), [128, 32, 128]), ALU.mult)
                jl = 127 if d == 0 else 0
                self.cp("dve", CL, CL[:], COS, COS[:, :, jl])
                self.cp("dve", SL, SL[:], SIN, SIN[:, :, jl])
                lre2 = sb2("slre2", [128, 16]); lim2 = sb2("slim2", [128, 16]); stp2 = sb2("sstp2", [128, 16])
                self.load(lre2, lre2[:], di["s5_lam_re"][l, d].rearrange("(gp go) p -> (go p) gp", go=2), "p7")
                self.load(lim2, lim2[:], di["s5_lam_im"][l, d].rearrange("(gp go) p -> (go p) gp", go=2), "p8")
                ls2 = di["s5_log_step"][l, d].rearrange("(gp go) -> go gp", go=2)
                for go in range(2):
                    self.load(stp2, stp2[go * 64:(go + 1) * 64, :], ls2[go].partition_broadcast(64), "p9")
                self.act(stp2, stp2[:], stp2, stp2[:], AF.Exp)
                self.ts("dve", lre2, lre2[:], lre2, lre2[:], -1e-4, None, ALU.min)
                pw = sb2("spw", [128, 16 * 12])
                c = lambda k: pw[:, k * 16:(k + 1) * 16]
                TTm = lambda o, a, b_, op: self.tt("dve", pw, o, pw, a, pw, b_, op)
                self.tt("dve", pw, c(0), lre2, lre2[:], stp2, stp2[:], ALU.mult)
                self.tt("dve", pw, c(1), lim2, lim2[:], stp2, stp2[:], ALU.mult)
                self.act(pw, c(2), pw, c(0), AF.Exp)
                sinred(pw, c(4), pw, c(1), arg2[:, 0, 0:16], argi[:, 0, 0:16], argf[:, 0, 0:16], 0.0)
                sinred(pw, c(5), pw, c(1), arg2[:, 0, 0:16], argi[:, 0, 0:16], argf[:, 0, 0:16], 0.25)
                TTm(c(5), c(5), c(2), ALU.mult)
                TTm(c(4), c(4), c(2), ALU.mult)
                self.ts("dve", pw, c(5), pw, c(5), -1.0, None, ALU.add)
                self.tt("dve", pw, c(6), lre2, lre2[:], lre2, lre2[:], ALU.mult)
                self.tt("dve", pw, c(7), lim2, lim2[:], lim2, lim2[:], ALU.mult)
                TTm(c(6), c(6), c(7), ALU.add)
                self.O("dve", lambda e, o=c(6), i_=c(6): e.reciprocal(out=o, in_=i_), reads=[pw], writes=[pw])
                self.tt("dve", pw, c(7), pw, c(5), lre2, lre2[:], ALU.mult)
                self.tt("dve", pw, c(8), pw, c(4), lim2, lim2[:], ALU.mult)
                TTm(c(7), c(7), c(8), ALU.add)
                TTm(c(7), c(7), c(6), ALU.mult)
                self.tt("dve", pw, c(8), pw, c(4), lre2, lre2[:], ALU.mult)
                self.tt("dve", pw, c(9), pw, c(5), lim2, lim2[:], ALU.mult)
                TTm(c(8), c(8), c(9), ALU.subtract)
                TTm(c(8), c(8), c(6), ALU.mult)
                Bre = sb2("sBre", [128, 16, 16]); Bim = sb2("sBim", [128, 16, 16])
                self.load(Bre, Bre[:], di["s5_b_re"][l, d].rearrange("(gp go) p c -> (go p) gp c", go=2), "p10")
                self.load(Bim, Bim[:], di["s5_b_im"][l, d].rearrange("(gp go) p c -> (go p) gp c", go=2), "p11")
                bbr = sb2("sbbr", [128, 16, 16]); bbi = sb2("sbbi", [128, 16, 16]); tq = sb2("stq", [128, 16, 16])
                cre = bc(c(7).unsqueeze(2), [128, 16, 16]); cim = bc(c(8).unsqueeze(2), [128, 16, 16])
                self.tt("dve", bbr, bbr[:], Bre, Bre[:], pw, cre, ALU.mult)
                self.tt("dve", tq, tq[:], Bim, Bim[:], pw, cim, ALU.mult)
                self.tt("dve", bbr, bbr[:], bbr, bbr[:], tq, tq[:], ALU.subtract)
                self.tt("dve", bbi, bbi[:], Bim, Bim[:], pw, cre, ALU.mult)
                self.tt("dve", tq, tq[:], Bre, Bre[:], pw, cim, ALU.mult)
                self.tt("dve", bbi, bbi[:], bbi, bbi[:], tq, tq[:], ALU.add)
                XPr = sb2("sXPr", [128, 8, 128], BF16); XPi = sb2("sXPi", [128, 8, 128], BF16)
                self.memset("pool", XPr, XPr[:], 0.0); self.memset("pool", XPi, XPi[:], 0.0)
                for g in range(32):
                    gp, go, gi = g // 2, g % 2, g % 8
                    self.cp("dve", XPr, XPr[:, gi, gi * 16:(gi + 1) * 16], bbr, bbr[:, gp, :])
                    self.cp("pool", XPi, XPi[:, gi, gi * 16:(gi + 1) * 16], bbi, bbi[:, gp, :])
                    selg = SELb[:, go * 64:(go + 1) * 64]
                    self.mm(psA, psA[:, 0:64], XPr, XPr[:, gi, :], SELb, selg)
                    self.mm(psA, psA[:, 64:128], XPi, XPi[:, gi, :], SELb, selg)
                    self.cp("act", BP1, BP1[:, g, :], psA, psA[:, 0:128])
                    self.act(BP2, BP2[:, g, 0:64], psA, psA[:, 64:128], AF.Copy, scale=-1.0)
                    self.cp("dve", BP2, BP2[:, g, 64:128], psA, psA[:, 0:64])
                Ca = sb2("sCa", [128, 4, 128]); Cb = sb2("sCb", [128, 4, 128])
                cre_d = di["s5_c_re"][l, d].rearrange("(a gi) c p -> (gi c) a p", gi=8)
                cim_d = di["s5_c_im"][l, d].rearrange("(a gi) c p -> (gi c) a p", gi=8)
                self.load(Ca, Ca[:, :, 0:64], cre_d, "p12"); self.load(Ca, Ca[:, :, 64:128], cim_d, "p13")
                self.load(Cb, Cb[:, :, 0:64], cim_d, "p14"); self.load(Cb, Cb[:, :, 64:128], cre_d, "p15")
                for a in range(4):
                    self.tr(psT, psT[:, a, :], Ca, Ca[:, a, :])
                self.cp("act", W1, W1[0:64, :], psT, psT[0:64, :, :].rearrange("p a b -> p (a b)"))
                self.act(W1, W1[64:128, :], psT, psT[64:128, :, :].rearrange("p a b -> p (a b)"), AF.Copy, scale=-1.0)
                for a in range(4):
                    self.tr(psT, psT[:, a, :], Cb, Cb[:, a, :])
                self.act(W2, W2[:], psT, psT[:].rearrange("p a b -> p (a b)"), AF.Copy, scale=-1.0)
                self.S.barrier()
                self.S.flush()
            u = [sb(f"su{i}", [128, 512]) for i in range(2)]
            uT = sb("suT", [128, 4, 128], BF16)
            t1 = sb("st1", [128, 1024]); t2 = sb("st2", [128, 1024])
            zin = sb("szin", [128, 32, 128]); z = sb("sz", [128, 32, 128])
            ZC = sb("sZC", [128, 32, 128], BF16); ZS = sb("sZS", [128, 32, 128], BF16)
            xst = sb("sxst", [128, 32]); xc = sb("sxc", [128, 32]); xs_ = sb("sxs", [128, 32]); tm = sb("stm", [128, 32])
            yo = [sb(f"syo{i}", [128, 512]) for i in range(2)]
            j0 = 0 if d == 0 else 127
            for it, (s, i, first) in enumerate(self.tiles(d)):
                b = it % 2
                p0 = self.prow(s, i)
                if first:
                    self.memset("pool", xst, xst[:], 0.0)
                self.load(u[b], u[b][:], self.PS[p0:p0 + 128, 2560:3072], f"l0{b}", reads=[("PS", s, i)])
                for ct in range(4):
                    self.tr(psT, psT[:, ct, :], u[b], u[b][:, ct * 128:(ct + 1) * 128])
                self.cp("act", uT, uT[:], psT, psT[:])
                for ct in range(4):
                    for gi in range(8):
                        g = ct * 8 + gi
                        self.mm(psA, psA[:, gi * 128:(gi + 1) * 128], BP1, BP1[:, g, :], uT, uT[:, ct, :])
                        self.mm(psB, psB[:, gi * 128:(gi + 1) * 128], BP2, BP2[:, g, :], uT, uT[:, ct, :])
                    gs = slice(ct * 8, ct * 8 + 8)
                    fl = lambda ap: ap.rearrange("p a b -> p (a b)")
                    self.tt("dve", t1, t1[:], psA, psA[:], COS, fl(COS[:, gs, :]), ALU.mult)
                    self.tt("dve", t2, t2[:], psB, psB[:], SIN, fl(SIN[:, gs, :]), ALU.mult)
                    self.tt("pool", zin, fl(zin[:, gs, :]), t1, t1[:], t2, t2[:], ALU.subtract)
                self.tt("dve", tm, tm[:], RHOF, RHOF[:], xst, xst[:], ALU.mult)
                self.tt("dve", zin, zin[:, :, j0], zin, zin[:, :, j0], tm, tm[:], ALU.add)
                zf = z[:].rearrange("p a b -> p (a b)"); zif = zin[:].rearrange("p a b -> p (a b)"); rf = RHO[:].rearrange("p a b -> p (a b)")
                if d == 1:
                    zf, zif, rf = zf[:, ::-1], zif[:, ::-1], rf[:, ::-1]
                self.O("dve", lambda e, o=zf, a=rf, b_=zif: e.tensor_tensor_scan(out=o, data0=a, data1=b_, initial=0.0, op0=ALU.mult, op1=ALU.add),
                       reads=[RHO, zin], writes=[z])
                self.tt("dve", ZC, ZC[:], z, z[:], COS, COS[:], ALU.mult)
                self.tt("pool", ZS, ZS[:], z, z[:], SIN, SIN[:], ALU.mult)
                for g in range(32):
                    self.mm(psY, psY[:, g * 16:(g + 1) * 16], ZC, ZC[:, g, :], W1, W1[:, g * 16:(g + 1) * 16], start=True, stop=False)
                    self.mm(psY, psY[:, g * 16:(g + 1) * 16], ZS, ZS[:, g, :], W2, W2[:, g * 16:(g + 1) * 16], start=False, stop=True)
                self.cp("act", yo[b], yo[b][:], psY, psY[:])
                r0 = self.rows(s, i)
                self.store(yo[b], self.YD[d, r0:r0 + 128, 1024:1536], yo[b][:], f"s0{b}", writes=[("YDs", d, s, i)])
                jl = 127 - j0
                self.tt("dve", xc, xc[:], z, z[:, :, jl], CL, CL[:], ALU.mult)
                self.tt("dve", xs_, xs_[:], z, z[:, :, jl], SL, SL[:], ALU.mult)
                self.mm(psX, psX[:, 0:32], SWAP, SWAP[:], xs_, xs_[:])
                self.tt("dve", xst, xst[:], xc, xc[:], psX, psX[:, 0:32], ALU.add)
            self.end_phase()

    def head_ln(self, y, eps, gw, gb, wk, st):
        v4 = lambda ap: ap.rearrange("p (a b) -> p a b", a=4)
        self.O("dve", lambda e, o=st[:, 0:4], i_=v4(y[:]): e.tensor_reduce(out=o, in_=i_, axis=AX.X, op=ALU.add), reads=[y], writes=[st])
        self.ts("dve", st, st[:, 0:4], st, st[:, 0:4], -1.0 / 64, None, ALU.mult)
        self.tt("dve", y, v4(y[:]), y, v4(y[:]), st, bc(st[:, 0:4].unsqueeze(2), [128, 4, 64]), ALU.add)
        self.tt("pool", wk, wk[:], y, y[:], y, y[:], ALU.mult)
        self.O("dve", lambda e, o=st[:, 4:8], i_=v4(wk[:]): e.tensor_reduce(out=o, in_=i_, axis=AX.X, op=ALU.add), reads=[wk], writes=[st])
        self.ts("dve", st, st[:, 4:8], st, st[:, 4:8], 1.0 / 64, eps, ALU.mult, ALU.add)
        self.act(st, st[:, 8:12], st, st[:, 4:8], AF.Sqrt)
        self.O("dve", lambda e, o=st[:, 12:16], i_=st[:, 8:12]: e.reciprocal(out=o, in_=i_), reads=[st], writes=[st])
        self.tt("dve", y, v4(y[:]), y, v4(y[:]), st, bc(st[:, 12:16].unsqueeze(2), [128, 4, 64]), ALU.mult)
        self.tt("pool", y, y[:], y, y[:], gw, gw[:], ALU.mult)
        self.tt("dve", y, y[:], y, y[:], gb, gb[:], ALU.add)

    def phase_O1(self, l):
        di = self.di
        with ExitStack() as es:
            sb = lambda n, sh, dt=F32: self.sb(es, n, sh, dt)
            stg = [sb(f"ostg{i}", [128, 1024]) for i in range(3)]
            WO = self.load_w_bf16(es, "oWO", di["w_out"][l], 8, D, stg)
            GL = self.load_w_bf16(es, "oGL", di["s5_glu_w"][l], 4, 512, stg)
            RGW = self.bcast_load(es, "oRGW", di["ret_gn_w"][l], 256); RGB = self.bcast_load(es, "oRGB", di["ret_gn_b"][l], 256)
            WGW = self.bcast_load(es, "oWGW", di["rwkv_gn_w"][l], 256); WGB = self.bcast_load(es, "oWGB", di["rwkv_gn_b"][l], 256)
            SD = self.bcast_load(es, "oSD", di["s5_d"][l], 512); GB = self.bcast_load(es, "oGB", di["s5_glu_b"][l], 512)
            y0 = [sb(f"oy0{i}", [128, YW]) for i in range(2)]
            y1 = [sb(f"oy1{i}", [128, YW]) for i in range(2)]
            gu = [sb(f"ogu{i}", [128, 768]) for i in range(2)]
            xt = [sb(f"oxt{i}", [128, D]) for i in range(2)]
            mix = sb("omix", [128, D]); wk = sb("owk", [128, 512]); wk2 = sb("owk2", [128, 512]); st = sb("ost", [128, 16])
            ygT = sb("oygT", [128, 4, 128], BF16); mixT = sb("omixT", [128, 8, 128], BF16)
            xo = [sb(f"oxo{i}", [128, D]) for i in range(2)]
            psT = [self.ps(es, f"opsT{i}", [128, 4, 128]) for i in range(2)]
            psG = self.ps(es, "opsG", [128, 512]); psO = [self.ps(es, f"opsO{i}", [128, 512]) for i in range(2)]
            nr, nw, ns = (self.flags.get(k) for k in ("no_ret", "no_rwkv", "no_s5"))
            self.S.barrier()
            for it, (s, i, first) in enumerate(self.tiles()):
                b = it % 2
                r0 = self.rows(s, i); p0 = self.prow(s, i)
                self.load(y0[b], y0[b][:], self.YD[0, r0:r0 + 128, :], f"l0{b}", reads=[("YDr", 0, s, i), ("YDw", 0, s, i), ("YDs", 0, s, i)])
                self.load(y1[b], y1[b][:], self.YD[1, r0:r0 + 128, :], f"l1{b}", reads=[("YDr", 1, s, i), ("YDw", 1, s, i), ("YDs", 1, s, i)])
                self.load(gu[b], gu[b][:, 0:256], self.PS[p0:p0 + 128, 1280:1536], f"l2{b}", reads=[("PS", s, i)])
                self.load(gu[b], gu[b][:, 256:768], self.PS[p0:p0 + 128, 2560:3072], f"l3{b}", reads=[("PS", s, i)])
                self.load(xt[b], xt[b][:], self.XS[r0:r0 + 128, :], f"l4{b}", reads=["XSall", ("XS", r0)])
                A, B, GU = y0[b], y1[b], gu[b]
                yr = mix
                if nr:
                    self.memset("pool", mix, mix[:, 0:256], 0.0)
                else:
                    self.tt("dve", mix, mix[:, 0:256], A, A[:, 0:256], B, B[:, 0:256], ALU.add)
                    self._ln_slice(mix, 0, 1e-5, RGW, RGB, wk, st)
                    self.act(wk2, wk2[:, 0:256], GU, GU[:, 0:256], AF.Silu)
                    self.tt("dve", mix, mix[:, 0:256], mix, mix[:, 0:256], wk2, wk2[:, 0:256], ALU.mult)
                if nw:
                    self.memset("pool", mix, mix[:, 256:512], 0.0)
                else:
                    self.tt("dve", mix, mix[:, 256:512], A, A[:, 256:512], B, B[:, 256:512], ALU.add)
                    self._ln_slice(mix, 256, 64e-5, WGW, WGB, wk, st)
                    self.tt("pool", wk2, wk2[:, 0:256], A, A[:, 512:768], B, B[:, 512:768], ALU.add)
                    self.tt("dve", mix, mix[:, 256:512], mix, mix[:, 256:512], wk2, wk2[:, 0:256], ALU.add)
                    self.tt("dve", mix, mix[:, 256:512], mix, mix[:, 256:512], A, A[:, 768:1024], ALU.mult)
                if ns:
                    self.memset("pool", mix, mix[:, 512:1024], 0.0)
                else:
                    self.tt("dve", wk, wk[:], GU, GU[:, 256:768], SD, SD[:], ALU.mult)
                    self.tt("pool", wk2, wk2[:], A, A[:, 1024:1536], B, B[:, 1024:1536], ALU.add)
                    self.tt("dve", wk, wk[:], wk, wk[:], wk2, wk2[:], ALU.add)
                    self.tt("pool", wk2, wk2[:], wk, wk[:], wk, wk[:], ALU.mult)
                    self.ts("dve", wk2, wk2[:], wk2, wk2[:], 0.044715, 1.0, ALU.mult, ALU.add)
                    self.tt("dve", wk2, wk2[:], wk2, wk2[:], wk, wk[:], ALU.mult)
                    self.act(wk2, wk2[:], wk2, wk2[:], AF.Sigmoid, scale=1.5957691216057308)
                    self.tt("dve", wk, wk[:], wk, wk[:], wk2, wk2[:], ALU.mult)
                    self.transpose_to(wk, 4, psT, ygT)
                    for k in range(4):
                        self.mm(psG, psG[:], ygT, ygT[:, k, :], GL, GL[:, k, :], start=(k == 0), stop=(k == 3))
                    self.tt("dve", wk2, wk2[:], psG, psG[:], GB, GB[:], ALU.add)
                    self.act(wk2, wk2[:], wk2, wk2[:], AF.Sigmoid)
                    self.tt("dve", mix, mix[:, 512:1024], wk, wk[:], wk2, wk2[:], ALU.mult)
                self.transpose_to(mix, 8, psT, mixT)
                for cb in range(2):
                    pp = psO[cb]
                    for k in range(8):
                        self.mm(pp, pp[:], mixT, mixT[:, k, :], WO, WO[:, k, cb * 512:(cb + 1) * 512], start=(k == 0), stop=(k == 7))
                    self.tt("dve", xo[b], xo[b][:, cb * 512:(cb + 1) * 512], xt[b], xt[b][:, cb * 512:(cb + 1) * 512], pp, pp[:], ALU.add)
                self.store(xo[b], self.XS[r0:r0 + 128, :], xo[b][:], f"s0{b}", writes=[("XS", r0)])
            self.end_phase()

    def _ln_slice(self, mix, c0, eps, gw, gb, wk, st):
        v4 = lambda ap: ap.rearrange("p (a b) -> p a b", a=4)
        y = mix[:, c0:c0 + 256]
        self.O("dve", lambda e, o=st[:, 0:4], i_=v4(y): e.tensor_reduce(out=o, in_=i_, axis=AX.X, op=ALU.add), reads=[mix], writes=[st])
        self.ts("dve", st, st[:, 0:4], st, st[:, 0:4], -1.0 / 64, None, ALU.mult)
        self.tt("dve", mix, v4(y), mix, v4(y), st, bc(st[:, 0:4].unsqueeze(2), [128, 4, 64]), ALU.add)
        self.tt("pool", wk, wk[:, 0:256], mix, y, mix, y, ALU.mult)
        self.O("dve", lambda e, o=st[:, 4:8], i_=v4(wk[:, 0:256]): e.tensor_reduce(out=o, in_=i_, axis=AX.X, op=ALU.add), reads=[wk], writes=[st])
        self.ts("dve", st, st[:, 4:8], st, st[:, 4:8], 1.0 / 64, eps, ALU.mult, ALU.add)
        self.act(st, st[:, 8:12], st, st[:, 4:8], AF.Sqrt)
        self.O("dve", lambda e, o=st[:, 12:16], i_=st[:, 8:12]: e.reciprocal(out=o, in_=i_), reads=[st], writes=[st])
        self.tt("dve", mix, v4(y), mix, v4(y), st, bc(st[:, 12:16].unsqueeze(2), [128, 4, 64]), ALU.mult)
        self.tt("pool", mix, y, mix, y, gw, gw[:], ALU.mult)
        self.tt("dve", mix, y, mix, y, gb, gb[:], ALU.add)

    def phase_O2(self, l, last):
        di = self.di
        finals = []
        with ExitStack() as es:
            sb = lambda n, sh, dt=F32: self.sb(es, n, sh, dt)
            stg = [sb(f"fstg{i}", [128, 1024]) for i in range(2)]
            WG = self.load_w_bf16(es, "fWG", di["ffn_w_gate"][l], 8, DFF, stg)
            WU = self.load_w_bf16(es, "fWU", di["ffn_w_up"][l], 8, DFF, stg)
            WD = self.load_w_bf16(es, "fWD", di["ffn_w_down"][l], 22, D, stg)
            G2 = self.bcast_load(es, "fG2", di["norm2_g"][l], D)
            GF = self.bcast_load(es, "fGF", di["final_g"], D) if last else None
            xb = [sb(f"fxb{i}", [128, D]) for i in range(2)]
            junk = sb("fjunk", [128, D], BF16); ss = [sb(f"fss{i}", [128, 4]) for i in range(2)]
            h = sb("fh", [128, D]); hT = sb("fhT", [128, 8, 128], BF16)
            sg = sb("fsg", [128, 4, 128]); aT = sb("faT", [128, 22, 128], BF16)
            xo = [sb("fxo0", [128, D])] * 2
            yo = [sb("fyo0", [128, D]) if last else None] * 2
            psT = [self.ps(es, f"fpsT{i}", [128, 4, 128]) for i in range(2)]
            psG = self.ps(es, "fpsG", [128, 4, 128]); psU = self.ps(es, "fpsU", [128, 4, 128])
            psO = [self.ps(es, f"fpsO{i}", [128, 512]) for i in range(2)]
            self.S.barrier()
            for it, (s, i, first) in enumerate(self.tiles()):
                b = it % 2
                r0 = self.rows(s, i)
                self.load(xb[b], xb[b][:], self.XS[r0:r0 + 128, :], f"l0{b}", reads=["XSall", ("XS", r0)])
                self.rmsnorm(xb[b], G2, h, junk, ss[b])
                self.transpose_to(h, 8, psT, hT)
                for f0 in range(0, 22, 4):
                    f1 = min(22, f0 + 4)
                    for f in range(f0, f1):
                        for k in range(8):
                            self.mm(psG, psG[:, f - f0, :], WG, WG[:, k, f * 128:(f + 1) * 128], hT, hT[:, k, :], start=(k == 0), stop=(k == 7))
                        for k in range(8):
                            self.mm(psU, psU[:, f - f0, :], WU, WU[:, k, f * 128:(f + 1) * 128], hT, hT[:, k, :], start=(k == 0), stop=(k == 7))
                    self.act(sg, sg[:, 0:f1 - f0, :], psG, psG[:, 0:f1 - f0, :], AF.Silu)
                    self.tt("dve", aT, aT[:, f0:f1, :], sg, sg[:, 0:f1 - f0, :], psU, psU[:, 0:f1 - f0, :], ALU.mult)
                for cb in range(2):
                    pp = psO[cb]
                    for k in range(22):
                        self.mm(pp, pp[:], aT, aT[:, k, :], WD, WD[:, k, cb * 512:(cb + 1) * 512], start=(k == 0), stop=(k == 21))
                    self.tt("dve", xo[b], xo[b][:, cb * 512:(cb + 1) * 512], xb[b], xb[b][:, cb * 512:(cb + 1) * 512], pp, pp[:], ALU.add)
                if not last:
                    self.store(xo[b], self.XS[r0:r0 + 128, :], xo[b][:], f"s0{b}", writes=[("XS", r0)])
                else:
                    self.rmsnorm(xo[b], GF, yo[b], junk, ss[b])
                    st_ = self.store(yo[b], self.yout[s][i * 128:(i + 1) * 128, :], yo[b][:], f"s0{b}")
                    finals.append(st_)
            self.end_phase()
        return finals


def _consts(Tmax):
    c = {}
    c["c_ident"] = np.eye(128, dtype=np.float32)
    inv = (1.0 / (10000.0 ** np.linspace(0.0, 1.0, 32, dtype=np.float32))).astype(np.float32)
    ang = np.arange(Tmax, dtype=np.float32)[:, None] * inv[None, :]
    cos, sin = np.cos(ang).astype(np.float32), np.sin(ang).astype(np.float32)
    rot = np.concatenate([0.125 * cos, 0.125 * cos, -0.125 * sin, 0.125 * sin, cos, cos, -sin, sin], axis=1)
    c["c_rot"] = np.ascontiguousarray(rot.astype(np.float32))
    lg = np.log(1.0 - 2.0 ** (-5.0 - np.arange(4, dtype=np.float64)))
    i = np.arange(128, dtype=np.float64)
    diff = np.abs(i[:, None] - i[None, :])
    c["c_retmask"] = np.concatenate([np.exp(diff * lg[h]) for h in range(4)], axis=1).astype(np.float32)
    qk = np.zeros((2, 128, 8), np.float32)
    for h in range(4):
        qk[0, :, h] = np.exp((i + 1) * lg[h]); qk[0, :, 4 + h] = np.exp((127 - i) * lg[h])
        qk[1, :, h] = np.exp((128 - i) * lg[h]); qk[1, :, 4 + h] = np.exp(i * lg[h])
    c["c_retqk"] = qk
    gc = np.zeros((128, 256), np.float32)
    for r in range(128):
        for pr in range(2):
            gc[r, pr * 128:(pr + 1) * 128] = np.exp(128 * lg[pr * 2 + r // 64])
    c["c_retgc"] = gc
    s_, t_ = np.meshgrid(np.arange(128), np.arange(128), indexing="ij")
    c["c_cs"] = np.stack([(s_ <= t_), (s_ >= t_)]).astype(np.float32)
    c["c_ones"] = np.ones((128, 128), np.float32)
    c["c_msi"] = np.stack([np.concatenate([(s_ < t_), (s_ <= t_)], axis=1),
                           np.concatenate([(s_ > t_), (s_ >= t_)], axis=1)]).astype(np.float32)
    c["c_mst"] = np.stack([(s_ > t_), (s_ < t_)]).astype(np.float32)
    j = np.arange(128, dtype=np.float32)
    c["c_j1"] = np.stack([np.tile(j + 1, (128, 1)), np.tile(128 - j, (128, 1))]).astype(np.float32)
    z0 = np.ones((2, 128, 128), np.float32); z0[0, :, 0] = 0; z0[1, :, 127] = 0
    c["c_z0"] = z0
    sw = np.zeros((128, 128), np.float32)
    for p in range(64):
        sw[64 + p, p] = -1.0; sw[p, 64 + p] = 1.0
    c["c_swap"] = sw
    sel = np.zeros((128, 128), np.float32)
    for go in range(2):
        for p in range(64):
            sel[go * 64 + p, go * 64 + p] = 1.0
    c["c_sel"] = sel
    return c


def _win_perm():
    idx = []
    def sw(base):
        out = []
        for h in range(4):
            out += list(range(base + h * 64 + 32, base + h * 64 + 64)) + list(range(base + h * 64, base + h * 64 + 32))
        return out
    idx += list(range(0, 256)) + sw(0) + list(range(256, 512)) + sw(256) + list(range(512, 768)) + list(range(768, 1024))
    idx += list(range(1024, 2560))
    return np.array(idx)


_CACHE = {}


def run(inputs, Tp, Ts, L, ncores, flags=None):
    key = (Tp, Ts, L, ncores, tuple(sorted((flags or {}).items())))
    if key not in _CACHE:
        _CACHE[key] = Builder(Tp, Ts, L, flags).build()
    nc = _CACHE[key]
    f = lambda a: np.ascontiguousarray(np.asarray(a, dtype=np.float32))
    shared = {k: f(v) for k, v in inputs.items() if k not in ("x_prompt", "x_sample", "w_in")}
    shared["w_in"] = np.ascontiguousarray(f(inputs["w_in"])[:, :, _win_perm()])
    shared.update(_consts(max(Tp, Ts)))
    xp, xs = f(inputs["x_prompt"]), f(inputs["x_sample"])
    nb_s = xs.shape[0]
    in_maps = []
    for c in range(ncores):
        m = dict(shared)
        m["x_prompt"] = xp[c % xp.shape[0]]
        m["x_sample"] = xs[c % nb_s]
        in_maps.append(m)
    res = run_bass_kernel_spmd(nc, in_maps, core_ids=list(range(ncores)))
    yp = np.stack([res.results[c]["y_prompt"] for c in range(xp.shape[0])])
    ys = np.stack([res.results[c]["y_sample"] for c in range(nb_s)])
    return yp.astype(np.float32), ys.astype(np.float32)


def kernel(**inputs):
    return run(inputs, 8192, 4096, 4, 8)
```

```python
import math
import numpy as np
import concourse.bass as bass
import concourse.mybir as mybir
from concourse.bass_utils import run_bass_kernel_spmd
from contextlib import ExitStack

F32 = mybir.dt.float32
BF16 = mybir.dt.bfloat16
ALU = mybir.AluOpType
AF = mybir.ActivationFunctionType
AX = mybir.AxisListType

D = 1024
NH = 4
HD = 64
PIN = 3072
DFF = 2816
YW = 1536
CDEC = -math.exp(-0.5)
ENGS = ("pe", "act", "dve", "pool", "sp")


class Op:
    __slots__ = ("eng", "fn", "deps", "signal", "idx", "semkey", "is_dma", "emitted", "val")

    def __init__(self, eng, fn, semkey, is_dma):
        self.eng = eng
        self.fn = fn
        self.deps = []
        self.signal = is_dma
        self.idx = -1
        self.semkey = semkey
        self.is_dma = is_dma
        self.emitted = False
        self.val = None


class Sched:
    def __init__(self, nc, es):
        self.nc = nc
        self.es = es
        self.ops = {e: [] for e in ENGS}
        self.semidx = {}
        self.sems = {}
        self.res = {}
        self.seen = {e: {} for e in ENGS}
        self.sigcount = {}
        self.last_emitted = {}
        self.last_op = {}
        self.nops = 0
        self.pe_skip = False

    def _dep(self, op, d):
        if d is None or d is op:
            return
        if self.pe_skip and d.eng == "pe" and op.eng == "pe" and not d.is_dma:
            return
        if d.emitted and not d.signal:
            d = self.last_emitted[d.semkey]
        seen = self.seen[op.eng]
        if seen.get(d.semkey, -1) >= d.idx:
            return
        seen[d.semkey] = d.idx
        d.signal = True
        op.deps.append(d)

    def op(self, eng, fn, reads=(), writes=(), dma=None, extra=()):
        is_dma = dma is not None
        semkey = ("dma", dma) if is_dma else ("eng", eng)
        o = Op(eng, fn, semkey, is_dma)
        o.idx = self.semidx.get(semkey, 0)
        self.semidx[semkey] = o.idx + 1
        for r in reads:
            st = self.res.get(r)
            if st is not None:
                self._dep(o, st[0])
        for w in writes:
            st = self.res.get(w)
            if st is not None:
                self._dep(o, st[0])
                for rd in st[1]:
                    self._dep(o, rd)
        for d in extra:
            self._dep(o, d)
        for r in reads:
            st = self.res.setdefault(r, [None, []])
            st[1].append(o)
        for w in writes:
            self.res[w] = [o, []]
        self.ops[eng].append(o)
        self.last_op[semkey] = o
        self.nops += 1
        return o

    def barrier(self):
        lasts = list(self.last_op.values())
        for e in ENGS:
            self.op(e, lambda en: en.nop(), extra=lasts)

    def flush(self, final=()):
        nc = self.nc
        for key in self.semidx:
            if key not in self.sems:
                nm = ("s_" + "_".join(map(str, key)))[:48]
                self.sems[key] = self.es.enter_context(nc.semaphore(nm))
        for e in ENGS:
            if self.ops[e]:
                for o in reversed(self.ops[e]):
                    if not o.is_dma:
                        o.signal = True
                        break
        for e in ENGS:
            for o in self.ops[e]:
                if o.is_dma:
                    o.val = 16 * (o.idx + 1)
                elif o.signal:
                    c = self.sigcount.get(o.semkey, 0) + 1
                    self.sigcount[o.semkey] = c
                    o.val = c
        sems = self.sems
        pending = self.ops

        def run(en, key):
            for o in pending[key]:
                for d in o.deps:
                    en.wait_ge(sems[d.semkey], d.val)
                ins = o.fn(en)
                if o.signal:
                    ins.then_inc(sems[o.semkey], 16 if o.is_dma else 1)
            if key == "sp":
                for d in final:
                    en.wait_ge(sems[d.semkey], d.val)

        with nc.Block() as block:
            @block.tensor
            def _(en):
                run(en, "pe")

            @block.scalar
            def _(en):
                run(en, "act")

            @block.vector
            def _(en):
                run(en, "dve")

            @block.gpsimd
            def _(en):
                run(en, "pool")

            @block.sync
            def _(en):
                run(en, "sp")
        for e in ENGS:
            for o in self.ops[e]:
                o.emitted = True
                if o.signal and not o.is_dma:
                    self.last_emitted[o.semkey] = o
            self.ops[e] = []


class Tl:
    def __init__(self, t, key):
        self.t = t
        self.k = key

    def __getitem__(self, idx):
        return self.t[idx]


def bc(ap, shape):
    return ap.to_broadcast(list(shape))


class Builder:
    def __init__(self, Tp, Ts, depth, flags=None):
        self.Tp, self.Ts, self.L = Tp, Ts, depth
        self.seqs = [(0, Tp), (Tp, Ts)]
        self.TT = Tp + Ts
        self.flags = flags or {}
        self.nc = bass.Bass("TRN2", target_bir_lowering=False)

    def sb(self, es, name, shape, dt=F32):
        self._uid = getattr(self, "_uid", 0) + 1
        name = f"{name}_{self._uid}"
        return Tl(es.enter_context(self.nc.sbuf_tensor(name, list(shape), dt)), name)

    def ps(self, es, name, shape, dt=F32):
        self._uid = getattr(self, "_uid", 0) + 1
        name = f"{name}_{self._uid}"
        return Tl(es.enter_context(self.nc.psum_tensor(name, list(shape), dt)), name)

    def O(self, eng, fn, reads=(), writes=(), dma=None):
        return self.S.op(eng, fn, [r.k if isinstance(r, Tl) else r for r in reads],
                         [w.k if isinstance(w, Tl) else w for w in writes], dma=dma)

    def load(self, dst, dst_ap, src_ap, key, reads=(), eng="sp"):
        return self.O(eng, lambda e, o=dst_ap, i=src_ap: e.dma_start(out=o, in_=i, allow_slow_non_contiguous=True), reads=reads, writes=[dst], dma=key)

    def store(self, src, dst_ap, src_ap, key, writes=(), eng="sp"):
        return self.O(eng, lambda e, o=dst_ap, i=src_ap: e.dma_start(out=o, in_=i), reads=[src], writes=writes, dma=key)

    def mm(self, out_t, out_ap, l_t, l_ap, r_t, r_ap, start=True, stop=True):
        return self.O("pe", lambda e, o=out_ap, a=l_ap, b=r_ap, s0=start, s1=stop: e.matmul(o, a, b, start=s0, stop=s1),
                      reads=[l_t, r_t] + ([] if start else [out_t]), writes=[out_t])

    def tr(self, out_t, out_ap, in_t, in_ap):
        idn = self.identf if in_ap.dtype == F32 else self.identb
        return self.O("pe", lambda e, o=out_ap, a=in_ap, i=idn: e.transpose(o, a, i[:]), reads=[in_t, idn], writes=[out_t])

    def act(self, out_t, out_ap, in_t, in_ap, func, scale=1.0, bias=None, accum=None, extra_r=(), extra_w=()):
        def fn(e, o=out_ap, i=in_ap, f=func, s=scale, b=bias, a=accum):
            kw = {}
            if b is not None:
                kw["bias"] = b
            if a is not None:
                kw["accum_out"] = a
            return e.activation(out=o, in_=i, func=f, scale=s, **kw)
        return self.O("act", fn, reads=[in_t] + list(extra_r), writes=[out_t] + list(extra_w))

    def tt(self, eng, out_t, out_ap, a_t, a_ap, b_t, b_ap, op):
        return self.O(eng, lambda e, o=out_ap, a=a_ap, b=b_ap, p=op: e.tensor_tensor(out=o, in0=a, in1=b, op=p),
                      reads=[a_t, b_t], writes=[out_t])

    def ts(self, eng, out_t, out_ap, a_t, a_ap, s1, s2, op0, op1=None, extra_r=()):
        def fn(e, o=out_ap, a=a_ap, x=s1, y=s2, p0=op0, p1=op1):
            if p1 is None:
                return e.tensor_single_scalar(out=o, in_=a, scalar=x, op=p0)
            return e.tensor_scalar(out=o, in0=a, scalar1=x, scalar2=y, op0=p0, op1=p1)
        return self.O(eng, fn, reads=[a_t] + list(extra_r), writes=[out_t])

    def stt(self, eng, out_t, out_ap, a_t, a_ap, scalar, b_t, b_ap, op0, op1, extra_r=()):
        eng = "dve"
        return self.O(eng, lambda e, o=out_ap, a=a_ap, s=scalar, b=b_ap, p0=op0, p1=op1:
                      e.scalar_tensor_tensor(out=o, in0=a, scalar=s, in1=b, op0=p0, op1=p1),
                      reads=[a_t, b_t] + list(extra_r), writes=[out_t])

    def cp(self, eng, out_t, out_ap, in_t, in_ap):
        if eng == "act":
            return self.act(out_t, out_ap, in_t, in_ap, AF.Copy)
        return self.O(eng, lambda e, o=out_ap, i=in_ap: e.tensor_copy(out=o, in_=i), reads=[in_t], writes=[out_t])

    def memset(self, eng, t, ap, v):
        return self.O(eng, lambda e, o=ap, x=v: e.memset(o, x), writes=[t])

    def load_w_bf16(self, es, name, dram_ap, K, C, stg, dst=None, dst_koff=0):
        if dst is None:
            dst = self.sb(es, name, [128, K, C], BF16)
        n = 0
        for k in range(K):
            for c0 in range(0, C, 1024):
                c1 = min(C, c0 + 1024)
                slot = self._stg_i % len(stg)
                st = stg[slot]
                self._stg_i += 1
                self.load(st, st[:, 0:c1 - c0], dram_ap[k * 128:(k + 1) * 128, c0:c1], f"stg{slot}")
                eng = ("pool", "dve", "act")[n % 3]
                n += 1
                self.cp(eng, dst, dst[:, dst_koff + k, c0:c1], st, st[:, 0:c1 - c0])
        return dst

    def bcast_load(self, es, name, dram_vec_ap, C, key="bcl"):
        t = self.sb(es, name, [128, C])
        self.load(t, t[:], dram_vec_ap.partition_broadcast(128), "setup")
        return t

    def rows(self, seq, i):
        off, T = self.seqs[seq]
        return off + i * 128

    def prow(self, seq, i):
        off, T = self.seqs[seq]
        return off + 2 * seq + 1 + i * 128

    def tiles(self, d=0):
        out = []
        for s, (off, T) in enumerate(self.seqs):
            n = T // 128
            rng = range(n) if d == 0 else range(n - 1, -1, -1)
            for j, i in enumerate(rng):
                out.append((s, i, j == 0))
        return out

    def build(self):
        nc = self.nc
        Tp, Ts, L, TT = self.Tp, self.Ts, self.L, self.TT
        di = {}

        def inp(name, shape):
            di[name] = nc.dram_tensor(name, list(shape), F32, kind="ExternalInput").ap()
            return di[name]
        inp("x_prompt", [Tp, D]); inp("x_sample", [Ts, D])
        inp("norm1_g", [L, D]); inp("w_in", [L, D, PIN])
        inp("ret_gn_w", [L, 256]); inp("ret_gn_b", [L, 256])
        inp("rwkv_mu", [L, 1024]); inp("rwkv_w0", [L, 2, 256]); inp("rwkv_w2", [L, 2, 64, 256])
        inp("rwkv_a0", [L, 2, 256]); inp("rwkv_a2", [L, 2, 64, 256]); inp("rwkv_g2", [L, 128, 256])
        inp("rwkv_k_k", [L, 256]); inp("rwkv_k_a", [L, 256]); inp("rwkv_r_k", [L, 256])
        inp("rwkv_gn_w", [L, 256]); inp("rwkv_gn_b", [L, 256])
        inp("s5_lam_re", [L, 2, 32, 64]); inp("s5_lam_im", [L, 2, 32, 64]); inp("s5_log_step", [L, 2, 32])
        inp("s5_b_re", [L, 2, 32, 64, 16]); inp("s5_b_im", [L, 2, 32, 64, 16])
        inp("s5_c_re", [L, 2, 32, 16, 64]); inp("s5_c_im", [L, 2, 32, 16, 64])
        inp("s5_d", [L, 512]); inp("s5_glu_w", [L, 512, 512]); inp("s5_glu_b", [L, 512])
        inp("w_out", [L, D, D]); inp("norm2_g", [L, D])
        inp("ffn_w_gate", [L, D, DFF]); inp("ffn_w_up", [L, D, DFF]); inp("ffn_w_down", [L, DFF, D])
        inp("final_g", [D])
        inp("c_ident", [128, 128]); inp("c_rot", [max(Tp, Ts), 256])
        inp("c_retmask", [128, 512]); inp("c_retqk", [2, 128, 8]); inp("c_retgc", [128, 256])
        inp("c_cs", [2, 128, 128]); inp("c_ones", [128, 128]); inp("c_msi", [2, 128, 256]); inp("c_mst", [2, 128, 128])
        inp("c_j1", [2, 128, 128]); inp("c_z0", [2, 128, 128]); inp("c_swap", [128, 128]); inp("c_sel", [128, 128])
        self.di = di
        yp = nc.dram_tensor("y_prompt", [Tp, D], F32, kind="ExternalOutput").ap()
        ys = nc.dram_tensor("y_sample", [Ts, D], F32, kind="ExternalOutput").ap()
        self.yout = [yp, ys]
        self.XS = nc.dram_tensor("XS", [TT, D], F32, kind="Internal").ap()
        self.PS = nc.dram_tensor("PSC", [TT + 4, PIN], F32, kind="Internal").ap()
        self.YD = nc.dram_tensor("YD", [2, TT, YW], F32, kind="Internal").ap()

        with ExitStack() as ges:
            self.S = Sched(nc, ges)
            self._stg_i = 0
            self.identf = self.sb(ges, "identf", [128, 128])
            self.identb = self.sb(ges, "identb", [128, 128], BF16)
            self.load(self.identf, self.identf[:], di["c_ident"][:, :], "setup")
            self.cp("dve", self.identb, self.identb[:], self.identf, self.identf[:])
            zero = self.sb(ges, "zero", [128, PIN])
            self.memset("pool", zero, zero[:], 0.0)
            self.O("sp", lambda e: e.dma_start(out=self.XS[0:Tp, :], in_=di["x_prompt"][:, :]), writes=["XSall"], dma="setup")
            self.O("sp", lambda e: e.dma_start(out=self.XS[Tp:TT, :], in_=di["x_sample"][:, :]), writes=["XSall"], dma="setup")
            for s, (off, T) in enumerate(self.seqs):
                for r in (off + 2 * s, off + 2 * s + T + 1):
                    self.store(zero, self.PS[r:r + 1, :], zero[0:1, :], "setup", writes=["PSall"])
            self.S.barrier()
            self.S.flush()
            final = []
            for l in range(L):
                self.phase_P(l)
                for d in range(2):
                    if not self.flags.get("no_ret"):
                        self.phase_ret(l, d)
                    if not self.flags.get("no_rwkv"):
                        self.phase_rwkv(l, d)
                    if not self.flags.get("no_s5"):
                        self.phase_s5(l, d)
                self.phase_O1(l)
                final = self.phase_O2(l, last=(l == L - 1))
            self.S.barrier()
            self.S.flush(final=final)
        return nc

    def end_phase(self):
        self.S.barrier()
        self.S.flush()

    def rmsnorm(self, xt, G, h_out, junk, ss, eng2="dve"):
        self.act(junk, junk[:], xt, xt[:], AF.Square, accum=ss[:, 0:1], extra_w=[ss])
        self.ts("dve", ss, ss[:, 1:2], ss, ss[:, 0:1], 1.0 / D, 1e-6, ALU.mult, ALU.add)
        self.act(ss, ss[:, 2:3], ss, ss[:, 1:2], AF.Sqrt)
        self.O("dve", lambda e, o=ss[:, 3:4], i=ss[:, 2:3]: e.reciprocal(out=o, in_=i), reads=[ss], writes=[ss])
        self.stt(eng2, h_out, h_out[:], xt, xt[:], ss[:, 3:4], G, G[:], ALU.mult, ALU.mult, extra_r=[ss])

    def transpose_to(self, src, ncol_tiles, psT, dstT, evac_engs=("act", "dve")):
        n = 0
        for g0 in range(0, ncol_tiles, 4):
            g1 = min(ncol_tiles, g0 + 4)
            pt = psT[(g0 // 4) % len(psT)]
            for c in range(g0, g1):
                self.tr(pt, pt[:, c - g0, :], src, src[:, c * 128:(c + 1) * 128])
            self.cp(evac_engs[n % len(evac_engs)], dstT, dstT[:, g0:g1, :], pt, pt[:, 0:g1 - g0, :])
            n += 1

    def pipeline(self, tiles, stages):
        K, N = len(stages), len(tiles)
        for n in range(N + K - 1):
            gens = []
            for k, st in enumerate(stages):
                j = n - k
                if 0 <= j < N:
                    s, i, first = tiles[j]
                    gens.append(st(j, s, i, first))
            while gens:
                alive = []
                for g in gens:
                    try:
                        next(g)
                        alive.append(g)
                    except StopIteration:
                        pass
                gens = alive

    def phase_P(self, l):
        di = self.di
        self.S.pe_skip = True
        with ExitStack() as es:
            stg = [self.sb(es, f"stg{i}", [128, 1024]) for i in range(3)]
            Wb = self.load_w_bf16(es, "Wb", di["w_in"][l], 8, PIN, stg)
            G1 = self.bcast_load(es, "G1", di["norm1_g"][l], D)
            xb = [self.sb(es, f"xb{i}", [128, D]) for i in range(2)]
            junk = self.sb(es, "junk", [128, D], BF16)
            ss = [self.sb(es, f"ss{i}", [128, 4]) for i in range(2)]
            h = [self.sb(es, f"h{i}", [128, D]) for i in range(2)]
            hT = [self.sb(es, f"hT{i}", [128, 8, 128], BF16) for i in range(2)]
            pt = [self.sb(es, f"pt{i}", [128, PIN]) for i in range(2)]
            psT = [self.ps(es, f"psT{i}", [128, 4, 128]) for i in range(2)]
            psP = [self.ps(es, f"psP{i}", [128, 512]) for i in range(4)]
            self.S.barrier()
            cnt = [0]

            def s0(j, s, i, first):
                b = j % 2
                r0 = self.rows(s, i)
                self.load(xb[b], xb[b][:], self.XS[r0:r0 + 128, :], f"l0{b}", reads=["XSall", ("XS", r0)])
                self.rmsnorm(xb[b], G1, h[b], junk, ss[b])
                yield
                self.transpose_to(h[b], 8, psT, hT[b])
                yield

            def s1(j, s, i, first):
                b = j % 2
                for cb in range(PIN // 512):
                    n = cnt[0]
                    cnt[0] += 1
                    pp = psP[n % 4]
                    for k in range(8):
                        self.mm(pp, pp[:], hT[b], hT[b][:, k, :], Wb, Wb[:, k, cb * 512:(cb + 1) * 512], start=(k == 0), stop=(k == 7))
                    self.cp(("act", "dve")[n % 2], pt[b], pt[b][:, cb * 512:(cb + 1) * 512], pp, pp[:])
                    if cb % 3 == 2:
                        yield
                pr = self.prow(s, i)
                self.store(pt[b], self.PS[pr:pr + 128, :], pt[b][:], f"s0{b}", writes=[("PS", s, i)])
                yield
            self.pipeline(self.tiles(), [s0, s1])
            self.end_phase()

    def phase_ret(self, l, d):
        di = self.di
        self.S.pe_skip = not self.flags.get("noskip_ret")
        with ExitStack() as es:
            sb = lambda n, sh, dt=F32: self.sb(es, n, sh, dt)
            mask = sb("rmask", [128, 512]); self.load(mask, mask[:], di["c_retmask"][:, :], "setup")
            qk = sb("rqk", [128, 8]); self.load(qk, qk[:], di["c_retqk"][d], "setup")
            gc = sb("rgc", [128, 256]); self.load(gc, gc[:], di["c_retgc"][:, :], "setup")
            R = sb("R", [128, 256]); Rb = sb("Rb", [128, 256], BF16)
            pr = [sb(f"pr{i}", [128, 1280]) for i in range(2)]
            rot = [sb(f"rot{i}", [128, 256]) for i in range(2)]
            qh = sb("qh", [128, 512]); t1 = sb("rt1", [128, 512])
            qkT = sb("qkT", [128, 4, 128], BF16)
            qz = sb("qz", [128, 4, 128], BF16)
            self.memset("pool", qz, qz[:], 0.0)
            sT = sb("sT", [128, 4, 128], BF16)
            vb = sb("vb", [128, 256], BF16); kdb = sb("kdb", [128, 256], BF16)
            tmp = sb("rtmp", [128, 256]); yo = [sb(f"ryo{i}", [128, 256]) for i in range(2)]
            psT = self.ps(es, "rpsT", [128, 4, 128]); psS = self.ps(es, "rpsS", [128, 4, 128])
            psO = self.ps(es, "rpsO", [128, 256]); psC = self.ps(es, "rpsC", [128, 256]); psR = self.ps(es, "rpsR", [128, 256])
            self.S.barrier()
            for it, (s, i, first) in enumerate(self.tiles(d)):
                b = it % 2
                p0 = self.prow(s, i)
                if first:
                    self.memset("pool", R, R[:], 0.0)
                    self.memset("pool", Rb, Rb[:], 0.0)
                self.load(pr[b], pr[b][:], self.PS[p0:p0 + 128, 0:1280], f"l0{b}", reads=[("PS", s, i)])
                self.load(rot[b], rot[b][:], di["c_rot"][i * 128:(i + 1) * 128, :], f"l1{b}")
                P, RT = pr[b], rot[b]
                for j, (c0, tb) in enumerate(((0, 0), (512, 128))):
                    o = j * 256
                    v4 = lambda ap: ap.rearrange("p (a b) -> p a b", a=4)
                    cosb = bc(RT[:, tb:tb + 64].unsqueeze(1), [128, 4, 64])
                    sinb = bc(RT[:, tb + 64:tb + 128].unsqueeze(1), [128, 4, 64])
                    self.tt("dve", qh, v4(qh[:, o:o + 256]), P, v4(P[:, c0:c0 + 256]), RT, cosb, ALU.mult)
                    self.tt("pool", t1, v4(t1[:, o:o + 256]), P, v4(P[:, c0 + 256:c0 + 512]), RT, sinb, ALU.mult)
                    self.tt("dve", qh, qh[:, o:o + 256], qh, qh[:, o:o + 256], t1, t1[:, o:o + 256], ALU.add)
                for c in range(4):
                    self.tr(psT, psT[:, c, :], qh, qh[:, c * 128:(c + 1) * 128])
                self.cp("act", qkT, qkT[:, 2:4, :], psT, psT[:, 2:4, :])
                self.cp("act", qz, qz[0:64, 0:4:2, :], psT, psT[0:64, 0:2, :])
                self.cp("dve", qz, qz[64:128, 1:4:2, :], psT, psT[64:128, 0:2, :])
                self.cp("act", vb, vb[:], P, P[:, 1024:1280])
                kv = qh[:, 256:512].rearrange("p (a b) -> p a b", a=4)
                self.tt("dve", kdb, kdb[:].rearrange("p (a b) -> p a b", a=4), qh, kv, qk, bc(qk[:, 4:8].unsqueeze(2), [128, 4, 64]), ALU.mult)
                hp = lambda hh: slice((hh % 2) * 64, (hh % 2) * 64 + 64)
                if d == 0:
                    for hh in range(4):
                        self.mm(psS, psS[:, hh, :], qkT, qkT[:, 2 + hh // 2, :], qz, qz[:, hh, :])
                    self.tt("dve", sT, sT[:], psS, psS[:], mask, mask[:].rearrange("p (a b) -> p a b", a=4), ALU.mult)
                    for hh in range(4):
                        self.mm(psO, psO[:, hh * 64:(hh + 1) * 64], sT, sT[:, hh, :], vb, vb[:, hh * 64:(hh + 1) * 64])
                    self.cp("act", tmp, tmp[:], psO, psO[:])
                for hh in range(4):
                    self.mm(psC, psC[:, hh * 64:(hh + 1) * 64], qz, qz[:, hh, :], Rb,
                            Rb[:, (hh // 2) * 128 + (hh % 2) * 64:(hh // 2) * 128 + (hh % 2) * 64 + 64])
                y = yo[b]
                self.tt("dve", y, y[:].rearrange("p (a b) -> p a b", a=4), psC, psC[:].rearrange("p (a b) -> p a b", a=4),
                        qk, bc(qk[:, 0:4].unsqueeze(2), [128, 4, 64]), ALU.mult)
                if d == 0:
                    self.tt("pool", y, y[:], y, y[:], tmp, tmp[:], ALU.add)
                r0 = self.rows(s, i)
                self.store(y, self.YD[d, r0:r0 + 128, 0:256], y[:], f"s0{b}", writes=[("YDr", d, s, i)])
                for pr_ in range(2):
                    self.mm(psR, psR[:, pr_ * 128:(pr_ + 1) * 128], kdb, kdb[:, pr_ * 128:(pr_ + 1) * 128], vb, vb[:, pr_ * 128:(pr_ + 1) * 128])
                self.tt("dve", R, R[:], R, R[:], gc, gc[:], ALU.mult)
                self.tt("dve", R, R[:], R, R[:], psR, psR[:], ALU.add)
                self.cp("act", Rb, Rb[:], R, R[:])
            self.end_phase()

    def phase_rwkv(self, l, d):
        di = self.di
        self.S.pe_skip = False
        with ExitStack() as es:
            sb = lambda n, sh, dt=F32: self.sb(es, n, sh, dt)
            v4 = lambda ap: ap.rearrange("p (a b) -> p a b", a=4)
            CS = sb("wCS", [128, 128]); self.load(CS, CS[:], di["c_cs"][d], "setup")
            ON = sb("wON", [128, 128]); self.load(ON, ON[:], di["c_ones"][:, :], "setup")
            MSI = sb("wMSI", [128, 256]); self.load(MSI, MSI[:], di["c_msi"][d], "setup")
            MST = sb("wMST", [128, 128]); self.load(MST, MST[:], di["c_mst"][d], "setup")
            MU = self.bcast_load(es, "wMU", di["rwkv_mu"][l], 1024)
            W0 = self.bcast_load(es, "wW0", di["rwkv_w0"][l, d], 256)
            A0 = self.bcast_load(es, "wA0", di["rwkv_a0"][l, d], 256)
            KK = self.bcast_load(es, "wKK", di["rwkv_k_k"][l], 256)
            KA = self.bcast_load(es, "wKA", di["rwkv_k_a"][l], 256)
            RK = self.bcast_load(es, "wRK", di["rwkv_r_k"][l], 256)
            stg = sb("wstg", [128, 256])
            LW = sb("wLW", [128, 256], BF16)
            self.load(stg, stg[0:64, :], di["rwkv_w2"][l, d], "setup")
            self.load(stg, stg[64:128, :], di["rwkv_a2"][l, d], "setup")
            stg2 = sb("wstg2", [128, 256])
            G2 = sb("wG2", [128, 256], BF16)
            self.load(stg2, stg2[:], di["rwkv_g2"][l], "setup")
            self.S.barrier()
            self.cp("dve", LW, LW[:], stg, stg[:])
            self.cp("dve", G2, G2[:], stg2, stg2[:])
            pm = [sb(f"wpm{i}", [128, 1024]) for i in range(2)]
            pc = [sb(f"wpc{i}", [128, 1024]) for i in range(2)]
            pn = [sb(f"wpn{i}", [128, 1024]) for i in range(2)]
            pp = sb("wpp", [128, 1024]); sh = sb("wsh", [128, 1024])
            ldT = sb("wldT", [128, 2, 128], BF16)
            targ = sb("wtarg", [128, 512]); sg = sb("wsg", [128, 256]); asg = sb("wasg", [128, 256])
            kk = sb("wkk", [128, 256]); kkr = sb("wkkr", [128, 256]); junk = sb("wjunk", [128, 64])
            ssq = sb("wssq", [128, 16]); kd = sb("wkd", [128, 256]); bv = sb("wbv", [128, 256])
            t1 = sb("wt1", [128, 256]); t2 = sb("wt2", [128, 256])
            yo = [sb(f"wyo{i}", [128, 768]) for i in range(2)]
            lws = sb("wlws", [128, 256]); lx = sb("wlx", [128, 256]); lt = sb("wlt", [128, 256])
            Wt = sb("wWt", [128, 256]); Wn = sb("wWn", [128, 256]); Wx = sb("wWx", [128, 256]); Wh = sb("wWh", [128, 256])
            X4 = sb("wX4", [128, 1024])
            Atb = sb("wAtb", [128, 256], BF16); Bhb = sb("wBhb", [128, 256], BF16); Khb = sb("wKhb", [128, 256], BF16)
            vb = sb("wvb", [128, 256], BF16)
            XT = sb("wXT", [128, 2, 4, 128], BF16)
            WCk = sb("wWCk", [128, 2])
            Nb = sb("wNb", [128, 4, 128], BF16); NTb = sb("wNTb", [128, 4, 128], BF16)
            Pb = [sb(f"wPb{i}", [128, 4, 128], BF16) for i in range(2)]
            PTb = [sb(f"wPTb{i}", [128, 4, 128], BF16) for i in range(2)]
            Mf = sb("wMf", [128, 4, 128]); Mb = sb("wMb", [128, 4, 128], BF16)
            Arb = sb("wArb", [128, 4, 128], BF16); Aak = sb("wAak", [128, 4, 128], BF16); Ark = sb("wArk", [128, 4, 128], BF16)
            Xb = sb("wXb", [128, 256], BF16); Ut = sb("wUt", [128, 256]); Ub = sb("wUb", [128, 256], BF16)
            WtT = sb("wWtT", [128, 4, 128], BF16)
            H = sb("wH", [128, 256]); Hb = sb("wHb", [128, 256], BF16)
            identq = sb("widq", [128, 4, 128])
            for hh in range(4):
                self.cp("pool", identq, identq[:, hh, :], self.identf, self.identf[:])
            psA = self.ps(es, "wpsA", [128, 512]); psB = self.ps(es, "wpsB", [128, 512])
            psC = self.ps(es, "wpsC", [128, 512]); psD = self.ps(es, "wpsD", [128, 512])
            psE = self.ps(es, "wpsE", [128, 512]); psF = self.ps(es, "wpsF", [128, 512])
            psG = self.ps(es, "wpsG", [128, 512]); psH = self.ps(es, "wpsH", [128, 512])
            hp = lambda hh: slice((hh % 2) * 64, (hh % 2) * 64 + 64)
            for it, (s, i, first) in enumerate(self.tiles(d)):
                b = it % 2
                p0 = self.prow(s, i)
                if first:
                    self.memset("pool", H, H[:], 0.0)
                    self.memset("pool", Hb, Hb[:], 0.0)
                self.load(pm[b], pm[b][:], self.PS[p0 - 1:p0 + 127, 1536:2560], f"l0{b}", reads=[("PS", s, i), ("PS", s, i - 1), "PSall"])
                self.load(pc[b], pc[b][:], self.PS[p0:p0 + 128, 1536:2560], f"l1{b}", reads=[("PS", s, i)])
                self.load(pn[b], pn[b][:], self.PS[p0 + 1:p0 + 129, 1536:2560], f"l2{b}", reads=[("PS", s, i), ("PS", s, i + 1), "PSall"])
                self.tt("pool", sh, sh[:], pm[b], pm[b][:], pn[b], pn[b][:], ALU.add)
                self.stt("dve", sh, sh[:], sh, sh[:], 0.5, pc[b], pc[b][:], ALU.mult, ALU.subtract)
                self.tt("pool", sh, sh[:], sh, sh[:], MU, MU[:], ALU.mult)
                self.tt("dve", pp, pp[:], pc[b], pc[b][:], sh, sh[:], ALU.add)
                r_, k_, v_ = pp[:, 0:256], pp[:, 256:512], pp[:, 512:768]
                Y = yo[b]
                self.tr(psA, psA[:, 0:128], pp, pp[:, 768:896])
                self.tr(psA, psA[:, 128:256], pp, pp[:, 896:1024])
                self.act(ldT, ldT[0:64, 0, :], psA, psA[0:64, 0:128], AF.Tanh)
                self.cp("dve", ldT, ldT[64:128, 0, :], psA, psA[64:128, 0:128])
                self.mm(psB, psB[:, 0:256], ldT, ldT[0:64, 0, :], LW, LW[0:64, :])
                self.mm(psB, psB[:, 256:512], ldT, ldT[64:128, 0, :], LW, LW[64:128, :])
                self.tt("dve", targ, targ[:, 0:256], psB, psB[:, 0:256], W0, W0[:], ALU.add)
                self.tt("dve", targ, targ[:, 256:512], psB, psB[:, 256:512], A0, A0[:], ALU.add)
                self.act(sg, sg[:], targ, targ[:, 0:256], AF.Sigmoid)
                self.act(asg, asg[:], targ, targ[:, 256:512], AF.Sigmoid)
                if d == 0:
                    self.act(ldT, ldT[:, 1, :], psA, psA[:, 128:256], AF.Sigmoid)
                    self.mm(psC, psC[:, 0:256], ldT, ldT[:, 1, :], G2, G2[:])
                    self.cp("act", Y, Y[:, 512:768], psC, psC[:, 0:256])
                self.tt("pool", kkr, kkr[:], pp, k_, KK, KK[:], ALU.mult)
                for hh in range(4):
                    self.act(junk, junk[:], kkr, kkr[:, hh * 64:(hh + 1) * 64], AF.Square, accum=ssq[:, hh:hh + 1], extra_w=[ssq])
                self.act(ssq, ssq[:, 4:8], ssq, ssq[:, 0:4], AF.Sqrt)
                self.ts("dve", ssq, ssq[:, 8:12], ssq, ssq[:, 4:8], 1e-12, None, ALU.max)
                self.O("dve", lambda e, o=ssq[:, 12:16], i_=ssq[:, 8:12]: e.reciprocal(out=o, in_=i_), reads=[ssq], writes=[ssq])
                self.tt("dve", kk, v4(kk[:]), kkr, v4(kkr[:]), ssq, bc(ssq[:, 12:16].unsqueeze(2), [128, 4, 64]), ALU.mult)
                self.stt("pool", t1, t1[:], asg, asg[:], -1.0, KA, KA[:], ALU.add, ALU.mult)
                self.stt("dve", kd, kd[:], t1, t1[:], 1.0, pp, k_, ALU.add, ALU.mult)
                self.tt("pool", bv, bv[:], kk, kk[:], asg, asg[:], ALU.mult)
                self.tt("dve", t1, t1[:], pp, r_, kd, kd[:], ALU.mult)
                self.tt("pool", t1, t1[:], t1, t1[:], RK, RK[:], ALU.mult)
                self.O("dve", lambda e, o=ssq[:, 0:4], i_=v4(t1[:]): e.tensor_reduce(out=o, in_=i_, axis=AX.X, op=ALU.add), reads=[t1], writes=[ssq])
                self.tt("dve", Y, v4(Y[:, 256:512]), pp, v4(v_), ssq, bc(ssq[:, 0:4].unsqueeze(2), [128, 4, 64]), ALU.mult)
                self.cp("act", vb, vb[:], pp, v_)
                self.mm(psC, psC[:, 256:512], CS, CS[:], sg, sg[:])
                self.mm(psD, psD[:, 0:256], ON, ON[:], sg, sg[:])
                for pr_ in range(2):
                    self.mm(psD, psD[:, 256 + pr_:257 + pr_], sg, sg[:, pr_ * 128:(pr_ + 1) * 128], ON, ON[:, 0:1])
                self.cp("act", lws, lws[:], psC, psC[:, 256:512])
                self.tt("dve", lx, lx[:], lws, lws[:], sg, sg[:], ALU.subtract)
                self.tt("dve", lt, lt[:], psD, psD[:, 0:256], lws, lws[:], ALU.subtract)
                self.act(Wt, Wt[:], lws, lws[:], AF.Exp, scale=CDEC)
                self.act(Wn, Wn[:], lws, lws[:], AF.Exp, scale=-CDEC)
                self.act(Wx, Wx[:], lx, lx[:], AF.Exp, scale=CDEC)
                self.act(Wh, Wh[:], lt, lt[:], AF.Exp, scale=CDEC)
                self.act(WCk, WCk[:], psD, psD[:, 256:258], AF.Exp, scale=CDEC)
                self.stt("dve", X4, X4[:, 0:256], kk, kk[:], -1.0, Wx, Wx[:], ALU.mult, ALU.mult)
                self.tt("pool", X4, X4[:, 256:512], pp, r_, Wt, Wt[:], ALU.mult)
                self.tt("dve", X4, X4[:, 512:768], bv, bv[:], Wn, Wn[:], ALU.mult)
                self.tt("pool", X4, X4[:, 768:1024], kd, kd[:], Wn, Wn[:], ALU.mult)
                self.cp("act", Atb, Atb[:], X4, X4[:, 0:256])
                self.tt("dve", Bhb, Bhb[:], bv, bv[:], Wh, Wh[:], ALU.mult)
                self.tt("pool", Khb, Khb[:], kd, kd[:], Wh, Wh[:], ALU.mult)
                for kind in range(4):
                    pt_ = psE if kind < 2 else psF
                    for pr_ in range(2):
                        c0 = kind * 256 + pr_ * 128
                        self.tr(pt_, pt_[:, ((kind % 2) * 2 + pr_) * 128:((kind % 2) * 2 + pr_ + 1) * 128], X4, X4[:, c0:c0 + 128])
                for kind in range(4):
                    pt_ = psE if kind < 2 else psF
                    self.cp(("act", "dve")[kind % 2], XT, XT[:, :, kind, :],
                            pt_, pt_[:, (kind % 2) * 256:(kind % 2) * 256 + 256].rearrange("p (a b) -> p a b", a=2))
                for hh in range(4):
                    pq = hh // 2
                    rhs = XT[hp(hh), pq, 0:2, :]
                    self.mm(psA, psA[:, hh * 128:(hh + 1) * 128], XT, XT[hp(hh), pq, 2, :], XT, XT[hp(hh), pq, 0, :])
                    self.mm(psB, psB[:, hh * 128:(hh + 1) * 128], XT, XT[hp(hh), pq, 2, :], XT, XT[hp(hh), pq, 1, :])
                    self.mm(psG, psG[:, hh * 128:(hh + 1) * 128], XT, XT[hp(hh), pq, 3, :], XT, XT[hp(hh), pq, 0, :])
                    self.mm(psH, psH[:, hh * 128:(hh + 1) * 128], XT, XT[hp(hh), pq, 3, :], XT, XT[hp(hh), pq, 1, :])
                    self.mm(psC, psC[:, hh * 128:(hh + 1) * 128], XT, XT[hp(hh), pq, 0, :], XT, XT[hp(hh), pq, 2, :])
                msb = bc(MSI[:, 0:128].unsqueeze(1), [128, 4, 128])
                mib = bc(MSI[:, 128:256].unsqueeze(1), [128, 4, 128])
                mtb = bc(MST[:].unsqueeze(1), [128, 4, 128])
                self.tt("dve", Nb, Nb[:], psA, v4(psA[:]), MSI, msb, ALU.mult)
                self.tt("dve", Mf, Mf[:], psA, v4(psA[:]), MSI, msb, ALU.mult)
                self.tt("dve", Arb, Arb[:], psB, v4(psB[:]), MSI, mib, ALU.mult)
                self.tt("dve", Aak, Aak[:], psG, v4(psG[:]), MSI, msb, ALU.mult)
                self.tt("dve", Ark, Ark[:], psH, v4(psH[:]), MSI, mib, ALU.mult)
                self.tt("dve", NTb, NTb[:], psC, v4(psC[:]), MST, mtb, ALU.mult)
                self.tt("pool", Mf, Mf[:], Mf, Mf[:], identq, identq[:], ALU.add)
                self.cp("act", Mb, Mb[:], Mf, Mf[:])
                Pc, PTc = Nb, NTb
                for st in range(6):
                    Pn, PTn = Pb[st % 2], PTb[st % 2]
                    for hh in range(4):
                        self.mm(psA, psA[:, hh * 128:(hh + 1) * 128], PTc, PTc[:, hh, :], Pc, Pc[:, hh, :])
                        self.mm(psB, psB[:, hh * 128:(hh + 1) * 128], Pc, Pc[:, hh, :], PTc, PTc[:, hh, :])
                    if st < 5:
                        self.cp("act", Pn, Pn[:], psA, v4(psA[:]))
                    self.cp("dve", PTn, PTn[:], psB, v4(psB[:]))
                    for hh in range(4):
                        self.mm(psG, psG[:, hh * 128:(hh + 1) * 128], PTn, PTn[:, hh, :], Mb, Mb[:, hh, :])
                    self.tt("dve", Mf, Mf[:], Mf, Mf[:], psG, v4(psG[:]), ALU.add)
                    self.cp("act", Mb, Mb[:], Mf, Mf[:])
                    Pc, PTc = Pn, PTn
                for hh in range(4):
                    self.mm(psH, psH[:, hh * 64:(hh + 1) * 64], Aak, Aak[:, hh, :], vb, vb[:, hh * 64:(hh + 1) * 64])
                self.cp("act", Xb, Xb[:], psH, psH[:, 0:256])
                for hh in range(4):
                    self.mm(psA, psA[:, hh * 64:(hh + 1) * 64], Mb, Mb[:, hh, :], Xb, Xb[:, hh * 64:(hh + 1) * 64])
                    self.mm(psB, psB[:, hh * 128:(hh + 1) * 128], Atb, Atb[:, (hh // 2) * 128:(hh // 2) * 128 + 128], Mb, Mb[:, hh, :])
                self.cp("act", Ut, Ut[:], psA, psA[:, 0:256])
                self.cp("dve", WtT, WtT[:], psB, v4(psB[:]))
                hcol = lambda hh: slice(hh * 64, (hh + 1) * 64)
                for hh in range(4):
                    self.mm(psC, psC[:, hcol(hh)], WtT, WtT[hp(hh), hh, :], Hb, Hb[hp(hh), hcol(hh)])
                self.tt("dve", Ut, Ut[:], Ut, Ut[:], psC, psC[:, 0:256], ALU.add)
                self.cp("act", Ub, Ub[:], Ut, Ut[:])
                for hh in range(4):
                    pq = hh // 2
                    self.mm(psD, psD[:, hcol(hh)], XT, XT[hp(hh), pq, 1, :], Hb, Hb[hp(hh), hcol(hh)], start=True, stop=False)
                    self.mm(psD, psD[:, hcol(hh)], Arb, Arb[:, hh, :], Ub, Ub[:, hcol(hh)], start=False, stop=False)
                    self.mm(psD, psD[:, hcol(hh)], Ark, Ark[:, hh, :], vb, vb[:, hcol(hh)], start=False, stop=True)
                self.cp("act", Y, Y[:, 0:256], psD, psD[:, 0:256])
                r0 = self.rows(s, i)
                ncol = 768 if d == 0 else 512
                self.store(Y, self.YD[d, r0:r0 + 128, 256:256 + ncol], Y[:, 0:ncol], f"s0{b}", writes=[("YDw", d, s, i)])
                for hh in range(4):
                    pq = hh // 2
                    self.mm(psE, psE[:, hcol(hh)], Bhb, Bhb[:, pq * 128:(pq + 1) * 128], Ub, Ub[:, hcol(hh)], start=True, stop=False)
                    self.mm(psE, psE[:, hcol(hh)], Khb, Khb[:, pq * 128:(pq + 1) * 128], vb, vb[:, hcol(hh)], start=False, stop=True)
                for pq in range(2):
                    self.ts("dve", H, H[:, pq * 128:(pq + 1) * 128], H, H[:, pq * 128:(pq + 1) * 128], WCk[:, pq:pq + 1], None, ALU.mult, extra_r=[WCk])
                self.tt("dve", H, H[:], H, H[:], psE, psE[:, 0:256], ALU.add)
                self.cp("act", Hb, Hb[:], H, H[:])
            self.end_phase()

    def phase_s5(self, l, d):
        di = self.di
        self.S.pe_skip = True
        PI = math.pi
        with ExitStack() as es:
            sb = lambda n, sh, dt=F32: self.sb(es, n, sh, dt)
            COS = sb("sCOS", [128, 32, 128]); SIN = sb("sSIN", [128, 32, 128]); RHO = sb("sRHO", [128, 32, 128])
            BP1 = sb("sBP1", [128, 32, 128], BF16); BP2 = sb("sBP2", [128, 32, 128], BF16)
            W1 = sb("sW1", [128, 512], BF16); W2 = sb("sW2", [128, 512], BF16)
            RHOF = sb("sRHOF", [128, 32]); CL = sb("sCL", [128, 32]); SL = sb("sSL", [128, 32])
            SWAP = sb("sSWAP", [128, 128]); self.load(SWAP, SWAP[:], di["c_swap"][:, :], "p0")
            psA = self.ps(es, "spsA", [128, 1024]); psB = self.ps(es, "spsB", [128, 1024])
            psY = self.ps(es, "spsY", [128, 512]); psT = self.ps(es, "spsT", [128, 4, 128]); psX = self.ps(es, "spsX", [128, 512])
            with ExitStack() as es2:
                sb2 = lambda n, sh, dt=F32: self.sb(es2, n, sh, dt)
                J1 = sb2("sJ1", [128, 128]); self.load(J1, J1[:], di["c_j1"][d], "p1")
                Z0 = sb2("sZ0", [128, 128]); self.load(Z0, Z0[:], di["c_z0"][d], "p2")
                SEL = sb2("sSEL", [128, 128]); self.load(SEL, SEL[:], di["c_sel"][:, :], "p3")
                SELb = sb2("sSELb", [128, 128], BF16); self.cp("dve", SELb, SELb[:], SEL, SEL[:])
                negpi = sb2("snegpi", [128, 1]); self.memset("pool", negpi, negpi[:], -PI)
                lre = sb2("slre", [128, 32]); lim = sb2("slim", [128, 32]); stp = sb2("sstp", [128, 32])
                lam_re_T = di["s5_lam_re"][l, d].rearrange("g p -> p g")
                lam_im_T = di["s5_lam_im"][l, d].rearrange("g p -> p g")
                for hf in range(2):
                    self.O("sp", lambda e, o=lre[hf * 64:(hf + 1) * 64, :], i_=lam_re_T: e.dma_start(out=o, in_=i_, allow_slow_non_contiguous=True), writes=[lre], dma="p4")
                    self.O("sp", lambda e, o=lim[hf * 64:(hf + 1) * 64, :], i_=lam_im_T: e.dma_start(out=o, in_=i_, allow_slow_non_contiguous=True), writes=[lim], dma="p5")
                self.load(stp, stp[:], di["s5_log_step"][l, d].partition_broadcast(128), "p6")
                self.act(stp, stp[:], stp, stp[:], AF.Exp)
                self.ts("dve", lre, lre[:], lre, lre[:], -1e-4, None, ALU.min)
                al = sb2("sal", [128, 32]); th = sb2("sth", [128, 32])
                self.tt("dve", al, al[:], lre, lre[:], stp, stp[:], ALU.mult)
                self.tt("dve", th, th[:], lim, lim[:], stp, stp[:], ALU.mult)
                self.act(RHOF, RHOF[:], al, al[:], AF.Exp)
                arg = sb2("sarg", [128, 32, 128]); arg2 = sb2("sarg2", [128, 32, 128])
                argi = sb2("sargi", [128, 32, 128], mybir.dt.int32); argf = sb2("sargf", [128, 32, 128])

                def sinred(o_t, o_ap, a_t, a_ap, w_ap, wi_ap, wf_ap, off):
                    self.ts("dve", arg2, w_ap, a_t, a_ap, 1.0 / (2 * PI), off, ALU.mult, ALU.add)
                    self.cp("dve", argi, wi_ap, arg2, w_ap)
                    self.cp("dve", argf, wf_ap, argi, wi_ap)
                    self.tt("dve", arg2, w_ap, arg2, w_ap, argf, wf_ap, ALU.subtract)
                    self.act(o_t, o_ap, arg2, w_ap, AF.Sin, scale=2 * PI)
                self.tt("dve", arg, arg[:], th, bc(th[:].unsqueeze(2), [128, 32, 128]), J1, bc(J1[:].unsqueeze(1), [128, 32, 128]), ALU.mult)
                sinred(SIN, SIN[:], arg, arg[:], arg2[:], argi[:], argf[:], 0.0)
                sinred(COS, COS[:], arg, arg[:], arg2[:], argi[:], argf[:], 0.25)
                self.tt("dve", RHO, RHO[:], RHOF, bc(RHOF[:].unsqueeze(2), [128, 32, 128]), Z0, bc(Z0[:].unsqueeze(1), [128, 32, 128]), ALU.mult)
                jl = 127 if d == 0 else 0
                self.cp("dve", CL, CL[:], COS, COS[:, :, jl])
                self.cp("dve", SL, SL[:], SIN, SIN[:, :, jl])
                lre2 = sb2("slre2", [128, 16]); lim2 = sb2("slim2", [128, 16]); stp2 = sb2("sstp2", [128, 16])
                self.load(lre2, lre2[:], di["s5_lam_re"][l, d].rearrange("(gp go) p -> (go p) gp", go=2), "p7")
                self.load(lim2, lim2[:], di["s5_lam_im"][l, d].rearrange("(gp go) p -> (go p) gp", go=2), "p8")
                ls2 = di["s5_log_step"][l, d].rearrange("(gp go) -> go gp", go=2)
                for go in range(2):
                    self.load(stp2, stp2[go * 64:(go + 1) * 64, :], ls2[go].partition_broadcast(64), "p9")
                self.act(stp2, stp2[:], stp2, stp2[:], AF.Exp)
                self.ts("dve", lre2, lre2[:], lre2, lre2[:], -1e-4, None, ALU.min)
                pw = sb2("spw", [128, 16 * 12])
                c = lambda k: pw[:, k * 16:(k + 1) * 16]
                TTm = lambda o, a, b_, op: self.tt("dve", pw, o, pw, a, pw, b_, op)
                self.tt("dve", pw, c(0), lre2, lre2[:], stp2, stp2[:], ALU.mult)
                self.tt("dve", pw, c(1), lim2, lim2[:], stp2, stp2[:], ALU.mult)
                self.act(pw, c(2), pw, c(0), AF.Exp)
                sinred(pw, c(4), pw, c(1), arg2[:, 0, 0:16], argi[:, 0, 0:16], argf[:, 0, 0:16], 0.0)
                sinred(pw, c(5), pw, c(1), arg2[:, 0, 0:16], argi[:, 0, 0:16], argf[:, 0, 0:16], 0.25)
                TTm(c(5), c(5), c(2), ALU.mult)
                TTm(c(4), c(4), c(2), ALU.mult)
                self.ts("dve", pw, c(5), pw, c(5), -1.0, None, ALU.add)
                self.tt("dve", pw, c(6), lre2, lre2[:], lre2, lre2[:], ALU.mult)
                self.tt("dve", pw, c(7), lim2, lim2[:], lim2, lim2[:], ALU.mult)
                TTm(c(6), c(6), c(7), ALU.add)
                self.O("dve", lambda e, o=c(6), i_=c(6): e.reciprocal(out=o, in_=i_), reads=[pw], writes=[pw])
                self.tt("dve", pw, c(7), pw, c(5), lre2, lre2[:], ALU.mult)
                self.tt("dve", pw, c(8), pw, c(4), lim2, lim2[:], ALU.mult)
                TTm(c(7), c(7), c(8), ALU.add)
                TTm(c(7), c(7), c(6), ALU.mult)
                self.tt("dve", pw, c(8), pw, c(4), lre2, lre2[:], ALU.mult)
                self.tt("dve", pw, c(9), pw, c(5), lim2, lim2[:], ALU.mult)
                TTm(c(8), c(8), c(9), ALU.subtract)
                TTm(c(8), c(8), c(6), ALU.mult)
                Bre = sb2("sBre", [128, 16, 16]); Bim = sb2("sBim", [128, 16, 16])
                self.load(Bre, Bre[:], di["s5_b_re"][l, d].rearrange("(gp go) p c -> (go p) gp c", go=2), "p10")
                self.load(Bim, Bim[:], di["s5_b_im"][l, d].rearrange("(gp go) p c -> (go p) gp c", go=2), "p11")
                bbr = sb2("sbbr", [128, 16, 16]); bbi = sb2("sbbi", [128, 16, 16]); tq = sb2("stq", [128, 16, 16])
                cre = bc(c(7).unsqueeze(2), [128, 16, 16]); cim = bc(c(8).unsqueeze(2), [128, 16, 16])
                self.tt("dve", bbr, bbr[:], Bre, Bre[:], pw, cre, ALU.mult)
                self.tt("dve", tq, tq[:], Bim, Bim[:], pw, cim, ALU.mult)
                self.tt("dve", bbr, bbr[:], bbr, bbr[:], tq, tq[:], ALU.subtract)
                self.tt("dve", bbi, bbi[:], Bim, Bim[:], pw, cre, ALU.mult)
                self.tt("dve", tq, tq[:], Bre, Bre[:], pw, cim, ALU.mult)
                self.tt("dve", bbi, bbi[:], bbi, bbi[:], tq, tq[:], ALU.add)
                XPr = sb2("sXPr", [128, 8, 128], BF16); XPi = sb2("sXPi", [128, 8, 128], BF16)
                self.memset("pool", XPr, XPr[:], 0.0); self.memset("pool", XPi, XPi[:], 0.0)
                for g in range(32):
                    gp, go, gi = g // 2, g % 2, g % 8
                    self.cp("dve", XPr, XPr[:, gi, gi * 16:(gi + 1) * 16], bbr, bbr[:, gp, :])
                    self.cp("pool", XPi, XPi[:, gi, gi * 16:(gi + 1) * 16], bbi, bbi[:, gp, :])
                    selg = SELb[:, go * 64:(go + 1) * 64]
                    self.mm(psA, psA[:, 0:64], XPr, XPr[:, gi, :], SELb, selg)
                    self.mm(psA, psA[:, 64:128], XPi, XPi[:, gi, :], SELb, selg)
                    self.cp("act", BP1, BP1[:, g, :], psA, psA[:, 0:128])
                    self.act(BP2, BP2[:, g, 0:64], psA, psA[:, 64:128], AF.Copy, scale=-1.0)
                    self.cp("dve", BP2, BP2[:, g, 64:128], psA, psA[:, 0:64])
                Ca = sb2("sCa", [128, 4, 128]); Cb = sb2("sCb", [128, 4, 128])
                cre_d = di["s5_c_re"][l, d].rearrange("(a gi) c p -> (gi c) a p", gi=8)
                cim_d = di["s5_c_im"][l, d].rearrange("(a gi) c p -> (gi c) a p", gi=8)
                self.load(Ca, Ca[:, :, 0:64], cre_d, "p12"); self.load(Ca, Ca[:, :, 64:128], cim_d, "p13")
                self.load(Cb, Cb[:, :, 0:64], cim_d, "p14"); self.load(Cb, Cb[:, :, 64:128], cre_d, "p15")
                for a in range(4):
                    self.tr(psT, psT[:, a, :], Ca, Ca[:, a, :])
                self.cp("act", W1, W1[0:64, :], psT, psT[0:64, :, :].rearrange("p a b -> p (a b)"))
                self.act(W1, W1[64:128, :], psT, psT[64:128, :, :].rearrange("p a b -> p (a b)"), AF.Copy, scale=-1.0)
                for a in range(4):
                    self.tr(psT, psT[:, a, :], Cb, Cb[:, a, :])
                self.act(W2, W2[:], psT, psT[:].rearrange("p a b -> p (a b)"), AF.Copy, scale=-1.0)
                self.S.barrier()
                self.S.flush()
            u = [sb(f"su{i}", [128, 512]) for i in range(2)]
            uT = sb("suT", [128, 4, 128], BF16)
            t1 = sb("st1", [128, 1024]); t2 = sb("st2", [128, 1024])
            zin = sb("szin", [128, 32, 128]); z = sb("sz", [128, 32, 128])
            ZC = sb("sZC", [128, 32, 128], BF16); ZS = sb("sZS", [128, 32, 128], BF16)
            xst = sb("sxst", [128, 32]); xc = sb("sxc", [128, 32]); xs_ = sb("sxs", [128, 32]); tm = sb("stm", [128, 32])
            yo = [sb(f"syo{i}", [128, 512]) for i in range(2)]
            j0 = 0 if d == 0 else 127
            for it, (s, i, first) in enumerate(self.tiles(d)):
                b = it % 2
                p0 = self.prow(s, i)
                if first:
                    self.memset("pool", xst, xst[:], 0.0)
                self.load(u[b], u[b][:], self.PS[p0:p0 + 128, 2560:3072], f"l0{b}", reads=[("PS", s, i)])
                for ct in range(4):
                    self.tr(psT, psT[:, ct, :], u[b], u[b][:, ct * 128:(ct + 1) * 128])
                self.cp("act", uT, uT[:], psT, psT[:])
                for ct in range(4):
                    for gi in range(8):
                        g = ct * 8 + gi
                        self.mm(psA, psA[:, gi * 128:(gi + 1) * 128], BP1, BP1[:, g, :], uT, uT[:, ct, :])
                        self.mm(psB, psB[:, gi * 128:(gi + 1) * 128], BP2, BP2[:, g, :], uT, uT[:, ct, :])
                    gs = slice(ct * 8, ct * 8 + 8)
                    fl = lambda ap: ap.rearrange("p a b -> p (a b)")
                    self.tt("dve", t1, t1[:], psA, psA[:], COS, fl(COS[:, gs, :]), ALU.mult)
                    self.tt("dve", t2, t2[:], psB, psB[:], SIN, fl(SIN[:, gs, :]), ALU.mult)
                    self.tt("pool", zin, fl(zin[:, gs, :]), t1, t1[:], t2, t2[:], ALU.subtract)
                self.tt("dve", tm, tm[:], RHOF, RHOF[:], xst, xst[:], ALU.mult)
                self.tt("dve", zin, zin[:, :, j0], zin, zin[:, :, j0], tm, tm[:], ALU.add)
                zf = z[:].rearrange("p a b -> p (a b)"); zif = zin[:].rearrange("p a b -> p (a b)"); rf = RHO[:].rearrange("p a b -> p (a b)")
                if d == 1:
                    zf, zif, rf = zf[:, ::-1], zif[:, ::-1], rf[:, ::-1]
                self.O("dve", lambda e, o=zf, a=rf, b_=zif: e.tensor_tensor_scan(out=o, data0=a, data1=b_, initial=0.0, op0=ALU.mult, op1=ALU.add),
                       reads=[RHO, zin], writes=[z])
                self.tt("dve", ZC, ZC[:], z, z[:], COS, COS[:], ALU.mult)
                self.tt("pool", ZS, ZS[:], z, z[:], SIN, SIN[:], ALU.mult)
                for g in range(32):
                    self.mm(psY, psY[:, g * 16:(g + 1) * 16], ZC, ZC[:, g, :], W1, W1[:, g * 16:(g + 1) * 16], start=True, stop=False)
                    self.mm(psY, psY[:, g * 16:(g + 1) * 16], ZS, ZS[:, g, :], W2, W2[:, g * 16:(g + 1) * 16], start=False, stop=True)
                self.cp("act", yo[b], yo[b][:], psY, psY[:])
                r0 = self.rows(s, i)
                self.store(yo[b], self.YD[d, r0:r0 + 128, 1024:1536], yo[b][:], f"s0{b}", writes=[("YDs", d, s, i)])
                jl = 127 - j0
                self.tt("dve", xc, xc[:], z, z[:, :, jl], CL, CL[:], ALU.mult)
                self.tt("dve", xs_, xs_[:], z, z[:, :, jl], SL, SL[:], ALU.mult)
                self.mm(psX, psX[:, 0:32], SWAP, SWAP[:], xs_, xs_[:])
                self.tt("dve", xst, xst[:], xc, xc[:], psX, psX[:, 0:32], ALU.add)
            self.end_phase()

    def head_ln(self, y, eps, gw, gb, wk, st):
        v4 = lambda ap: ap.rearrange("p (a b) -> p a b", a=4)
        self.O("dve", lambda e, o=st[:, 0:4], i_=v4(y[:]): e.tensor_reduce(out=o, in_=i_, axis=AX.X, op=ALU.add), reads=[y], writes=[st])
        self.ts("dve", st, st[:, 0:4], st, st[:, 0:4], -1.0 / 64, None, ALU.mult)
        self.tt("dve", y, v4(y[:]), y, v4(y[:]), st, bc(st[:, 0:4].unsqueeze(2), [128, 4, 64]), ALU.add)
        self.tt("pool", wk, wk[:], y, y[:], y, y[:], ALU.mult)
        self.O("dve", lambda e, o=st[:, 4:8], i_=v4(wk[:]): e.tensor_reduce(out=o, in_=i_, axis=AX.X, op=ALU.add), reads=[wk], writes=[st])
        self.ts("dve", st, st[:, 4:8], st, st[:, 4:8], 1.0 / 64, eps, ALU.mult, ALU.add)
        self.act(st, st[:, 8:12], st, st[:, 4:8], AF.Sqrt)
        self.O("dve", lambda e, o=st[:, 12:16], i_=st[:, 8:12]: e.reciprocal(out=o, in_=i_), reads=[st], writes=[st])
        self.tt("dve", y, v4(y[:]), y, v4(y[:]), st, bc(st[:, 12:16].unsqueeze(2), [128, 4, 64]), ALU.mult)
        self.tt("pool", y, y[:], y, y[:], gw, gw[:], ALU.mult)
        self.tt("dve", y, y[:], y, y[:], gb, gb[:], ALU.add)

    def phase_O1(self, l):
        di = self.di
        self.S.pe_skip = True
        with ExitStack() as es:
            sb = lambda n, sh, dt=F32: self.sb(es, n, sh, dt)
            stg = [sb(f"ostg{i}", [128, 1024]) for i in range(3)]
            WO = self.load_w_bf16(es, "oWO", di["w_out"][l], 8, D, stg)
            GL = self.load_w_bf16(es, "oGL", di["s5_glu_w"][l], 4, 512, stg)
            RGW = self.bcast_load(es, "oRGW", di["ret_gn_w"][l], 256); RGB = self.bcast_load(es, "oRGB", di["ret_gn_b"][l], 256)
            WGW = self.bcast_load(es, "oWGW", di["rwkv_gn_w"][l], 256); WGB = self.bcast_load(es, "oWGB", di["rwkv_gn_b"][l], 256)
            SD = self.bcast_load(es, "oSD", di["s5_d"][l], 512); GB = self.bcast_load(es, "oGB", di["s5_glu_b"][l], 512)
            y0 = [sb(f"oy0{i}", [128, YW]) for i in range(2)]
            y1 = [sb(f"oy1{i}", [128, YW]) for i in range(2)]
            gu = [sb(f"ogu{i}", [128, 768]) for i in range(2)]
            xt = [sb(f"oxt{i}", [128, D]) for i in range(2)]
            mix = sb("omix", [128, D]); wk = sb("owk", [128, 512]); wk2 = sb("owk2", [128, 512]); st = sb("ost", [128, 16])
            ygT = sb("oygT", [128, 4, 128], BF16); mixT = sb("omixT", [128, 8, 128], BF16)
            xo = [sb(f"oxo{i}", [128, D]) for i in range(2)]
            psT = [self.ps(es, f"opsT{i}", [128, 4, 128]) for i in range(2)]
            psG = self.ps(es, "opsG", [128, 512]); psO = [self.ps(es, f"opsO{i}", [128, 512]) for i in range(2)]
            nr, nw, ns = (self.flags.get(k) for k in ("no_ret", "no_rwkv", "no_s5"))
            self.S.barrier()
            for it, (s, i, first) in enumerate(self.tiles()):
                b = it % 2
                r0 = self.rows(s, i); p0 = self.prow(s, i)
                self.load(y0[b], y0[b][:], self.YD[0, r0:r0 + 128, :], f"l0{b}", reads=[("YDr", 0, s, i), ("YDw", 0, s, i), ("YDs", 0, s, i)])
                self.load(y1[b], y1[b][:], self.YD[1, r0:r0 + 128, :], f"l1{b}", reads=[("YDr", 1, s, i), ("YDw", 1, s, i), ("YDs", 1, s, i)])
                self.load(gu[b], gu[b][:, 0:256], self.PS[p0:p0 + 128, 1280:1536], f"l2{b}", reads=[("PS", s, i)])
                self.load(gu[b], gu[b][:, 256:768], self.PS[p0:p0 + 128, 2560:3072], f"l3{b}", reads=[("PS", s, i)])
                self.load(xt[b], xt[b][:], self.XS[r0:r0 + 128, :], f"l4{b}", reads=["XSall", ("XS", r0)])
                A, B, GU = y0[b], y1[b], gu[b]
                yr = mix
                if nr:
                    self.memset("pool", mix, mix[:, 0:256], 0.0)
                else:
                    self.tt("dve", mix, mix[:, 0:256], A, A[:, 0:256], B, B[:, 0:256], ALU.add)
                    self._ln_slice(mix, 0, 1e-5, RGW, RGB, wk, st)
                    self.act(wk2, wk2[:, 0:256], GU, GU[:, 0:256], AF.Silu)
                    self.tt("dve", mix, mix[:, 0:256], mix, mix[:, 0:256], wk2, wk2[:, 0:256], ALU.mult)
                if nw:
                    self.memset("pool", mix, mix[:, 256:512], 0.0)
                else:
                    self.tt("dve", mix, mix[:, 256:512], A, A[:, 256:512], B, B[:, 256:512], ALU.add)
                    self._ln_slice(mix, 256, 64e-5, WGW, WGB, wk, st)
                    self.tt("pool", wk2, wk2[:, 0:256], A, A[:, 512:768], B, B[:, 512:768], ALU.add)
                    self.tt("dve", mix, mix[:, 256:512], mix, mix[:, 256:512], wk2, wk2[:, 0:256], ALU.add)
                    self.tt("dve", mix, mix[:, 256:512], mix, mix[:, 256:512], A, A[:, 768:1024], ALU.mult)
                if ns:
                    self.memset("pool", mix, mix[:, 512:1024], 0.0)
                else:
                    self.tt("dve", wk, wk[:], GU, GU[:, 256:768], SD, SD[:], ALU.mult)
                    self.tt("pool", wk2, wk2[:], A, A[:, 1024:1536], B, B[:, 1024:1536], ALU.add)
                    self.tt("dve", wk, wk[:], wk, wk[:], wk2, wk2[:], ALU.add)
                    self.tt("pool", wk2, wk2[:], wk, wk[:], wk, wk[:], ALU.mult)
                    self.ts("dve", wk2, wk2[:], wk2, wk2[:], 0.044715, 1.0, ALU.mult, ALU.add)
                    self.tt("dve", wk2, wk2[:], wk2, wk2[:], wk, wk[:], ALU.mult)
                    self.act(wk2, wk2[:], wk2, wk2[:], AF.Sigmoid, scale=1.5957691216057308)
                    self.tt("dve", wk, wk[:], wk, wk[:], wk2, wk2[:], ALU.mult)
                    self.transpose_to(wk, 4, psT, ygT)
                    for k in range(4):
                        self.mm(psG, psG[:], ygT, ygT[:, k, :], GL, GL[:, k, :], start=(k == 0), stop=(k == 3))
                    self.tt("dve", wk2, wk2[:], psG, psG[:], GB, GB[:], ALU.add)
                    self.act(wk2, wk2[:], wk2, wk2[:], AF.Sigmoid)
                    self.tt("dve", mix, mix[:, 512:1024], wk, wk[:], wk2, wk2[:], ALU.mult)
                self.transpose_to(mix, 8, psT, mixT)
                for cb in range(2):
                    pp = psO[cb]
                    for k in range(8):
                        self.mm(pp, pp[:], mixT, mixT[:, k, :], WO, WO[:, k, cb * 512:(cb + 1) * 512], start=(k == 0), stop=(k == 7))
                    self.tt("dve", xo[b], xo[b][:, cb * 512:(cb + 1) * 512], xt[b], xt[b][:, cb * 512:(cb + 1) * 512], pp, pp[:], ALU.add)
                self.store(xo[b], self.XS[r0:r0 + 128, :], xo[b][:], f"s0{b}", writes=[("XS", r0)])
            self.end_phase()

    def _ln_slice(self, mix, c0, eps, gw, gb, wk, st):
        v4 = lambda ap: ap.rearrange("p (a b) -> p a b", a=4)
        y = mix[:, c0:c0 + 256]
        self.O("dve", lambda e, o=st[:, 0:4], i_=v4(y): e.tensor_reduce(out=o, in_=i_, axis=AX.X, op=ALU.add), reads=[mix], writes=[st])
        self.ts("dve", st, st[:, 0:4], st, st[:, 0:4], -1.0 / 64, None, ALU.mult)
        self.tt("dve", mix, v4(y), mix, v4(y), st, bc(st[:, 0:4].unsqueeze(2), [128, 4, 64]), ALU.add)
        self.tt("pool", wk, wk[:, 0:256], mix, y, mix, y, ALU.mult)
        self.O("dve", lambda e, o=st[:, 4:8], i_=v4(wk[:, 0:256]): e.tensor_reduce(out=o, in_=i_, axis=AX.X, op=ALU.add), reads=[wk], writes=[st])
        self.ts("dve", st, st[:, 4:8], st, st[:, 4:8], 1.0 / 64, eps, ALU.mult, ALU.add)
        self.act(st, st[:, 8:12], st, st[:, 4:8], AF.Sqrt)
        self.O("dve", lambda e, o=st[:, 12:16], i_=st[:, 8:12]: e.reciprocal(out=o, in_=i_), reads=[st], writes=[st])
        self.tt("dve", mix, v4(y), mix, v4(y), st, bc(st[:, 12:16].unsqueeze(2), [128, 4, 64]), ALU.mult)
        self.tt("pool", mix, y, mix, y, gw, gw[:], ALU.mult)
        self.tt("dve", mix, y, mix, y, gb, gb[:], ALU.add)

    def phase_O2(self, l, last):
        di = self.di
        self.S.pe_skip = True
        finals = []
        with ExitStack() as es:
            sb = lambda n, sh, dt=F32: self.sb(es, n, sh, dt)
            stg = [sb(f"fstg{i}", [128, 1024]) for i in range(2)]
            WG = self.load_w_bf16(es, "fWG", di["ffn_w_gate"][l], 8, DFF, stg)
            WU = self.load_w_bf16(es, "fWU", di["ffn_w_up"][l], 8, DFF, stg)
            WD = self.load_w_bf16(es, "fWD", di["ffn_w_down"][l], 22, D, stg)
            G2 = self.bcast_load(es, "fG2", di["norm2_g"][l], D)
            GF = self.bcast_load(es, "fGF", di["final_g"], D) if last else None
            xb = [sb(f"fxb{i}", [128, D]) for i in range(2)]
            junk = sb("fjunk", [128, D], BF16); ss = [sb(f"fss{i}", [128, 4]) for i in range(2)]
            junk2 = sb("fjunk2", [128, D], BF16) if last else None
            ss2 = sb("fss2", [128, 4]) if last else None
            h = [sb(f"fh{i}", [128, D]) for i in range(2)]
            hT = [sb(f"fhT{i}", [128, 8, 128], BF16) for i in range(2)]
            sg = [sb(f"fsg{i}", [128, 4, 128]) for i in range(2)]
            aT = sb("faT", [128, 22, 128], BF16)
            xo = sb("fxo0", [128, D])
            yo = sb("fyo0", [128, D]) if last else None
            psT = [self.ps(es, f"fpsT{i}", [128, 4, 128]) for i in range(2)]
            psG = [self.ps(es, f"fpsG{i}", [128, 4, 128]) for i in range(2)]
            psU = [self.ps(es, f"fpsU{i}", [128, 4, 128]) for i in range(2)]
            psO = [self.ps(es, f"fpsO{i}", [128, 512]) for i in range(2)]
            self.S.barrier()

            def s0(j, s, i, first):
                b = j % 2
                r0 = self.rows(s, i)
                self.load(xb[b], xb[b][:], self.XS[r0:r0 + 128, :], f"l0{b}", reads=["XSall", ("XS", r0)])
                self.rmsnorm(xb[b], G2, h[b], junk, ss[b])
                yield
                self.transpose_to(h[b], 8, psT, hT[b])
                yield

            def s1(j, s, i, first):
                b = j % 2
                r0 = self.rows(s, i)
                HT = hT[b]
                for gi, f0 in enumerate(range(0, 22, 4)):
                    f1 = min(22, f0 + 4)
                    pg, pu, sgg = psG[gi % 2], psU[gi % 2], sg[gi % 2]
                    for f in range(f0, f1):
                        for k in range(8):
                            self.mm(pg, pg[:, f - f0, :], WG, WG[:, k, f * 128:(f + 1) * 128], HT, HT[:, k, :], start=(k == 0), stop=(k == 7))
                        for k in range(8):
                            self.mm(pu, pu[:, f - f0, :], WU, WU[:, k, f * 128:(f + 1) * 128], HT, HT[:, k, :], start=(k == 0), stop=(k == 7))
                    self.act(sgg, sgg[:, 0:f1 - f0, :], pg, pg[:, 0:f1 - f0, :], AF.Silu)
                    self.tt("dve", aT, aT[:, f0:f1, :], sgg, sgg[:, 0:f1 - f0, :], pu, pu[:, 0:f1 - f0, :], ALU.mult)
                    if gi % 2 == 1:
                        yield
                for cb in range(2):
                    pp = psO[cb]
                    for k in range(22):
                        self.mm(pp, pp[:], aT, aT[:, k, :], WD, WD[:, k, cb * 512:(cb + 1) * 512], start=(k == 0), stop=(k == 21))
                    self.tt("dve", xo, xo[:, cb * 512:(cb + 1) * 512], xb[b], xb[b][:, cb * 512:(cb + 1) * 512], pp, pp[:], ALU.add)
                yield
                if not last:
                    self.store(xo, self.XS[r0:r0 + 128, :], xo[:], "s00", writes=[("XS", r0)])
                else:
                    self.rmsnorm(xo, GF, yo, junk2, ss2)
                    st_ = self.store(yo, self.yout[s][i * 128:(i + 1) * 128, :], yo[:], "s00")
                    finals.append(st_)
                yield
            self.pipeline(self.tiles(), [s0, s1])
            self.end_phase()
        return finals


def _consts(Tmax):
    c = {}
    c["c_ident"] = np.eye(128, dtype=np.float32)
    inv = (1.0 / (10000.0 ** np.linspace(0.0, 1.0, 32, dtype=np.float32))).astype(np.float32)
    ang = np.arange(Tmax, dtype=np.float32)[:, None] * inv[None, :]
    cos, sin = np.cos(ang).astype(np.float32), np.sin(ang).astype(np.float32)
    rot = np.concatenate([0.125 * cos, 0.125 * cos, -0.125 * sin, 0.125 * sin, cos, cos, -sin, sin], axis=1)
    c["c_rot"] = np.ascontiguousarray(rot.astype(np.float32))
    lg = np.log(1.0 - 2.0 ** (-5.0 - np.arange(4, dtype=np.float64)))
    i = np.arange(128, dtype=np.float64)
    diff = np.abs(i[:, None] - i[None, :])
    c["c_retmask"] = np.concatenate([np.exp(diff * lg[h]) for h in range(4)], axis=1).astype(np.float32)
    qk = np.zeros((2, 128, 8), np.float32)
    for h in range(4):
        qk[0, :, h] = np.exp((i + 1) * lg[h]); qk[0, :, 4 + h] = np.exp((127 - i) * lg[h])
        qk[1, :, h] = np.exp((128 - i) * lg[h]); qk[1, :, 4 + h] = np.exp(i * lg[h])
    c["c_retqk"] = qk
    gc = np.zeros((128, 256), np.float32)
    for r in range(128):
        for pr in range(2):
            gc[r, pr * 128:(pr + 1) * 128] = np.exp(128 * lg[pr * 2 + r // 64])
    c["c_retgc"] = gc
    s_, t_ = np.meshgrid(np.arange(128), np.arange(128), indexing="ij")
    c["c_cs"] = np.stack([(s_ <= t_), (s_ >= t_)]).astype(np.float32)
    c["c_ones"] = np.ones((128, 128), np.float32)
    c["c_msi"] = np.stack([np.concatenate([(s_ < t_), (s_ <= t_)], axis=1),
                           np.concatenate([(s_ > t_), (s_ >= t_)], axis=1)]).astype(np.float32)
    c["c_mst"] = np.stack([(s_ > t_), (s_ < t_)]).astype(np.float32)
    j = np.arange(128, dtype=np.float32)
    c["c_j1"] = np.stack([np.tile(j + 1, (128, 1)), np.tile(128 - j, (128, 1))]).astype(np.float32)
    z0 = np.ones((2, 128, 128), np.float32); z0[0, :, 0] = 0; z0[1, :, 127] = 0
    c["c_z0"] = z0
    sw = np.zeros((128, 128), np.float32)
    for p in range(64):
        sw[64 + p, p] = -1.0; sw[p, 64 + p] = 1.0
    c["c_swap"] = sw
    sel = np.zeros((128, 128), np.float32)
    for go in range(2):
        for p in range(64):
            sel[go * 64 + p, go * 64 + p] = 1.0
    c["c_sel"] = sel
    return c


def _win_perm():
    idx = []
    def sw(base):
        out = []
        for h in range(4):
            out += list(range(base + h * 64 + 32, base + h * 64 + 64)) + list(range(base + h * 64, base + h * 64 + 32))
        return out
    idx += list(range(0, 256)) + sw(0) + list(range(256, 512)) + sw(256) + list(range(512, 768)) + list(range(768, 1024))
    idx += list(range(1024, 2560))
    return np.array(idx)


_CACHE = {}


def run(inputs, Tp, Ts, L, ncores, flags=None):
    key = (Tp, Ts, L, ncores, tuple(sorted((flags or {}).items())))
    if key not in _CACHE:
        _CACHE[key] = Builder(Tp, Ts, L, flags).build()
    nc = _CACHE[key]
    f = lambda a: np.ascontiguousarray(np.asarray(a, dtype=np.float32))
    shared = {k: f(v) for k, v in inputs.items() if k not in ("x_prompt", "x_sample", "w_in")}
    shared["w_in"] = np.ascontiguousarray(f(inputs["w_in"])[:, :, _win_perm()])
    shared.update(_consts(max(Tp, Ts)))
    xp, xs = f(inputs["x_prompt"]), f(inputs["x_sample"])
    nb_s = xs.shape[0]
    in_maps = []
    for c in range(ncores):
        m = dict(shared)
        m["x_prompt"] = xp[c % xp.shape[0]]
        m["x_sample"] = xs[c % nb_s]
        in_maps.append(m)
    res = run_bass_kernel_spmd(nc, in_maps, core_ids=list(range(ncores)))
    yp = np.stack([res.results[c]["y_prompt"] for c in range(xp.shape[0])])
    ys = np.stack([res.results[c]["y_sample"] for c in range(nb_s)])
    return yp.astype(np.float32), ys.astype(np.float32)


def kernel(**inputs):
    return run(inputs, 8192, 4096, 4, 8)
```

```python
import math
import numpy as np
import concourse.bass as bass
import concourse.mybir as mybir
from concourse.bass_utils import run_bass_kernel_spmd
from contextlib import ExitStack

F32 = mybir.dt.float32
BF16 = mybir.dt.bfloat16
ALU = mybir.AluOpType
AF = mybir.ActivationFunctionType
AX = mybir.AxisListType

D = 1024
NH = 4
HD = 64
PIN = 3072
DFF = 2816
YW = 1536
CDEC = -math.exp(-0.5)
ENGS = ("pe", "act", "dve", "pool", "sp")


class Op:
    __slots__ = ("eng", "fn", "deps", "signal", "idx", "semkey", "is_dma", "emitted", "val")

    def __init__(self, eng, fn, semkey, is_dma):
        self.eng = eng
        self.fn = fn
        self.deps = []
        self.signal = is_dma
        self.idx = -1
        self.semkey = semkey
        self.is_dma = is_dma
        self.emitted = False
        self.val = None


class Sched:
    def __init__(self, nc, es):
        self.nc = nc
        self.es = es
        self.ops = {e: [] for e in ENGS}
        self.semidx = {}
        self.sems = {}
        self.res = {}
        self.seen = {e: {} for e in ENGS}
        self.sigcount = {}
        self.last_emitted = {}
        self.last_op = {}
        self.nops = 0
        self.pe_skip = False
        self.pe_fence_pending = False

    def _dep(self, op, d, force=False):
        if d is None or d is op:
            return
        if self.pe_skip and not force and d.eng == "pe" and op.eng == "pe" and not d.is_dma:
            return
        if d.emitted and not d.signal:
            d = self.last_emitted[d.semkey]
        seen = self.seen[op.eng]
        if seen.get(d.semkey, -1) >= d.idx:
            return
        seen[d.semkey] = d.idx
        d.signal = True
        op.deps.append(d)

    def op(self, eng, fn, reads=(), writes=(), dma=None, extra=(), fence=False):
        is_dma = dma is not None
        semkey = ("dma", dma) if is_dma else ("eng", eng)
        o = Op(eng, fn, semkey, is_dma)
        o.idx = self.semidx.get(semkey, 0)
        self.semidx[semkey] = o.idx + 1
        for r in reads:
            st = self.res.get(r)
            if st is not None:
                self._dep(o, st[0])
        for w in writes:
            st = self.res.get(w)
            if st is not None:
                self._dep(o, st[0])
                for rd in st[1]:
                    self._dep(o, rd)
        for d in extra:
            self._dep(o, d)
        if eng == "pe":
            if fence or self.pe_fence_pending:
                self._dep(o, self.last_op.get(semkey), force=True)
            self.pe_fence_pending = fence
        for r in reads:
            st = self.res.setdefault(r, [None, []])
            st[1].append(o)
        for w in writes:
            self.res[w] = [o, []]
        self.ops[eng].append(o)
        self.last_op[semkey] = o
        self.nops += 1
        return o

    def barrier(self):
        lasts = list(self.last_op.values())
        for e in ENGS:
            self.op(e, lambda en: en.nop(), extra=lasts)

    def flush(self, final=()):
        nc = self.nc
        for key in self.semidx:
            if key not in self.sems:
                nm = ("s_" + "_".join(map(str, key)))[:48]
                self.sems[key] = self.es.enter_context(nc.semaphore(nm))
        for e in ENGS:
            if self.ops[e]:
                for o in reversed(self.ops[e]):
                    if not o.is_dma:
                        o.signal = True
                        break
        for e in ENGS:
            for o in self.ops[e]:
                if o.is_dma:
                    o.val = 16 * (o.idx + 1)
                elif o.signal:
                    c = self.sigcount.get(o.semkey, 0) + 1
                    self.sigcount[o.semkey] = c
                    o.val = c
        sems = self.sems
        pending = self.ops

        def run(en, key):
            for o in pending[key]:
                for d in o.deps:
                    en.wait_ge(sems[d.semkey], d.val)
                ins = o.fn(en)
                if o.signal:
                    ins.then_inc(sems[o.semkey], 16 if o.is_dma else 1)
            if key == "sp":
                for d in final:
                    en.wait_ge(sems[d.semkey], d.val)

        with nc.Block() as block:
            @block.tensor
            def _(en):
                run(en, "pe")

            @block.scalar
            def _(en):
                run(en, "act")

            @block.vector
            def _(en):
                run(en, "dve")

            @block.gpsimd
            def _(en):
                run(en, "pool")

            @block.sync
            def _(en):
                run(en, "sp")
        for e in ENGS:
            for o in self.ops[e]:
                o.emitted = True
                if o.signal and not o.is_dma:
                    self.last_emitted[o.semkey] = o
            self.ops[e] = []


class Tl:
    def __init__(self, t, key):
        self.t = t
        self.k = key

    def __getitem__(self, idx):
        return self.t[idx]


def bc(ap, shape):
    return ap.to_broadcast(list(shape))


class Builder:
    def __init__(self, Tp, Ts, depth, flags=None):
        self.Tp, self.Ts, self.L = Tp, Ts, depth
        self.seqs = [(0, Tp), (Tp, Ts)]
        self.TT = Tp + Ts
        self.flags = flags or {}
        self.nc = bass.Bass("TRN2", target_bir_lowering=False)

    def sb(self, es, name, shape, dt=F32):
        self._uid = getattr(self, "_uid", 0) + 1
        name = f"{name}_{self._uid}"
        return Tl(es.enter_context(self.nc.sbuf_tensor(name, list(shape), dt)), name)

    def ps(self, es, name, shape, dt=F32):
        self._uid = getattr(self, "_uid", 0) + 1
        name = f"{name}_{self._uid}"
        return Tl(es.enter_context(self.nc.psum_tensor(name, list(shape), dt)), name)

    def O(self, eng, fn, reads=(), writes=(), dma=None, fence=False):
        return self.S.op(eng, fn, [r.k if isinstance(r, Tl) else r for r in reads],
                         [w.k if isinstance(w, Tl) else w for w in writes], dma=dma, fence=fence)

    def load(self, dst, dst_ap, src_ap, key, reads=(), eng="sp"):
        return self.O(eng, lambda e, o=dst_ap, i=src_ap: e.dma_start(out=o, in_=i, allow_slow_non_contiguous=True), reads=reads, writes=[dst], dma=key)

    def store(self, src, dst_ap, src_ap, key, writes=(), eng="sp"):
        return self.O(eng, lambda e, o=dst_ap, i=src_ap: e.dma_start(out=o, in_=i), reads=[src], writes=writes, dma=key)

    def mm(self, out_t, out_ap, l_t, l_ap, r_t, r_ap, start=True, stop=True):
        return self.O("pe", lambda e, o=out_ap, a=l_ap, b=r_ap, s0=start, s1=stop: e.matmul(o, a, b, start=s0, stop=s1),
                      reads=[l_t, r_t] + ([] if start else [out_t]), writes=[out_t], fence=(l_ap.shape[0] != 128))

    def tr(self, out_t, out_ap, in_t, in_ap):
        idn = self.identf if in_ap.dtype == F32 else self.identb
        return self.O("pe", lambda e, o=out_ap, a=in_ap, i=idn: e.transpose(o, a, i[:]), reads=[in_t, idn], writes=[out_t])

    def act(self, out_t, out_ap, in_t, in_ap, func, scale=1.0, bias=None, accum=None, extra_r=(), extra_w=()):
        def fn(e, o=out_ap, i=in_ap, f=func, s=scale, b=bias, a=accum):
            kw = {}
            if b is not None:
                kw["bias"] = b
            if a is not None:
                kw["accum_out"] = a
            return e.activation(out=o, in_=i, func=f, scale=s, **kw)
        return self.O("act", fn, reads=[in_t] + list(extra_r), writes=[out_t] + list(extra_w))

    def tt(self, eng, out_t, out_ap, a_t, a_ap, b_t, b_ap, op):
        return self.O(eng, lambda e, o=out_ap, a=a_ap, b=b_ap, p=op: e.tensor_tensor(out=o, in0=a, in1=b, op=p),
                      reads=[a_t, b_t], writes=[out_t])

    def ts(self, eng, out_t, out_ap, a_t, a_ap, s1, s2, op0, op1=None, extra_r=()):
        def fn(e, o=out_ap, a=a_ap, x=s1, y=s2, p0=op0, p1=op1):
            if p1 is None:
                return e.tensor_single_scalar(out=o, in_=a, scalar=x, op=p0)
            return e.tensor_scalar(out=o, in0=a, scalar1=x, scalar2=y, op0=p0, op1=p1)
        return self.O(eng, fn, reads=[a_t] + list(extra_r), writes=[out_t])

    def stt(self, eng, out_t, out_ap, a_t, a_ap, scalar, b_t, b_ap, op0, op1, extra_r=()):
        eng = "dve"
        return self.O(eng, lambda e, o=out_ap, a=a_ap, s=scalar, b=b_ap, p0=op0, p1=op1:
                      e.scalar_tensor_tensor(out=o, in0=a, scalar=s, in1=b, op0=p0, op1=p1),
                      reads=[a_t, b_t] + list(extra_r), writes=[out_t])

    def cp(self, eng, out_t, out_ap, in_t, in_ap):
        if eng == "act":
            return self.act(out_t, out_ap, in_t, in_ap, AF.Copy)
        return self.O(eng, lambda e, o=out_ap, i=in_ap: e.tensor_copy(out=o, in_=i), reads=[in_t], writes=[out_t])

    def memset(self, eng, t, ap, v):
        return self.O(eng, lambda e, o=ap, x=v: e.memset(o, x), writes=[t])

    def load_w_bf16(self, es, name, dram_ap, K, C, stg, dst=None, dst_koff=0):
        if dst is None:
            dst = self.sb(es, name, [128, K, C], BF16)
        n = 0
        for k in range(K):
            for c0 in range(0, C, 1024):
                c1 = min(C, c0 + 1024)
                slot = self._stg_i % len(stg)
                st = stg[slot]
                self._stg_i += 1
                self.load(st, st[:, 0:c1 - c0], dram_ap[k * 128:(k + 1) * 128, c0:c1], f"stg{slot}")
                eng = ("pool", "dve", "act")[n % 3]
                n += 1
                self.cp(eng, dst, dst[:, dst_koff + k, c0:c1], st, st[:, 0:c1 - c0])
        return dst

    def bcast_load(self, es, name, dram_vec_ap, C, key="bcl"):
        t = self.sb(es, name, [128, C])
        self.load(t, t[:], dram_vec_ap.partition_broadcast(128), "setup")
        return t

    def rows(self, seq, i):
        off, T = self.seqs[seq]
        return off + i * 128

    def prow(self, seq, i):
        off, T = self.seqs[seq]
        return off + 2 * seq + 1 + i * 128

    def tiles(self, d=0):
        out = []
        for s, (off, T) in enumerate(self.seqs):
            n = T // 128
            rng = range(n) if d == 0 else range(n - 1, -1, -1)
            for j, i in enumerate(rng):
                out.append((s, i, j == 0))
        return out

    def build(self):
        nc = self.nc
        Tp, Ts, L, TT = self.Tp, self.Ts, self.L, self.TT
        di = {}

        def inp(name, shape):
            di[name] = nc.dram_tensor(name, list(shape), F32, kind="ExternalInput").ap()
            return di[name]
        inp("x_prompt", [Tp, D]); inp("x_sample", [Ts, D])
        inp("norm1_g", [L, D]); inp("w_in", [L, D, PIN])
        inp("ret_gn_w", [L, 256]); inp("ret_gn_b", [L, 256])
        inp("rwkv_mu", [L, 1024]); inp("rwkv_w0", [L, 2, 256]); inp("rwkv_w2", [L, 2, 64, 256])
        inp("rwkv_a0", [L, 2, 256]); inp("rwkv_a2", [L, 2, 64, 256]); inp("rwkv_g2", [L, 128, 256])
        inp("rwkv_k_k", [L, 256]); inp("rwkv_k_a", [L, 256]); inp("rwkv_r_k", [L, 256])
        inp("rwkv_gn_w", [L, 256]); inp("rwkv_gn_b", [L, 256])
        inp("s5_lam_re", [L, 2, 32, 64]); inp("s5_lam_im", [L, 2, 32, 64]); inp("s5_log_step", [L, 2, 32])
        inp("s5_b_re", [L, 2, 32, 64, 16]); inp("s5_b_im", [L, 2, 32, 64, 16])
        inp("s5_c_re", [L, 2, 32, 16, 64]); inp("s5_c_im", [L, 2, 32, 16, 64])
        inp("s5_d", [L, 512]); inp("s5_glu_w", [L, 512, 512]); inp("s5_glu_b", [L, 512])
        inp("w_out", [L, D, D]); inp("norm2_g", [L, D])
        inp("ffn_w_gate", [L, D, DFF]); inp("ffn_w_up", [L, D, DFF]); inp("ffn_w_down", [L, DFF, D])
        inp("final_g", [D])
        inp("c_ident", [128, 128]); inp("c_rot", [max(Tp, Ts), 256])
        inp("c_retmask", [128, 512]); inp("c_retqk", [2, 128, 8]); inp("c_retgc", [128, 256])
        inp("c_cs", [2, 128, 128]); inp("c_ones", [128, 128]); inp("c_msi", [2, 128, 256]); inp("c_mst", [2, 128, 128])
        inp("c_j1", [2, 128, 128]); inp("c_z0", [2, 128, 128]); inp("c_swap", [128, 128]); inp("c_sel", [128, 128])
        self.di = di
        yp = nc.dram_tensor("y_prompt", [Tp, D], F32, kind="ExternalOutput").ap()
        ys = nc.dram_tensor("y_sample", [Ts, D], F32, kind="ExternalOutput").ap()
        self.yout = [yp, ys]
        self.XS = nc.dram_tensor("XS", [TT, D], F32, kind="Internal").ap()
        self.PS = nc.dram_tensor("PSC", [TT + 4, PIN], F32, kind="Internal").ap()
        self.YD = nc.dram_tensor("YD", [2, TT, YW], F32, kind="Internal").ap()

        with ExitStack() as ges:
            self.S = Sched(nc, ges)
            self._stg_i = 0
            self.identf = self.sb(ges, "identf", [128, 128])
            self.identb = self.sb(ges, "identb", [128, 128], BF16)
            self.load(self.identf, self.identf[:], di["c_ident"][:, :], "setup")
            self.cp("dve", self.identb, self.identb[:], self.identf, self.identf[:])
            zero = self.sb(ges, "zero", [128, PIN])
            self.memset("pool", zero, zero[:], 0.0)
            self.O("sp", lambda e: e.dma_start(out=self.XS[0:Tp, :], in_=di["x_prompt"][:, :]), writes=["XSall"], dma="setup")
            self.O("sp", lambda e: e.dma_start(out=self.XS[Tp:TT, :], in_=di["x_sample"][:, :]), writes=["XSall"], dma="setup")
            for s, (off, T) in enumerate(self.seqs):
                for r in (off + 2 * s, off + 2 * s + T + 1):
                    self.store(zero, self.PS[r:r + 1, :], zero[0:1, :], "setup", writes=["PSall"])
            self.S.barrier()
            self.S.flush()
            final = []
            for l in range(L):
                self.phase_P(l)
                for d in range(2):
                    if not self.flags.get("no_ret"):
                        self.phase_ret(l, d)
                    if not self.flags.get("no_rwkv"):
                        self.phase_rwkv(l, d)
                    if not self.flags.get("no_s5"):
                        self.phase_s5(l, d)
                self.phase_O1(l)
                final = self.phase_O2(l, last=(l == L - 1))
            self.S.barrier()
            self.S.flush(final=final)
        return nc

    def end_phase(self):
        self.S.barrier()
        self.S.flush()

    def rmsnorm(self, xt, G, h_out, junk, ss, eng2="dve"):
        self.act(junk, junk[:], xt, xt[:], AF.Square, accum=ss[:, 0:1], extra_w=[ss])
        self.ts("dve", ss, ss[:, 1:2], ss, ss[:, 0:1], 1.0 / D, 1e-6, ALU.mult, ALU.add)
        self.act(ss, ss[:, 2:3], ss, ss[:, 1:2], AF.Sqrt)
        self.O("dve", lambda e, o=ss[:, 3:4], i=ss[:, 2:3]: e.reciprocal(out=o, in_=i), reads=[ss], writes=[ss])
        self.stt(eng2, h_out, h_out[:], xt, xt[:], ss[:, 3:4], G, G[:], ALU.mult, ALU.mult, extra_r=[ss])

    def transpose_to(self, src, ncol_tiles, psT, dstT, evac_engs=("act", "dve")):
        n = 0
        for g0 in range(0, ncol_tiles, 4):
            g1 = min(ncol_tiles, g0 + 4)
            pt = psT[(g0 // 4) % len(psT)]
            for c in range(g0, g1):
                self.tr(pt, pt[:, c - g0, :], src, src[:, c * 128:(c + 1) * 128])
            self.cp(evac_engs[n % len(evac_engs)], dstT, dstT[:, g0:g1, :], pt, pt[:, 0:g1 - g0, :])
            n += 1

    def pipeline(self, tiles, stages):
        K, N = len(stages), len(tiles)
        for n in range(N + K - 1):
            gens = []
            for k, st in enumerate(stages):
                j = n - k
                if 0 <= j < N:
                    s, i, first = tiles[j]
                    gens.append(st(j, s, i, first))
            while gens:
                alive = []
                for g in gens:
                    try:
                        next(g)
                        alive.append(g)
                    except StopIteration:
                        pass
                gens = alive

    def phase_P(self, l):
        di = self.di
        self.S.pe_skip = True
        with ExitStack() as es:
            stg = [self.sb(es, f"stg{i}", [128, 1024]) for i in range(3)]
            Wb = self.load_w_bf16(es, "Wb", di["w_in"][l], 8, PIN, stg)
            G1 = self.bcast_load(es, "G1", di["norm1_g"][l], D)
            xb = [self.sb(es, f"xb{i}", [128, D]) for i in range(2)]
            junk = self.sb(es, "junk", [128, D], BF16)
            ss = [self.sb(es, f"ss{i}", [128, 4]) for i in range(2)]
            h = [self.sb(es, f"h{i}", [128, D]) for i in range(2)]
            hT = [self.sb(es, f"hT{i}", [128, 8, 128], BF16) for i in range(2)]
            pt = [self.sb(es, f"pt{i}", [128, PIN]) for i in range(2)]
            psT = [self.ps(es, f"psT{i}", [128, 4, 128]) for i in range(2)]
            psP = [self.ps(es, f"psP{i}", [128, 512]) for i in range(4)]
            self.S.barrier()
            cnt = [0]

            def s0(j, s, i, first):
                b = j % 2
                r0 = self.rows(s, i)
                self.load(xb[b], xb[b][:], self.XS[r0:r0 + 128, :], f"l0{b}", reads=["XSall", ("XS", r0)])
                self.rmsnorm(xb[b], G1, h[b], junk, ss[b])
                yield
                self.transpose_to(h[b], 8, psT, hT[b])
                yield

            def s1(j, s, i, first):
                b = j % 2
                for cb in range(PIN // 512):
                    n = cnt[0]
                    cnt[0] += 1
                    pp = psP[n % 4]
                    for k in range(8):
                        self.mm(pp, pp[:], hT[b], hT[b][:, k, :], Wb, Wb[:, k, cb * 512:(cb + 1) * 512], start=(k == 0), stop=(k == 7))
                    self.cp(("act", "dve")[n % 2], pt[b], pt[b][:, cb * 512:(cb + 1) * 512], pp, pp[:])
                    if cb % 3 == 2:
                        yield
                pr = self.prow(s, i)
                self.store(pt[b], self.PS[pr:pr + 128, :], pt[b][:], f"s0{b}", writes=[("PS", s, i)])
                yield
            self.pipeline(self.tiles(), [s0, s1])
            self.end_phase()

    def phase_ret(self, l, d):
        di = self.di
        self.S.pe_skip = not self.flags.get("noskip_ret")
        with ExitStack() as es:
            sb = lambda n, sh, dt=F32: self.sb(es, n, sh, dt)
            mask = sb("rmask", [128, 512]); self.load(mask, mask[:], di["c_retmask"][:, :], "setup")
            qk = sb("rqk", [128, 8]); self.load(qk, qk[:], di["c_retqk"][d], "setup")
            gc = sb("rgc", [128, 256]); self.load(gc, gc[:], di["c_retgc"][:, :], "setup")
            R = sb("R", [128, 256]); Rb = sb("Rb", [128, 256], BF16)
            pr = [sb(f"pr{i}", [128, 1280]) for i in range(2)]
            rot = [sb(f"rot{i}", [128, 256]) for i in range(2)]
            qh = sb("qh", [128, 512]); t1 = sb("rt1", [128, 512])
            qkT = sb("qkT", [128, 4, 128], BF16)
            qz = sb("qz", [128, 4, 128], BF16)
            self.memset("pool", qz, qz[:], 0.0)
            sT = sb("sT", [128, 4, 128], BF16)
            vb = sb("vb", [128, 256], BF16); kdb = sb("kdb", [128, 256], BF16)
            tmp = sb("rtmp", [128, 256]); yo = [sb(f"ryo{i}", [128, 256]) for i in range(2)]
            psT = self.ps(es, "rpsT", [128, 4, 128]); psS = self.ps(es, "rpsS", [128, 4, 128])
            psO = self.ps(es, "rpsO", [128, 256]); psC = self.ps(es, "rpsC", [128, 256]); psR = self.ps(es, "rpsR", [128, 256])
            self.S.barrier()
            for it, (s, i, first) in enumerate(self.tiles(d)):
                b = it % 2
                p0 = self.prow(s, i)
                if first:
                    self.memset("pool", R, R[:], 0.0)
                    self.memset("pool", Rb, Rb[:], 0.0)
                self.load(pr[b], pr[b][:], self.PS[p0:p0 + 128, 0:1280], f"l0{b}", reads=[("PS", s, i)])
                self.load(rot[b], rot[b][:], di["c_rot"][i * 128:(i + 1) * 128, :], f"l1{b}")
                P, RT = pr[b], rot[b]
                for j, (c0, tb) in enumerate(((0, 0), (512, 128))):
                    o = j * 256
                    v4 = lambda ap: ap.rearrange("p (a b) -> p a b", a=4)
                    cosb = bc(RT[:, tb:tb + 64].unsqueeze(1), [128, 4, 64])
                    sinb = bc(RT[:, tb + 64:tb + 128].unsqueeze(1), [128, 4, 64])
                    self.tt("dve", qh, v4(qh[:, o:o + 256]), P, v4(P[:, c0:c0 + 256]), RT, cosb, ALU.mult)
                    self.tt("pool", t1, v4(t1[:, o:o + 256]), P, v4(P[:, c0 + 256:c0 + 512]), RT, sinb, ALU.mult)
                    self.tt("dve", qh, qh[:, o:o + 256], qh, qh[:, o:o + 256], t1, t1[:, o:o + 256], ALU.add)
                for c in range(4):
                    self.tr(psT, psT[:, c, :], qh, qh[:, c * 128:(c + 1) * 128])
                self.cp("act", qkT, qkT[:, 2:4, :], psT, psT[:, 2:4, :])
                self.cp("act", qz, qz[0:64, 0:4:2, :], psT, psT[0:64, 0:2, :])
                self.cp("dve", qz, qz[64:128, 1:4:2, :], psT, psT[64:128, 0:2, :])
                self.cp("act", vb, vb[:], P, P[:, 1024:1280])
                kv = qh[:, 256:512].rearrange("p (a b) -> p a b", a=4)
                self.tt("dve", kdb, kdb[:].rearrange("p (a b) -> p a b", a=4), qh, kv, qk, bc(qk[:, 4:8].unsqueeze(2), [128, 4, 64]), ALU.mult)
                hp = lambda hh: slice((hh % 2) * 64, (hh % 2) * 64 + 64)
                if d == 0:
                    for hh in range(4):
                        self.mm(psS, psS[:, hh, :], qkT, qkT[:, 2 + hh // 2, :], qz, qz[:, hh, :])
                    self.tt("dve", sT, sT[:], psS, psS[:], mask, mask[:].rearrange("p (a b) -> p a b", a=4), ALU.mult)
                    for hh in range(4):
                        self.mm(psO, psO[:, hh * 64:(hh + 1) * 64], sT, sT[:, hh, :], vb, vb[:, hh * 64:(hh + 1) * 64])
                    self.cp("act", tmp, tmp[:], psO, psO[:])
                for hh in range(4):
                    self.mm(psC, psC[:, hh * 64:(hh + 1) * 64], qz, qz[:, hh, :], Rb,
                            Rb[:, (hh // 2) * 128 + (hh % 2) * 64:(hh // 2) * 128 + (hh % 2) * 64 + 64])
                y = yo[b]
                self.tt("dve", y, y[:].rearrange("p (a b) -> p a b", a=4), psC, psC[:].rearrange("p (a b) -> p a b", a=4),
                        qk, bc(qk[:, 0:4].unsqueeze(2), [128, 4, 64]), ALU.mult)
                if d == 0:
                    self.tt("pool", y, y[:], y, y[:], tmp, tmp[:], ALU.add)
                r0 = self.rows(s, i)
                self.store(y, self.YD[d, r0:r0 + 128, 0:256], y[:], f"s0{b}", writes=[("YDr", d, s, i)])
                for pr_ in range(2):
                    self.mm(psR, psR[:, pr_ * 128:(pr_ + 1) * 128], kdb, kdb[:, pr_ * 128:(pr_ + 1) * 128], vb, vb[:, pr_ * 128:(pr_ + 1) * 128])
                self.tt("dve", R, R[:], R, R[:], gc, gc[:], ALU.mult)
                self.tt("dve", R, R[:], R, R[:], psR, psR[:], ALU.add)
                self.cp("act", Rb, Rb[:], R, R[:])
            self.end_phase()

    def phase_rwkv(self, l, d):
        di = self.di
        self.S.pe_skip = not self.flags.get("noskip_rwkv")
        with ExitStack() as es:
            sb = lambda n, sh, dt=F32: self.sb(es, n, sh, dt)
            v4 = lambda ap: ap.rearrange("p (a b) -> p a b", a=4)
            CS = sb("wCS", [128, 128]); self.load(CS, CS[:], di["c_cs"][d], "setup")
            ON = sb("wON", [128, 128]); self.load(ON, ON[:], di["c_ones"][:, :], "setup")
            MSI = sb("wMSI", [128, 256]); self.load(MSI, MSI[:], di["c_msi"][d], "setup")
            MST = sb("wMST", [128, 128]); self.load(MST, MST[:], di["c_mst"][d], "setup")
            MU = self.bcast_load(es, "wMU", di["rwkv_mu"][l], 1024)
            W0 = self.bcast_load(es, "wW0", di["rwkv_w0"][l, d], 256)
            A0 = self.bcast_load(es, "wA0", di["rwkv_a0"][l, d], 256)
            KK = self.bcast_load(es, "wKK", di["rwkv_k_k"][l], 256)
            KA = self.bcast_load(es, "wKA", di["rwkv_k_a"][l], 256)
            RK = self.bcast_load(es, "wRK", di["rwkv_r_k"][l], 256)
            stg = sb("wstg", [128, 256])
            LW = sb("wLW", [128, 256], BF16)
            self.load(stg, stg[0:64, :], di["rwkv_w2"][l, d], "setup")
            self.load(stg, stg[64:128, :], di["rwkv_a2"][l, d], "setup")
            stg2 = sb("wstg2", [128, 256])
            G2 = sb("wG2", [128, 256], BF16)
            self.load(stg2, stg2[:], di["rwkv_g2"][l], "setup")
            self.S.barrier()
            self.cp("dve", LW, LW[:], stg, stg[:])
            self.cp("dve", G2, G2[:], stg2, stg2[:])
            R2 = range(2); R3 = range(3)
            pm = [sb(f"wpm{i}", [128, 1024]) for i in R2]
            pc = [sb(f"wpc{i}", [128, 1024]) for i in R2]
            pn = [sb(f"wpn{i}", [128, 1024]) for i in R2]
            pp = sb("wpp", [128, 1024]); sh = sb("wsh", [128, 1024])
            ldT = sb("wldT", [128, 2, 128], BF16)
            targ = sb("wtarg", [128, 512]); sg = sb("wsg", [128, 256]); asg = sb("wasg", [128, 256])
            kk = sb("wkk", [128, 256]); kkr = sb("wkkr", [128, 256]); junk = sb("wjunk", [128, 64])
            ssq = sb("wssq", [128, 16]); kd = sb("wkd", [128, 256]); bv = sb("wbv", [128, 256])
            t1 = sb("wt1", [128, 256])
            yo = [sb(f"wyo{i}", [128, 768]) for i in R3]
            lws = sb("wlws", [128, 256]); lx = sb("wlx", [128, 256]); lt = sb("wlt", [128, 256])
            Wt = sb("wWt", [128, 256]); Wn = sb("wWn", [128, 256]); Wx = sb("wWx", [128, 256]); Wh = sb("wWh", [128, 256])
            X4 = sb("wX4", [128, 1024])
            Atb = [sb(f"wAtb{i}", [128, 256], BF16) for i in R2]
            Bhb = [sb(f"wBhb{i}", [128, 256], BF16) for i in R3]; Khb = [sb(f"wKhb{i}", [128, 256], BF16) for i in R3]
            vb = [sb(f"wvb{i}", [128, 256], BF16) for i in R3]
            XT = [sb(f"wXT{i}", [128, 2, 4, 128], BF16) for i in R3]
            WCk = [sb(f"wWCk{i}", [128, 2]) for i in R3]
            Nb = [sb(f"wNb{i}", [128, 4, 128], BF16) for i in R2]; NTb = [sb(f"wNTb{i}", [128, 4, 128], BF16) for i in R2]
            Pb = [sb(f"wPb{i}", [128, 4, 128], BF16) for i in R2]
            PTb = [sb(f"wPTb{i}", [128, 4, 128], BF16) for i in R2]
            Mf = [sb(f"wMf{i}", [128, 4, 128]) for i in R2]; Mb = [sb(f"wMb{i}", [128, 4, 128], BF16) for i in R2]
            Arb = [sb(f"wArb{i}", [128, 4, 128], BF16) for i in R3]; Ark = [sb(f"wArk{i}", [128, 4, 128], BF16) for i in R3]
            Aak = [sb(f"wAak{i}", [128, 4, 128], BF16) for i in R2]
            Xb = sb("wXb", [128, 256], BF16); Ub = sb("wUb", [128, 256], BF16)
            Ut = [sb(f"wUt{i}", [128, 256]) for i in R2]
            WtT = [sb(f"wWtT{i}", [128, 4, 128], BF16) for i in R2]
            H = sb("wH", [128, 256]); Hb = sb("wHb", [128, 256], BF16)
            identq = sb("widq", [128, 4, 128])
            for hh in range(4):
                self.cp("pool", identq, identq[:, hh, :], self.identf, self.identf[:])
            pA = [self.ps(es, f"wpA{i}", [128, 512]) for i in range(4)]
            pB = [self.ps(es, f"wpB{i}", [128, 512]) for i in range(2)]
            pC = [self.ps(es, f"wpC{i}", [128, 512]) for i in range(2)]
            hp = lambda hh: slice((hh % 2) * 64, (hh % 2) * 64 + 64)
            hcol = lambda hh: slice(hh * 64, (hh + 1) * 64)
            msb = bc(MSI[:, 0:128].unsqueeze(1), [128, 4, 128])
            mib = bc(MSI[:, 128:256].unsqueeze(1), [128, 4, 128])
            mtb = bc(MST[:].unsqueeze(1), [128, 4, 128])

            def sA(j, s, i, first):
                b = j % 2; c = j % 3
                p0 = self.prow(s, i)
                self.load(pm[b], pm[b][:], self.PS[p0 - 1:p0 + 127, 1536:2560], f"l0{b}", reads=[("PS", s, i), ("PS", s, i - 1), "PSall"])
                self.load(pc[b], pc[b][:], self.PS[p0:p0 + 128, 1536:2560], f"l1{b}", reads=[("PS", s, i)])
                self.load(pn[b], pn[b][:], self.PS[p0 + 1:p0 + 129, 1536:2560], f"l2{b}", reads=[("PS", s, i), ("PS", s, i + 1), "PSall"])
                self.tt("pool", sh, sh[:], pm[b], pm[b][:], pn[b], pn[b][:], ALU.add)
                self.stt("dve", sh, sh[:], sh, sh[:], 0.5, pc[b], pc[b][:], ALU.mult, ALU.subtract)
                self.tt("pool", sh, sh[:], sh, sh[:], MU, MU[:], ALU.mult)
                self.tt("dve", pp, pp[:], pc[b], pc[b][:], sh, sh[:], ALU.add)
                yield
                r_, k_, v_ = pp[:, 0:256], pp[:, 256:512], pp[:, 512:768]
                Y = yo[c]
                self.tr(pA[0], pA[0][:, 0:128], pp, pp[:, 768:896])
                self.tr(pA[0], pA[0][:, 128:256], pp, pp[:, 896:1024])
                self.act(ldT, ldT[0:64, 0, :], pA[0], pA[0][0:64, 0:128], AF.Tanh)
                self.cp("dve", ldT, ldT[64:128, 0, :], pA[0], pA[0][64:128, 0:128])
                self.mm(pA[1], pA[1][:, 0:256], ldT, ldT[0:64, 0, :], LW, LW[0:64, :])
                self.mm(pA[1], pA[1][:, 256:512], ldT, ldT[64:128, 0, :], LW, LW[64:128, :])
                self.tt("dve", targ, targ[:, 0:256], pA[1], pA[1][:, 0:256], W0, W0[:], ALU.add)
                self.tt("dve", targ, targ[:, 256:512], pA[1], pA[1][:, 256:512], A0, A0[:], ALU.add)
                self.act(sg, sg[:], targ, targ[:, 0:256], AF.Sigmoid)
                self.act(asg, asg[:], targ, targ[:, 256:512], AF.Sigmoid)
                if d == 0:
                    self.act(ldT, ldT[:, 1, :], pA[0], pA[0][:, 128:256], AF.Sigmoid)
                    self.mm(pA[2], pA[2][:, 0:256], ldT, ldT[:, 1, :], G2, G2[:])
                    self.cp("act", Y, Y[:, 512:768], pA[2], pA[2][:, 0:256])
                yield
                self.tt("pool", kkr, kkr[:], pp, k_, KK, KK[:], ALU.mult)
                for hh in range(4):
                    self.act(junk, junk[:], kkr, kkr[:, hh * 64:(hh + 1) * 64], AF.Square, accum=ssq[:, hh:hh + 1], extra_w=[ssq])
                self.act(ssq, ssq[:, 4:8], ssq, ssq[:, 0:4], AF.Sqrt)
                self.ts("dve", ssq, ssq[:, 8:12], ssq, ssq[:, 4:8], 1e-12, None, ALU.max)
                self.O("dve", lambda e, o=ssq[:, 12:16], i_=ssq[:, 8:12]: e.reciprocal(out=o, in_=i_), reads=[ssq], writes=[ssq])
                self.tt("dve", kk, v4(kk[:]), kkr, v4(kkr[:]), ssq, bc(ssq[:, 12:16].unsqueeze(2), [128, 4, 64]), ALU.mult)
                self.stt("dve", t1, t1[:], asg, asg[:], -1.0, KA, KA[:], ALU.add, ALU.mult)
                self.stt("dve", kd, kd[:], t1, t1[:], 1.0, pp, k_, ALU.add, ALU.mult)
                self.tt("pool", bv, bv[:], kk, kk[:], asg, asg[:], ALU.mult)
                yield
                self.tt("dve", t1, t1[:], pp, r_, kd, kd[:], ALU.mult)
                self.tt("pool", t1, t1[:], t1, t1[:], RK, RK[:], ALU.mult)
                self.O("dve", lambda e, o=ssq[:, 0:4], i_=v4(t1[:]): e.tensor_reduce(out=o, in_=i_, axis=AX.X, op=ALU.add), reads=[t1], writes=[ssq])
                self.tt("dve", Y, v4(Y[:, 256:512]), pp, v4(v_), ssq, bc(ssq[:, 0:4].unsqueeze(2), [128, 4, 64]), ALU.mult)
                self.cp("act", vb[c], vb[c][:], pp, v_)
                self.mm(pA[2], pA[2][:, 256:512], CS, CS[:], sg, sg[:])
                self.mm(pA[3], pA[3][:, 0:256], ON, ON[:], sg, sg[:])
                for pr_ in range(2):
                    self.mm(pA[3], pA[3][:, 256 + pr_:257 + pr_], sg, sg[:, pr_ * 128:(pr_ + 1) * 128], ON, ON[:, 0:1])
                self.cp("act", lws, lws[:], pA[2], pA[2][:, 256:512])
                self.tt("dve", lx, lx[:], lws, lws[:], sg, sg[:], ALU.subtract)
                self.tt("dve", lt, lt[:], pA[3], pA[3][:, 0:256], lws, lws[:], ALU.subtract)
                self.act(Wt, Wt[:], lws, lws[:], AF.Exp, scale=CDEC)
                self.act(Wn, Wn[:], lws, lws[:], AF.Exp, scale=-CDEC)
                self.act(Wx, Wx[:], lx, lx[:], AF.Exp, scale=CDEC)
                self.act(Wh, Wh[:], lt, lt[:], AF.Exp, scale=CDEC)
                self.act(WCk[c], WCk[c][:], pA[3], pA[3][:, 256:258], AF.Exp, scale=CDEC)
                yield
                self.stt("dve", X4, X4[:, 0:256], kk, kk[:], -1.0, Wx, Wx[:], ALU.mult, ALU.mult)
                self.tt("pool", X4, X4[:, 256:512], pp, r_, Wt, Wt[:], ALU.mult)
                self.tt("dve", X4, X4[:, 512:768], bv, bv[:], Wn, Wn[:], ALU.mult)
                self.tt("pool", X4, X4[:, 768:1024], kd, kd[:], Wn, Wn[:], ALU.mult)
                self.cp("act", Atb[b], Atb[b][:], X4, X4[:, 0:256])
                self.tt("dve", Bhb[c], Bhb[c][:], bv, bv[:], Wh, Wh[:], ALU.mult)
                self.tt("pool", Khb[c], Khb[c][:], kd, kd[:], Wh, Wh[:], ALU.mult)
                yield
                X = XT[c]
                for kind in range(4):
                    pt_ = pA[0] if kind < 2 else pA[1]
                    for pr_ in range(2):
                        c0 = kind * 256 + pr_ * 128
                        self.tr(pt_, pt_[:, ((kind % 2) * 2 + pr_) * 128:((kind % 2) * 2 + pr_ + 1) * 128], X4, X4[:, c0:c0 + 128])
                for kind in range(4):
                    pt_ = pA[0] if kind < 2 else pA[1]
                    self.cp(("act", "dve")[kind % 2], X, X[:, :, kind, :],
                            pt_, pt_[:, (kind % 2) * 256:(kind % 2) * 256 + 256].rearrange("p (a b) -> p a b", a=2))
                yield
                def amat(ps_, lk, rk):
                    for hh in range(4):
                        pq = hh // 2
                        self.mm(ps_, ps_[:, hh * 128:(hh + 1) * 128], X, X[hp(hh), pq, lk, :], X, X[hp(hh), pq, rk, :])
                amat(pA[2], 2, 0)
                self.tt("dve", Nb[b], Nb[b][:], pA[2], v4(pA[2][:]), MSI, msb, ALU.mult)
                self.tt("dve", Mf[b], Mf[b][:], pA[2], v4(pA[2][:]), MSI, msb, ALU.mult)
                amat(pA[3], 2, 1)
                self.tt("dve", Arb[c], Arb[c][:], pA[3], v4(pA[3][:]), MSI, mib, ALU.mult)
                yield
                amat(pA[0], 3, 0)
                self.tt("dve", Aak[b], Aak[b][:], pA[0], v4(pA[0][:]), MSI, msb, ALU.mult)
                amat(pA[1], 3, 1)
                self.tt("dve", Ark[c], Ark[c][:], pA[1], v4(pA[1][:]), MSI, mib, ALU.mult)
                amat(pA[2], 0, 2)
                self.tt("dve", NTb[b], NTb[b][:], pA[2], v4(pA[2][:]), MST, mtb, ALU.mult)
                self.tt("pool", Mf[b], Mf[b][:], Mf[b], Mf[b][:], identq, identq[:], ALU.add)
                self.cp("act", Mb[b], Mb[b][:], Mf[b], Mf[b][:])
                yield

            def sB(j, s, i, first):
                b = j % 2; c = j % 3
                MF, MB = Mf[b], Mb[b]
                Pc, PTc = Nb[b], NTb[b]
                for st in range(6):
                    Pn, PTn = Pb[st % 2], PTb[st % 2]
                    for hh in range(4):
                        if st < 5:
                            self.mm(pB[0], pB[0][:, hh * 128:(hh + 1) * 128], PTc, PTc[:, hh, :], Pc, Pc[:, hh, :])
                        self.mm(pB[1], pB[1][:, hh * 128:(hh + 1) * 128], Pc, Pc[:, hh, :], PTc, PTc[:, hh, :])
                    if st < 5:
                        self.cp("act", Pn, Pn[:], pB[0], v4(pB[0][:]))
                    self.cp("dve", PTn, PTn[:], pB[1], v4(pB[1][:]))
                    yield
                    for hh in range(4):
                        self.mm(pB[0], pB[0][:, hh * 128:(hh + 1) * 128], PTn, PTn[:, hh, :], MB, MB[:, hh, :])
                    self.tt("dve", MF, MF[:], MF, MF[:], pB[0], v4(pB[0][:]), ALU.add)
                    self.cp("act", MB, MB[:], MF, MF[:])
                    yield
                    Pc, PTc = Pn, PTn
                V = vb[c]
                for hh in range(4):
                    self.mm(pB[1], pB[1][:, hh * 64:(hh + 1) * 64], Aak[b], Aak[b][:, hh, :], V, V[:, hh * 64:(hh + 1) * 64])
                self.cp("act", Xb, Xb[:], pB[1], pB[1][:, 0:256])
                for hh in range(4):
                    self.mm(pB[0], pB[0][:, hh * 64:(hh + 1) * 64], MB, MB[:, hh, :], Xb, Xb[:, hh * 64:(hh + 1) * 64])
                self.cp("act", Ut[b], Ut[b][:], pB[0], pB[0][:, 0:256])
                yield
                for hh in range(4):
                    self.mm(pB[1], pB[1][:, hh * 128:(hh + 1) * 128], Atb[b], Atb[b][:, (hh // 2) * 128:(hh // 2) * 128 + 128], MB, MB[:, hh, :])
                self.cp("dve", WtT[b], WtT[b][:], pB[1], v4(pB[1][:]))
                yield

            def sC(j, s, i, first):
                b = j % 2; c = j % 3
                if first:
                    self.memset("pool", H, H[:], 0.0)
                    self.memset("pool", Hb, Hb[:], 0.0)
                X, V, Y, UT = XT[c], vb[c], yo[c], Ut[b]
                for hh in range(4):
                    self.mm(pC[0], pC[0][:, hcol(hh)], WtT[b], WtT[b][hp(hh), hh, :], Hb, Hb[hp(hh), hcol(hh)])
                self.tt("dve", UT, UT[:], UT, UT[:], pC[0], pC[0][:, 0:256], ALU.add)
                self.cp("act", Ub, Ub[:], UT, UT[:])
                yield
                for hh in range(4):
                    pq = hh // 2
                    oc = slice(256 + hh * 64, 256 + (hh + 1) * 64)
                    self.mm(pC[0], pC[0][:, oc], X, X[hp(hh), pq, 1, :], Hb, Hb[hp(hh), hcol(hh)], start=True, stop=False)
                    self.mm(pC[0], pC[0][:, oc], Arb[c], Arb[c][:, hh, :], Ub, Ub[:, hcol(hh)], start=False, stop=False)
                    self.mm(pC[0], pC[0][:, oc], Ark[c], Ark[c][:, hh, :], V, V[:, hcol(hh)], start=False, stop=True)
                for hh in range(4):
                    pq = hh // 2
                    self.mm(pC[1], pC[1][:, hcol(hh)], Bhb[c], Bhb[c][:, pq * 128:(pq + 1) * 128], Ub, Ub[:, hcol(hh)], start=True, stop=False)
                    self.mm(pC[1], pC[1][:, hcol(hh)], Khb[c], Khb[c][:, pq * 128:(pq + 1) * 128], V, V[:, hcol(hh)], start=False, stop=True)
                for pq in range(2):
                    self.ts("dve", H, H[:, pq * 128:(pq + 1) * 128], H, H[:, pq * 128:(pq + 1) * 128], WCk[c][:, pq:pq + 1], None, ALU.mult, extra_r=[WCk[c]])
                self.tt("dve", H, H[:], H, H[:], pC[1], pC[1][:, 0:256], ALU.add)
                self.cp("act", Hb, Hb[:], H, H[:])
                yield
                self.cp("act", Y, Y[:, 0:256], pC[0], pC[0][:, 256:512])
                r0 = self.rows(s, i)
                ncol = 768 if d == 0 else 512
                self.store(Y, self.YD[d, r0:r0 + 128, 256:256 + ncol], Y[:, 0:ncol], f"s0{c}", writes=[("YDw", d, s, i)])
                yield
            self.pipeline(self.tiles(d), [sA, sB, sC])
            self.end_phase()

    def phase_s5(self, l, d):
        di = self.di
        self.S.pe_skip = True
        PI = math.pi
        with ExitStack() as es:
            sb = lambda n, sh, dt=F32: self.sb(es, n, sh, dt)
            COS = sb("sCOS", [128, 32, 128]); SIN = sb("sSIN", [128, 32, 128]); RHO = sb("sRHO", [128, 32, 128])
            BP1 = sb("sBP1", [128, 32, 128], BF16); BP2 = sb("sBP2", [128, 32, 128], BF16)
            W1 = sb("sW1", [128, 512], BF16); W2 = sb("sW2", [128, 512], BF16)
            RHOF = sb("sRHOF", [128, 32]); CL = sb("sCL", [128, 32]); SL = sb("sSL", [128, 32])
            SWAP = sb("sSWAP", [128, 128]); self.load(SWAP, SWAP[:], di["c_swap"][:, :], "p0")
            psA = self.ps(es, "spsA", [128, 1024]); psB = self.ps(es, "spsB", [128, 1024])
            psY = self.ps(es, "spsY", [128, 512]); psT = self.ps(es, "spsT", [128, 4, 128]); psX = self.ps(es, "spsX", [128, 512])
            with ExitStack() as es2:
                sb2 = lambda n, sh, dt=F32: self.sb(es2, n, sh, dt)
                J1 = sb2("sJ1", [128, 128]); self.load(J1, J1[:], di["c_j1"][d], "p1")
                Z0 = sb2("sZ0", [128, 128]); self.load(Z0, Z0[:], di["c_z0"][d], "p2")
                SEL = sb2("sSEL", [128, 128]); self.load(SEL, SEL[:], di["c_sel"][:, :], "p3")
                SELb = sb2("sSELb", [128, 128], BF16); self.cp("dve", SELb, SELb[:], SEL, SEL[:])
                negpi = sb2("snegpi", [128, 1]); self.memset("pool", negpi, negpi[:], -PI)
                lre = sb2("slre", [128, 32]); lim = sb2("slim", [128, 32]); stp = sb2("sstp", [128, 32])
                lam_re_T = di["s5_lam_re"][l, d].rearrange("g p -> p g")
                lam_im_T = di["s5_lam_im"][l, d].rearrange("g p -> p g")
                for hf in range(2):
                    self.O("sp", lambda e, o=lre[hf * 64:(hf + 1) * 64, :], i_=lam_re_T: e.dma_start(out=o, in_=i_, allow_slow_non_contiguous=True), writes=[lre], dma="p4")
                    self.O("sp", lambda e, o=lim[hf * 64:(hf + 1) * 64, :], i_=lam_im_T: e.dma_start(out=o, in_=i_, allow_slow_non_contiguous=True), writes=[lim], dma="p5")
                self.load(stp, stp[:], di["s5_log_step"][l, d].partition_broadcast(128), "p6")
                self.act(stp, stp[:], stp, stp[:], AF.Exp)
                self.ts("dve", lre, lre[:], lre, lre[:], -1e-4, None, ALU.min)
                al = sb2("sal", [128, 32]); th = sb2("sth", [128, 32])
                self.tt("dve", al, al[:], lre, lre[:], stp, stp[:], ALU.mult)
                self.tt("dve", th, th[:], lim, lim[:], stp, stp[:], ALU.mult)
                self.act(RHOF, RHOF[:], al, al[:], AF.Exp)
                arg = sb2("sarg", [128, 32, 128]); arg2 = sb2("sarg2", [128, 32, 128])
                argi = sb2("sargi", [128, 32, 128], mybir.dt.int32); argf = sb2("sargf", [128, 32, 128])

                def sinred(o_t, o_ap, a_t, a_ap, w_ap, wi_ap, wf_ap, off):
                    self.ts("dve", arg2, w_ap, a_t, a_ap, 1.0 / (2 * PI), off, ALU.mult, ALU.add)
                    self.cp("dve", argi, wi_ap, arg2, w_ap)
                    self.cp("dve", argf, wf_ap, argi, wi_ap)
                    self.tt("dve", arg2, w_ap, arg2, w_ap, argf, wf_ap, ALU.subtract)
                    self.act(o_t, o_ap, arg2, w_ap, AF.Sin, scale=2 * PI)
                self.tt("dve", arg, arg[:], th, bc(th[:].unsqueeze(2), [128, 32, 128]), J1, bc(J1[:].unsqueeze(1), [128, 32, 128]), ALU.mult)
                sinred(SIN, SIN[:], arg, arg[:], arg2[:], argi[:], argf[:], 0.0)
                sinred(COS, COS[:], arg, arg[:], arg2[:], argi[:], argf[:], 0.25)
                self.tt("dve", RHO, RHO[:], RHOF, bc(RHOF[:].unsqueeze(2), [128, 32, 128]), Z0, bc(Z0[:].unsqueeze(1), [128, 32, 128]), ALU.mult)
                jl = 127 if d == 0 else 0
                self.cp("dve", CL, CL[:], COS, COS[:, :, jl])
                self.cp("dve", SL, SL[:], SIN, SIN[:, :, jl])
                lre2 = sb2("slre2", [128, 16]); lim2 = sb2("slim2", [128, 16]); stp2 = sb2("sstp2", [128, 16])
                self.load(lre2, lre2[:], di["s5_lam_re"][l, d].rearrange("(gp go) p -> (go p) gp", go=2), "p7")
                self.load(lim2, lim2[:], di["s5_lam_im"][l, d].rearrange("(gp go) p -> (go p) gp", go=2), "p8")
                ls2 = di["s5_log_step"][l, d].rearrange("(gp go) -> go gp", go=2)
                for go in range(2):
                    self.load(stp2, stp2[go * 64:(go + 1) * 64, :], ls2[go].partition_broadcast(64), "p9")
                self.act(stp2, stp2[:], stp2, stp2[:], AF.Exp)
                self.ts("dve", lre2, lre2[:], lre2, lre2[:], -1e-4, None, ALU.min)
                pw = sb2("spw", [128, 16 * 12])
                c = lambda k: pw[:, k * 16:(k + 1) * 16]
                TTm = lambda o, a, b_, op: self.tt("dve", pw, o, pw, a, pw, b_, op)
                self.tt("dve", pw, c(0), lre2, lre2[:], stp2, stp2[:], ALU.mult)
                self.tt("dve", pw, c(1), lim2, lim2[:], stp2, stp2[:], ALU.mult)
                self.act(pw, c(2), pw, c(0), AF.Exp)
                sinred(pw, c(4), pw, c(1), arg2[:, 0, 0:16], argi[:, 0, 0:16], argf[:, 0, 0:16], 0.0)
                sinred(pw, c(5), pw, c(1), arg2[:, 0, 0:16], argi[:, 0, 0:16], argf[:, 0, 0:16], 0.25)
                TTm(c(5), c(5), c(2), ALU.mult)
                TTm(c(4), c(4), c(2), ALU.mult)
                self.ts("dve", pw, c(5), pw, c(5), -1.0, None, ALU.add)
                self.tt("dve", pw, c(6), lre2, lre2[:], lre2, lre2[:], ALU.mult)
                self.tt("dve", pw, c(7), lim2, lim2[:], lim2, lim2[:], ALU.mult)
                TTm(c(6), c(6), c(7), ALU.add)
                self.O("dve", lambda e, o=c(6), i_=c(6): e.reciprocal(out=o, in_=i_), reads=[pw], writes=[pw])
                self.tt("dve", pw, c(7), pw, c(5), lre2, lre2[:], ALU.mult)
                self.tt("dve", pw, c(8), pw, c(4), lim2, lim2[:], ALU.mult)
                TTm(c(7), c(7), c(8), ALU.add)
                TTm(c(7), c(7), c(6), ALU.mult)
                self.tt("dve", pw, c(8), pw, c(4), lre2, lre2[:], ALU.mult)
                self.tt("dve", pw, c(9), pw, c(5), lim2, lim2[:], ALU.mult)
                TTm(c(8), c(8), c(9), ALU.subtract)
                TTm(c(8), c(8), c(6), ALU.mult)
                Bre = sb2("sBre", [128, 16, 16]); Bim = sb2("sBim", [128, 16, 16])
                self.load(Bre, Bre[:], di["s5_b_re"][l, d].rearrange("(gp go) p c -> (go p) gp c", go=2), "p10")
                self.load(Bim, Bim[:], di["s5_b_im"][l, d].rearrange("(gp go) p c -> (go p) gp c", go=2), "p11")
                bbr = sb2("sbbr", [128, 16, 16]); bbi = sb2("sbbi", [128, 16, 16]); tq = sb2("stq", [128, 16, 16])
                cre = bc(c(7).unsqueeze(2), [128, 16, 16]); cim = bc(c(8).unsqueeze(2), [128, 16, 16])
                self.tt("dve", bbr, bbr[:], Bre, Bre[:], pw, cre, ALU.mult)
                self.tt("dve", tq, tq[:], Bim, Bim[:], pw, cim, ALU.mult)
                self.tt("dve", bbr, bbr[:], bbr, bbr[:], tq, tq[:], ALU.subtract)
                self.tt("dve", bbi, bbi[:], Bim, Bim[:], pw, cre, ALU.mult)
                self.tt("dve", tq, tq[:], Bre, Bre[:], pw, cim, ALU.mult)
                self.tt("dve", bbi, bbi[:], bbi, bbi[:], tq, tq[:], ALU.add)
                XPr = sb2("sXPr", [128, 8, 128], BF16); XPi = sb2("sXPi", [128, 8, 128], BF16)
                self.memset("pool", XPr, XPr[:], 0.0); self.memset("pool", XPi, XPi[:], 0.0)
                for g in range(32):
                    gp, go, gi = g // 2, g % 2, g % 8
                    self.cp("dve", XPr, XPr[:, gi, gi * 16:(gi + 1) * 16], bbr, bbr[:, gp, :])
                    self.cp("pool", XPi, XPi[:, gi, gi * 16:(gi + 1) * 16], bbi, bbi[:, gp, :])
                    selg = SELb[:, go * 64:(go + 1) * 64]
                    self.mm(psA, psA[:, 0:64], XPr, XPr[:, gi, :], SELb, selg)
                    self.mm(psA, psA[:, 64:128], XPi, XPi[:, gi, :], SELb, selg)
                    self.cp("act", BP1, BP1[:, g, :], psA, psA[:, 0:128])
                    self.act(BP2, BP2[:, g, 0:64], psA, psA[:, 64:128], AF.Copy, scale=-1.0)
                    self.cp("dve", BP2, BP2[:, g, 64:128], psA, psA[:, 0:64])
                Ca = sb2("sCa", [128, 4, 128]); Cb = sb2("sCb", [128, 4, 128])
                cre_d = di["s5_c_re"][l, d].rearrange("(a gi) c p -> (gi c) a p", gi=8)
                cim_d = di["s5_c_im"][l, d].rearrange("(a gi) c p -> (gi c) a p", gi=8)
                self.load(Ca, Ca[:, :, 0:64], cre_d, "p12"); self.load(Ca, Ca[:, :, 64:128], cim_d, "p13")
                self.load(Cb, Cb[:, :, 0:64], cim_d, "p14"); self.load(Cb, Cb[:, :, 64:128], cre_d, "p15")
                for a in range(4):
                    self.tr(psT, psT[:, a, :], Ca, Ca[:, a, :])
                self.cp("act", W1, W1[0:64, :], psT, psT[0:64, :, :].rearrange("p a b -> p (a b)"))
                self.act(W1, W1[64:128, :], psT, psT[64:128, :, :].rearrange("p a b -> p (a b)"), AF.Copy, scale=-1.0)
                for a in range(4):
                    self.tr(psT, psT[:, a, :], Cb, Cb[:, a, :])
                self.act(W2, W2[:], psT, psT[:].rearrange("p a b -> p (a b)"), AF.Copy, scale=-1.0)
                self.S.barrier()
                self.S.flush()
            u = [sb(f"su{i}", [128, 512]) for i in range(2)]
            uT = sb("suT", [128, 4, 128], BF16)
            t1 = [sb(f"st1{i}", [128, 1024]) for i in range(2)]; t2 = [sb(f"st2{i}", [128, 1024]) for i in range(2)]
            zin = [sb(f"szin{i}", [128, 32, 128]) for i in range(2)]; z = sb("sz", [128, 32, 128])
            ZC = sb("sZC", [128, 32, 128], BF16); ZS = sb("sZS", [128, 32, 128], BF16)
            xst = sb("sxst", [128, 32]); xc = sb("sxc", [128, 32]); xs_ = sb("sxs", [128, 32]); tm = sb("stm", [128, 32])
            yo = [sb(f"syo{i}", [128, 512]) for i in range(2)]
            j0 = 0 if d == 0 else 127
            fl = lambda ap: ap.rearrange("p a b -> p (a b)")

            def s0(j, s, i, first):
                b = j % 2
                p0 = self.prow(s, i)
                self.load(u[b], u[b][:], self.PS[p0:p0 + 128, 2560:3072], f"l0{b}", reads=[("PS", s, i)])
                for ct in range(4):
                    self.tr(psT, psT[:, ct, :], u[b], u[b][:, ct * 128:(ct + 1) * 128])
                self.cp("act", uT, uT[:], psT, psT[:])
                yield
                for ct in range(4):
                    for gi in range(8):
                        g = ct * 8 + gi
                        self.mm(psA, psA[:, gi * 128:(gi + 1) * 128], BP1, BP1[:, g, :], uT, uT[:, ct, :])
                        self.mm(psB, psB[:, gi * 128:(gi + 1) * 128], BP2, BP2[:, g, :], uT, uT[:, ct, :])
                    gs = slice(ct * 8, ct * 8 + 8)
                    self.tt("dve", t1[ct % 2], t1[ct % 2][:], psA, psA[:], COS, fl(COS[:, gs, :]), ALU.mult)
                    self.tt("dve", t2[ct % 2], t2[ct % 2][:], psB, psB[:], SIN, fl(SIN[:, gs, :]), ALU.mult)
                    self.tt("pool", zin[b], fl(zin[b][:, gs, :]), t1[ct % 2], t1[ct % 2][:], t2[ct % 2], t2[ct % 2][:], ALU.subtract)
                    yield

            def s1(j, s, i, first):
                b = j % 2
                Z = zin[b]
                if first:
                    self.memset("pool", xst, xst[:], 0.0)
                self.tt("dve", tm, tm[:], RHOF, RHOF[:], xst, xst[:], ALU.mult)
                self.tt("dve", Z, Z[:, :, j0], Z, Z[:, :, j0], tm, tm[:], ALU.add)
                zf = z[:].rearrange("p a b -> p (a b)"); zif = Z[:].rearrange("p a b -> p (a b)"); rf = RHO[:].rearrange("p a b -> p (a b)")
                if d == 1:
                    zf, zif, rf = zf[:, ::-1], zif[:, ::-1], rf[:, ::-1]
                self.O("dve", lambda e, o=zf, a=rf, b_=zif: e.tensor_tensor_scan(out=o, data0=a, data1=b_, initial=0.0, op0=ALU.mult, op1=ALU.add),
                       reads=[RHO, Z], writes=[z])
                yield
                jl = 127 - j0
                self.tt("dve", xc, xc[:], z, z[:, :, jl], CL, CL[:], ALU.mult)
                self.tt("dve", xs_, xs_[:], z, z[:, :, jl], SL, SL[:], ALU.mult)
                self.mm(psX, psX[:, 0:32], SWAP, SWAP[:], xs_, xs_[:])
                self.tt("dve", xst, xst[:], xc, xc[:], psX, psX[:, 0:32], ALU.add)
                yield
                for hf in range(2):
                    gs = slice(hf * 16, hf * 16 + 16)
                    self.tt("dve", ZC, ZC[:, gs, :], z, z[:, gs, :], COS, COS[:, gs, :], ALU.mult)
                    self.tt("pool" if hf == 0 else "dve", ZS, ZS[:, gs, :], z, z[:, gs, :], SIN, SIN[:, gs, :], ALU.mult)
                    yield
                for g in range(32):
                    self.mm(psY, psY[:, g * 16:(g + 1) * 16], ZC, ZC[:, g, :], W1, W1[:, g * 16:(g + 1) * 16], start=True, stop=False)
                    self.mm(psY, psY[:, g * 16:(g + 1) * 16], ZS, ZS[:, g, :], W2, W2[:, g * 16:(g + 1) * 16], start=False, stop=True)
                    if g % 8 == 7:
                        yield
                self.cp("act", yo[b], yo[b][:], psY, psY[:])
                r0 = self.rows(s, i)
                self.store(yo[b], self.YD[d, r0:r0 + 128, 1024:1536], yo[b][:], f"s0{b}", writes=[("YDs", d, s, i)])
                yield
            self.pipeline(self.tiles(d), [s0, s1])
            self.end_phase()

    def head_ln(self, y, eps, gw, gb, wk, st):
        v4 = lambda ap: ap.rearrange("p (a b) -> p a b", a=4)
        self.O("dve", lambda e, o=st[:, 0:4], i_=v4(y[:]): e.tensor_reduce(out=o, in_=i_, axis=AX.X, op=ALU.add), reads=[y], writes=[st])
        self.ts("dve", st, st[:, 0:4], st, st[:, 0:4], -1.0 / 64, None, ALU.mult)
        self.tt("dve", y, v4(y[:]), y, v4(y[:]), st, bc(st[:, 0:4].unsqueeze(2), [128, 4, 64]), ALU.add)
        self.tt("pool", wk, wk[:], y, y[:], y, y[:], ALU.mult)
        self.O("dve", lambda e, o=st[:, 4:8], i_=v4(wk[:]): e.tensor_reduce(out=o, in_=i_, axis=AX.X, op=ALU.add), reads=[wk], writes=[st])
        self.ts("dve", st, st[:, 4:8], st, st[:, 4:8], 1.0 / 64, eps, ALU.mult, ALU.add)
        self.act(st, st[:, 8:12], st, st[:, 4:8], AF.Sqrt)
        self.O("dve", lambda e, o=st[:, 12:16], i_=st[:, 8:12]: e.reciprocal(out=o, in_=i_), reads=[st], writes=[st])
        self.tt("dve", y, v4(y[:]), y, v4(y[:]), st, bc(st[:, 12:16].unsqueeze(2), [128, 4, 64]), ALU.mult)
        self.tt("pool", y, y[:], y, y[:], gw, gw[:], ALU.mult)
        self.tt("dve", y, y[:], y, y[:], gb, gb[:], ALU.add)

    def phase_O1(self, l):
        di = self.di
        self.S.pe_skip = True
        with ExitStack() as es:
            sb = lambda n, sh, dt=F32: self.sb(es, n, sh, dt)
            stg = [sb(f"ostg{i}", [128, 1024]) for i in range(3)]
            WO = self.load_w_bf16(es, "oWO", di["w_out"][l], 8, D, stg)
            GL = self.load_w_bf16(es, "oGL", di["s5_glu_w"][l], 4, 512, stg)
            RGW = self.bcast_load(es, "oRGW", di["ret_gn_w"][l], 256); RGB = self.bcast_load(es, "oRGB", di["ret_gn_b"][l], 256)
            WGW = self.bcast_load(es, "oWGW", di["rwkv_gn_w"][l], 256); WGB = self.bcast_load(es, "oWGB", di["rwkv_gn_b"][l], 256)
            SD = self.bcast_load(es, "oSD", di["s5_d"][l], 512); GB = self.bcast_load(es, "oGB", di["s5_glu_b"][l], 512)
            y0 = [sb(f"oy0{i}", [128, YW]) for i in range(2)]
            y1 = [sb(f"oy1{i}", [128, YW]) for i in range(2)]
            gu = [sb(f"ogu{i}", [128, 768]) for i in range(2)]
            xt = [sb(f"oxt{i}", [128, D]) for i in range(2)]
            mix = sb("omix", [128, D]); wk = sb("owk", [128, 512]); wk2 = sb("owk2", [128, 512]); st = sb("ost", [128, 16])
            ygT = sb("oygT", [128, 4, 128], BF16); mixT = sb("omixT", [128, 8, 128], BF16)
            xo = [sb(f"oxo{i}", [128, D]) for i in range(2)]
            psT = [self.ps(es, f"opsT{i}", [128, 4, 128]) for i in range(2)]
            psG = self.ps(es, "opsG", [128, 512]); psO = [self.ps(es, f"opsO{i}", [128, 512]) for i in range(2)]
            nr, nw, ns = (self.flags.get(k) for k in ("no_ret", "no_rwkv", "no_s5"))
            self.S.barrier()
            for it, (s, i, first) in enumerate(self.tiles()):
                b = it % 2
                r0 = self.rows(s, i); p0 = self.prow(s, i)
                self.load(y0[b], y0[b][:], self.YD[0, r0:r0 + 128, :], f"l0{b}", reads=[("YDr", 0, s, i), ("YDw", 0, s, i), ("YDs", 0, s, i)])
                self.load(y1[b], y1[b][:], self.YD[1, r0:r0 + 128, :], f"l1{b}", reads=[("YDr", 1, s, i), ("YDw", 1, s, i), ("YDs", 1, s, i)])
                self.load(gu[b], gu[b][:, 0:256], self.PS[p0:p0 + 128, 1280:1536], f"l2{b}", reads=[("PS", s, i)])
                self.load(gu[b], gu[b][:, 256:768], self.PS[p0:p0 + 128, 2560:3072], f"l3{b}", reads=[("PS", s, i)])
                self.load(xt[b], xt[b][:], self.XS[r0:r0 + 128, :], f"l4{b}", reads=["XSall", ("XS", r0)])
                A, B, GU = y0[b], y1[b], gu[b]
                yr = mix
                if nr:
                    self.memset("pool", mix, mix[:, 0:256], 0.0)
                else:
                    self.tt("dve", mix, mix[:, 0:256], A, A[:, 0:256], B, B[:, 0:256], ALU.add)
                    self._ln_slice(mix, 0, 1e-5, RGW, RGB, wk, st)
                    self.act(wk2, wk2[:, 0:256], GU, GU[:, 0:256], AF.Silu)
                    self.tt("dve", mix, mix[:, 0:256], mix, mix[:, 0:256], wk2, wk2[:, 0:256], ALU.mult)
                if nw:
                    self.memset("pool", mix, mix[:, 256:512], 0.0)
                else:
                    self.tt("dve", mix, mix[:, 256:512], A, A[:, 256:512], B, B[:, 256:512], ALU.add)
                    self._ln_slice(mix, 256, 64e-5, WGW, WGB, wk, st)
                    self.tt("pool", wk2, wk2[:, 0:256], A, A[:, 512:768], B, B[:, 512:768], ALU.add)
                    self.tt("dve", mix, mix[:, 256:512], mix, mix[:, 256:512], wk2, wk2[:, 0:256], ALU.add)
                    self.tt("dve", mix, mix[:, 256:512], mix, mix[:, 256:512], A, A[:, 768:1024], ALU.mult)
                if ns:
                    self.memset("pool", mix, mix[:, 512:1024], 0.0)
                else:
                    self.tt("dve", wk, wk[:], GU, GU[:, 256:768], SD, SD[:], ALU.mult)
                    self.tt("pool", wk2, wk2[:], A, A[:, 1024:1536], B, B[:, 1024:1536], ALU.add)
                    self.tt("dve", wk, wk[:], wk, wk[:], wk2, wk2[:], ALU.add)
                    self.tt("pool", wk2, wk2[:], wk, wk[:], wk, wk[:], ALU.mult)
                    self.ts("dve", wk2, wk2[:], wk2, wk2[:], 0.044715, 1.0, ALU.mult, ALU.add)
                    self.tt("dve", wk2, wk2[:], wk2, wk2[:], wk, wk[:], ALU.mult)
                    self.act(wk2, wk2[:], wk2, wk2[:], AF.Sigmoid, scale=1.5957691216057308)
                    self.tt("dve", wk, wk[:], wk, wk[:], wk2, wk2[:], ALU.mult)
                    self.transpose_to(wk, 4, psT, ygT)
                    for k in range(4):
                        self.mm(psG, psG[:], ygT, ygT[:, k, :], GL, GL[:, k, :], start=(k == 0), stop=(k == 3))
                    self.tt("dve", wk2, wk2[:], psG, psG[:], GB, GB[:], ALU.add)
                    self.act(wk2, wk2[:], wk2, wk2[:], AF.Sigmoid)
                    self.tt("dve", mix, mix[:, 512:1024], wk, wk[:], wk2, wk2[:], ALU.mult)
                self.transpose_to(mix, 8, psT, mixT)
                for cb in range(2):
                    pp = psO[cb]
                    for k in range(8):
                        self.mm(pp, pp[:], mixT, mixT[:, k, :], WO, WO[:, k, cb * 512:(cb + 1) * 512], start=(k == 0), stop=(k == 7))
                    self.tt("dve", xo[b], xo[b][:, cb * 512:(cb + 1) * 512], xt[b], xt[b][:, cb * 512:(cb + 1) * 512], pp, pp[:], ALU.add)
                self.store(xo[b], self.XS[r0:r0 + 128, :], xo[b][:], f"s0{b}", writes=[("XS", r0)])
            self.end_phase()

    def _ln_slice(self, mix, c0, eps, gw, gb, wk, st):
        v4 = lambda ap: ap.rearrange("p (a b) -> p a b", a=4)
        y = mix[:, c0:c0 + 256]
        self.O("dve", lambda e, o=st[:, 0:4], i_=v4(y): e.tensor_reduce(out=o, in_=i_, axis=AX.X, op=ALU.add), reads=[mix], writes=[st])
        self.ts("dve", st, st[:, 0:4], st, st[:, 0:4], -1.0 / 64, None, ALU.mult)
        self.tt("dve", mix, v4(y), mix, v4(y), st, bc(st[:, 0:4].unsqueeze(2), [128, 4, 64]), ALU.add)
        self.tt("pool", wk, wk[:, 0:256], mix, y, mix, y, ALU.mult)
        self.O("dve", lambda e, o=st[:, 4:8], i_=v4(wk[:, 0:256]): e.tensor_reduce(out=o, in_=i_, axis=AX.X, op=ALU.add), reads=[wk], writes=[st])
        self.ts("dve", st, st[:, 4:8], st, st[:, 4:8], 1.0 / 64, eps, ALU.mult, ALU.add)
        self.act(st, st[:, 8:12], st, st[:, 4:8], AF.Sqrt)
        self.O("dve", lambda e, o=st[:, 12:16], i_=st[:, 8:12]: e.reciprocal(out=o, in_=i_), reads=[st], writes=[st])
        self.tt("dve", mix, v4(y), mix, v4(y), st, bc(st[:, 12:16].unsqueeze(2), [128, 4, 64]), ALU.mult)
        self.tt("pool", mix, y, mix, y, gw, gw[:], ALU.mult)
        self.tt("dve", mix, y, mix, y, gb, gb[:], ALU.add)

    def phase_O2(self, l, last):
        di = self.di
        self.S.pe_skip = True
        finals = []
        with ExitStack() as es:
            sb = lambda n, sh, dt=F32: self.sb(es, n, sh, dt)
            stg = [sb(f"fstg{i}", [128, 1024]) for i in range(2)]
            WG = self.load_w_bf16(es, "fWG", di["ffn_w_gate"][l], 8, DFF, stg)
            WU = self.load_w_bf16(es, "fWU", di["ffn_w_up"][l], 8, DFF, stg)
            WD = self.load_w_bf16(es, "fWD", di["ffn_w_down"][l], 22, D, stg)
            G2 = self.bcast_load(es, "fG2", di["norm2_g"][l], D)
            GF = self.bcast_load(es, "fGF", di["final_g"], D) if last else None
            xb = [sb(f"fxb{i}", [128, D]) for i in range(2)]
            junk = sb("fjunk", [128, D], BF16); ss = [sb(f"fss{i}", [128, 4]) for i in range(2)]
            junk2 = sb("fjunk2", [128, D], BF16) if last else None
            ss2 = sb("fss2", [128, 4]) if last else None
            h = [sb(f"fh{i}", [128, D]) for i in range(2)]
            hT = [sb(f"fhT{i}", [128, 8, 128], BF16) for i in range(2)]
            sg = [sb(f"fsg{i}", [128, 4, 128]) for i in range(2)]
            aT = sb("faT", [128, 22, 128], BF16)
            xo = sb("fxo0", [128, D])
            yo = sb("fyo0", [128, D]) if last else None
            psT = [self.ps(es, f"fpsT{i}", [128, 4, 128]) for i in range(2)]
            psG = [self.ps(es, f"fpsG{i}", [128, 4, 128]) for i in range(2)]
            psU = [self.ps(es, f"fpsU{i}", [128, 4, 128]) for i in range(2)]
            psO = [self.ps(es, f"fpsO{i}", [128, 512]) for i in range(2)]
            self.S.barrier()

            def s0(j, s, i, first):
                b = j % 2
                r0 = self.rows(s, i)
                self.load(xb[b], xb[b][:], self.XS[r0:r0 + 128, :], f"l0{b}", reads=["XSall", ("XS", r0)])
                self.rmsnorm(xb[b], G2, h[b], junk, ss[b])
                yield
                self.transpose_to(h[b], 8, psT, hT[b])
                yield

            def s1(j, s, i, first):
                b = j % 2
                r0 = self.rows(s, i)
                HT = hT[b]
                for gi, f0 in enumerate(range(0, 22, 4)):
                    f1 = min(22, f0 + 4)
                    pg, pu, sgg = psG[gi % 2], psU[gi % 2], sg[gi % 2]
                    for f in range(f0, f1):
                        for k in range(8):
                            self.mm(pg, pg[:, f - f0, :], WG, WG[:, k, f * 128:(f + 1) * 128], HT, HT[:, k, :], start=(k == 0), stop=(k == 7))
                        for k in range(8):
                            self.mm(pu, pu[:, f - f0, :], WU, WU[:, k, f * 128:(f + 1) * 128], HT, HT[:, k, :], start=(k == 0), stop=(k == 7))
                    self.act(sgg, sgg[:, 0:f1 - f0, :], pg, pg[:, 0:f1 - f0, :], AF.Silu)
                    self.tt("dve", aT, aT[:, f0:f1, :], sgg, sgg[:, 0:f1 - f0, :], pu, pu[:, 0:f1 - f0, :], ALU.mult)
                    if gi % 2 == 1:
                        yield
                for cb in range(2):
                    pp = psO[cb]
                    for k in range(22):
                        self.mm(pp, pp[:], aT, aT[:, k, :], WD, WD[:, k, cb * 512:(cb + 1) * 512], start=(k == 0), stop=(k == 21))
                    self.tt("dve", xo, xo[:, cb * 512:(cb + 1) * 512], xb[b], xb[b][:, cb * 512:(cb + 1) * 512], pp, pp[:], ALU.add)
                yield
                if not last:
                    self.store(xo, self.XS[r0:r0 + 128, :], xo[:], "s00", writes=[("XS", r0)])
                else:
                    self.rmsnorm(xo, GF, yo, junk2, ss2)
                    st_ = self.store(yo, self.yout[s][i * 128:(i + 1) * 128, :], yo[:], "s00")
                    finals.append(st_)
                yield
            self.pipeline(self.tiles(), [s0, s1])
            self.end_phase()
        return finals


def _consts(Tmax):
    c = {}
    c["c_ident"] = np.eye(128, dtype=np.float32)
    inv = (1.0 / (10000.0 ** np.linspace(0.0, 1.0, 32, dtype=np.float32))).astype(np.float32)
    ang = np.arange(Tmax, dtype=np.float32)[:, None] * inv[None, :]
    cos, sin = np.cos(ang).astype(np.float32), np.sin(ang).astype(np.float32)
    rot = np.concatenate([0.125 * cos, 0.125 * cos, -0.125 * sin, 0.125 * sin, cos, cos, -sin, sin], axis=1)
    c["c_rot"] = np.ascontiguousarray(rot.astype(np.float32))
    lg = np.log(1.0 - 2.0 ** (-5.0 - np.arange(4, dtype=np.float64)))
    i = np.arange(128, dtype=np.float64)
    diff = np.abs(i[:, None] - i[None, :])
    c["c_retmask"] = np.concatenate([np.exp(diff * lg[h]) for h in range(4)], axis=1).astype(np.float32)
    qk = np.zeros((2, 128, 8), np.float32)
    for h in range(4):
        qk[0, :, h] = np.exp((i + 1) * lg[h]); qk[0, :, 4 + h] = np.exp((127 - i) * lg[h])
        qk[1, :, h] = np.exp((128 - i) * lg[h]); qk[1, :, 4 + h] = np.exp(i * lg[h])
    c["c_retqk"] = qk
    gc = np.zeros((128, 256), np.float32)
    for r in range(128):
        for pr in range(2):
            gc[r, pr * 128:(pr + 1) * 128] = np.exp(128 * lg[pr * 2 + r // 64])
    c["c_retgc"] = gc
    s_, t_ = np.meshgrid(np.arange(128), np.arange(128), indexing="ij")
    c["c_cs"] = np.stack([(s_ <= t_), (s_ >= t_)]).astype(np.float32)
    c["c_ones"] = np.ones((128, 128), np.float32)
    c["c_msi"] = np.stack([np.concatenate([(s_ < t_), (s_ <= t_)], axis=1),
                           np.concatenate([(s_ > t_), (s_ >= t_)], axis=1)]).astype(np.float32)
    c["c_mst"] = np.stack([(s_ > t_), (s_ < t_)]).astype(np.float32)
    j = np.arange(128, dtype=np.float32)
    c["c_j1"] = np.stack([np.tile(j + 1, (128, 1)), np.tile(128 - j, (128, 1))]).astype(np.float32)
    z0 = np.ones((2, 128, 128), np.float32); z0[0, :, 0] = 0; z0[1, :, 127] = 0
    c["c_z0"] = z0
    sw = np.zeros((128, 128), np.float32)
    for p in range(64):
        sw[64 + p, p] = -1.0; sw[p, 64 + p] = 1.0
    c["c_swap"] = sw
    sel = np.zeros((128, 128), np.float32)
    for go in range(2):
        for p in range(64):
            sel[go * 64 + p, go * 64 + p] = 1.0
    c["c_sel"] = sel
    return c


def _win_perm():
    idx = []
    def sw(base):
        out = []
        for h in range(4):
            out += list(range(base + h * 64 + 32, base + h * 64 + 64)) + list(range(base + h * 64, base + h * 64 + 32))
        return out
    idx += list(range(0, 256)) + sw(0) + list(range(256, 512)) + sw(256) + list(range(512, 768)) + list(range(768, 1024))
    idx += list(range(1024, 2560))
    return np.array(idx)


_CACHE = {}


def run(inputs, Tp, Ts, L, ncores, flags=None):
    key = (Tp, Ts, L, ncores, tuple(sorted((flags or {}).items())))
    if key not in _CACHE:
        _CACHE[key] = Builder(Tp, Ts, L, flags).build()
    nc = _CACHE[key]
    f = lambda a: np.ascontiguousarray(np.asarray(a, dtype=np.float32))
    shared = {k: f(v) for k, v in inputs.items() if k not in ("x_prompt", "x_sample", "w_in")}
    shared["w_in"] = np.ascontiguousarray(f(inputs["w_in"])[:, :, _win_perm()])
    shared.update(_consts(max(Tp, Ts)))
    xp, xs = f(inputs["x_prompt"]), f(inputs["x_sample"])
    nb_s = xs.shape[0]
    in_maps = []
    for c in range(ncores):
        m = dict(shared)
        m["x_prompt"] = xp[c % xp.shape[0]]
        m["x_sample"] = xs[c % nb_s]
        in_maps.append(m)
    res = run_bass_kernel_spmd(nc, in_maps, core_ids=list(range(ncores)))
    yp = np.stack([res.results[c]["y_prompt"] for c in range(xp.shape[0])])
    ys = np.stack([res.results[c]["y_sample"] for c in range(nb_s)])
    return yp.astype(np.float32), ys.astype(np.float32)


def kernel(**inputs):
    return run(inputs, 8192, 4096, 4, 8)
```

```python
import math
import numpy as np
import concourse.bass as bass
import concourse.mybir as mybir
from concourse.bass_utils import run_bass_kernel_spmd
from contextlib import ExitStack

F32 = mybir.dt.float32
BF16 = mybir.dt.bfloat16
ALU = mybir.AluOpType
AF = mybir.ActivationFunctionType
AX = mybir.AxisListType

D = 1024
NH = 4
HD = 64
PIN = 3072
DFF = 2816
YW = 1536
CDEC = -math.exp(-0.5)
ENGS = ("pe", "act", "dve", "pool", "sp")


class Op:
    __slots__ = ("eng", "fn", "deps", "signal", "idx", "semkey", "is_dma", "emitted", "val")

    def __init__(self, eng, fn, semkey, is_dma):
        self.eng = eng
        self.fn = fn
        self.deps = []
        self.signal = is_dma
        self.idx = -1
        self.semkey = semkey
        self.is_dma = is_dma
        self.emitted = False
        self.val = None


class Sched:
    def __init__(self, nc, es):
        self.nc = nc
        self.es = es
        self.ops = {e: [] for e in ENGS}
        self.semidx = {}
        self.sems = {}
        self.res = {}
        self.seen = {e: {} for e in ENGS}
        self.sigcount = {}
        self.last_emitted = {}
        self.last_op = {}
        self.nops = 0
        self.pe_skip = False
        self.pe_fence_pending = False

    def _dep(self, op, d, force=False):
        if d is None or d is op:
            return
        if self.pe_skip and not force and d.eng == "pe" and op.eng == "pe" and not d.is_dma:
            return
        if d.emitted and not d.signal:
            d = self.last_emitted[d.semkey]
        seen = self.seen[op.eng]
        if seen.get(d.semkey, -1) >= d.idx:
            return
        seen[d.semkey] = d.idx
        d.signal = True
        op.deps.append(d)

    def op(self, eng, fn, reads=(), writes=(), dma=None, extra=(), fence=False):
        is_dma = dma is not None
        semkey = ("dma", dma) if is_dma else ("eng", eng)
        o = Op(eng, fn, semkey, is_dma)
        o.idx = self.semidx.get(semkey, 0)
        self.semidx[semkey] = o.idx + 1
        for r in reads:
            st = self.res.get(r)
            if st is not None:
                self._dep(o, st[0])
        for w in writes:
            st = self.res.get(w)
            if st is not None:
                self._dep(o, st[0])
                for rd in st[1]:
                    self._dep(o, rd)
        for d in extra:
            self._dep(o, d)
        if eng == "pe":
            if fence or self.pe_fence_pending:
                self._dep(o, self.last_op.get(semkey), force=True)
            self.pe_fence_pending = fence
        for r in reads:
            st = self.res.setdefault(r, [None, []])
            st[1].append(o)
        for w in writes:
            self.res[w] = [o, []]
        self.ops[eng].append(o)
        self.last_op[semkey] = o
        self.nops += 1
        return o

    def barrier(self):
        lasts = list(self.last_op.values())
        for e in ENGS:
            self.op(e, lambda en: en.nop(), extra=lasts)

    def flush(self, final=()):
        nc = self.nc
        for key in self.semidx:
            if key not in self.sems:
                nm = ("s_" + "_".join(map(str, key)))[:48]
                self.sems[key] = self.es.enter_context(nc.semaphore(nm))
        for e in ENGS:
            if self.ops[e]:
                for o in reversed(self.ops[e]):
                    if not o.is_dma:
                        o.signal = True
                        break
        for e in ENGS:
            for o in self.ops[e]:
                if o.is_dma:
                    o.val = 16 * (o.idx + 1)
                elif o.signal:
                    c = self.sigcount.get(o.semkey, 0) + 1
                    self.sigcount[o.semkey] = c
                    o.val = c
        sems = self.sems
        pending = self.ops

        def run(en, key):
            for o in pending[key]:
                for d in o.deps:
                    en.wait_ge(sems[d.semkey], d.val)
                ins = o.fn(en)
                if o.signal:
                    ins.then_inc(sems[o.semkey], 16 if o.is_dma else 1)
            if key == "sp":
                for d in final:
                    en.wait_ge(sems[d.semkey], d.val)

        with nc.Block() as block:
            @block.tensor
            def _(en):
                run(en, "pe")

            @block.scalar
            def _(en):
                run(en, "act")

            @block.vector
            def _(en):
                run(en, "dve")

            @block.gpsimd
            def _(en):
                run(en, "pool")

            @block.sync
            def _(en):
                run(en, "sp")
        for e in ENGS:
            for o in self.ops[e]:
                o.emitted = True
                if o.signal and not o.is_dma:
                    self.last_emitted[o.semkey] = o
            self.ops[e] = []


class Tl:
    def __init__(self, t, key):
        self.t = t
        self.k = key

    def __getitem__(self, idx):
        return self.t[idx]


def bc(ap, shape):
    return ap.to_broadcast(list(shape))


class Builder:
    def __init__(self, Tp, Ts, depth, flags=None):
        self.Tp, self.Ts, self.L = Tp, Ts, depth
        self.seqs = [(0, Tp), (Tp, Ts)]
        self.TT = Tp + Ts
        self.flags = flags or {}
        self.nc = bass.Bass("TRN2", target_bir_lowering=False)

    def sb(self, es, name, shape, dt=F32):
        self._uid = getattr(self, "_uid", 0) + 1
        name = f"{name}_{self._uid}"
        return Tl(es.enter_context(self.nc.sbuf_tensor(name, list(shape), dt)), name)

    def ps(self, es, name, shape, dt=F32):
        self._uid = getattr(self, "_uid", 0) + 1
        name = f"{name}_{self._uid}"
        return Tl(es.enter_context(self.nc.psum_tensor(name, list(shape), dt)), name)

    def O(self, eng, fn, reads=(), writes=(), dma=None, fence=False):
        return self.S.op(eng, fn, [r.k if isinstance(r, Tl) else r for r in reads],
                         [w.k if isinstance(w, Tl) else w for w in writes], dma=dma, fence=fence)

    def load(self, dst, dst_ap, src_ap, key, reads=(), eng="sp"):
        return self.O(eng, lambda e, o=dst_ap, i=src_ap: e.dma_start(out=o, in_=i, allow_slow_non_contiguous=True), reads=reads, writes=[dst], dma=key)

    def store(self, src, dst_ap, src_ap, key, writes=(), eng="sp"):
        return self.O(eng, lambda e, o=dst_ap, i=src_ap: e.dma_start(out=o, in_=i), reads=[src], writes=writes, dma=key)

    def mm(self, out_t, out_ap, l_t, l_ap, r_t, r_ap, start=True, stop=True):
        return self.O("pe", lambda e, o=out_ap, a=l_ap, b=r_ap, s0=start, s1=stop: e.matmul(o, a, b, start=s0, stop=s1),
                      reads=[l_t, r_t] + ([] if start else [out_t]), writes=[out_t], fence=(l_ap.shape[0] != 128))

    def tr(self, out_t, out_ap, in_t, in_ap):
        idn = self.identf if in_ap.dtype == F32 else self.identb
        return self.O("pe", lambda e, o=out_ap, a=in_ap, i=idn: e.transpose(o, a, i[:]), reads=[in_t, idn], writes=[out_t])

    def act(self, out_t, out_ap, in_t, in_ap, func, scale=1.0, bias=None, accum=None, extra_r=(), extra_w=()):
        def fn(e, o=out_ap, i=in_ap, f=func, s=scale, b=bias, a=accum):
            kw = {}
            if b is not None:
                kw["bias"] = b
            if a is not None:
                kw["accum_out"] = a
            return e.activation(out=o, in_=i, func=f, scale=s, **kw)
        return self.O("act", fn, reads=[in_t] + list(extra_r), writes=[out_t] + list(extra_w))

    def tt(self, eng, out_t, out_ap, a_t, a_ap, b_t, b_ap, op):
        return self.O(eng, lambda e, o=out_ap, a=a_ap, b=b_ap, p=op: e.tensor_tensor(out=o, in0=a, in1=b, op=p),
                      reads=[a_t, b_t], writes=[out_t])

    def ts(self, eng, out_t, out_ap, a_t, a_ap, s1, s2, op0, op1=None, extra_r=()):
        def fn(e, o=out_ap, a=a_ap, x=s1, y=s2, p0=op0, p1=op1):
            if p1 is None:
                return e.tensor_single_scalar(out=o, in_=a, scalar=x, op=p0)
            return e.tensor_scalar(out=o, in0=a, scalar1=x, scalar2=y, op0=p0, op1=p1)
        return self.O(eng, fn, reads=[a_t] + list(extra_r), writes=[out_t])

    def stt(self, eng, out_t, out_ap, a_t, a_ap, scalar, b_t, b_ap, op0, op1, extra_r=()):
        eng = "dve"
        return self.O(eng, lambda e, o=out_ap, a=a_ap, s=scalar, b=b_ap, p0=op0, p1=op1:
                      e.scalar_tensor_tensor(out=o, in0=a, scalar=s, in1=b, op0=p0, op1=p1),
                      reads=[a_t, b_t] + list(extra_r), writes=[out_t])

    def cp(self, eng, out_t, out_ap, in_t, in_ap):
        if eng == "act":
            return self.act(out_t, out_ap, in_t, in_ap, AF.Copy)
        return self.O(eng, lambda e, o=out_ap, i=in_ap: e.tensor_copy(out=o, in_=i), reads=[in_t], writes=[out_t])

    def memset(self, eng, t, ap, v):
        return self.O(eng, lambda e, o=ap, x=v: e.memset(o, x), writes=[t])

    def load_w_bf16(self, es, name, dram_ap, K, C, stg, dst=None, dst_koff=0):
        if dst is None:
            dst = self.sb(es, name, [128, K, C], BF16)
        n = 0
        for k in range(K):
            cw = stg[0].t.shape[-1]
            for c0 in range(0, C, cw):
                c1 = min(C, c0 + cw)
                slot = self._stg_i % len(stg)
                st = stg[slot]
                self._stg_i += 1
                self.load(st, st[:, 0:c1 - c0], dram_ap[k * 128:(k + 1) * 128, c0:c1], f"stg{slot}")
                eng = ("pool", "dve", "act")[n % 3]
                n += 1
                self.cp(eng, dst, dst[:, dst_koff + k, c0:c1], st, st[:, 0:c1 - c0])
        return dst

    def bcast_load(self, es, name, dram_vec_ap, C, key="bcl"):
        t = self.sb(es, name, [128, C])
        self.load(t, t[:], dram_vec_ap.partition_broadcast(128), "setup")
        return t

    def rows(self, seq, i):
        off, T = self.seqs[seq]
        return off + i * 128

    def prow(self, seq, i):
        off, T = self.seqs[seq]
        return off + 2 * seq + 1 + i * 128

    def tiles(self, d=0):
        out = []
        for s, (off, T) in enumerate(self.seqs):
            n = T // 128
            rng = range(n) if d == 0 else range(n - 1, -1, -1)
            for j, i in enumerate(rng):
                out.append((s, i, j == 0))
        return out

    def build(self):
        nc = self.nc
        Tp, Ts, L, TT = self.Tp, self.Ts, self.L, self.TT
        di = {}

        def inp(name, shape):
            di[name] = nc.dram_tensor(name, list(shape), F32, kind="ExternalInput").ap()
            return di[name]
        inp("x_prompt", [Tp, D]); inp("x_sample", [Ts, D])
        inp("norm1_g", [L, D]); inp("w_in", [L, D, PIN])
        inp("ret_gn_w", [L, 256]); inp("ret_gn_b", [L, 256])
        inp("rwkv_mu", [L, 1024]); inp("rwkv_w0", [L, 2, 256]); inp("rwkv_w2", [L, 2, 64, 256])
        inp("rwkv_a0", [L, 2, 256]); inp("rwkv_a2", [L, 2, 64, 256]); inp("rwkv_g2", [L, 128, 256])
        inp("rwkv_k_k", [L, 256]); inp("rwkv_k_a", [L, 256]); inp("rwkv_r_k", [L, 256])
        inp("rwkv_gn_w", [L, 256]); inp("rwkv_gn_b", [L, 256])
        inp("s5_lam_re", [L, 2, 32, 64]); inp("s5_lam_im", [L, 2, 32, 64]); inp("s5_log_step", [L, 2, 32])
        inp("s5_b_re", [L, 2, 32, 64, 16]); inp("s5_b_im", [L, 2, 32, 64, 16])
        inp("s5_c_re", [L, 2, 32, 16, 64]); inp("s5_c_im", [L, 2, 32, 16, 64])
        inp("s5_d", [L, 512]); inp("s5_glu_w", [L, 512, 512]); inp("s5_glu_b", [L, 512])
        inp("w_out", [L, D, D]); inp("norm2_g", [L, D])
        inp("ffn_w_gate", [L, D, DFF]); inp("ffn_w_up", [L, D, DFF]); inp("ffn_w_down", [L, DFF, D])
        inp("final_g", [D])
        inp("c_ident", [128, 128]); inp("c_rot", [max(Tp, Ts), 256])
        inp("c_retmask", [128, 512]); inp("c_retqk", [2, 128, 8]); inp("c_retgc", [128, 256])
        inp("c_cs", [2, 128, 128]); inp("c_ones", [128, 128]); inp("c_msi", [2, 128, 256]); inp("c_mst", [2, 128, 128])
        inp("c_j1", [2, 128, 128]); inp("c_z0", [2, 128, 128]); inp("c_swap", [128, 128]); inp("c_sel", [128, 128])
        self.di = di
        yp = nc.dram_tensor("y_prompt", [Tp, D], F32, kind="ExternalOutput").ap()
        ys = nc.dram_tensor("y_sample", [Ts, D], F32, kind="ExternalOutput").ap()
        self.yout = [yp, ys]
        self.XS = nc.dram_tensor("XS", [TT, D], F32, kind="Internal").ap()
        self.PS = nc.dram_tensor("PSC", [TT + 4, PIN], F32, kind="Internal").ap()
        self.YD = nc.dram_tensor("YD", [2, TT, YW], F32, kind="Internal").ap()

        with ExitStack() as ges:
            self.S = Sched(nc, ges)
            self._stg_i = 0
            self.identf = self.sb(ges, "identf", [128, 128])
            self.identb = self.sb(ges, "identb", [128, 128], BF16)
            self.load(self.identf, self.identf[:], di["c_ident"][:, :], "setup")
            self.cp("dve", self.identb, self.identb[:], self.identf, self.identf[:])
            zero = self.sb(ges, "zero", [128, PIN])
            self.memset("pool", zero, zero[:], 0.0)
            self.O("sp", lambda e: e.dma_start(out=self.XS[0:Tp, :], in_=di["x_prompt"][:, :]), writes=["XSall"], dma="setup")
            self.O("sp", lambda e: e.dma_start(out=self.XS[Tp:TT, :], in_=di["x_sample"][:, :]), writes=["XSall"], dma="setup")
            for s, (off, T) in enumerate(self.seqs):
                for r in (off + 2 * s, off + 2 * s + T + 1):
                    self.store(zero, self.PS[r:r + 1, :], zero[0:1, :], "setup", writes=["PSall"])
            self.S.barrier()
            self.S.flush()
            final = []
            for l in range(L):
                self.phase_P(l)
                for d in range(2):
                    if not self.flags.get("no_ret"):
                        self.phase_ret(l, d)
                    if not self.flags.get("no_rwkv"):
                        self.phase_rwkv(l, d)
                    if not self.flags.get("no_s5"):
                        self.phase_s5(l, d)
                self.phase_O1(l)
                final = self.phase_O2(l, last=(l == L - 1))
            self.S.barrier()
            self.S.flush(final=final)
        return nc

    def end_phase(self):
        self.S.barrier()
        self.S.flush()

    def rmsnorm(self, xt, G, h_out, junk, ss, eng2="dve"):
        self.act(junk, junk[:], xt, xt[:], AF.Square, accum=ss[:, 0:1], extra_w=[ss])
        self.ts("dve", ss, ss[:, 1:2], ss, ss[:, 0:1], 1.0 / D, 1e-6, ALU.mult, ALU.add)
        self.act(ss, ss[:, 2:3], ss, ss[:, 1:2], AF.Sqrt)
        self.O("dve", lambda e, o=ss[:, 3:4], i=ss[:, 2:3]: e.reciprocal(out=o, in_=i), reads=[ss], writes=[ss])
        self.stt(eng2, h_out, h_out[:], xt, xt[:], ss[:, 3:4], G, G[:], ALU.mult, ALU.mult, extra_r=[ss])

    def transpose_to(self, src, ncol_tiles, psT, dstT, evac_engs=("act", "dve")):
        n = 0
        for g0 in range(0, ncol_tiles, 4):
            g1 = min(ncol_tiles, g0 + 4)
            pt = psT[(g0 // 4) % len(psT)]
            for c in range(g0, g1):
                self.tr(pt, pt[:, c - g0, :], src, src[:, c * 128:(c + 1) * 128])
            self.cp(evac_engs[n % len(evac_engs)], dstT, dstT[:, g0:g1, :], pt, pt[:, 0:g1 - g0, :])
            n += 1

    def pipeline(self, tiles, stages, counts=None):
        K, N = len(stages), len(tiles)
        counts = counts or [1] * K
        for n in range(N + K - 1):
            gens = []
            for k in range(K - 1, -1, -1):
                j = n - k
                if 0 <= j < N:
                    s, i, first = tiles[j]
                    gens.append([stages[k](j, s, i, first), 0, float(counts[k])])
            while gens:
                g = min(gens, key=lambda x: x[1] / x[2])
                try:
                    next(g[0])
                    g[1] += 1
                except StopIteration:
                    gens.remove(g)

    def phase_P(self, l):
        di = self.di
        self.S.pe_skip = True
        with ExitStack() as es:
            stg = [self.sb(es, f"stg{i}", [128, 1024]) for i in range(3)]
            Wb = self.load_w_bf16(es, "Wb", di["w_in"][l], 8, PIN, stg)
            G1 = self.bcast_load(es, "G1", di["norm1_g"][l], D)
            xb = [self.sb(es, f"xb{i}", [128, D]) for i in range(2)]
            junk = self.sb(es, "junk", [128, D], BF16)
            ss = [self.sb(es, f"ss{i}", [128, 4]) for i in range(2)]
            h = [self.sb(es, f"h{i}", [128, D]) for i in range(2)]
            hT = [self.sb(es, f"hT{i}", [128, 8, 128], BF16) for i in range(2)]
            pt = [self.sb(es, f"pt{i}", [128, PIN]) for i in range(2)]
            psT = [self.ps(es, f"psT{i}", [128, 4, 128]) for i in range(2)]
            psP = [self.ps(es, f"psP{i}", [128, 512]) for i in range(4)]
            self.S.barrier()
            cnt = [0]

            def s0(j, s, i, first):
                b = j % 2
                r0 = self.rows(s, i)
                self.load(xb[b], xb[b][:], self.XS[r0:r0 + 128, :], f"l0{b}", reads=["XSall", ("XS", r0)])
                self.rmsnorm(xb[b], G1, h[b], junk, ss[b])
                yield
                self.transpose_to(h[b], 8, psT, hT[b])
                yield

            def s1(j, s, i, first):
                b = j % 2
                for cb in range(PIN // 512):
                    n = cnt[0]
                    cnt[0] += 1
                    pp = psP[n % 4]
                    for k in range(8):
                        self.mm(pp, pp[:], hT[b], hT[b][:, k, :], Wb, Wb[:, k, cb * 512:(cb + 1) * 512], start=(k == 0), stop=(k == 7))
                    self.cp(("act", "dve")[n % 2], pt[b], pt[b][:, cb * 512:(cb + 1) * 512], pp, pp[:])
                    if cb % 3 == 2:
                        yield
                pr = self.prow(s, i)
                self.store(pt[b], self.PS[pr:pr + 128, :], pt[b][:], f"s0{b}", writes=[("PS", s, i)])
                yield
            self.pipeline(self.tiles(), [s0, s1])
            self.end_phase()

    def phase_ret(self, l, d):
        di = self.di
        self.S.pe_skip = not self.flags.get("noskip_ret")
        with ExitStack() as es:
            sb = lambda n, sh, dt=F32: self.sb(es, n, sh, dt)
            mask = sb("rmask", [128, 512]); self.load(mask, mask[:], di["c_retmask"][:, :], "setup")
            qk = sb("rqk", [128, 8]); self.load(qk, qk[:], di["c_retqk"][d], "setup")
            gc = sb("rgc", [128, 256]); self.load(gc, gc[:], di["c_retgc"][:, :], "setup")
            R = sb("R", [128, 256]); Rb = sb("Rb", [128, 256], BF16)
            R2 = range(2)
            pr = [sb(f"pr{i}", [128, 1280]) for i in R2]
            rot = [sb(f"rot{i}", [128, 256]) for i in R2]
            qh = sb("qh", [128, 512]); t1 = sb("rt1", [128, 512])
            kT = [sb(f"rkT{i}", [128, 2, 128], BF16) for i in R2]
            qz = [sb(f"qz{i}", [128, 4, 128], BF16) for i in R2]
            for i in R2:
                self.memset("pool", qz[i], qz[i][:], 0.0)
            sT = sb("sT", [128, 4, 128], BF16)
            vb = [sb(f"vb{i}", [128, 256], BF16) for i in R2]; kdb = [sb(f"kdb{i}", [128, 256], BF16) for i in R2]
            tmp = [sb(f"rtmp{i}", [128, 256]) for i in R2]; yo = [sb(f"ryo{i}", [128, 256]) for i in R2]
            psT = self.ps(es, "rpsT", [128, 4, 128]); psS = self.ps(es, "rpsS", [128, 4, 128])
            psO = self.ps(es, "rpsO", [128, 256]); psC = self.ps(es, "rpsC", [128, 256]); psR = self.ps(es, "rpsR", [128, 256])
            self.S.barrier()
            v4 = lambda ap: ap.rearrange("p (a b) -> p a b", a=4)

            def s0(j, s, i, first):
                b = j % 2
                p0 = self.prow(s, i)
                self.load(pr[b], pr[b][:], self.PS[p0:p0 + 128, 0:1280], f"l0{b}", reads=[("PS", s, i)])
                self.load(rot[b], rot[b][:], di["c_rot"][i * 128:(i + 1) * 128, :], f"l1{b}")
                P, RT = pr[b], rot[b]
                for jj, (c0, tb) in enumerate(((0, 0), (512, 128))):
                    o = jj * 256
                    cosb = bc(RT[:, tb:tb + 64].unsqueeze(1), [128, 4, 64])
                    sinb = bc(RT[:, tb + 64:tb + 128].unsqueeze(1), [128, 4, 64])
                    self.tt("dve", qh, v4(qh[:, o:o + 256]), P, v4(P[:, c0:c0 + 256]), RT, cosb, ALU.mult)
                    self.tt("pool", t1, v4(t1[:, o:o + 256]), P, v4(P[:, c0 + 256:c0 + 512]), RT, sinb, ALU.mult)
                    self.tt("dve", qh, qh[:, o:o + 256], qh, qh[:, o:o + 256], t1, t1[:, o:o + 256], ALU.add)
                yield
                for c in range(4):
                    self.tr(psT, psT[:, c, :], qh, qh[:, c * 128:(c + 1) * 128])
                self.cp("act", kT[b], kT[b][:], psT, psT[:, 2:4, :])
                self.cp("act", qz[b], qz[b][0:64, 0:4:2, :], psT, psT[0:64, 0:2, :])
                self.cp("dve", qz[b], qz[b][64:128, 1:4:2, :], psT, psT[64:128, 0:2, :])
                self.cp("act", vb[b], vb[b][:], P, P[:, 1024:1280])
                kv = qh[:, 256:512].rearrange("p (a b) -> p a b", a=4)
                self.tt("dve", kdb[b], v4(kdb[b][:]), qh, kv, qk, bc(qk[:, 4:8].unsqueeze(2), [128, 4, 64]), ALU.mult)
                yield
                if d == 0:
                    for hh in range(4):
                        self.mm(psS, psS[:, hh, :], kT[b], kT[b][:, hh // 2, :], qz[b], qz[b][:, hh, :])
                    self.tt("dve", sT, sT[:], psS, psS[:], mask, v4(mask[:]), ALU.mult)
                    for hh in range(4):
                        self.mm(psO, psO[:, hh * 64:(hh + 1) * 64], sT, sT[:, hh, :], vb[b], vb[b][:, hh * 64:(hh + 1) * 64])
                    self.cp("act", tmp[b], tmp[b][:], psO, psO[:])
                yield

            def s1(j, s, i, first):
                b = j % 2
                if first:
                    self.memset("pool", R, R[:], 0.0)
                    self.memset("pool", Rb, Rb[:], 0.0)
                for hh in range(4):
                    self.mm(psC, psC[:, hh * 64:(hh + 1) * 64], qz[b], qz[b][:, hh, :], Rb,
                            Rb[:, (hh // 2) * 128 + (hh % 2) * 64:(hh // 2) * 128 + (hh % 2) * 64 + 64])
                for pr_ in range(2):
                    self.mm(psR, psR[:, pr_ * 128:(pr_ + 1) * 128], kdb[b], kdb[b][:, pr_ * 128:(pr_ + 1) * 128], vb[b], vb[b][:, pr_ * 128:(pr_ + 1) * 128])
                self.tt("dve", R, R[:], R, R[:], gc, gc[:], ALU.mult)
                self.tt("dve", R, R[:], R, R[:], psR, psR[:], ALU.add)
                self.cp("act", Rb, Rb[:], R, R[:])
                yield
                y = yo[b]
                self.tt("dve", y, v4(y[:]), psC, v4(psC[:]), qk, bc(qk[:, 0:4].unsqueeze(2), [128, 4, 64]), ALU.mult)
                if d == 0:
                    self.tt("pool", y, y[:], y, y[:], tmp[b], tmp[b][:], ALU.add)
                r0 = self.rows(s, i)
                self.store(y, self.YD[d, r0:r0 + 128, 0:256], y[:], f"s0{b}", writes=[("YDr", d, s, i)])
                yield
            self.pipeline(self.tiles(d), [s0, s1])
            self.end_phase()

    def phase_rwkv(self, l, d):
        di = self.di
        self.S.pe_skip = not self.flags.get("noskip_rwkv")
        with ExitStack() as es:
            sb = lambda n, sh, dt=F32: self.sb(es, n, sh, dt)
            v4 = lambda ap: ap.rearrange("p (a b) -> p a b", a=4)
            CS = sb("wCS", [128, 128]); self.load(CS, CS[:], di["c_cs"][d], "setup")
            ON = sb("wON", [128, 128]); self.load(ON, ON[:], di["c_ones"][:, :], "setup")
            MSI = sb("wMSI", [128, 256]); self.load(MSI, MSI[:], di["c_msi"][d], "setup")
            MST = sb("wMST", [128, 128]); self.load(MST, MST[:], di["c_mst"][d], "setup")
            MU = self.bcast_load(es, "wMU", di["rwkv_mu"][l], 1024)
            W0 = self.bcast_load(es, "wW0", di["rwkv_w0"][l, d], 256)
            A0 = self.bcast_load(es, "wA0", di["rwkv_a0"][l, d], 256)
            KK = self.bcast_load(es, "wKK", di["rwkv_k_k"][l], 256)
            KA = self.bcast_load(es, "wKA", di["rwkv_k_a"][l], 256)
            RK = self.bcast_load(es, "wRK", di["rwkv_r_k"][l], 256)
            stg = sb("wstg", [128, 256])
            LW = sb("wLW", [128, 256], BF16)
            self.load(stg, stg[0:64, :], di["rwkv_w2"][l, d], "setup")
            self.load(stg, stg[64:128, :], di["rwkv_a2"][l, d], "setup")
            stg2 = sb("wstg2", [128, 256])
            G2 = sb("wG2", [128, 256], BF16)
            self.load(stg2, stg2[:], di["rwkv_g2"][l], "setup")
            self.S.barrier()
            self.cp("dve", LW, LW[:], stg, stg[:])
            self.cp("dve", G2, G2[:], stg2, stg2[:])
            R2 = range(2); R3 = range(3)
            pm = [sb(f"wpm{i}", [128, 1024]) for i in R2]
            pc = [sb(f"wpc{i}", [128, 1024]) for i in R2]
            pn = [sb(f"wpn{i}", [128, 1024]) for i in R2]
            pp = sb("wpp", [128, 1024]); sh = sb("wsh", [128, 1024])
            ldT = sb("wldT", [128, 2, 128], BF16)
            targ = sb("wtarg", [128, 512]); sg = sb("wsg", [128, 256]); asg = sb("wasg", [128, 256])
            kk = sb("wkk", [128, 256]); kkr = sb("wkkr", [128, 256]); junk = sb("wjunk", [128, 64])
            ssq = sb("wssq", [128, 16]); kd = sb("wkd", [128, 256]); bv = sb("wbv", [128, 256])
            t1 = sb("wt1", [128, 256])
            yo = [sb(f"wyo{i}", [128, 768]) for i in R3]
            lws = sb("wlws", [128, 256]); lx = sb("wlx", [128, 256]); lt = sb("wlt", [128, 256])
            Wt = sb("wWt", [128, 256]); Wn = sb("wWn", [128, 256]); Wx = sb("wWx", [128, 256]); Wh = sb("wWh", [128, 256])
            X4 = sb("wX4", [128, 1024])
            Atb = [sb(f"wAtb{i}", [128, 256], BF16) for i in R2]
            Bhb = [sb(f"wBhb{i}", [128, 256], BF16) for i in R3]; Khb = [sb(f"wKhb{i}", [128, 256], BF16) for i in R3]
            vb = [sb(f"wvb{i}", [128, 256], BF16) for i in R3]
            XT = [sb(f"wXT{i}", [128, 2, 4, 128], BF16) for i in R3]
            WCk = [sb(f"wWCk{i}", [128, 2]) for i in R3]
            Nb = [sb(f"wNb{i}", [128, 4, 128], BF16) for i in R2]; NTb = [sb(f"wNTb{i}", [128, 4, 128], BF16) for i in R2]
            Pb = [sb(f"wPb{i}", [128, 4, 128], BF16) for i in R2]
            PTb = [sb(f"wPTb{i}", [128, 4, 128], BF16) for i in R2]
            Mf = [sb(f"wMf{i}", [128, 4, 128]) for i in R2]; Mb = [sb(f"wMb{i}", [128, 4, 128], BF16) for i in R2]
            Arb = [sb(f"wArb{i}", [128, 4, 128], BF16) for i in R3]; Ark = [sb(f"wArk{i}", [128, 4, 128], BF16) for i in R3]
            Aak = [sb(f"wAak{i}", [128, 4, 128], BF16) for i in R2]
            Xb = sb("wXb", [128, 256], BF16); Ub = sb("wUb", [128, 256], BF16)
            Ut = [sb(f"wUt{i}", [128, 256]) for i in R2]
            WtT = [sb(f"wWtT{i}", [128, 4, 128], BF16) for i in R2]
            H = sb("wH", [128, 256]); Hb = sb("wHb", [128, 256], BF16)
            identq = sb("widq", [128, 4, 128])
            for hh in range(4):
                self.cp("pool", identq, identq[:, hh, :], self.identf, self.identf[:])
            pA = [self.ps(es, f"wpA{i}", [128, 512]) for i in range(4)]
            pB = [self.ps(es, f"wpB{i}", [128, 512]) for i in range(2)]
            pC = [self.ps(es, f"wpC{i}", [128, 512]) for i in range(2)]
            hp = lambda hh: slice((hh % 2) * 64, (hh % 2) * 64 + 64)
            hcol = lambda hh: slice(hh * 64, (hh + 1) * 64)
            msb = bc(MSI[:, 0:128].unsqueeze(1), [128, 4, 128])
            mib = bc(MSI[:, 128:256].unsqueeze(1), [128, 4, 128])
            mtb = bc(MST[:].unsqueeze(1), [128, 4, 128])

            def sA(j, s, i, first):
                b = j % 2; c = j % 3
                p0 = self.prow(s, i)
                self.load(pm[b], pm[b][:], self.PS[p0 - 1:p0 + 127, 1536:2560], f"l0{b}", reads=[("PS", s, i), ("PS", s, i - 1), "PSall"])
                self.load(pc[b], pc[b][:], self.PS[p0:p0 + 128, 1536:2560], f"l1{b}", reads=[("PS", s, i)])
                self.load(pn[b], pn[b][:], self.PS[p0 + 1:p0 + 129, 1536:2560], f"l2{b}", reads=[("PS", s, i), ("PS", s, i + 1), "PSall"])
                self.tt("pool", sh, sh[:], pm[b], pm[b][:], pn[b], pn[b][:], ALU.add)
                self.stt("dve", sh, sh[:], sh, sh[:], 0.5, pc[b], pc[b][:], ALU.mult, ALU.subtract)
                self.tt("pool", sh, sh[:], sh, sh[:], MU, MU[:], ALU.mult)
                self.tt("dve", pp, pp[:], pc[b], pc[b][:], sh, sh[:], ALU.add)
                yield
                r_, k_, v_ = pp[:, 0:256], pp[:, 256:512], pp[:, 512:768]
                Y = yo[c]
                self.tr(pA[0], pA[0][:, 0:128], pp, pp[:, 768:896])
                self.tr(pA[0], pA[0][:, 128:256], pp, pp[:, 896:1024])
                self.act(ldT, ldT[0:64, 0, :], pA[0], pA[0][0:64, 0:128], AF.Tanh)
                self.cp("dve", ldT, ldT[64:128, 0, :], pA[0], pA[0][64:128, 0:128])
                self.mm(pA[1], pA[1][:, 0:256], ldT, ldT[0:64, 0, :], LW, LW[0:64, :])
                self.mm(pA[1], pA[1][:, 256:512], ldT, ldT[64:128, 0, :], LW, LW[64:128, :])
                self.tt("dve", targ, targ[:, 0:256], pA[1], pA[1][:, 0:256], W0, W0[:], ALU.add)
                self.tt("dve", targ, targ[:, 256:512], pA[1], pA[1][:, 256:512], A0, A0[:], ALU.add)
                self.act(sg, sg[:], targ, targ[:, 0:256], AF.Sigmoid)
                self.act(asg, asg[:], targ, targ[:, 256:512], AF.Sigmoid)
                if d == 0:
                    self.act(ldT, ldT[:, 1, :], pA[0], pA[0][:, 128:256], AF.Sigmoid)
                    self.mm(pA[2], pA[2][:, 0:256], ldT, ldT[:, 1, :], G2, G2[:])
                    self.cp("act", Y, Y[:, 512:768], pA[2], pA[2][:, 0:256])
                yield
                self.tt("pool", kkr, kkr[:], pp, k_, KK, KK[:], ALU.mult)
                for hh in range(4):
                    self.act(junk, junk[:], kkr, kkr[:, hh * 64:(hh + 1) * 64], AF.Square, accum=ssq[:, hh:hh + 1], extra_w=[ssq])
                self.act(ssq, ssq[:, 4:8], ssq, ssq[:, 0:4], AF.Sqrt)
                self.ts("dve", ssq, ssq[:, 8:12], ssq, ssq[:, 4:8], 1e-12, None, ALU.max)
                self.O("dve", lambda e, o=ssq[:, 12:16], i_=ssq[:, 8:12]: e.reciprocal(out=o, in_=i_), reads=[ssq], writes=[ssq])
                self.tt("dve", kk, v4(kk[:]), kkr, v4(kkr[:]), ssq, bc(ssq[:, 12:16].unsqueeze(2), [128, 4, 64]), ALU.mult)
                self.stt("dve", t1, t1[:], asg, asg[:], -1.0, KA, KA[:], ALU.add, ALU.mult)
                self.stt("dve", kd, kd[:], t1, t1[:], 1.0, pp, k_, ALU.add, ALU.mult)
                self.tt("pool", bv, bv[:], kk, kk[:], asg, asg[:], ALU.mult)
                yield
                self.tt("dve", t1, t1[:], pp, r_, kd, kd[:], ALU.mult)
                self.tt("pool", t1, t1[:], t1, t1[:], RK, RK[:], ALU.mult)
                self.O("dve", lambda e, o=ssq[:, 0:4], i_=v4(t1[:]): e.tensor_reduce(out=o, in_=i_, axis=AX.X, op=ALU.add), reads=[t1], writes=[ssq])
                self.tt("dve", Y, v4(Y[:, 256:512]), pp, v4(v_), ssq, bc(ssq[:, 0:4].unsqueeze(2), [128, 4, 64]), ALU.mult)
                self.cp("act", vb[c], vb[c][:], pp, v_)
                self.mm(pA[2], pA[2][:, 256:512], CS, CS[:], sg, sg[:])
                self.mm(pA[3], pA[3][:, 0:256], ON, ON[:], sg, sg[:])
                for pr_ in range(2):
                    self.mm(pA[3], pA[3][:, 256 + pr_:257 + pr_], sg, sg[:, pr_ * 128:(pr_ + 1) * 128], ON, ON[:, 0:1])
                self.cp("act", lws, lws[:], pA[2], pA[2][:, 256:512])
                self.tt("dve", lx, lx[:], lws, lws[:], sg, sg[:], ALU.subtract)
                self.tt("dve", lt, lt[:], pA[3], pA[3][:, 0:256], lws, lws[:], ALU.subtract)
                self.act(Wt, Wt[:], lws, lws[:], AF.Exp, scale=CDEC)
                self.act(Wn, Wn[:], lws, lws[:], AF.Exp, scale=-CDEC)
                self.act(Wx, Wx[:], lx, lx[:], AF.Exp, scale=CDEC)
                self.act(Wh, Wh[:], lt, lt[:], AF.Exp, scale=CDEC)
                self.act(WCk[c], WCk[c][:], pA[3], pA[3][:, 256:258], AF.Exp, scale=CDEC)
                yield
                self.stt("dve", X4, X4[:, 0:256], kk, kk[:], -1.0, Wx, Wx[:], ALU.mult, ALU.mult)
                self.tt("pool", X4, X4[:, 256:512], pp, r_, Wt, Wt[:], ALU.mult)
                self.tt("dve", X4, X4[:, 512:768], bv, bv[:], Wn, Wn[:], ALU.mult)
                self.tt("pool", X4, X4[:, 768:1024], kd, kd[:], Wn, Wn[:], ALU.mult)
                self.cp("act", Atb[b], Atb[b][:], X4, X4[:, 0:256])
                self.tt("dve", Bhb[c], Bhb[c][:], bv, bv[:], Wh, Wh[:], ALU.mult)
                self.tt("pool", Khb[c], Khb[c][:], kd, kd[:], Wh, Wh[:], ALU.mult)
                yield
                X = XT[c]
                for kind in range(4):
                    pt_ = pA[0] if kind < 2 else pA[1]
                    for pr_ in range(2):
                        c0 = kind * 256 + pr_ * 128
                        self.tr(pt_, pt_[:, ((kind % 2) * 2 + pr_) * 128:((kind % 2) * 2 + pr_ + 1) * 128], X4, X4[:, c0:c0 + 128])
                for kind in range(4):
                    pt_ = pA[0] if kind < 2 else pA[1]
                    self.cp(("act", "dve")[kind % 2], X, X[:, :, kind, :],
                            pt_, pt_[:, (kind % 2) * 256:(kind % 2) * 256 + 256].rearrange("p (a b) -> p a b", a=2))
                yield
                def amat(ps_, lk, rk):
                    for hh in range(4):
                        pq = hh // 2
                        self.mm(ps_, ps_[:, hh * 128:(hh + 1) * 128], X, X[hp(hh), pq, lk, :], X, X[hp(hh), pq, rk, :])
                amat(pA[2], 2, 0)
                self.tt("dve", Nb[b], Nb[b][:], pA[2], v4(pA[2][:]), MSI, msb, ALU.mult)
                self.tt("dve", Mf[b], Mf[b][:], pA[2], v4(pA[2][:]), MSI, msb, ALU.mult)
                amat(pA[3], 2, 1)
                self.tt("dve", Arb[c], Arb[c][:], pA[3], v4(pA[3][:]), MSI, mib, ALU.mult)
                yield
                amat(pA[0], 3, 0)
                self.tt("dve", Aak[b], Aak[b][:], pA[0], v4(pA[0][:]), MSI, msb, ALU.mult)
                amat(pA[1], 3, 1)
                self.tt("dve", Ark[c], Ark[c][:], pA[1], v4(pA[1][:]), MSI, mib, ALU.mult)
                amat(pA[2], 0, 2)
                self.tt("dve", NTb[b], NTb[b][:], pA[2], v4(pA[2][:]), MST, mtb, ALU.mult)
                self.tt("pool", Mf[b], Mf[b][:], Mf[b], Mf[b][:], identq, identq[:], ALU.add)
                self.cp("act", Mb[b], Mb[b][:], Mf[b], Mf[b][:])
                yield

            def sB(j, s, i, first):
                b = j % 2; c = j % 3
                MF, MB = Mf[b], Mb[b]
                Pc, PTc = Nb[b], NTb[b]
                for st in range(6):
                    Pn, PTn = Pb[st % 2], PTb[st % 2]
                    for hh in range(4):
                        if st < 5:
                            self.mm(pB[0], pB[0][:, hh * 128:(hh + 1) * 128], PTc, PTc[:, hh, :], Pc, Pc[:, hh, :])
                        self.mm(pB[1], pB[1][:, hh * 128:(hh + 1) * 128], Pc, Pc[:, hh, :], PTc, PTc[:, hh, :])
                    if st < 5:
                        self.cp("act", Pn, Pn[:], pB[0], v4(pB[0][:]))
                    self.cp("dve", PTn, PTn[:], pB[1], v4(pB[1][:]))
                    yield
                    for hh in range(4):
                        self.mm(pB[0], pB[0][:, hh * 128:(hh + 1) * 128], PTn, PTn[:, hh, :], MB, MB[:, hh, :])
                    self.tt("dve", MF, MF[:], MF, MF[:], pB[0], v4(pB[0][:]), ALU.add)
                    self.cp("act", MB, MB[:], MF, MF[:])
                    yield
                    Pc, PTc = Pn, PTn
                V = vb[c]
                for hh in range(4):
                    self.mm(pB[1], pB[1][:, hh * 64:(hh + 1) * 64], Aak[b], Aak[b][:, hh, :], V, V[:, hh * 64:(hh + 1) * 64])
                self.cp("act", Xb, Xb[:], pB[1], pB[1][:, 0:256])
                for hh in range(4):
                    self.mm(pB[0], pB[0][:, hh * 64:(hh + 1) * 64], MB, MB[:, hh, :], Xb, Xb[:, hh * 64:(hh + 1) * 64])
                self.cp("act", Ut[b], Ut[b][:], pB[0], pB[0][:, 0:256])
                yield
                for hh in range(4):
                    self.mm(pB[1], pB[1][:, hh * 128:(hh + 1) * 128], Atb[b], Atb[b][:, (hh // 2) * 128:(hh // 2) * 128 + 128], MB, MB[:, hh, :])
                self.cp("dve", WtT[b], WtT[b][:], pB[1], v4(pB[1][:]))
                yield

            def sC(j, s, i, first):
                b = j % 2; c = j % 3
                if first:
                    self.memset("pool", H, H[:], 0.0)
                    self.memset("pool", Hb, Hb[:], 0.0)
                X, V, Y, UT = XT[c], vb[c], yo[c], Ut[b]
                for hh in range(4):
                    self.mm(pC[0], pC[0][:, hcol(hh)], WtT[b], WtT[b][hp(hh), hh, :], Hb, Hb[hp(hh), hcol(hh)])
                self.tt("dve", UT, UT[:], UT, UT[:], pC[0], pC[0][:, 0:256], ALU.add)
                self.cp("act", Ub, Ub[:], UT, UT[:])
                yield
                for hh in range(4):
                    pq = hh // 2
                    oc = slice(256 + hh * 64, 256 + (hh + 1) * 64)
                    self.mm(pC[0], pC[0][:, oc], X, X[hp(hh), pq, 1, :], Hb, Hb[hp(hh), hcol(hh)], start=True, stop=False)
                    self.mm(pC[0], pC[0][:, oc], Arb[c], Arb[c][:, hh, :], Ub, Ub[:, hcol(hh)], start=False, stop=False)
                    self.mm(pC[0], pC[0][:, oc], Ark[c], Ark[c][:, hh, :], V, V[:, hcol(hh)], start=False, stop=True)
                for hh in range(4):
                    pq = hh // 2
                    self.mm(pC[1], pC[1][:, hcol(hh)], Bhb[c], Bhb[c][:, pq * 128:(pq + 1) * 128], Ub, Ub[:, hcol(hh)], start=True, stop=False)
                    self.mm(pC[1], pC[1][:, hcol(hh)], Khb[c], Khb[c][:, pq * 128:(pq + 1) * 128], V, V[:, hcol(hh)], start=False, stop=True)
                for pq in range(2):
                    self.ts("dve", H, H[:, pq * 128:(pq + 1) * 128], H, H[:, pq * 128:(pq + 1) * 128], WCk[c][:, pq:pq + 1], None, ALU.mult, extra_r=[WCk[c]])
                self.tt("dve", H, H[:], H, H[:], pC[1], pC[1][:, 0:256], ALU.add)
                self.cp("act", Hb, Hb[:], H, H[:])
                yield
                self.cp("act", Y, Y[:, 0:256], pC[0], pC[0][:, 256:512])
                r0 = self.rows(s, i)
                ncol = 768 if d == 0 else 512
                self.store(Y, self.YD[d, r0:r0 + 128, 256:256 + ncol], Y[:, 0:ncol], f"s0{c}", writes=[("YDw", d, s, i)])
                yield
            self.pipeline(self.tiles(d), [sA, sB, sC], counts=[9, 15, 4])
            self.end_phase()

    def phase_s5(self, l, d):
        di = self.di
        self.S.pe_skip = True
        PI = math.pi
        with ExitStack() as es:
            sb = lambda n, sh, dt=F32: self.sb(es, n, sh, dt)
            COS = sb("sCOS", [128, 32, 128]); SIN = sb("sSIN", [128, 32, 128]); RHO = sb("sRHO", [128, 32, 128])
            BP1 = sb("sBP1", [128, 32, 128], BF16); BP2 = sb("sBP2", [128, 32, 128], BF16)
            W1 = sb("sW1", [128, 512], BF16); W2 = sb("sW2", [128, 512], BF16)
            RHOF = sb("sRHOF", [128, 32]); CL = sb("sCL", [128, 32]); SL = sb("sSL", [128, 32])
            SWAP = sb("sSWAP", [128, 128]); self.load(SWAP, SWAP[:], di["c_swap"][:, :], "p0")
            psA2 = [self.ps(es, f"spsA{i}", [128, 512]) for i in range(2)]; psB2 = [self.ps(es, f"spsB{i}", [128, 512]) for i in range(2)]
            psA = psA2[0]
            psY = self.ps(es, "spsY", [128, 512]); psT = self.ps(es, "spsT", [128, 4, 128]); psX = self.ps(es, "spsX", [128, 512])
            with ExitStack() as es2:
                sb2 = lambda n, sh, dt=F32: self.sb(es2, n, sh, dt)
                J1 = sb2("sJ1", [128, 128]); self.load(J1, J1[:], di["c_j1"][d], "p1")
                Z0 = sb2("sZ0", [128, 128]); self.load(Z0, Z0[:], di["c_z0"][d], "p2")
                SEL = sb2("sSEL", [128, 128]); self.load(SEL, SEL[:], di["c_sel"][:, :], "p3")
                SELb = sb2("sSELb", [128, 128], BF16); self.cp("dve", SELb, SELb[:], SEL, SEL[:])
                negpi = sb2("snegpi", [128, 1]); self.memset("pool", negpi, negpi[:], -PI)
                lre = sb2("slre", [128, 32]); lim = sb2("slim", [128, 32]); stp = sb2("sstp", [128, 32])
                lam_re_T = di["s5_lam_re"][l, d].rearrange("g p -> p g")
                lam_im_T = di["s5_lam_im"][l, d].rearrange("g p -> p g")
                for hf in range(2):
                    self.O("sp", lambda e, o=lre[hf * 64:(hf + 1) * 64, :], i_=lam_re_T: e.dma_start(out=o, in_=i_, allow_slow_non_contiguous=True), writes=[lre], dma="p4")
                    self.O("sp", lambda e, o=lim[hf * 64:(hf + 1) * 64, :], i_=lam_im_T: e.dma_start(out=o, in_=i_, allow_slow_non_contiguous=True), writes=[lim], dma="p5")
                self.load(stp, stp[:], di["s5_log_step"][l, d].partition_broadcast(128), "p6")
                self.act(stp, stp[:], stp, stp[:], AF.Exp)
                self.ts("dve", lre, lre[:], lre, lre[:], -1e-4, None, ALU.min)
                al = sb2("sal", [128, 32]); th = sb2("sth", [128, 32])
                self.tt("dve", al, al[:], lre, lre[:], stp, stp[:], ALU.mult)
                self.tt("dve", th, th[:], lim, lim[:], stp, stp[:], ALU.mult)
                self.act(RHOF, RHOF[:], al, al[:], AF.Exp)
                arg = sb2("sarg", [128, 32, 128]); arg2 = sb2("sarg2", [128, 32, 128])
                argi = sb2("sargi", [128, 32, 128], mybir.dt.int32); argf = sb2("sargf", [128, 32, 128])

                def sinred(o_t, o_ap, a_t, a_ap, w_ap, wi_ap, wf_ap, off):
                    self.ts("dve", arg2, w_ap, a_t, a_ap, 1.0 / (2 * PI), off, ALU.mult, ALU.add)
                    self.cp("dve", argi, wi_ap, arg2, w_ap)
                    self.cp("dve", argf, wf_ap, argi, wi_ap)
                    self.tt("dve", arg2, w_ap, arg2, w_ap, argf, wf_ap, ALU.subtract)
                    self.act(o_t, o_ap, arg2, w_ap, AF.Sin, scale=2 * PI)
                self.tt("dve", arg, arg[:], th, bc(th[:].unsqueeze(2), [128, 32, 128]), J1, bc(J1[:].unsqueeze(1), [128, 32, 128]), ALU.mult)
                sinred(SIN, SIN[:], arg, arg[:], arg2[:], argi[:], argf[:], 0.0)
                sinred(COS, COS[:], arg, arg[:], arg2[:], argi[:], argf[:], 0.25)
                self.tt("dve", RHO, RHO[:], RHOF, bc(RHOF[:].unsqueeze(2), [128, 32, 128]), Z0, bc(Z0[:].unsqueeze(1), [128, 32, 128]), ALU.mult)
                jl = 127 if d == 0 else 0
                self.cp("dve", CL, CL[:], COS, COS[:, :, jl])
                self.cp("dve", SL, SL[:], SIN, SIN[:, :, jl])
                lre2 = sb2("slre2", [128, 16]); lim2 = sb2("slim2", [128, 16]); stp2 = sb2("sstp2", [128, 16])
                self.load(lre2, lre2[:], di["s5_lam_re"][l, d].rearrange("(gp go) p -> (go p) gp", go=2), "p7")
                self.load(lim2, lim2[:], di["s5_lam_im"][l, d].rearrange("(gp go) p -> (go p) gp", go=2), "p8")
                ls2 = di["s5_log_step"][l, d].rearrange("(gp go) -> go gp", go=2)
                for go in range(2):
                    self.load(stp2, stp2[go * 64:(go + 1) * 64, :], ls2[go].partition_broadcast(64), "p9")
                self.act(stp2, stp2[:], stp2, stp2[:], AF.Exp)
                self.ts("dve", lre2, lre2[:], lre2, lre2[:], -1e-4, None, ALU.min)
                pw = sb2("spw", [128, 16 * 12])
                c = lambda k: pw[:, k * 16:(k + 1) * 16]
                TTm = lambda o, a, b_, op: self.tt("dve", pw, o, pw, a, pw, b_, op)
                self.tt("dve", pw, c(0), lre2, lre2[:], stp2, stp2[:], ALU.mult)
                self.tt("dve", pw, c(1), lim2, lim2[:], stp2, stp2[:], ALU.mult)
                self.act(pw, c(2), pw, c(0), AF.Exp)
                sinred(pw, c(4), pw, c(1), arg2[:, 0, 0:16], argi[:, 0, 0:16], argf[:, 0, 0:16], 0.0)
                sinred(pw, c(5), pw, c(1), arg2[:, 0, 0:16], argi[:, 0, 0:16], argf[:, 0, 0:16], 0.25)
                TTm(c(5), c(5), c(2), ALU.mult)
                TTm(c(4), c(4), c(2), ALU.mult)
                self.ts("dve", pw, c(5), pw, c(5), -1.0, None, ALU.add)
                self.tt("dve", pw, c(6), lre2, lre2[:], lre2, lre2[:], ALU.mult)
                self.tt("dve", pw, c(7), lim2, lim2[:], lim2, lim2[:], ALU.mult)
                TTm(c(6), c(6), c(7), ALU.add)
                self.O("dve", lambda e, o=c(6), i_=c(6): e.reciprocal(out=o, in_=i_), reads=[pw], writes=[pw])
                self.tt("dve", pw, c(7), pw, c(5), lre2, lre2[:], ALU.mult)
                self.tt("dve", pw, c(8), pw, c(4), lim2, lim2[:], ALU.mult)
                TTm(c(7), c(7), c(8), ALU.add)
                TTm(c(7), c(7), c(6), ALU.mult)
                self.tt("dve", pw, c(8), pw, c(4), lre2, lre2[:], ALU.mult)
                self.tt("dve", pw, c(9), pw, c(5), lim2, lim2[:], ALU.mult)
                TTm(c(8), c(8), c(9), ALU.subtract)
                TTm(c(8), c(8), c(6), ALU.mult)
                Bre = sb2("sBre", [128, 16, 16]); Bim = sb2("sBim", [128, 16, 16])
                self.load(Bre, Bre[:], di["s5_b_re"][l, d].rearrange("(gp go) p c -> (go p) gp c", go=2), "p10")
                self.load(Bim, Bim[:], di["s5_b_im"][l, d].rearrange("(gp go) p c -> (go p) gp c", go=2), "p11")
                bbr = sb2("sbbr", [128, 16, 16]); bbi = sb2("sbbi", [128, 16, 16]); tq = sb2("stq", [128, 16, 16])
                cre = bc(c(7).unsqueeze(2), [128, 16, 16]); cim = bc(c(8).unsqueeze(2), [128, 16, 16])
                self.tt("dve", bbr, bbr[:], Bre, Bre[:], pw, cre, ALU.mult)
                self.tt("dve", tq, tq[:], Bim, Bim[:], pw, cim, ALU.mult)
                self.tt("dve", bbr, bbr[:], bbr, bbr[:], tq, tq[:], ALU.subtract)
                self.tt("dve", bbi, bbi[:], Bim, Bim[:], pw, cre, ALU.mult)
                self.tt("dve", tq, tq[:], Bre, Bre[:], pw, cim, ALU.mult)
                self.tt("dve", bbi, bbi[:], bbi, bbi[:], tq, tq[:], ALU.add)
                XPr = sb2("sXPr", [128, 8, 128], BF16); XPi = sb2("sXPi", [128, 8, 128], BF16)
                self.memset("pool", XPr, XPr[:], 0.0); self.memset("pool", XPi, XPi[:], 0.0)
                for g in range(32):
                    gp, go, gi = g // 2, g % 2, g % 8
                    self.cp("dve", XPr, XPr[:, gi, gi * 16:(gi + 1) * 16], bbr, bbr[:, gp, :])
                    self.cp("pool", XPi, XPi[:, gi, gi * 16:(gi + 1) * 16], bbi, bbi[:, gp, :])
                    selg = SELb[:, go * 64:(go + 1) * 64]
                    self.mm(psA, psA[:, 0:64], XPr, XPr[:, gi, :], SELb, selg)
                    self.mm(psA, psA[:, 64:128], XPi, XPi[:, gi, :], SELb, selg)
                    self.cp("act", BP1, BP1[:, g, :], psA, psA[:, 0:128])
                    self.act(BP2, BP2[:, g, 0:64], psA, psA[:, 64:128], AF.Copy, scale=-1.0)
                    self.cp("dve", BP2, BP2[:, g, 64:128], psA, psA[:, 0:64])
                Ca = sb2("sCa", [128, 4, 128]); Cb = sb2("sCb", [128, 4, 128])
                cre_d = di["s5_c_re"][l, d].rearrange("(a gi) c p -> (gi c) a p", gi=8)
                cim_d = di["s5_c_im"][l, d].rearrange("(a gi) c p -> (gi c) a p", gi=8)
                self.load(Ca, Ca[:, :, 0:64], cre_d, "p12"); self.load(Ca, Ca[:, :, 64:128], cim_d, "p13")
                self.load(Cb, Cb[:, :, 0:64], cim_d, "p14"); self.load(Cb, Cb[:, :, 64:128], cre_d, "p15")
                for a in range(4):
                    self.tr(psT, psT[:, a, :], Ca, Ca[:, a, :])
                self.cp("act", W1, W1[0:64, :], psT, psT[0:64, :, :].rearrange("p a b -> p (a b)"))
                self.act(W1, W1[64:128, :], psT, psT[64:128, :, :].rearrange("p a b -> p (a b)"), AF.Copy, scale=-1.0)
                for a in range(4):
                    self.tr(psT, psT[:, a, :], Cb, Cb[:, a, :])
                self.act(W2, W2[:], psT, psT[:].rearrange("p a b -> p (a b)"), AF.Copy, scale=-1.0)
                self.S.barrier()
                self.S.flush()
            u = [sb(f"su{i}", [128, 512]) for i in range(2)]
            uT = sb("suT", [128, 4, 128], BF16)
            t1 = [sb(f"st1{i}", [128, 512]) for i in range(2)]; t2 = [sb(f"st2{i}", [128, 512]) for i in range(2)]
            zin = [sb(f"szin{i}", [128, 32, 128]) for i in range(2)]; z = sb("sz", [128, 32, 128])
            ZC = sb("sZC", [128, 32, 128], BF16); ZS = sb("sZS", [128, 32, 128], BF16)
            xst = sb("sxst", [128, 32]); xc = sb("sxc", [128, 32]); xs_ = sb("sxs", [128, 32]); tm = sb("stm", [128, 32])
            yo = [sb(f"syo{i}", [128, 512]) for i in range(2)]
            j0 = 0 if d == 0 else 127
            fl = lambda ap: ap.rearrange("p a b -> p (a b)")

            def s0(j, s, i, first):
                b = j % 2
                p0 = self.prow(s, i)
                self.load(u[b], u[b][:], self.PS[p0:p0 + 128, 2560:3072], f"l0{b}", reads=[("PS", s, i)])
                for ct in range(4):
                    self.tr(psT, psT[:, ct, :], u[b], u[b][:, ct * 128:(ct + 1) * 128])
                self.cp("act", uT, uT[:], psT, psT[:])
                yield
                for oc in range(8):
                    ct = oc // 2
                    pa, pb_ = psA2[oc % 2], psB2[oc % 2]
                    for gi in range(4):
                        g = oc * 4 + gi
                        self.mm(pa, pa[:, gi * 128:(gi + 1) * 128], BP1, BP1[:, g, :], uT, uT[:, ct, :])
                        self.mm(pb_, pb_[:, gi * 128:(gi + 1) * 128], BP2, BP2[:, g, :], uT, uT[:, ct, :])
                    gs = slice(oc * 4, oc * 4 + 4)
                    self.tt("dve", t1[oc % 2], t1[oc % 2][:], pa, pa[:], COS, fl(COS[:, gs, :]), ALU.mult)
                    self.tt("dve", t2[oc % 2], t2[oc % 2][:], pb_, pb_[:], SIN, fl(SIN[:, gs, :]), ALU.mult)
                    self.tt("pool", zin[b], fl(zin[b][:, gs, :]), t1[oc % 2], t1[oc % 2][:], t2[oc % 2], t2[oc % 2][:], ALU.subtract)
                    if oc % 2 == 1:
                        yield

            def s1(j, s, i, first):
                b = j % 2
                Z = zin[b]
                if first:
                    self.memset("pool", xst, xst[:], 0.0)
                self.tt("dve", tm, tm[:], RHOF, RHOF[:], xst, xst[:], ALU.mult)
                self.tt("dve", Z, Z[:, :, j0], Z, Z[:, :, j0], tm, tm[:], ALU.add)
                zf = z[:].rearrange("p a b -> p (a b)"); zif = Z[:].rearrange("p a b -> p (a b)"); rf = RHO[:].rearrange("p a b -> p (a b)")
                if d == 1:
                    zf, zif, rf = zf[:, ::-1], zif[:, ::-1], rf[:, ::-1]
                self.O("dve", lambda e, o=zf, a=rf, b_=zif: e.tensor_tensor_scan(out=o, data0=a, data1=b_, initial=0.0, op0=ALU.mult, op1=ALU.add),
                       reads=[RHO, Z], writes=[z])
                yield
                jl = 127 - j0
                self.tt("dve", xc, xc[:], z, z[:, :, jl], CL, CL[:], ALU.mult)
                self.tt("dve", xs_, xs_[:], z, z[:, :, jl], SL, SL[:], ALU.mult)
                self.mm(psX, psX[:, 0:32], SWAP, SWAP[:], xs_, xs_[:])
                self.tt("dve", xst, xst[:], xc, xc[:], psX, psX[:, 0:32], ALU.add)
                yield
                for hf in range(2):
                    gs = slice(hf * 16, hf * 16 + 16)
                    self.tt("dve", ZC, ZC[:, gs, :], z, z[:, gs, :], COS, COS[:, gs, :], ALU.mult)
                    self.tt("pool" if hf == 0 else "dve", ZS, ZS[:, gs, :], z, z[:, gs, :], SIN, SIN[:, gs, :], ALU.mult)
                    yield
                for g in range(32):
                    self.mm(psY, psY[:, g * 16:(g + 1) * 16], ZC, ZC[:, g, :], W1, W1[:, g * 16:(g + 1) * 16], start=True, stop=False)
                    self.mm(psY, psY[:, g * 16:(g + 1) * 16], ZS, ZS[:, g, :], W2, W2[:, g * 16:(g + 1) * 16], start=False, stop=True)
                    if g % 8 == 7:
                        yield
                self.cp("act", yo[b], yo[b][:], psY, psY[:])
                r0 = self.rows(s, i)
                self.store(yo[b], self.YD[d, r0:r0 + 128, 1024:1536], yo[b][:], f"s0{b}", writes=[("YDs", d, s, i)])
                yield
            self.pipeline(self.tiles(d), [s0, s1], counts=[6, 10])
            self.end_phase()

    def head_ln(self, y, eps, gw, gb, wk, st):
        v4 = lambda ap: ap.rearrange("p (a b) -> p a b", a=4)
        self.O("dve", lambda e, o=st[:, 0:4], i_=v4(y[:]): e.tensor_reduce(out=o, in_=i_, axis=AX.X, op=ALU.add), reads=[y], writes=[st])
        self.ts("dve", st, st[:, 0:4], st, st[:, 0:4], -1.0 / 64, None, ALU.mult)
        self.tt("dve", y, v4(y[:]), y, v4(y[:]), st, bc(st[:, 0:4].unsqueeze(2), [128, 4, 64]), ALU.add)
        self.tt("pool", wk, wk[:], y, y[:], y, y[:], ALU.mult)
        self.O("dve", lambda e, o=st[:, 4:8], i_=v4(wk[:]): e.tensor_reduce(out=o, in_=i_, axis=AX.X, op=ALU.add), reads=[wk], writes=[st])
        self.ts("dve", st, st[:, 4:8], st, st[:, 4:8], 1.0 / 64, eps, ALU.mult, ALU.add)
        self.act(st, st[:, 8:12], st, st[:, 4:8], AF.Sqrt)
        self.O("dve", lambda e, o=st[:, 12:16], i_=st[:, 8:12]: e.reciprocal(out=o, in_=i_), reads=[st], writes=[st])
        self.tt("dve", y, v4(y[:]), y, v4(y[:]), st, bc(st[:, 12:16].unsqueeze(2), [128, 4, 64]), ALU.mult)
        self.tt("pool", y, y[:], y, y[:], gw, gw[:], ALU.mult)
        self.tt("dve", y, y[:], y, y[:], gb, gb[:], ALU.add)

    def phase_O1(self, l):
        di = self.di
        self.S.pe_skip = True
        with ExitStack() as es:
            sb = lambda n, sh, dt=F32: self.sb(es, n, sh, dt)
            stg = [sb(f"ostg{i}", [128, 1024]) for i in range(3)]
            WO = self.load_w_bf16(es, "oWO", di["w_out"][l], 8, D, stg)
            GL = self.load_w_bf16(es, "oGL", di["s5_glu_w"][l], 4, 512, stg)
            RGW = self.bcast_load(es, "oRGW", di["ret_gn_w"][l], 256); RGB = self.bcast_load(es, "oRGB", di["ret_gn_b"][l], 256)
            WGW = self.bcast_load(es, "oWGW", di["rwkv_gn_w"][l], 256); WGB = self.bcast_load(es, "oWGB", di["rwkv_gn_b"][l], 256)
            SD = self.bcast_load(es, "oSD", di["s5_d"][l], 512); GB = self.bcast_load(es, "oGB", di["s5_glu_b"][l], 512)
            y0 = [sb(f"oy0{i}", [128, YW]) for i in range(2)]
            y1 = [sb(f"oy1{i}", [128, YW]) for i in range(2)]
            gu = [sb(f"ogu{i}", [128, 768]) for i in range(2)]
            xt = [sb(f"oxt{i}", [128, D]) for i in range(2)]
            mix = sb("omix", [128, D]); wk = sb("owk", [128, 512]); wk2 = sb("owk2", [128, 512]); st = sb("ost", [128, 16])
            ygT = sb("oygT", [128, 4, 128], BF16); mixT = sb("omixT", [128, 8, 128], BF16)
            xo = [sb(f"oxo{i}", [128, D]) for i in range(2)]
            psT = [self.ps(es, f"opsT{i}", [128, 4, 128]) for i in range(2)]
            psG = self.ps(es, "opsG", [128, 512]); psO = [self.ps(es, f"opsO{i}", [128, 512]) for i in range(2)]
            nr, nw, ns = (self.flags.get(k) for k in ("no_ret", "no_rwkv", "no_s5"))
            self.S.barrier()
            for it, (s, i, first) in enumerate(self.tiles()):
                b = it % 2
                r0 = self.rows(s, i); p0 = self.prow(s, i)
                self.load(y0[b], y0[b][:], self.YD[0, r0:r0 + 128, :], f"l0{b}", reads=[("YDr", 0, s, i), ("YDw", 0, s, i), ("YDs", 0, s, i)])
                self.load(y1[b], y1[b][:], self.YD[1, r0:r0 + 128, :], f"l1{b}", reads=[("YDr", 1, s, i), ("YDw", 1, s, i), ("YDs", 1, s, i)])
                self.load(gu[b], gu[b][:, 0:256], self.PS[p0:p0 + 128, 1280:1536], f"l2{b}", reads=[("PS", s, i)])
                self.load(gu[b], gu[b][:, 256:768], self.PS[p0:p0 + 128, 2560:3072], f"l3{b}", reads=[("PS", s, i)])
                self.load(xt[b], xt[b][:], self.XS[r0:r0 + 128, :], f"l4{b}", reads=["XSall", ("XS", r0)])
                A, B, GU = y0[b], y1[b], gu[b]
                yr = mix
                if nr:
                    self.memset("pool", mix, mix[:, 0:256], 0.0)
                else:
                    self.tt("dve", mix, mix[:, 0:256], A, A[:, 0:256], B, B[:, 0:256], ALU.add)
                    self._ln_slice(mix, 0, 1e-5, RGW, RGB, wk, st)
                    self.act(wk2, wk2[:, 0:256], GU, GU[:, 0:256], AF.Silu)
                    self.tt("dve", mix, mix[:, 0:256], mix, mix[:, 0:256], wk2, wk2[:, 0:256], ALU.mult)
                if nw:
                    self.memset("pool", mix, mix[:, 256:512], 0.0)
                else:
                    self.tt("dve", mix, mix[:, 256:512], A, A[:, 256:512], B, B[:, 256:512], ALU.add)
                    self._ln_slice(mix, 256, 64e-5, WGW, WGB, wk, st)
                    self.tt("pool", wk2, wk2[:, 0:256], A, A[:, 512:768], B, B[:, 512:768], ALU.add)
                    self.tt("dve", mix, mix[:, 256:512], mix, mix[:, 256:512], wk2, wk2[:, 0:256], ALU.add)
                    self.tt("dve", mix, mix[:, 256:512], mix, mix[:, 256:512], A, A[:, 768:1024], ALU.mult)
                if ns:
                    self.memset("pool", mix, mix[:, 512:1024], 0.0)
                else:
                    self.tt("dve", wk, wk[:], GU, GU[:, 256:768], SD, SD[:], ALU.mult)
                    self.tt("pool", wk2, wk2[:], A, A[:, 1024:1536], B, B[:, 1024:1536], ALU.add)
                    self.tt("dve", wk, wk[:], wk, wk[:], wk2, wk2[:], ALU.add)
                    self.tt("pool", wk2, wk2[:], wk, wk[:], wk, wk[:], ALU.mult)
                    self.ts("dve", wk2, wk2[:], wk2, wk2[:], 0.044715, 1.0, ALU.mult, ALU.add)
                    self.tt("dve", wk2, wk2[:], wk2, wk2[:], wk, wk[:], ALU.mult)
                    self.act(wk2, wk2[:], wk2, wk2[:], AF.Sigmoid, scale=1.5957691216057308)
                    self.tt("dve", wk, wk[:], wk, wk[:], wk2, wk2[:], ALU.mult)
                    self.transpose_to(wk, 4, psT, ygT)
                    for k in range(4):
                        self.mm(psG, psG[:], ygT, ygT[:, k, :], GL, GL[:, k, :], start=(k == 0), stop=(k == 3))
                    self.tt("dve", wk2, wk2[:], psG, psG[:], GB, GB[:], ALU.add)
                    self.act(wk2, wk2[:], wk2, wk2[:], AF.Sigmoid)
                    self.tt("dve", mix, mix[:, 512:1024], wk, wk[:], wk2, wk2[:], ALU.mult)
                self.transpose_to(mix, 8, psT, mixT)
                for cb in range(2):
                    pp = psO[cb]
                    for k in range(8):
                        self.mm(pp, pp[:], mixT, mixT[:, k, :], WO, WO[:, k, cb * 512:(cb + 1) * 512], start=(k == 0), stop=(k == 7))
                    self.tt("dve", xo[b], xo[b][:, cb * 512:(cb + 1) * 512], xt[b], xt[b][:, cb * 512:(cb + 1) * 512], pp, pp[:], ALU.add)
                self.store(xo[b], self.XS[r0:r0 + 128, :], xo[b][:], f"s0{b}", writes=[("XS", r0)])
            self.end_phase()

    def _ln_slice(self, mix, c0, eps, gw, gb, wk, st):
        v4 = lambda ap: ap.rearrange("p (a b) -> p a b", a=4)
        y = mix[:, c0:c0 + 256]
        self.O("dve", lambda e, o=st[:, 0:4], i_=v4(y): e.tensor_reduce(out=o, in_=i_, axis=AX.X, op=ALU.add), reads=[mix], writes=[st])
        self.ts("dve", st, st[:, 0:4], st, st[:, 0:4], -1.0 / 64, None, ALU.mult)
        self.tt("dve", mix, v4(y), mix, v4(y), st, bc(st[:, 0:4].unsqueeze(2), [128, 4, 64]), ALU.add)
        self.tt("pool", wk, wk[:, 0:256], mix, y, mix, y, ALU.mult)
        self.O("dve", lambda e, o=st[:, 4:8], i_=v4(wk[:, 0:256]): e.tensor_reduce(out=o, in_=i_, axis=AX.X, op=ALU.add), reads=[wk], writes=[st])
        self.ts("dve", st, st[:, 4:8], st, st[:, 4:8], 1.0 / 64, eps, ALU.mult, ALU.add)
        self.act(st, st[:, 8:12], st, st[:, 4:8], AF.Sqrt)
        self.O("dve", lambda e, o=st[:, 12:16], i_=st[:, 8:12]: e.reciprocal(out=o, in_=i_), reads=[st], writes=[st])
        self.tt("dve", mix, v4(y), mix, v4(y), st, bc(st[:, 12:16].unsqueeze(2), [128, 4, 64]), ALU.mult)
        self.tt("pool", mix, y, mix, y, gw, gw[:], ALU.mult)
        self.tt("dve", mix, y, mix, y, gb, gb[:], ALU.add)

    def phase_O2(self, l, last):
        di = self.di
        self.S.pe_skip = True
        finals = []
        with ExitStack() as es:
            sb = lambda n, sh, dt=F32: self.sb(es, n, sh, dt)
            stg = [sb(f"fstg{i}", [128, 512]) for i in range(2)]
            WG = self.load_w_bf16(es, "fWG", di["ffn_w_gate"][l], 8, DFF, stg)
            WU = self.load_w_bf16(es, "fWU", di["ffn_w_up"][l], 8, DFF, stg)
            WD = self.load_w_bf16(es, "fWD", di["ffn_w_down"][l], 22, D, stg)
            G2 = self.bcast_load(es, "fG2", di["norm2_g"][l], D)
            GF = self.bcast_load(es, "fGF", di["final_g"], D) if last else None
            xb = [sb(f"fxb{i}", [128, D]) for i in range(2)]
            junk = sb("fjunk", [128, D], BF16); ss = [sb(f"fss{i}", [128, 4]) for i in range(2)]
            junk2 = sb("fjunk2", [128, D], BF16) if last else None
            ss2 = sb("fss2", [128, 4]) if last else None
            h = [sb(f"fh{i}", [128, D]) for i in range(2)]
            hT = [sb(f"fhT{i}", [128, 8, 128], BF16) for i in range(2)]
            sg = [sb(f"fsg{i}", [128, 512]) for i in range(2)]
            atok = sb("fatok", [128, DFF], BF16)
            aT = sb("faT", [128, 22, 128], BF16)
            xo = sb("fxo0", [128, D])
            yo = sb("fyo0", [128, D]) if last else None
            psT = [self.ps(es, "fpsT0", [128, 4, 128])]
            psG = [self.ps(es, f"fpsG{i}", [128, 512]) for i in range(2)]
            psU = [self.ps(es, f"fpsU{i}", [128, 512]) for i in range(2)]
            psTb = self.ps(es, "fpsTb", [128, 8, 128], BF16)
            psO = [self.ps(es, f"fpsO{i}", [128, 512]) for i in range(2)]
            self.S.barrier()

            def s0(j, s, i, first):
                b = j % 2
                r0 = self.rows(s, i)
                self.load(xb[b], xb[b][:], self.XS[r0:r0 + 128, :], f"l0{b}", reads=["XSall", ("XS", r0)])
                self.rmsnorm(xb[b], G2, h[b], junk, ss[b])
                yield
                self.transpose_to(h[b], 8, psT, hT[b])
                yield

            def s1(j, s, i, first):
                b = j % 2
                r0 = self.rows(s, i)
                HT = hT[b]
                for gi, c0 in enumerate(range(0, DFF, 512)):
                    c1 = min(DFF, c0 + 512)
                    w = c1 - c0
                    pg, pu, sgg = psG[gi % 2], psU[gi % 2], sg[gi % 2]
                    for k in range(8):
                        self.mm(pg, pg[:, 0:w], HT, HT[:, k, :], WG, WG[:, k, c0:c1], start=(k == 0), stop=(k == 7))
                    for k in range(8):
                        self.mm(pu, pu[:, 0:w], HT, HT[:, k, :], WU, WU[:, k, c0:c1], start=(k == 0), stop=(k == 7))
                    self.act(sgg, sgg[:, 0:w], pg, pg[:, 0:w], AF.Silu)
                    self.tt("dve", atok, atok[:, c0:c1], sgg, sgg[:, 0:w], pu, pu[:, 0:w], ALU.mult)
                    if gi % 2 == 1:
                        yield
                for g0 in range(0, 22, 8):
                    g1 = min(22, g0 + 8)
                    for f in range(g0, g1):
                        self.tr(psTb, psTb[:, f - g0, :], atok, atok[:, f * 128:(f + 1) * 128])
                    self.cp(("act", "dve", "act")[g0 // 8], aT, aT[:, g0:g1, :], psTb, psTb[:, 0:g1 - g0, :])
                yield
                for cb in range(2):
                    pp = psO[cb]
                    for k in range(22):
                        self.mm(pp, pp[:], aT, aT[:, k, :], WD, WD[:, k, cb * 512:(cb + 1) * 512], start=(k == 0), stop=(k == 21))
                    self.tt("dve", xo, xo[:, cb * 512:(cb + 1) * 512], xb[b], xb[b][:, cb * 512:(cb + 1) * 512], pp, pp[:], ALU.add)
                yield
                if not last:
                    self.store(xo, self.XS[r0:r0 + 128, :], xo[:], "s00", writes=[("XS", r0)])
                else:
                    self.rmsnorm(xo, GF, yo, junk2, ss2)
                    st_ = self.store(yo, self.yout[s][i * 128:(i + 1) * 128, :], yo[:], "s00")
                    finals.append(st_)
                yield
            self.pipeline(self.tiles(), [s0, s1])
            self.end_phase()
        return finals


def _consts(Tmax):
    c = {}
    c["c_ident"] = np.eye(128, dtype=np.float32)
    inv = (1.0 / (10000.0 ** np.linspace(0.0, 1.0, 32, dtype=np.float32))).astype(np.float32)
    ang = np.arange(Tmax, dtype=np.float32)[:, None] * inv[None, :]
    cos, sin = np.cos(ang).astype(np.float32), np.sin(ang).astype(np.float32)
    rot = np.concatenate([0.125 * cos, 0.125 * cos, -0.125 * sin, 0.125 * sin, cos, cos, -sin, sin], axis=1)
    c["c_rot"] = np.ascontiguousarray(rot.astype(np.float32))
    lg = np.log(1.0 - 2.0 ** (-5.0 - np.arange(4, dtype=np.float64)))
    i = np.arange(128, dtype=np.float64)
    diff = np.abs(i[:, None] - i[None, :])
    c["c_retmask"] = np.concatenate([np.exp(diff * lg[h]) for h in range(4)], axis=1).astype(np.float32)
    qk = np.zeros((2, 128, 8), np.float32)
    for h in range(4):
        qk[0, :, h] = np.exp((i + 1) * lg[h]); qk[0, :, 4 + h] = np.exp((127 - i) * lg[h])
        qk[1, :, h] = np.exp((128 - i) * lg[h]); qk[1, :, 4 + h] = np.exp(i * lg[h])
    c["c_retqk"] = qk
    gc = np.zeros((128, 256), np.float32)
    for r in range(128):
        for pr in range(2):
            gc[r, pr * 128:(pr + 1) * 128] = np.exp(128 * lg[pr * 2 + r // 64])
    c["c_retgc"] = gc
    s_, t_ = np.meshgrid(np.arange(128), np.arange(128), indexing="ij")
    c["c_cs"] = np.stack([(s_ <= t_), (s_ >= t_)]).astype(np.float32)
    c["c_ones"] = np.ones((128, 128), np.float32)
    c["c_msi"] = np.stack([np.concatenate([(s_ < t_), (s_ <= t_)], axis=1),
                           np.concatenate([(s_ > t_), (s_ >= t_)], axis=1)]).astype(np.float32)
    c["c_mst"] = np.stack([(s_ > t_), (s_ < t_)]).astype(np.float32)
    j = np.arange(128, dtype=np.float32)
    c["c_j1"] = np.stack([np.tile(j + 1, (128, 1)), np.tile(128 - j, (128, 1))]).astype(np.float32)
    z0 = np.ones((2, 128, 128), np.float32); z0[0, :, 0] = 0; z0[1, :, 127] = 0
    c["c_z0"] = z0
    sw = np.zeros((128, 128), np.float32)
    for p in range(64):
        sw[64 + p, p] = -1.0; sw[p, 64 + p] = 1.0
    c["c_swap"] = sw
    sel = np.zeros((128, 128), np.float32)
    for go in range(2):
        for p in range(64):
            sel[go * 64 + p, go * 64 + p] = 1.0
    c["c_sel"] = sel
    return c


def _win_perm():
    idx = []
    def sw(base):
        out = []
        for h in range(4):
            out += list(range(base + h * 64 + 32, base + h * 64 + 64)) + list(range(base + h * 64, base + h * 64 + 32))
        return out
    idx += list(range(0, 256)) + sw(0) + list(range(256, 512)) + sw(256) + list(range(512, 768)) + list(range(768, 1024))
    idx += list(range(1024, 2560))
    return np.array(idx)


_CACHE = {}


def run(inputs, Tp, Ts, L, ncores, flags=None):
    key = (Tp, Ts, L, ncores, tuple(sorted((flags or {}).items())))
    if key not in _CACHE:
        _CACHE[key] = Builder(Tp, Ts, L, flags).build()
    nc = _CACHE[key]
    f = lambda a: np.ascontiguousarray(np.asarray(a, dtype=np.float32))
    shared = {k: f(v) for k, v in inputs.items() if k not in ("x_prompt", "x_sample", "w_in")}
    shared["w_in"] = np.ascontiguousarray(f(inputs["w_in"])[:, :, _win_perm()])
    shared.update(_consts(max(Tp, Ts)))
    xp, xs = f(inputs["x_prompt"]), f(inputs["x_sample"])
    nb_s = xs.shape[0]
    in_maps = []
    for c in range(ncores):
        m = dict(shared)
        m["x_prompt"] = xp[c % xp.shape[0]]
        m["x_sample"] = xs[c % nb_s]
        in_maps.append(m)
    res = run_bass_kernel_spmd(nc, in_maps, core_ids=list(range(ncores)))
    yp = np.stack([res.results[c]["y_prompt"] for c in range(xp.shape[0])])
    ys = np.stack([res.results[c]["y_sample"] for c in range(nb_s)])
    return yp.astype(np.float32), ys.astype(np.float32)


def kernel(**inputs):
    return run(inputs, 8192, 4096, 4, 8)
```

```python
import math
import numpy as np
import concourse.bass as bass
import concourse.mybir as mybir
from concourse.bass_utils import run_bass_kernel_spmd
from contextlib import ExitStack

F32 = mybir.dt.float32
BF16 = mybir.dt.bfloat16
ALU = mybir.AluOpType
AF = mybir.ActivationFunctionType
AX = mybir.AxisListType

D = 1024
NH = 4
HD = 64
PIN = 3072
DFF = 2816
YW = 1536
CDEC = -math.exp(-0.5)
ENGS = ("pe", "act", "dve", "pool", "sp")


class Op:
    __slots__ = ("eng", "fn", "deps", "signal", "idx", "semkey", "is_dma", "emitted", "val")

    def __init__(self, eng, fn, semkey, is_dma):
        self.eng = eng
        self.fn = fn
        self.deps = []
        self.signal = is_dma
        self.idx = -1
        self.semkey = semkey
        self.is_dma = is_dma
        self.emitted = False
        self.val = None


class Sched:
    def __init__(self, nc, es):
        self.nc = nc
        self.es = es
        self.ops = {e: [] for e in ENGS}
        self.semidx = {}
        self.sems = {}
        self.res = {}
        self.seen = {e: {} for e in ENGS}
        self.sigcount = {}
        self.last_emitted = {}
        self.last_op = {}
        self.nops = 0
        self.pe_skip = False
        self.pe_fence_pending = False

    def _dep(self, op, d, force=False):
        if d is None or d is op:
            return
        if self.pe_skip and not force and d.eng == "pe" and op.eng == "pe" and not d.is_dma:
            return
        if d.emitted and not d.signal:
            d = self.last_emitted[d.semkey]
        seen = self.seen[op.eng]
        if seen.get(d.semkey, -1) >= d.idx:
            return
        seen[d.semkey] = d.idx
        d.signal = True
        op.deps.append(d)

    def op(self, eng, fn, reads=(), writes=(), dma=None, extra=(), fence=False):
        is_dma = dma is not None
        semkey = ("dma", dma) if is_dma else ("eng", eng)
        o = Op(eng, fn, semkey, is_dma)
        o.idx = self.semidx.get(semkey, 0)
        self.semidx[semkey] = o.idx + 1
        for r in reads:
            st = self.res.get(r)
            if st is not None:
                self._dep(o, st[0])
        for w in writes:
            st = self.res.get(w)
            if st is not None:
                self._dep(o, st[0])
                for rd in st[1]:
                    self._dep(o, rd)
        for d in extra:
            self._dep(o, d)
        if eng == "pe":
            if fence or self.pe_fence_pending:
                self._dep(o, self.last_op.get(semkey), force=True)
            self.pe_fence_pending = fence
        for r in reads:
            st = self.res.setdefault(r, [None, []])
            st[1].append(o)
        for w in writes:
            self.res[w] = [o, []]
        self.ops[eng].append(o)
        self.last_op[semkey] = o
        self.nops += 1
        return o

    def barrier(self):
        lasts = list(self.last_op.values())
        for e in ENGS:
            self.op(e, lambda en: en.nop(), extra=lasts)

    def flush(self, final=()):
        nc = self.nc
        for key in self.semidx:
            if key not in self.sems:
                nm = ("s_" + "_".join(map(str, key)))[:48]
                self.sems[key] = self.es.enter_context(nc.semaphore(nm))
        for e in ENGS:
            if self.ops[e]:
                for o in reversed(self.ops[e]):
                    if not o.is_dma:
                        o.signal = True
                        break
        for e in ENGS:
            for o in self.ops[e]:
                if o.is_dma:
                    o.val = 16 * (o.idx + 1)
                elif o.signal:
                    c = self.sigcount.get(o.semkey, 0) + 1
                    self.sigcount[o.semkey] = c
                    o.val = c
        sems = self.sems
        pending = self.ops

        def run(en, key):
            for o in pending[key]:
                for d in o.deps:
                    en.wait_ge(sems[d.semkey], d.val)
                ins = o.fn(en)
                if o.signal:
                    ins.then_inc(sems[o.semkey], 16 if o.is_dma else 1)
            if key == "sp":
                for d in final:
                    en.wait_ge(sems[d.semkey], d.val)

        with nc.Block() as block:
            @block.tensor
            def _(en):
                run(en, "pe")

            @block.scalar
            def _(en):
                run(en, "act")

            @block.vector
            def _(en):
                run(en, "dve")

            @block.gpsimd
            def _(en):
                run(en, "pool")

            @block.sync
            def _(en):
                run(en, "sp")
        for e in ENGS:
            for o in self.ops[e]:
                o.emitted = True
                if o.signal and not o.is_dma:
                    self.last_emitted[o.semkey] = o
            self.ops[e] = []


class Tl:
    def __init__(self, t, key):
        self.t = t
        self.k = key

    def __getitem__(self, idx):
        return self.t[idx]


def bc(ap, shape):
    return ap.to_broadcast(list(shape))


class Builder:
    def __init__(self, Tp, Ts, depth, flags=None):
        self.Tp, self.Ts, self.L = Tp, Ts, depth
        self.seqs = [(0, Tp), (Tp, Ts)]
        self.TT = Tp + Ts
        self.flags = flags or {}
        self.nc = bass.Bass("TRN2", target_bir_lowering=False)

    def sb(self, es, name, shape, dt=F32):
        self._uid = getattr(self, "_uid", 0) + 1
        name = f"{name}_{self._uid}"
        return Tl(es.enter_context(self.nc.sbuf_tensor(name, list(shape), dt)), name)

    def ps(self, es, name, shape, dt=F32):
        self._uid = getattr(self, "_uid", 0) + 1
        name = f"{name}_{self._uid}"
        return Tl(es.enter_context(self.nc.psum_tensor(name, list(shape), dt)), name)

    def O(self, eng, fn, reads=(), writes=(), dma=None, fence=False):
        return self.S.op(eng, fn, [r.k if isinstance(r, Tl) else r for r in reads],
                         [w.k if isinstance(w, Tl) else w for w in writes], dma=dma, fence=fence)

    def load(self, dst, dst_ap, src_ap, key, reads=(), eng="sp"):
        return self.O(eng, lambda e, o=dst_ap, i=src_ap: e.dma_start(out=o, in_=i, allow_slow_non_contiguous=True), reads=reads, writes=[dst], dma=key)

    def store(self, src, dst_ap, src_ap, key, writes=(), eng="sp"):
        return self.O(eng, lambda e, o=dst_ap, i=src_ap: e.dma_start(out=o, in_=i), reads=[src], writes=writes, dma=key)

    def mm(self, out_t, out_ap, l_t, l_ap, r_t, r_ap, start=True, stop=True):
        return self.O("pe", lambda e, o=out_ap, a=l_ap, b=r_ap, s0=start, s1=stop: e.matmul(o, a, b, start=s0, stop=s1),
                      reads=[l_t, r_t] + ([] if start else [out_t]), writes=[out_t], fence=(l_ap.shape[0] != 128))

    def tr(self, out_t, out_ap, in_t, in_ap):
        idn = self.identf if in_ap.dtype == F32 else self.identb
        return self.O("pe", lambda e, o=out_ap, a=in_ap, i=idn: e.transpose(o, a, i[:]), reads=[in_t, idn], writes=[out_t])

    def act(self, out_t, out_ap, in_t, in_ap, func, scale=1.0, bias=None, accum=None, extra_r=(), extra_w=()):
        def fn(e, o=out_ap, i=in_ap, f=func, s=scale, b=bias, a=accum):
            kw = {}
            if b is not None:
                kw["bias"] = b
            if a is not None:
                kw["accum_out"] = a
            return e.activation(out=o, in_=i, func=f, scale=s, **kw)
        return self.O("act", fn, reads=[in_t] + list(extra_r), writes=[out_t] + list(extra_w))

    def tt(self, eng, out_t, out_ap, a_t, a_ap, b_t, b_ap, op):
        return self.O(eng, lambda e, o=out_ap, a=a_ap, b=b_ap, p=op: e.tensor_tensor(out=o, in0=a, in1=b, op=p),
                      reads=[a_t, b_t], writes=[out_t])

    def ts(self, eng, out_t, out_ap, a_t, a_ap, s1, s2, op0, op1=None, extra_r=()):
        def fn(e, o=out_ap, a=a_ap, x=s1, y=s2, p0=op0, p1=op1):
            if p1 is None:
                return e.tensor_single_scalar(out=o, in_=a, scalar=x, op=p0)
            return e.tensor_scalar(out=o, in0=a, scalar1=x, scalar2=y, op0=p0, op1=p1)
        return self.O(eng, fn, reads=[a_t] + list(extra_r), writes=[out_t])

    def stt(self, eng, out_t, out_ap, a_t, a_ap, scalar, b_t, b_ap, op0, op1, extra_r=()):
        eng = "dve"
        return self.O(eng, lambda e, o=out_ap, a=a_ap, s=scalar, b=b_ap, p0=op0, p1=op1:
                      e.scalar_tensor_tensor(out=o, in0=a, scalar=s, in1=b, op0=p0, op1=p1),
                      reads=[a_t, b_t] + list(extra_r), writes=[out_t])

    def cp(self, eng, out_t, out_ap, in_t, in_ap):
        if eng == "act":
            return self.act(out_t, out_ap, in_t, in_ap, AF.Copy)
        return self.O(eng, lambda e, o=out_ap, i=in_ap: e.tensor_copy(out=o, in_=i), reads=[in_t], writes=[out_t])

    def memset(self, eng, t, ap, v):
        return self.O(eng, lambda e, o=ap, x=v: e.memset(o, x), writes=[t])

    def load_w_bf16(self, es, name, dram_ap, K, C, stg, dst=None, dst_koff=0):
        if dst is None:
            dst = self.sb(es, name, [128, K, C], BF16)
        n = 0
        for k in range(K):
            cw = stg[0].t.shape[-1]
            for c0 in range(0, C, cw):
                c1 = min(C, c0 + cw)
                slot = self._stg_i % len(stg)
                st = stg[slot]
                self._stg_i += 1
                self.load(st, st[:, 0:c1 - c0], dram_ap[k * 128:(k + 1) * 128, c0:c1], f"stg{slot}")
                eng = ("pool", "dve", "act")[n % 3]
                n += 1
                self.cp(eng, dst, dst[:, dst_koff + k, c0:c1], st, st[:, 0:c1 - c0])
        return dst

    def bcast_load(self, es, name, dram_vec_ap, C, key="bcl"):
        t = self.sb(es, name, [128, C])
        self.load(t, t[:], dram_vec_ap.partition_broadcast(128), "setup")
        return t

    def rows(self, seq, i):
        off, T = self.seqs[seq]
        return off + i * 128

    def prow(self, seq, i):
        off, T = self.seqs[seq]
        return off + 2 * seq + 1 + i * 128

    def tiles(self, d=0):
        out = []
        for s, (off, T) in enumerate(self.seqs):
            n = T // 128
            rng = range(n) if d == 0 else range(n - 1, -1, -1)
            for j, i in enumerate(rng):
                out.append((s, i, j == 0))
        return out

    def build(self):
        nc = self.nc
        Tp, Ts, L, TT = self.Tp, self.Ts, self.L, self.TT
        di = {}

        def inp(name, shape):
            di[name] = nc.dram_tensor(name, list(shape), F32, kind="ExternalInput").ap()
            return di[name]
        inp("x_prompt", [Tp, D]); inp("x_sample", [Ts, D])
        inp("norm1_g", [L, D]); inp("w_in", [L, D, PIN])
        inp("ret_gn_w", [L, 256]); inp("ret_gn_b", [L, 256])
        inp("rwkv_mu", [L, 1024]); inp("rwkv_w0", [L, 2, 256]); inp("rwkv_w2", [L, 2, 64, 256])
        inp("rwkv_a0", [L, 2, 256]); inp("rwkv_a2", [L, 2, 64, 256]); inp("rwkv_g2", [L, 128, 256])
        inp("rwkv_k_k", [L, 256]); inp("rwkv_k_a", [L, 256]); inp("rwkv_r_k", [L, 256])
        inp("rwkv_gn_w", [L, 256]); inp("rwkv_gn_b", [L, 256])
        inp("s5_lam_re", [L, 2, 32, 64]); inp("s5_lam_im", [L, 2, 32, 64]); inp("s5_log_step", [L, 2, 32])
        inp("s5_b_re", [L, 2, 32, 64, 16]); inp("s5_b_im", [L, 2, 32, 64, 16])
        inp("s5_c_re", [L, 2, 32, 16, 64]); inp("s5_c_im", [L, 2, 32, 16, 64])
        inp("s5_d", [L, 512]); inp("s5_glu_w", [L, 512, 512]); inp("s5_glu_b", [L, 512])
        inp("w_out", [L, D, D]); inp("norm2_g", [L, D])
        inp("ffn_w_gate", [L, D, DFF]); inp("ffn_w_up", [L, D, DFF]); inp("ffn_w_down", [L, DFF, D])
        inp("final_g", [D])
        inp("c_ident", [128, 128]); inp("c_rot", [max(Tp, Ts), 256])
        inp("c_retmask", [128, 512]); inp("c_retqk", [2, 128, 8]); inp("c_retgc", [128, 256])
        inp("c_cs", [2, 128, 128]); inp("c_ones", [128, 128]); inp("c_msi", [2, 128, 256]); inp("c_mst", [2, 128, 128])
        inp("c_j1", [2, 128, 128]); inp("c_z0", [2, 128, 128]); inp("c_swap", [128, 128]); inp("c_sel", [128, 128])
        self.di = di
        yp = nc.dram_tensor("y_prompt", [Tp, D], F32, kind="ExternalOutput").ap()
        ys = nc.dram_tensor("y_sample", [Ts, D], F32, kind="ExternalOutput").ap()
        self.yout = [yp, ys]
        self.XS = nc.dram_tensor("XS", [TT, D], F32, kind="Internal").ap()
        self.PS = nc.dram_tensor("PSC", [TT + 4, PIN], F32, kind="Internal").ap()
        self.YD = nc.dram_tensor("YD", [2, TT, YW], F32, kind="Internal").ap()

        with ExitStack() as ges:
            self.S = Sched(nc, ges)
            self._stg_i = 0
            self.identf = self.sb(ges, "identf", [128, 128])
            self.identb = self.sb(ges, "identb", [128, 128], BF16)
            self.load(self.identf, self.identf[:], di["c_ident"][:, :], "setup")
            self.cp("dve", self.identb, self.identb[:], self.identf, self.identf[:])
            zero = self.sb(ges, "zero", [128, PIN])
            self.memset("pool", zero, zero[:], 0.0)
            self.O("sp", lambda e: e.dma_start(out=self.XS[0:Tp, :], in_=di["x_prompt"][:, :]), writes=["XSall"], dma="setup")
            self.O("sp", lambda e: e.dma_start(out=self.XS[Tp:TT, :], in_=di["x_sample"][:, :]), writes=["XSall"], dma="setup")
            for s, (off, T) in enumerate(self.seqs):
                for r in (off + 2 * s, off + 2 * s + T + 1):
                    self.store(zero, self.PS[r:r + 1, :], zero[0:1, :], "setup", writes=["PSall"])
            self.S.barrier()
            self.S.flush()
            final = []
            for l in range(L):
                self.phase_P(l)
                for d in range(2):
                    if not self.flags.get("no_ret"):
                        self.phase_ret(l, d)
                    if not self.flags.get("no_rwkv"):
                        self.phase_rwkv(l, d)
                    if not self.flags.get("no_s5"):
                        self.phase_s5(l, d)
                self.phase_O1(l)
                final = self.phase_O2(l, last=(l == L - 1))
            self.S.barrier()
            self.S.flush(final=final)
        return nc

    def end_phase(self):
        self.S.barrier()
        self.S.flush()

    def rmsnorm(self, xt, G, h_out, junk, ss, eng2="dve"):
        self.act(junk, junk[:], xt, xt[:], AF.Square, accum=ss[:, 0:1], extra_w=[ss])
        self.ts("dve", ss, ss[:, 1:2], ss, ss[:, 0:1], 1.0 / D, 1e-6, ALU.mult, ALU.add)
        self.act(ss, ss[:, 2:3], ss, ss[:, 1:2], AF.Sqrt)
        self.O("dve", lambda e, o=ss[:, 3:4], i=ss[:, 2:3]: e.reciprocal(out=o, in_=i), reads=[ss], writes=[ss])
        self.stt(eng2, h_out, h_out[:], xt, xt[:], ss[:, 3:4], G, G[:], ALU.mult, ALU.mult, extra_r=[ss])

    def transpose_to(self, src, ncol_tiles, psT, dstT, evac_engs=("act", "dve")):
        n = 0
        for g0 in range(0, ncol_tiles, 4):
            g1 = min(ncol_tiles, g0 + 4)
            pt = psT[(g0 // 4) % len(psT)]
            for c in range(g0, g1):
                self.tr(pt, pt[:, c - g0, :], src, src[:, c * 128:(c + 1) * 128])
            self.cp(evac_engs[n % len(evac_engs)], dstT, dstT[:, g0:g1, :], pt, pt[:, 0:g1 - g0, :])
            n += 1

    def pipeline(self, tiles, stages, counts=None):
        K, N = len(stages), len(tiles)
        counts = counts or [1] * K
        for n in range(N + K - 1):
            gens = []
            for k in range(K - 1, -1, -1):
                j = n - k
                if 0 <= j < N:
                    s, i, first = tiles[j]
                    gens.append([stages[k](j, s, i, first), 0, float(counts[k])])
            while gens:
                g = min(gens, key=lambda x: x[1] / x[2])
                try:
                    next(g[0])
                    g[1] += 1
                except StopIteration:
                    gens.remove(g)

    def phase_P(self, l):
        di = self.di
        self.S.pe_skip = True
        with ExitStack() as es:
            stg = [self.sb(es, f"stg{i}", [128, 1024]) for i in range(3)]
            Wb = self.load_w_bf16(es, "Wb", di["w_in"][l], 8, PIN, stg)
            G1 = self.bcast_load(es, "G1", di["norm1_g"][l], D)
            xb = [self.sb(es, f"xb{i}", [128, D]) for i in range(2)]
            junk = self.sb(es, "junk", [128, D], BF16)
            ss = [self.sb(es, f"ss{i}", [128, 4]) for i in range(2)]
            h = [self.sb(es, f"h{i}", [128, D]) for i in range(2)]
            hT = [self.sb(es, f"hT{i}", [128, 8, 128], BF16) for i in range(2)]
            pt = [self.sb(es, f"pt{i}", [128, PIN]) for i in range(2)]
            psT = [self.ps(es, f"psT{i}", [128, 4, 128]) for i in range(2)]
            psP = [self.ps(es, f"psP{i}", [128, 512]) for i in range(4)]
            self.S.barrier()
            cnt = [0]

            def s0(j, s, i, first):
                b = j % 2
                r0 = self.rows(s, i)
                self.load(xb[b], xb[b][:], self.XS[r0:r0 + 128, :], f"l0{b}", reads=["XSall", ("XS", r0)])
                self.rmsnorm(xb[b], G1, h[b], junk, ss[b])
                yield
                self.transpose_to(h[b], 8, psT, hT[b])
                yield

            def s1(j, s, i, first):
                b = j % 2
                for cb in range(PIN // 512):
                    n = cnt[0]
                    cnt[0] += 1
                    pp = psP[n % 4]
                    for k in range(8):
                        self.mm(pp, pp[:], hT[b], hT[b][:, k, :], Wb, Wb[:, k, cb * 512:(cb + 1) * 512], start=(k == 0), stop=(k == 7))
                    self.cp(("act", "dve")[n % 2], pt[b], pt[b][:, cb * 512:(cb + 1) * 512], pp, pp[:])
                    if cb % 3 == 2:
                        yield
                pr = self.prow(s, i)
                self.store(pt[b], self.PS[pr:pr + 128, :], pt[b][:], f"s0{b}", writes=[("PS", s, i)], eng="act")
                yield
            self.pipeline(self.tiles(), [s0, s1])
            self.end_phase()

    def phase_ret(self, l, d):
        di = self.di
        self.S.pe_skip = not self.flags.get("noskip_ret")
        with ExitStack() as es:
            sb = lambda n, sh, dt=F32: self.sb(es, n, sh, dt)
            mask = sb("rmask", [128, 512]); self.load(mask, mask[:], di["c_retmask"][:, :], "setup")
            qk = sb("rqk", [128, 8]); self.load(qk, qk[:], di["c_retqk"][d], "setup")
            gc = sb("rgc", [128, 256]); self.load(gc, gc[:], di["c_retgc"][:, :], "setup")
            R = sb("R", [128, 256]); Rb = sb("Rb", [128, 256], BF16)
            R2 = range(2)
            pr = [sb(f"pr{i}", [128, 1280]) for i in R2]
            rot = [sb(f"rot{i}", [128, 256]) for i in R2]
            qh = sb("qh", [128, 512]); t1 = sb("rt1", [128, 512])
            kT = [sb(f"rkT{i}", [128, 2, 128], BF16) for i in R2]
            qz = [sb(f"qz{i}", [128, 4, 128], BF16) for i in R2]
            for i in R2:
                self.memset("pool", qz[i], qz[i][:], 0.0)
            sT = sb("sT", [128, 4, 128], BF16)
            vb = [sb(f"vb{i}", [128, 256], BF16) for i in R2]; kdb = [sb(f"kdb{i}", [128, 256], BF16) for i in R2]
            tmp = [sb(f"rtmp{i}", [128, 256]) for i in R2]; yo = [sb(f"ryo{i}", [128, 256]) for i in R2]
            psT = self.ps(es, "rpsT", [128, 4, 128]); psS = self.ps(es, "rpsS", [128, 4, 128])
            psO = self.ps(es, "rpsO", [128, 256]); psC = self.ps(es, "rpsC", [128, 256]); psR = self.ps(es, "rpsR", [128, 256])
            self.S.barrier()
            v4 = lambda ap: ap.rearrange("p (a b) -> p a b", a=4)

            def sL(j, s, i, first):
                b = j % 2
                p0 = self.prow(s, i)
                self.load(pr[b], pr[b][:], self.PS[p0:p0 + 128, 0:1280], f"l0{b}", reads=[("PS", s, i)])
                self.load(rot[b], rot[b][:], di["c_rot"][i * 128:(i + 1) * 128, :], f"l1{b}")
                yield

            def s0(j, s, i, first):
                b = j % 2
                P, RT = pr[b], rot[b]
                for jj, (c0, tb) in enumerate(((0, 0), (512, 128))):
                    o = jj * 256
                    cosb = bc(RT[:, tb:tb + 64].unsqueeze(1), [128, 4, 64])
                    sinb = bc(RT[:, tb + 64:tb + 128].unsqueeze(1), [128, 4, 64])
                    self.tt("dve", qh, v4(qh[:, o:o + 256]), P, v4(P[:, c0:c0 + 256]), RT, cosb, ALU.mult)
                    self.tt("pool", t1, v4(t1[:, o:o + 256]), P, v4(P[:, c0 + 256:c0 + 512]), RT, sinb, ALU.mult)
                    self.tt("dve", qh, qh[:, o:o + 256], qh, qh[:, o:o + 256], t1, t1[:, o:o + 256], ALU.add)
                yield
                for c in range(4):
                    self.tr(psT, psT[:, c, :], qh, qh[:, c * 128:(c + 1) * 128])
                self.cp("act", kT[b], kT[b][:], psT, psT[:, 2:4, :])
                self.cp("act", qz[b], qz[b][0:64, 0:4:2, :], psT, psT[0:64, 0:2, :])
                self.cp("dve", qz[b], qz[b][64:128, 1:4:2, :], psT, psT[64:128, 0:2, :])
                self.cp("act", vb[b], vb[b][:], P, P[:, 1024:1280])
                kv = qh[:, 256:512].rearrange("p (a b) -> p a b", a=4)
                self.tt("dve", kdb[b], v4(kdb[b][:]), qh, kv, qk, bc(qk[:, 4:8].unsqueeze(2), [128, 4, 64]), ALU.mult)
                yield
                if d == 0:
                    for hh in range(4):
                        self.mm(psS, psS[:, hh, :], kT[b], kT[b][:, hh // 2, :], qz[b], qz[b][:, hh, :])
                    self.tt("dve", sT, sT[:], psS, psS[:], mask, v4(mask[:]), ALU.mult)
                    for hh in range(4):
                        self.mm(psO, psO[:, hh * 64:(hh + 1) * 64], sT, sT[:, hh, :], vb[b], vb[b][:, hh * 64:(hh + 1) * 64])
                    self.cp("act", tmp[b], tmp[b][:], psO, psO[:])
                yield

            def s1(j, s, i, first):
                b = j % 2
                if first:
                    self.memset("pool", R, R[:], 0.0)
                    self.memset("pool", Rb, Rb[:], 0.0)
                for hh in range(4):
                    self.mm(psC, psC[:, hh * 64:(hh + 1) * 64], qz[b], qz[b][:, hh, :], Rb,
                            Rb[:, (hh // 2) * 128 + (hh % 2) * 64:(hh // 2) * 128 + (hh % 2) * 64 + 64])
                for pr_ in range(2):
                    self.mm(psR, psR[:, pr_ * 128:(pr_ + 1) * 128], kdb[b], kdb[b][:, pr_ * 128:(pr_ + 1) * 128], vb[b], vb[b][:, pr_ * 128:(pr_ + 1) * 128])
                self.tt("dve", R, R[:], R, R[:], gc, gc[:], ALU.mult)
                self.tt("dve", R, R[:], R, R[:], psR, psR[:], ALU.add)
                self.cp("act", Rb, Rb[:], R, R[:])
                yield
                y = yo[b]
                self.tt("dve", y, v4(y[:]), psC, v4(psC[:]), qk, bc(qk[:, 0:4].unsqueeze(2), [128, 4, 64]), ALU.mult)
                if d == 0:
                    self.tt("pool", y, y[:], y, y[:], tmp[b], tmp[b][:], ALU.add)
                r0 = self.rows(s, i)
                self.store(y, self.YD[d, r0:r0 + 128, 0:256], y[:], f"s0{b}", writes=[("YDr", d, s, i)], eng="pool")
                yield
            self.pipeline(self.tiles(d), [sL, s0, s1], counts=[1, 4, 3])
            self.end_phase()

    def phase_rwkv(self, l, d):
        di = self.di
        self.S.pe_skip = not self.flags.get("noskip_rwkv")
        with ExitStack() as es:
            sb = lambda n, sh, dt=F32: self.sb(es, n, sh, dt)
            v4 = lambda ap: ap.rearrange("p (a b) -> p a b", a=4)
            CS = sb("wCS", [128, 128]); self.load(CS, CS[:], di["c_cs"][d], "setup")
            ON = sb("wON", [128, 128]); self.load(ON, ON[:], di["c_ones"][:, :], "setup")
            MSI = sb("wMSI", [128, 256]); self.load(MSI, MSI[:], di["c_msi"][d], "setup")
            MST = sb("wMST", [128, 128]); self.load(MST, MST[:], di["c_mst"][d], "setup")
            MU = self.bcast_load(es, "wMU", di["rwkv_mu"][l], 1024)
            W0 = self.bcast_load(es, "wW0", di["rwkv_w0"][l, d], 256)
            A0 = self.bcast_load(es, "wA0", di["rwkv_a0"][l, d], 256)
            KK = self.bcast_load(es, "wKK", di["rwkv_k_k"][l], 256)
            KA = self.bcast_load(es, "wKA", di["rwkv_k_a"][l], 256)
            RK = self.bcast_load(es, "wRK", di["rwkv_r_k"][l], 256)
            stg = sb("wstg", [128, 256])
            LW = sb("wLW", [128, 256], BF16)
            self.load(stg, stg[0:64, :], di["rwkv_w2"][l, d], "setup")
            self.load(stg, stg[64:128, :], di["rwkv_a2"][l, d], "setup")
            stg2 = sb("wstg2", [128, 256])
            G2 = sb("wG2", [128, 256], BF16)
            self.load(stg2, stg2[:], di["rwkv_g2"][l], "setup")
            self.S.barrier()
            self.cp("dve", LW, LW[:], stg, stg[:])
            self.cp("dve", G2, G2[:], stg2, stg2[:])
            R2 = range(2); R3 = range(3)
            pm = [sb(f"wpm{i}", [128, 1024]) for i in R2]
            pc = [sb(f"wpc{i}", [128, 1024]) for i in R2]
            pn = [sb(f"wpn{i}", [128, 1024]) for i in R2]
            pp = sb("wpp", [128, 1024]); sh = sb("wsh", [128, 1024])
            ldT = sb("wldT", [128, 2, 128], BF16)
            targ = sb("wtarg", [128, 512]); sg = sb("wsg", [128, 256]); asg = sb("wasg", [128, 256])
            kk = sb("wkk", [128, 256]); kkr = sb("wkkr", [128, 256]); junk = sb("wjunk", [128, 64])
            ssq = sb("wssq", [128, 16]); kd = sb("wkd", [128, 256]); bv = sb("wbv", [128, 256])
            t1 = sb("wt1", [128, 256])
            yo = [sb(f"wyo{i}", [128, 768]) for i in R3]
            lws = sb("wlws", [128, 256]); lx = sb("wlx", [128, 256]); lt = sb("wlt", [128, 256])
            Wt = sb("wWt", [128, 256]); Wn = sb("wWn", [128, 256]); Wx = sb("wWx", [128, 256]); Wh = sb("wWh", [128, 256])
            X4 = sb("wX4", [128, 1024])
            Atb = [sb(f"wAtb{i}", [128, 256], BF16) for i in R2]
            Bhb = [sb(f"wBhb{i}", [128, 256], BF16) for i in R3]; Khb = [sb(f"wKhb{i}", [128, 256], BF16) for i in R3]
            vb = [sb(f"wvb{i}", [128, 256], BF16) for i in R3]
            XT = [sb(f"wXT{i}", [128, 2, 4, 128], BF16) for i in R3]
            WCk = [sb(f"wWCk{i}", [128, 2]) for i in R3]
            Nb = [sb(f"wNb{i}", [128, 4, 128], BF16) for i in R2]; NTb = [sb(f"wNTb{i}", [128, 4, 128], BF16) for i in R2]
            Pb = [sb(f"wPb{i}", [128, 4, 128], BF16) for i in R2]
            PTb = [sb(f"wPTb{i}", [128, 4, 128], BF16) for i in R2]
            Mf = [sb(f"wMf{i}", [128, 4, 128]) for i in R2]; Mb = [sb(f"wMb{i}", [128, 4, 128], BF16) for i in R2]
            Arb = [sb(f"wArb{i}", [128, 4, 128], BF16) for i in R3]; Ark = [sb(f"wArk{i}", [128, 4, 128], BF16) for i in R3]
            Aak = [sb(f"wAak{i}", [128, 4, 128], BF16) for i in R2]
            Xb = sb("wXb", [128, 256], BF16); Ub = sb("wUb", [128, 256], BF16)
            Ut = [sb(f"wUt{i}", [128, 256]) for i in R2]
            WtT = [sb(f"wWtT{i}", [128, 4, 128], BF16) for i in R2]
            H = sb("wH", [128, 256]); Hb = sb("wHb", [128, 256], BF16)
            identq = sb("widq", [128, 4, 128])
            for hh in range(4):
                self.cp("pool", identq, identq[:, hh, :], self.identf, self.identf[:])
            pA = [self.ps(es, f"wpA{i}", [128, 512]) for i in range(4)]
            pB = [self.ps(es, f"wpB{i}", [128, 512]) for i in range(2)]
            pC = [self.ps(es, f"wpC{i}", [128, 512]) for i in range(2)]
            hp = lambda hh: slice((hh % 2) * 64, (hh % 2) * 64 + 64)
            hcol = lambda hh: slice(hh * 64, (hh + 1) * 64)
            msb = bc(MSI[:, 0:128].unsqueeze(1), [128, 4, 128])
            mib = bc(MSI[:, 128:256].unsqueeze(1), [128, 4, 128])
            mtb = bc(MST[:].unsqueeze(1), [128, 4, 128])

            def sA(j, s, i, first):
                b = j % 2; c = j % 3
                p0 = self.prow(s, i)
                self.load(pm[b], pm[b][:], self.PS[p0 - 1:p0 + 127, 1536:2560], f"l0{b}", reads=[("PS", s, i), ("PS", s, i - 1), "PSall"])
                self.load(pc[b], pc[b][:], self.PS[p0:p0 + 128, 1536:2560], f"l1{b}", reads=[("PS", s, i)])
                self.load(pn[b], pn[b][:], self.PS[p0 + 1:p0 + 129, 1536:2560], f"l2{b}", reads=[("PS", s, i), ("PS", s, i + 1), "PSall"])
                self.tt("pool", sh, sh[:], pm[b], pm[b][:], pn[b], pn[b][:], ALU.add)
                self.stt("dve", sh, sh[:], sh, sh[:], 0.5, pc[b], pc[b][:], ALU.mult, ALU.subtract)
                self.tt("pool", sh, sh[:], sh, sh[:], MU, MU[:], ALU.mult)
                self.tt("dve", pp, pp[:], pc[b], pc[b][:], sh, sh[:], ALU.add)
                yield
                r_, k_, v_ = pp[:, 0:256], pp[:, 256:512], pp[:, 512:768]
                Y = yo[c]
                self.tr(pA[0], pA[0][:, 0:128], pp, pp[:, 768:896])
                self.tr(pA[0], pA[0][:, 128:256], pp, pp[:, 896:1024])
                self.act(ldT, ldT[0:64, 0, :], pA[0], pA[0][0:64, 0:128], AF.Tanh)
                self.cp("dve", ldT, ldT[64:128, 0, :], pA[0], pA[0][64:128, 0:128])
                self.mm(pA[1], pA[1][:, 0:256], ldT, ldT[0:64, 0, :], LW, LW[0:64, :])
                self.mm(pA[1], pA[1][:, 256:512], ldT, ldT[64:128, 0, :], LW, LW[64:128, :])
                self.tt("dve", targ, targ[:, 0:256], pA[1], pA[1][:, 0:256], W0, W0[:], ALU.add)
                self.tt("dve", targ, targ[:, 256:512], pA[1], pA[1][:, 256:512], A0, A0[:], ALU.add)
                self.act(sg, sg[:], targ, targ[:, 0:256], AF.Sigmoid)
                self.act(asg, asg[:], targ, targ[:, 256:512], AF.Sigmoid)
                if d == 0:
                    self.act(ldT, ldT[:, 1, :], pA[0], pA[0][:, 128:256], AF.Sigmoid)
                    self.mm(pA[2], pA[2][:, 0:256], ldT, ldT[:, 1, :], G2, G2[:])
                    self.cp("act", Y, Y[:, 512:768], pA[2], pA[2][:, 0:256])
                yield
                self.tt("pool", kkr, kkr[:], pp, k_, KK, KK[:], ALU.mult)
                for hh in range(4):
                    self.act(junk, junk[:], kkr, kkr[:, hh * 64:(hh + 1) * 64], AF.Square, accum=ssq[:, hh:hh + 1], extra_w=[ssq])
                self.act(ssq, ssq[:, 4:8], ssq, ssq[:, 0:4], AF.Sqrt)
                self.ts("dve", ssq, ssq[:, 8:12], ssq, ssq[:, 4:8], 1e-12, None, ALU.max)
                self.O("dve", lambda e, o=ssq[:, 12:16], i_=ssq[:, 8:12]: e.reciprocal(out=o, in_=i_), reads=[ssq], writes=[ssq])
                self.tt("dve", kk, v4(kk[:]), kkr, v4(kkr[:]), ssq, bc(ssq[:, 12:16].unsqueeze(2), [128, 4, 64]), ALU.mult)
                self.stt("dve", t1, t1[:], asg, asg[:], -1.0, KA, KA[:], ALU.add, ALU.mult)
                self.stt("dve", kd, kd[:], t1, t1[:], 1.0, pp, k_, ALU.add, ALU.mult)
                self.tt("pool", bv, bv[:], kk, kk[:], asg, asg[:], ALU.mult)
                yield
                self.tt("dve", t1, t1[:], pp, r_, kd, kd[:], ALU.mult)
                self.tt("pool", t1, t1[:], t1, t1[:], RK, RK[:], ALU.mult)
                self.O("dve", lambda e, o=ssq[:, 0:4], i_=v4(t1[:]): e.tensor_reduce(out=o, in_=i_, axis=AX.X, op=ALU.add), reads=[t1], writes=[ssq])
                self.tt("dve", Y, v4(Y[:, 256:512]), pp, v4(v_), ssq, bc(ssq[:, 0:4].unsqueeze(2), [128, 4, 64]), ALU.mult)
                self.cp("act", vb[c], vb[c][:], pp, v_)
                self.mm(pA[2], pA[2][:, 256:512], CS, CS[:], sg, sg[:])
                self.mm(pA[3], pA[3][:, 0:256], ON, ON[:], sg, sg[:])
                for pr_ in range(2):
                    self.mm(pA[3], pA[3][:, 256 + pr_:257 + pr_], sg, sg[:, pr_ * 128:(pr_ + 1) * 128], ON, ON[:, 0:1])
                self.cp("act", lws, lws[:], pA[2], pA[2][:, 256:512])
                self.tt("dve", lx, lx[:], lws, lws[:], sg, sg[:], ALU.subtract)
                self.tt("dve", lt, lt[:], pA[3], pA[3][:, 0:256], lws, lws[:], ALU.subtract)
                self.act(Wt, Wt[:], lws, lws[:], AF.Exp, scale=CDEC)
                self.act(Wn, Wn[:], lws, lws[:], AF.Exp, scale=-CDEC)
                self.act(Wx, Wx[:], lx, lx[:], AF.Exp, scale=CDEC)
                self.act(Wh, Wh[:], lt, lt[:], AF.Exp, scale=CDEC)
                self.act(WCk[c], WCk[c][:], pA[3], pA[3][:, 256:258], AF.Exp, scale=CDEC)
                yield
                self.stt("dve", X4, X4[:, 0:256], kk, kk[:], -1.0, Wx, Wx[:], ALU.mult, ALU.mult)
                self.tt("pool", X4, X4[:, 256:512], pp, r_, Wt, Wt[:], ALU.mult)
                self.tt("dve", X4, X4[:, 512:768], bv, bv[:], Wn, Wn[:], ALU.mult)
                self.tt("pool", X4, X4[:, 768:1024], kd, kd[:], Wn, Wn[:], ALU.mult)
                self.cp("act", Atb[b], Atb[b][:], X4, X4[:, 0:256])
                self.tt("dve", Bhb[c], Bhb[c][:], bv, bv[:], Wh, Wh[:], ALU.mult)
                self.tt("pool", Khb[c], Khb[c][:], kd, kd[:], Wh, Wh[:], ALU.mult)
                yield
                X = XT[c]
                for kind in range(4):
                    pt_ = pA[0] if kind < 2 else pA[1]
                    for pr_ in range(2):
                        c0 = kind * 256 + pr_ * 128
                        self.tr(pt_, pt_[:, ((kind % 2) * 2 + pr_) * 128:((kind % 2) * 2 + pr_ + 1) * 128], X4, X4[:, c0:c0 + 128])
                for kind in range(4):
                    pt_ = pA[0] if kind < 2 else pA[1]
                    self.cp(("act", "dve")[kind % 2], X, X[:, :, kind, :],
                            pt_, pt_[:, (kind % 2) * 256:(kind % 2) * 256 + 256].rearrange("p (a b) -> p a b", a=2))
                yield
                def amat(ps_, lk, rk):
                    for hh in range(4):
                        pq = hh // 2
                        self.mm(ps_, ps_[:, hh * 128:(hh + 1) * 128], X, X[hp(hh), pq, lk, :], X, X[hp(hh), pq, rk, :])
                amat(pA[2], 2, 0)
                self.tt("dve", Nb[b], Nb[b][:], pA[2], v4(pA[2][:]), MSI, msb, ALU.mult)
                self.tt("dve", Mf[b], Mf[b][:], pA[2], v4(pA[2][:]), MSI, msb, ALU.mult)
                amat(pA[3], 2, 1)
                self.tt("dve", Arb[c], Arb[c][:], pA[3], v4(pA[3][:]), MSI, mib, ALU.mult)
                yield
                amat(pA[0], 3, 0)
                self.tt("dve", Aak[b], Aak[b][:], pA[0], v4(pA[0][:]), MSI, msb, ALU.mult)
                amat(pA[1], 3, 1)
                self.tt("dve", Ark[c], Ark[c][:], pA[1], v4(pA[1][:]), MSI, mib, ALU.mult)
                amat(pA[2], 0, 2)
                self.tt("dve", NTb[b], NTb[b][:], pA[2], v4(pA[2][:]), MST, mtb, ALU.mult)
                self.tt("pool", Mf[b], Mf[b][:], Mf[b], Mf[b][:], identq, identq[:], ALU.add)
                self.cp("act", Mb[b], Mb[b][:], Mf[b], Mf[b][:])
                yield

            def sB(j, s, i, first):
                b = j % 2; c = j % 3
                MF, MB = Mf[b], Mb[b]
                Pc, PTc = Nb[b], NTb[b]
                for st in range(6):
                    Pn, PTn = Pb[st % 2], PTb[st % 2]
                    for hh in range(4):
                        if st < 5:
                            self.mm(pB[0], pB[0][:, hh * 128:(hh + 1) * 128], PTc, PTc[:, hh, :], Pc, Pc[:, hh, :])
                        self.mm(pB[1], pB[1][:, hh * 128:(hh + 1) * 128], Pc, Pc[:, hh, :], PTc, PTc[:, hh, :])
                    if st < 5:
                        self.cp("act", Pn, Pn[:], pB[0], v4(pB[0][:]))
                    self.cp("dve", PTn, PTn[:], pB[1], v4(pB[1][:]))
                    yield
                    for hh in range(4):
                        self.mm(pB[0], pB[0][:, hh * 128:(hh + 1) * 128], PTn, PTn[:, hh, :], MB, MB[:, hh, :])
                    self.tt("dve", MF, MF[:], MF, MF[:], pB[0], v4(pB[0][:]), ALU.add)
                    self.cp("act", MB, MB[:], MF, MF[:])
                    yield
                    Pc, PTc = Pn, PTn
                V = vb[c]
                for hh in range(4):
                    self.mm(pB[1], pB[1][:, hh * 64:(hh + 1) * 64], Aak[b], Aak[b][:, hh, :], V, V[:, hh * 64:(hh + 1) * 64])
                self.cp("act", Xb, Xb[:], pB[1], pB[1][:, 0:256])
                for hh in range(4):
                    self.mm(pB[0], pB[0][:, hh * 64:(hh + 1) * 64], MB, MB[:, hh, :], Xb, Xb[:, hh * 64:(hh + 1) * 64])
                self.cp("act", Ut[b], Ut[b][:], pB[0], pB[0][:, 0:256])
                yield
                for hh in range(4):
                    self.mm(pB[1], pB[1][:, hh * 128:(hh + 1) * 128], Atb[b], Atb[b][:, (hh // 2) * 128:(hh // 2) * 128 + 128], MB, MB[:, hh, :])
                self.cp("dve", WtT[b], WtT[b][:], pB[1], v4(pB[1][:]))
                yield

            def sC(j, s, i, first):
                b = j % 2; c = j % 3
                if first:
                    self.memset("pool", H, H[:], 0.0)
                    self.memset("pool", Hb, Hb[:], 0.0)
                X, V, Y, UT = XT[c], vb[c], yo[c], Ut[b]
                for hh in range(4):
                    self.mm(pC[0], pC[0][:, hcol(hh)], WtT[b], WtT[b][hp(hh), hh, :], Hb, Hb[hp(hh), hcol(hh)])
                self.tt("dve", UT, UT[:], UT, UT[:], pC[0], pC[0][:, 0:256], ALU.add)
                self.cp("act", Ub, Ub[:], UT, UT[:])
                yield
                for hh in range(4):
                    pq = hh // 2
                    oc = slice(256 + hh * 64, 256 + (hh + 1) * 64)
                    self.mm(pC[0], pC[0][:, oc], X, X[hp(hh), pq, 1, :], Hb, Hb[hp(hh), hcol(hh)], start=True, stop=False)
                    self.mm(pC[0], pC[0][:, oc], Arb[c], Arb[c][:, hh, :], Ub, Ub[:, hcol(hh)], start=False, stop=False)
                    self.mm(pC[0], pC[0][:, oc], Ark[c], Ark[c][:, hh, :], V, V[:, hcol(hh)], start=False, stop=True)
                for hh in range(4):
                    pq = hh // 2
                    self.mm(pC[1], pC[1][:, hcol(hh)], Bhb[c], Bhb[c][:, pq * 128:(pq + 1) * 128], Ub, Ub[:, hcol(hh)], start=True, stop=False)
                    self.mm(pC[1], pC[1][:, hcol(hh)], Khb[c], Khb[c][:, pq * 128:(pq + 1) * 128], V, V[:, hcol(hh)], start=False, stop=True)
                for pq in range(2):
                    self.ts("dve", H, H[:, pq * 128:(pq + 1) * 128], H, H[:, pq * 128:(pq + 1) * 128], WCk[c][:, pq:pq + 1], None, ALU.mult, extra_r=[WCk[c]])
                self.tt("dve", H, H[:], H, H[:], pC[1], pC[1][:, 0:256], ALU.add)
                self.cp("act", Hb, Hb[:], H, H[:])
                yield
                self.cp("act", Y, Y[:, 0:256], pC[0], pC[0][:, 256:512])
                r0 = self.rows(s, i)
                ncol = 768 if d == 0 else 512
                self.store(Y, self.YD[d, r0:r0 + 128, 256:256 + ncol], Y[:, 0:ncol], f"s0{c}", writes=[("YDw", d, s, i)])
                yield
            self.pipeline(self.tiles(d), [sA, sB, sC], counts=[9, 15, 4])
            self.end_phase()

    def phase_s5(self, l, d):
        di = self.di
        self.S.pe_skip = True
        PI = math.pi
        with ExitStack() as es:
            sb = lambda n, sh, dt=F32: self.sb(es, n, sh, dt)
            COS = sb("sCOS", [128, 32, 128]); SIN = sb("sSIN", [128, 32, 128]); RHO = sb("sRHO", [128, 32, 128])
            BP1 = sb("sBP1", [128, 32, 128], BF16); BP2 = sb("sBP2", [128, 32, 128], BF16)
            W1 = sb("sW1", [128, 512], BF16); W2 = sb("sW2", [128, 512], BF16)
            RHOF = sb("sRHOF", [128, 32]); CL = sb("sCL", [128, 32]); SL = sb("sSL", [128, 32])
            SWAP = sb("sSWAP", [128, 128]); self.load(SWAP, SWAP[:], di["c_swap"][:, :], "p0")
            psA2 = [self.ps(es, f"spsA{i}", [128, 512]) for i in range(2)]; psB2 = [self.ps(es, f"spsB{i}", [128, 512]) for i in range(2)]
            psA = psA2[0]
            psY = self.ps(es, "spsY", [128, 512]); psT = self.ps(es, "spsT", [128, 4, 128]); psX = self.ps(es, "spsX", [128, 512])
            with ExitStack() as es2:
                sb2 = lambda n, sh, dt=F32: self.sb(es2, n, sh, dt)
                J1 = sb2("sJ1", [128, 128]); self.load(J1, J1[:], di["c_j1"][d], "p1")
                Z0 = sb2("sZ0", [128, 128]); self.load(Z0, Z0[:], di["c_z0"][d], "p2")
                SEL = sb2("sSEL", [128, 128]); self.load(SEL, SEL[:], di["c_sel"][:, :], "p3")
                SELb = sb2("sSELb", [128, 128], BF16); self.cp("dve", SELb, SELb[:], SEL, SEL[:])
                negpi = sb2("snegpi", [128, 1]); self.memset("pool", negpi, negpi[:], -PI)
                lre = sb2("slre", [128, 32]); lim = sb2("slim", [128, 32]); stp = sb2("sstp", [128, 32])
                lam_re_T = di["s5_lam_re"][l, d].rearrange("g p -> p g")
                lam_im_T = di["s5_lam_im"][l, d].rearrange("g p -> p g")
                for hf in range(2):
                    self.O("sp", lambda e, o=lre[hf * 64:(hf + 1) * 64, :], i_=lam_re_T: e.dma_start(out=o, in_=i_, allow_slow_non_contiguous=True), writes=[lre], dma="p4")
                    self.O("sp", lambda e, o=lim[hf * 64:(hf + 1) * 64, :], i_=lam_im_T: e.dma_start(out=o, in_=i_, allow_slow_non_contiguous=True), writes=[lim], dma="p5")
                self.load(stp, stp[:], di["s5_log_step"][l, d].partition_broadcast(128), "p6")
                self.act(stp, stp[:], stp, stp[:], AF.Exp)
                self.ts("dve", lre, lre[:], lre, lre[:], -1e-4, None, ALU.min)
                al = sb2("sal", [128, 32]); th = sb2("sth", [128, 32])
                self.tt("dve", al, al[:], lre, lre[:], stp, stp[:], ALU.mult)
                self.tt("dve", th, th[:], lim, lim[:], stp, stp[:], ALU.mult)
                self.act(RHOF, RHOF[:], al, al[:], AF.Exp)
                arg = sb2("sarg", [128, 32, 128]); arg2 = sb2("sarg2", [128, 32, 128])
                argi = sb2("sargi", [128, 32, 128], mybir.dt.int32); argf = sb2("sargf", [128, 32, 128])

                def sinred(o_t, o_ap, a_t, a_ap, w_ap, wi_ap, wf_ap, off):
                    self.ts("dve", arg2, w_ap, a_t, a_ap, 1.0 / (2 * PI), off, ALU.mult, ALU.add)
                    self.cp("dve", argi, wi_ap, arg2, w_ap)
                    self.cp("dve", argf, wf_ap, argi, wi_ap)
                    self.tt("dve", arg2, w_ap, arg2, w_ap, argf, wf_ap, ALU.subtract)
                    self.act(o_t, o_ap, arg2, w_ap, AF.Sin, scale=2 * PI)
                self.tt("dve", arg, arg[:], th, bc(th[:].unsqueeze(2), [128, 32, 128]), J1, bc(J1[:].unsqueeze(1), [128, 32, 128]), ALU.mult)
                sinred(SIN, SIN[:], arg, arg[:], arg2[:], argi[:], argf[:], 0.0)
                sinred(COS, COS[:], arg, arg[:], arg2[:], argi[:], argf[:], 0.25)
                self.tt("dve", RHO, RHO[:], RHOF, bc(RHOF[:].unsqueeze(2), [128, 32, 128]), Z0, bc(Z0[:].unsqueeze(1), [128, 32, 128]), ALU.mult)
                jl = 127 if d == 0 else 0
                self.cp("dve", CL, CL[:], COS, COS[:, :, jl])
                self.cp("dve", SL, SL[:], SIN, SIN[:, :, jl])
                lre2 = sb2("slre2", [128, 16]); lim2 = sb2("slim2", [128, 16]); stp2 = sb2("sstp2", [128, 16])
                self.load(lre2, lre2[:], di["s5_lam_re"][l, d].rearrange("(gp go) p -> (go p) gp", go=2), "p7")
                self.load(lim2, lim2[:], di["s5_lam_im"][l, d].rearrange("(gp go) p -> (go p) gp", go=2), "p8")
                ls2 = di["s5_log_step"][l, d].rearrange("(gp go) -> go gp", go=2)
                for go in range(2):
                    self.load(stp2, stp2[go * 64:(go + 1) * 64, :], ls2[go].partition_broadcast(64), "p9")
                self.act(stp2, stp2[:], stp2, stp2[:], AF.Exp)
                self.ts("dve", lre2, lre2[:], lre2, lre2[:], -1e-4, None, ALU.min)
                pw = sb2("spw", [128, 16 * 12])
                c = lambda k: pw[:, k * 16:(k + 1) * 16]
                TTm = lambda o, a, b_, op: self.tt("dve", pw, o, pw, a, pw, b_, op)
                self.tt("dve", pw, c(0), lre2, lre2[:], stp2, stp2[:], ALU.mult)
                self.tt("dve", pw, c(1), lim2, lim2[:], stp2, stp2[:], ALU.mult)
                self.act(pw, c(2), pw, c(0), AF.Exp)
                sinred(pw, c(4), pw, c(1), arg2[:, 0, 0:16], argi[:, 0, 0:16], argf[:, 0, 0:16], 0.0)
                sinred(pw, c(5), pw, c(1), arg2[:, 0, 0:16], argi[:, 0, 0:16], argf[:, 0, 0:16], 0.25)
                TTm(c(5), c(5), c(2), ALU.mult)
                TTm(c(4), c(4), c(2), ALU.mult)
                self.ts("dve", pw, c(5), pw, c(5), -1.0, None, ALU.add)
                self.tt("dve", pw, c(6), lre2, lre2[:], lre2, lre2[:], ALU.mult)
                self.tt("dve", pw, c(7), lim2, lim2[:], lim2, lim2[:], ALU.mult)
                TTm(c(6), c(6), c(7), ALU.add)
                self.O("dve", lambda e, o=c(6), i_=c(6): e.reciprocal(out=o, in_=i_), reads=[pw], writes=[pw])
                self.tt("dve", pw, c(7), pw, c(5), lre2, lre2[:], ALU.mult)
                self.tt("dve", pw, c(8), pw, c(4), lim2, lim2[:], ALU.mult)
                TTm(c(7), c(7), c(8), ALU.add)
                TTm(c(7), c(7), c(6), ALU.mult)
                self.tt("dve", pw, c(8), pw, c(4), lre2, lre2[:], ALU.mult)
                self.tt("dve", pw, c(9), pw, c(5), lim2, lim2[:], ALU.mult)
                TTm(c(8), c(8), c(9), ALU.subtract)
                TTm(c(8), c(8), c(6), ALU.mult)
                Bre = sb2("sBre", [128, 16, 16]); Bim = sb2("sBim", [128, 16, 16])
                self.load(Bre, Bre[:], di["s5_b_re"][l, d].rearrange("(gp go) p c -> (go p) gp c", go=2), "p10")
                self.load(Bim, Bim[:], di["s5_b_im"][l, d].rearrange("(gp go) p c -> (go p) gp c", go=2), "p11")
                bbr = sb2("sbbr", [128, 16, 16]); bbi = sb2("sbbi", [128, 16, 16]); tq = sb2("stq", [128, 16, 16])
                cre = bc(c(7).unsqueeze(2), [128, 16, 16]); cim = bc(c(8).unsqueeze(2), [128, 16, 16])
                self.tt("dve", bbr, bbr[:], Bre, Bre[:], pw, cre, ALU.mult)
                self.tt("dve", tq, tq[:], Bim, Bim[:], pw, cim, ALU.mult)
                self.tt("dve", bbr, bbr[:], bbr, bbr[:], tq, tq[:], ALU.subtract)
                self.tt("dve", bbi, bbi[:], Bim, Bim[:], pw, cre, ALU.mult)
                self.tt("dve", tq, tq[:], Bre, Bre[:], pw, cim, ALU.mult)
                self.tt("dve", bbi, bbi[:], bbi, bbi[:], tq, tq[:], ALU.add)
                XPr = sb2("sXPr", [128, 8, 128], BF16); XPi = sb2("sXPi", [128, 8, 128], BF16)
                self.memset("pool", XPr, XPr[:], 0.0); self.memset("pool", XPi, XPi[:], 0.0)
                for g in range(32):
                    gp, go, gi = g // 2, g % 2, g % 8
                    self.cp("dve", XPr, XPr[:, gi, gi * 16:(gi + 1) * 16], bbr, bbr[:, gp, :])
                    self.cp("pool", XPi, XPi[:, gi, gi * 16:(gi + 1) * 16], bbi, bbi[:, gp, :])
                    selg = SELb[:, go * 64:(go + 1) * 64]
                    self.mm(psA, psA[:, 0:64], XPr, XPr[:, gi, :], SELb, selg)
                    self.mm(psA, psA[:, 64:128], XPi, XPi[:, gi, :], SELb, selg)
                    self.cp("act", BP1, BP1[:, g, :], psA, psA[:, 0:128])
                    self.act(BP2, BP2[:, g, 0:64], psA, psA[:, 64:128], AF.Copy, scale=-1.0)
                    self.cp("dve", BP2, BP2[:, g, 64:128], psA, psA[:, 0:64])
                Ca = sb2("sCa", [128, 4, 128]); Cb = sb2("sCb", [128, 4, 128])
                cre_d = di["s5_c_re"][l, d].rearrange("(a gi) c p -> (gi c) a p", gi=8)
                cim_d = di["s5_c_im"][l, d].rearrange("(a gi) c p -> (gi c) a p", gi=8)
                self.load(Ca, Ca[:, :, 0:64], cre_d, "p12"); self.load(Ca, Ca[:, :, 64:128], cim_d, "p13")
                self.load(Cb, Cb[:, :, 0:64], cim_d, "p14"); self.load(Cb, Cb[:, :, 64:128], cre_d, "p15")
                for a in range(4):
                    self.tr(psT, psT[:, a, :], Ca, Ca[:, a, :])
                self.cp("act", W1, W1[0:64, :], psT, psT[0:64, :, :].rearrange("p a b -> p (a b)"))
                self.act(W1, W1[64:128, :], psT, psT[64:128, :, :].rearrange("p a b -> p (a b)"), AF.Copy, scale=-1.0)
                for a in range(4):
                    self.tr(psT, psT[:, a, :], Cb, Cb[:, a, :])
                self.act(W2, W2[:], psT, psT[:].rearrange("p a b -> p (a b)"), AF.Copy, scale=-1.0)
                self.S.barrier()
                self.S.flush()
            u = [sb(f"su{i}", [128, 512]) for i in range(2)]
            uT = sb("suT", [128, 4, 128], BF16)
            t1 = [sb(f"st1{i}", [128, 512]) for i in range(2)]; t2 = [sb(f"st2{i}", [128, 512]) for i in range(2)]
            zin = [sb(f"szin{i}", [128, 32, 128]) for i in range(2)]; z = sb("sz", [128, 32, 128])
            ZC = sb("sZC", [128, 32, 128], BF16); ZS = sb("sZS", [128, 32, 128], BF16)
            xst = sb("sxst", [128, 32]); xc = sb("sxc", [128, 32]); xs_ = sb("sxs", [128, 32]); tm = sb("stm", [128, 32])
            yo = [sb(f"syo{i}", [128, 512]) for i in range(2)]
            j0 = 0 if d == 0 else 127
            fl = lambda ap: ap.rearrange("p a b -> p (a b)")

            def s0(j, s, i, first):
                b = j % 2
                p0 = self.prow(s, i)
                self.load(u[b], u[b][:], self.PS[p0:p0 + 128, 2560:3072], f"l0{b}", reads=[("PS", s, i)])
                for ct in range(4):
                    self.tr(psT, psT[:, ct, :], u[b], u[b][:, ct * 128:(ct + 1) * 128])
                self.cp("act", uT, uT[:], psT, psT[:])
                yield
                for oc in range(8):
                    ct = oc // 2
                    pa, pb_ = psA2[oc % 2], psB2[oc % 2]
                    for gi in range(4):
                        g = oc * 4 + gi
                        self.mm(pa, pa[:, gi * 128:(gi + 1) * 128], BP1, BP1[:, g, :], uT, uT[:, ct, :])
                        self.mm(pb_, pb_[:, gi * 128:(gi + 1) * 128], BP2, BP2[:, g, :], uT, uT[:, ct, :])
                    gs = slice(oc * 4, oc * 4 + 4)
                    self.tt("dve", t1[oc % 2], t1[oc % 2][:], pa, pa[:], COS, fl(COS[:, gs, :]), ALU.mult)
                    self.tt("dve", t2[oc % 2], t2[oc % 2][:], pb_, pb_[:], SIN, fl(SIN[:, gs, :]), ALU.mult)
                    self.tt("pool", zin[b], fl(zin[b][:, gs, :]), t1[oc % 2], t1[oc % 2][:], t2[oc % 2], t2[oc % 2][:], ALU.subtract)
                    if oc % 2 == 1:
                        yield

            def s1(j, s, i, first):
                b = j % 2
                Z = zin[b]
                if first:
                    self.memset("pool", xst, xst[:], 0.0)
                self.tt("dve", tm, tm[:], RHOF, RHOF[:], xst, xst[:], ALU.mult)
                self.tt("dve", Z, Z[:, :, j0], Z, Z[:, :, j0], tm, tm[:], ALU.add)
                zf = z[:].rearrange("p a b -> p (a b)"); zif = Z[:].rearrange("p a b -> p (a b)"); rf = RHO[:].rearrange("p a b -> p (a b)")
                if d == 1:
                    zf, zif, rf = zf[:, ::-1], zif[:, ::-1], rf[:, ::-1]
                self.O("dve", lambda e, o=zf, a=rf, b_=zif: e.tensor_tensor_scan(out=o, data0=a, data1=b_, initial=0.0, op0=ALU.mult, op1=ALU.add),
                       reads=[RHO, Z], writes=[z])
                yield
                jl = 127 - j0
                self.tt("dve", xc, xc[:], z, z[:, :, jl], CL, CL[:], ALU.mult)
                self.tt("dve", xs_, xs_[:], z, z[:, :, jl], SL, SL[:], ALU.mult)
                self.mm(psX, psX[:, 0:32], SWAP, SWAP[:], xs_, xs_[:])
                self.tt("dve", xst, xst[:], xc, xc[:], psX, psX[:, 0:32], ALU.add)
                yield
                for hf in range(2):
                    gs = slice(hf * 16, hf * 16 + 16)
                    self.tt("dve", ZC, ZC[:, gs, :], z, z[:, gs, :], COS, COS[:, gs, :], ALU.mult)
                    self.tt("pool" if hf == 0 else "dve", ZS, ZS[:, gs, :], z, z[:, gs, :], SIN, SIN[:, gs, :], ALU.mult)
                    yield
                for g in range(32):
                    self.mm(psY, psY[:, g * 16:(g + 1) * 16], ZC, ZC[:, g, :], W1, W1[:, g * 16:(g + 1) * 16], start=True, stop=False)
                    self.mm(psY, psY[:, g * 16:(g + 1) * 16], ZS, ZS[:, g, :], W2, W2[:, g * 16:(g + 1) * 16], start=False, stop=True)
                    if g % 8 == 7:
                        yield
                self.cp("act", yo[b], yo[b][:], psY, psY[:])
                r0 = self.rows(s, i)
                self.store(yo[b], self.YD[d, r0:r0 + 128, 1024:1536], yo[b][:], f"s0{b}", writes=[("YDs", d, s, i)])
                yield
            self.pipeline(self.tiles(d), [s0, s1], counts=[6, 10])
            self.end_phase()

    def head_ln(self, y, eps, gw, gb, wk, st):
        v4 = lambda ap: ap.rearrange("p (a b) -> p a b", a=4)
        self.O("dve", lambda e, o=st[:, 0:4], i_=v4(y[:]): e.tensor_reduce(out=o, in_=i_, axis=AX.X, op=ALU.add), reads=[y], writes=[st])
        self.ts("dve", st, st[:, 0:4], st, st[:, 0:4], -1.0 / 64, None, ALU.mult)
        self.tt("dve", y, v4(y[:]), y, v4(y[:]), st, bc(st[:, 0:4].unsqueeze(2), [128, 4, 64]), ALU.add)
        self.tt("pool", wk, wk[:], y, y[:], y, y[:], ALU.mult)
        self.O("dve", lambda e, o=st[:, 4:8], i_=v4(wk[:]): e.tensor_reduce(out=o, in_=i_, axis=AX.X, op=ALU.add), reads=[wk], writes=[st])
        self.ts("dve", st, st[:, 4:8], st, st[:, 4:8], 1.0 / 64, eps, ALU.mult, ALU.add)
        self.act(st, st[:, 8:12], st, st[:, 4:8], AF.Sqrt)
        self.O("dve", lambda e, o=st[:, 12:16], i_=st[:, 8:12]: e.reciprocal(out=o, in_=i_), reads=[st], writes=[st])
        self.tt("dve", y, v4(y[:]), y, v4(y[:]), st, bc(st[:, 12:16].unsqueeze(2), [128, 4, 64]), ALU.mult)
        self.tt("pool", y, y[:], y, y[:], gw, gw[:], ALU.mult)
        self.tt("dve", y, y[:], y, y[:], gb, gb[:], ALU.add)

    def phase_O1(self, l):
        di = self.di
        self.S.pe_skip = True
        with ExitStack() as es:
            sb = lambda n, sh, dt=F32: self.sb(es, n, sh, dt)
            stg = [sb(f"ostg{i}", [128, 1024]) for i in range(3)]
            WO = self.load_w_bf16(es, "oWO", di["w_out"][l], 8, D, stg)
            GL = self.load_w_bf16(es, "oGL", di["s5_glu_w"][l], 4, 512, stg)
            RGW = self.bcast_load(es, "oRGW", di["ret_gn_w"][l], 256); RGB = self.bcast_load(es, "oRGB", di["ret_gn_b"][l], 256)
            WGW = self.bcast_load(es, "oWGW", di["rwkv_gn_w"][l], 256); WGB = self.bcast_load(es, "oWGB", di["rwkv_gn_b"][l], 256)
            SD = self.bcast_load(es, "oSD", di["s5_d"][l], 512); GB = self.bcast_load(es, "oGB", di["s5_glu_b"][l], 512)
            y0 = [sb(f"oy0{i}", [128, YW]) for i in range(2)]
            y1 = [sb(f"oy1{i}", [128, YW]) for i in range(2)]
            gu = [sb(f"ogu{i}", [128, 768]) for i in range(2)]
            xt = [sb(f"oxt{i}", [128, D]) for i in range(2)]
            mix = sb("omix", [128, D]); wk = sb("owk", [128, 512]); wk2 = sb("owk2", [128, 512]); st = sb("ost", [128, 16])
            ygT = sb("oygT", [128, 4, 128], BF16); mixT = sb("omixT", [128, 8, 128], BF16)
            xo = [sb(f"oxo{i}", [128, D]) for i in range(2)]
            psT = [self.ps(es, f"opsT{i}", [128, 4, 128]) for i in range(2)]
            psG = self.ps(es, "opsG", [128, 512]); psO = [self.ps(es, f"opsO{i}", [128, 512]) for i in range(2)]
            nr, nw, ns = (self.flags.get(k) for k in ("no_ret", "no_rwkv", "no_s5"))
            self.S.barrier()
            for it, (s, i, first) in enumerate(self.tiles()):
                b = it % 2
                r0 = self.rows(s, i); p0 = self.prow(s, i)
                self.load(y0[b], y0[b][:], self.YD[0, r0:r0 + 128, :], f"l0{b}", reads=[("YDr", 0, s, i), ("YDw", 0, s, i), ("YDs", 0, s, i)])
                self.load(y1[b], y1[b][:], self.YD[1, r0:r0 + 128, :], f"l1{b}", reads=[("YDr", 1, s, i), ("YDw", 1, s, i), ("YDs", 1, s, i)])
                self.load(gu[b], gu[b][:, 0:256], self.PS[p0:p0 + 128, 1280:1536], f"l2{b}", reads=[("PS", s, i)])
                self.load(gu[b], gu[b][:, 256:768], self.PS[p0:p0 + 128, 2560:3072], f"l3{b}", reads=[("PS", s, i)])
                self.load(xt[b], xt[b][:], self.XS[r0:r0 + 128, :], f"l4{b}", reads=["XSall", ("XS", r0)])
                A, B, GU = y0[b], y1[b], gu[b]
                yr = mix
                if nr:
                    self.memset("pool", mix, mix[:, 0:256], 0.0)
                else:
                    self.tt("dve", mix, mix[:, 0:256], A, A[:, 0:256], B, B[:, 0:256], ALU.add)
                    self._ln_slice(mix, 0, 1e-5, RGW, RGB, wk, st)
                    self.act(wk2, wk2[:, 0:256], GU, GU[:, 0:256], AF.Silu)
                    self.tt("dve", mix, mix[:, 0:256], mix, mix[:, 0:256], wk2, wk2[:, 0:256], ALU.mult)
                if nw:
                    self.memset("pool", mix, mix[:, 256:512], 0.0)
                else:
                    self.tt("dve", mix, mix[:, 256:512], A, A[:, 256:512], B, B[:, 256:512], ALU.add)
                    self._ln_slice(mix, 256, 64e-5, WGW, WGB, wk, st)
                    self.tt("pool", wk2, wk2[:, 0:256], A, A[:, 512:768], B, B[:, 512:768], ALU.add)
                    self.tt("dve", mix, mix[:, 256:512], mix, mix[:, 256:512], wk2, wk2[:, 0:256], ALU.add)
                    self.tt("dve", mix, mix[:, 256:512], mix, mix[:, 256:512], A, A[:, 768:1024], ALU.mult)
                if ns:
                    self.memset("pool", mix, mix[:, 512:1024], 0.0)
                else:
                    self.tt("dve", wk, wk[:], GU, GU[:, 256:768], SD, SD[:], ALU.mult)
                    self.tt("pool", wk2, wk2[:], A, A[:, 1024:1536], B, B[:, 1024:1536], ALU.add)
                    self.tt("dve", wk, wk[:], wk, wk[:], wk2, wk2[:], ALU.add)
                    self.tt("pool", wk2, wk2[:], wk, wk[:], wk, wk[:], ALU.mult)
                    self.ts("dve", wk2, wk2[:], wk2, wk2[:], 0.044715, 1.0, ALU.mult, ALU.add)
                    self.tt("dve", wk2, wk2[:], wk2, wk2[:], wk, wk[:], ALU.mult)
                    self.act(wk2, wk2[:], wk2, wk2[:], AF.Sigmoid, scale=1.5957691216057308)
                    self.tt("dve", wk, wk[:], wk, wk[:], wk2, wk2[:], ALU.mult)
                    self.transpose_to(wk, 4, psT, ygT)
                    for k in range(4):
                        self.mm(psG, psG[:], ygT, ygT[:, k, :], GL, GL[:, k, :], start=(k == 0), stop=(k == 3))
                    self.tt("dve", wk2, wk2[:], psG, psG[:], GB, GB[:], ALU.add)
                    self.act(wk2, wk2[:], wk2, wk2[:], AF.Sigmoid)
                    self.tt("dve", mix, mix[:, 512:1024], wk, wk[:], wk2, wk2[:], ALU.mult)
                self.transpose_to(mix, 8, psT, mixT)
                for cb in range(2):
                    pp = psO[cb]
                    for k in range(8):
                        self.mm(pp, pp[:], mixT, mixT[:, k, :], WO, WO[:, k, cb * 512:(cb + 1) * 512], start=(k == 0), stop=(k == 7))
                    self.tt("dve", xo[b], xo[b][:, cb * 512:(cb + 1) * 512], xt[b], xt[b][:, cb * 512:(cb + 1) * 512], pp, pp[:], ALU.add)
                self.store(xo[b], self.XS[r0:r0 + 128, :], xo[b][:], f"s0{b}", writes=[("XS", r0)])
            self.end_phase()

    def _ln_slice(self, mix, c0, eps, gw, gb, wk, st):
        v4 = lambda ap: ap.rearrange("p (a b) -> p a b", a=4)
        y = mix[:, c0:c0 + 256]
        self.O("dve", lambda e, o=st[:, 0:4], i_=v4(y): e.tensor_reduce(out=o, in_=i_, axis=AX.X, op=ALU.add), reads=[mix], writes=[st])
        self.ts("dve", st, st[:, 0:4], st, st[:, 0:4], -1.0 / 64, None, ALU.mult)
        self.tt("dve", mix, v4(y), mix, v4(y), st, bc(st[:, 0:4].unsqueeze(2), [128, 4, 64]), ALU.add)
        self.tt("pool", wk, wk[:, 0:256], mix, y, mix, y, ALU.mult)
        self.O("dve", lambda e, o=st[:, 4:8], i_=v4(wk[:, 0:256]): e.tensor_reduce(out=o, in_=i_, axis=AX.X, op=ALU.add), reads=[wk], writes=[st])
        self.ts("dve", st, st[:, 4:8], st, st[:, 4:8], 1.0 / 64, eps, ALU.mult, ALU.add)
        self.act(st, st[:, 8:12], st, st[:, 4:8], AF.Sqrt)
        self.O("dve", lambda e, o=st[:, 12:16], i_=st[:, 8:12]: e.reciprocal(out=o, in_=i_), reads=[st], writes=[st])
        self.tt("dve", mix, v4(y), mix, v4(y), st, bc(st[:, 12:16].unsqueeze(2), [128, 4, 64]), ALU.mult)
        self.tt("pool", mix, y, mix, y, gw, gw[:], ALU.mult)
        self.tt("dve", mix, y, mix, y, gb, gb[:], ALU.add)

    def phase_O2(self, l, last):
        di = self.di
        self.S.pe_skip = True
        finals = []
        with ExitStack() as es:
            sb = lambda n, sh, dt=F32: self.sb(es, n, sh, dt)
            stg = [sb(f"fstg{i}", [128, 512]) for i in range(2)]
            WG = self.load_w_bf16(es, "fWG", di["ffn_w_gate"][l], 8, DFF, stg)
            WU = self.load_w_bf16(es, "fWU", di["ffn_w_up"][l], 8, DFF, stg)
            WD = self.load_w_bf16(es, "fWD", di["ffn_w_down"][l], 22, D, stg)
            G2 = self.bcast_load(es, "fG2", di["norm2_g"][l], D)
            GF = self.bcast_load(es, "fGF", di["final_g"], D) if last else None
            xb = [sb(f"fxb{i}", [128, D]) for i in range(2)]
            junk = sb("fjunk", [128, D], BF16); ss = [sb(f"fss{i}", [128, 4]) for i in range(2)]
            junk2 = sb("fjunk2", [128, D], BF16) if last else None
            ss2 = sb("fss2", [128, 4]) if last else None
            h = [sb(f"fh{i}", [128, D]) for i in range(2)]
            hT = [sb(f"fhT{i}", [128, 8, 128], BF16) for i in range(2)]
            sg = [sb(f"fsg{i}", [128, 512]) for i in range(2)]
            atok = sb("fatok", [128, DFF], BF16)
            aT = sb("faT", [128, 22, 128], BF16)
            xo = sb("fxo0", [128, D])
            yo = sb("fyo0", [128, D]) if last else None
            psT = [self.ps(es, "fpsT0", [128, 4, 128])]
            psG = [self.ps(es, f"fpsG{i}", [128, 512]) for i in range(2)]
            psU = [self.ps(es, f"fpsU{i}", [128, 512]) for i in range(2)]
            psTb = self.ps(es, "fpsTb", [128, 8, 128], BF16)
            psO = [self.ps(es, f"fpsO{i}", [128, 512]) for i in range(2)]
            self.S.barrier()

            def s0(j, s, i, first):
                b = j % 2
                r0 = self.rows(s, i)
                self.load(xb[b], xb[b][:], self.XS[r0:r0 + 128, :], f"l0{b}", reads=["XSall", ("XS", r0)])
                self.rmsnorm(xb[b], G2, h[b], junk, ss[b])
                yield
                self.transpose_to(h[b], 8, psT, hT[b])
                yield

            def s1(j, s, i, first):
                b = j % 2
                r0 = self.rows(s, i)
                HT = hT[b]
                for gi, c0 in enumerate(range(0, DFF, 512)):
                    c1 = min(DFF, c0 + 512)
                    w = c1 - c0
                    pg, pu, sgg = psG[gi % 2], psU[gi % 2], sg[gi % 2]
                    for k in range(8):
                        self.mm(pg, pg[:, 0:w], HT, HT[:, k, :], WG, WG[:, k, c0:c1], start=(k == 0), stop=(k == 7))
                    for k in range(8):
                        self.mm(pu, pu[:, 0:w], HT, HT[:, k, :], WU, WU[:, k, c0:c1], start=(k == 0), stop=(k == 7))
                    self.act(sgg, sgg[:, 0:w], pg, pg[:, 0:w], AF.Silu)
                    self.tt("dve", atok, atok[:, c0:c1], sgg, sgg[:, 0:w], pu, pu[:, 0:w], ALU.mult)
                    if gi % 2 == 1:
                        yield
                for g0 in range(0, 22, 8):
                    g1 = min(22, g0 + 8)
                    for f in range(g0, g1):
                        self.tr(psTb, psTb[:, f - g0, :], atok, atok[:, f * 128:(f + 1) * 128])
                    self.cp(("act", "dve", "act")[g0 // 8], aT, aT[:, g0:g1, :], psTb, psTb[:, 0:g1 - g0, :])
                yield
                for cb in range(2):
                    pp = psO[cb]
                    for k in range(22):
                        self.mm(pp, pp[:], aT, aT[:, k, :], WD, WD[:, k, cb * 512:(cb + 1) * 512], start=(k == 0), stop=(k == 21))
                    self.tt("dve", xo, xo[:, cb * 512:(cb + 1) * 512], xb[b], xb[b][:, cb * 512:(cb + 1) * 512], pp, pp[:], ALU.add)
                yield
                if not last:
                    self.store(xo, self.XS[r0:r0 + 128, :], xo[:], "s00", writes=[("XS", r0)])
                else:
                    self.rmsnorm(xo, GF, yo, junk2, ss2)
                    st_ = self.store(yo, self.yout[s][i * 128:(i + 1) * 128, :], yo[:], "s00")
                    finals.append(st_)
                yield
            self.pipeline(self.tiles(), [s0, s1])
            self.end_phase()
        return finals


def _consts(Tmax):
    c = {}
    c["c_ident"] = np.eye(128, dtype=np.float32)
    inv = (1.0 / (10000.0 ** np.linspace(0.0, 1.0, 32, dtype=np.float32))).astype(np.float32)
    ang = np.arange(Tmax, dtype=np.float32)[:, None] * inv[None, :]
    cos, sin = np.cos(ang).astype(np.float32), np.sin(ang).astype(np.float32)
    rot = np.concatenate([0.125 * cos, 0.125 * cos, -0.125 * sin, 0.125 * sin, cos, cos, -sin, sin], axis=1)
    c["c_rot"] = np.ascontiguousarray(rot.astype(np.float32))
    lg = np.log(1.0 - 2.0 ** (-5.0 - np.arange(4, dtype=np.float64)))
    i = np.arange(128, dtype=np.float64)
    diff = np.abs(i[:, None] - i[None, :])
    c["c_retmask"] = np.concatenate([np.exp(diff * lg[h]) for h in range(4)], axis=1).astype(np.float32)
    qk = np.zeros((2, 128, 8), np.float32)
    for h in range(4):
        qk[0, :, h] = np.exp((i + 1) * lg[h]); qk[0, :, 4 + h] = np.exp((127 - i) * lg[h])
        qk[1, :, h] = np.exp((128 - i) * lg[h]); qk[1, :, 4 + h] = np.exp(i * lg[h])
    c["c_retqk"] = qk
    gc = np.zeros((128, 256), np.float32)
    for r in range(128):
        for pr in range(2):
            gc[r, pr * 128:(pr + 1) * 128] = np.exp(128 * lg[pr * 2 + r // 64])
    c["c_retgc"] = gc
    s_, t_ = np.meshgrid(np.arange(128), np.arange(128), indexing="ij")
    c["c_cs"] = np.stack([(s_ <= t_), (s_ >= t_)]).astype(np.float32)
    c["c_ones"] = np.ones((128, 128), np.float32)
    c["c_msi"] = np.stack([np.concatenate([(s_ < t_), (s_ <= t_)], axis=1),
                           np.concatenate([(s_ > t_), (s_ >= t_)], axis=1)]).astype(np.float32)
    c["c_mst"] = np.stack([(s_ > t_), (s_ < t_)]).astype(np.float32)
    j = np.arange(128, dtype=np.float32)
    c["c_j1"] = np.stack([np.tile(j + 1, (128, 1)), np.tile(128 - j, (128, 1))]).astype(np.float32)
    z0 = np.ones((2, 128, 128), np.float32); z0[0, :, 0] = 0; z0[1, :, 127] = 0
    c["c_z0"] = z0
    sw = np.zeros((128, 128), np.float32)
    for p in range(64):
        sw[64 + p, p] = -1.0; sw[p, 64 + p] = 1.0
    c["c_swap"] = sw
    sel = np.zeros((128, 128), np.float32)
    for go in range(2):
        for p in range(64):
            sel[go * 64 + p, go * 64 + p] = 1.0
    c["c_sel"] = sel
    return c


def _win_perm():
    idx = []
    def sw(base):
        out = []
        for h in range(4):
            out += list(range(base + h * 64 + 32, base + h * 64 + 64)) + list(range(base + h * 64, base + h * 64 + 32))
        return out
    idx += list(range(0, 256)) + sw(0) + list(range(256, 512)) + sw(256) + list(range(512, 768)) + list(range(768, 1024))
    idx += list(range(1024, 2560))
    return np.array(idx)


_CACHE = {}


def run(inputs, Tp, Ts, L, ncores, flags=None):
    key = (Tp, Ts, L, ncores, tuple(sorted((flags or {}).items())))
    if key not in _CACHE:
        _CACHE[key] = Builder(Tp, Ts, L, flags).build()
    nc = _CACHE[key]
    f = lambda a: np.ascontiguousarray(np.asarray(a, dtype=np.float32))
    shared = {k: f(v) for k, v in inputs.items() if k not in ("x_prompt", "x_sample", "w_in")}
    shared["w_in"] = np.ascontiguousarray(f(inputs["w_in"])[:, :, _win_perm()])
    shared.update(_consts(max(Tp, Ts)))
    xp, xs = f(inputs["x_prompt"]), f(inputs["x_sample"])
    nb_s = xs.shape[0]
    in_maps = []
    for c in range(ncores):
        m = dict(shared)
        m["x_prompt"] = xp[c % xp.shape[0]]
        m["x_sample"] = xs[c % nb_s]
        in_maps.append(m)
    res = run_bass_kernel_spmd(nc, in_maps, core_ids=list(range(ncores)))
    yp = np.stack([res.results[c]["y_prompt"] for c in range(xp.shape[0])])
    ys = np.stack([res.results[c]["y_sample"] for c in range(nb_s)])
    return yp.astype(np.float32), ys.astype(np.float32)


def kernel(**inputs):
    return run(inputs, 8192, 4096, 4, 8)
```
